# Optimizing a Trainium2 kernel written in Bass

```python
import jax, jax.numpy as jnp
from jax import lax
import numpy as np

D_MODEL = 1024
BATCH = 8
SEQ = 2048
DEPTH = 1
DEC_BATCH = 128
DEC_SEQ = 4
PAST_LEN = 16384
PAGE_SIZE = 128

N_HEADS_A = 8
N_KV_A = 2
Q_PER_KV = N_HEADS_A // N_KV_A
HEAD_DIM_A = 64
WINDOW = 128
ATTN_WIDTH = N_HEADS_A * HEAD_DIM_A
KV_WIDTH = N_KV_A * HEAD_DIM_A
D_INNER = D_MODEL
HEAD_DIM_M = 64
N_HEADS_M = D_INNER // HEAD_DIM_M
N_GROUPS_M = 2
HEADS_PER_GROUP = N_HEADS_M // N_GROUPS_M
D_STATE = 128
CONV_W = 4
CONV_DIM = D_INNER + 2 * N_GROUPS_M * D_STATE
CHUNK = 128
D_FF = 4 * D_MODEL
EPS = 1e-6
IN_WIDTH = ATTN_WIDTH + 2 * KV_WIDTH + D_INNER + CONV_DIM + N_HEADS_M + 2 * D_MODEL

kernel_name = "hybrid_swa_sink_mamba2_gated_step"


def _rmsnorm(x, w):
    xf = x.astype(jnp.float32)
    y = xf * lax.rsqrt(jnp.mean(xf * xf, axis=-1, keepdims=True) + EPS)
    return (y * w.astype(jnp.float32)).astype(x.dtype)


def _split(u):
    sizes = (ATTN_WIDTH, KV_WIDTH, KV_WIDTH, D_INNER, CONV_DIM, N_HEADS_M, D_MODEL, D_MODEL)
    out = []
    off = 0
    for s in sizes:
        out.append(u[..., off:off + s])
        off += s
    return out


def _sink_attend(q, k, v, mask, sinks):
    scale = HEAD_DIM_A ** -0.5
    s = jnp.einsum("bnqkgd,bnskd->bnkgqs", q, k).astype(jnp.float32) * scale
    s = jnp.where(mask[None, :, None, None], s, -jnp.inf)
    sk = jnp.broadcast_to(sinks.astype(jnp.float32).reshape(N_KV_A, Q_PER_KV, 1, 1), s.shape[:-1] + (1,))
    p = jax.nn.softmax(jnp.concatenate([s, sk], axis=-1), axis=-1)[..., :-1]
    return jnp.einsum("bnkgqs,bnskd->bnqkgd", p.astype(v.dtype), v)


def _swa_prompt(q, k, v, sinks):
    b, L = q.shape[:2]
    nb = L // WINDOW
    qb = q.reshape(b, nb, WINDOW, N_KV_A, Q_PER_KV, HEAD_DIM_A)
    kb = k.reshape(b, nb, WINDOW, N_KV_A, HEAD_DIM_A)
    vb = v.reshape(b, nb, WINDOW, N_KV_A, HEAD_DIM_A)
    padw = ((0, 0), (1, 0), (0, 0), (0, 0), (0, 0))
    kk = jnp.concatenate([jnp.pad(kb, padw)[:, :-1], kb], axis=2)
    vv = jnp.concatenate([jnp.pad(vb, padw)[:, :-1], vb], axis=2)
    qpos = jnp.arange(nb)[:, None] * WINDOW + jnp.arange(WINDOW)[None]
    kpos = (jnp.arange(nb)[:, None] - 1) * WINDOW + jnp.arange(2 * WINDOW)[None]
    d = qpos[:, :, None] - kpos[:, None, :]
    mask = (d >= 0) & (d <= WINDOW) & (kpos[:, None, :] >= 0)
    o = _sink_attend(qb, kk, vv, mask, sinks)
    return o.reshape(b, L, ATTN_WIDTH), k[:, -WINDOW:], v[:, -WINDOW:]


def _swa_sample(q, k, v, k_buf, v_buf, sinks):
    b, L = q.shape[:2]
    kk = jnp.concatenate([k_buf, k], axis=1)
    vv = jnp.concatenate([v_buf, v], axis=1)
    qpos = WINDOW + jnp.arange(L)
    kpos = jnp.arange(WINDOW + L)
    d = qpos[:, None] - kpos[None, :]
    mask = ((d >= 0) & (d <= WINDOW))[None]
    o = _sink_attend(q[:, None], kk[:, None], vv[:, None], mask, sinks)
    return o.reshape(b, L, ATTN_WIDTH), kk[:, -WINDOW:], vv[:, -WINDOW:]


def _causal_conv(xbc, prev, w, bias):
    L = xbc.shape[1]
    xp = jnp.concatenate([prev, xbc], axis=1)
    y = xp[:, 0:L] * w[0]
    for i in range(1, CONV_W):
        y = y + xp[:, i:i + L] * w[i]
    return jax.nn.silu(y + bias), xp[:, -(CONV_W - 1):]


def _segsum_exp(a_cs):
    T = a_cs.shape[-1]
    diff = a_cs[..., :, None] - a_cs[..., None, :]
    mask = jnp.tril(jnp.ones((T, T), dtype=bool))
    return jnp.exp(jnp.where(mask, diff, -jnp.inf))


def _ssd(xh, dt, a, bmat, cmat, h0):
    b, L = xh.shape[:2]
    T = CHUNK if L >= CHUNK else L
    Lp = -(-L // T) * T
    pad = Lp - L
    xf = xh.astype(jnp.float32)
    bf = bmat.astype(jnp.float32)
    cf = cmat.astype(jnp.float32)
    if pad:
        xf = jnp.pad(xf, ((0, 0), (0, pad), (0, 0), (0, 0)))
        dt = jnp.pad(dt, ((0, 0), (0, pad), (0, 0)))
        bf = jnp.pad(bf, ((0, 0), (0, pad), (0, 0), (0, 0)))
        cf = jnp.pad(cf, ((0, 0), (0, pad), (0, 0), (0, 0)))
    nc = Lp // T
    G, HG = N_GROUPS_M, HEADS_PER_GROUP
    x = (xf * dt[..., None]).reshape(b, nc, T, G, HG, HEAD_DIM_M)
    a_cs = jnp.cumsum((dt * a).reshape(b, nc, T, G, HG).transpose(0, 1, 3, 4, 2), axis=-1)
    Bc = bf.reshape(b, nc, T, G, D_STATE)
    Cc = cf.reshape(b, nc, T, G, D_STATE)
    cb = jnp.einsum("bclgn,bcsgn->bcgls", Cc, Bc)
    wmat = cb[:, :, :, None] * _segsum_exp(a_cs)
    y_diag = jnp.einsum("bcghls,bcsghp->bclghp", wmat, x)
    decay = jnp.exp(a_cs[..., -1:] - a_cs)
    st = jnp.einsum("bclgn,bcghl,bclghp->bcghpn", Bc, decay, x)
    chunk_decay = jnp.exp(a_cs[..., -1])

    def step(h, inp):
        s_c, d_c = inp
        return h * d_c[..., None, None] + s_c, h

    h_init = h0.astype(jnp.float32).reshape(b, G, HG, HEAD_DIM_M, D_STATE)
    h_last, h_in = lax.scan(step, h_init, (st.transpose(1, 0, 2, 3, 4, 5), chunk_decay.transpose(1, 0, 2, 3)))
    h_in = h_in.transpose(1, 0, 2, 3, 4, 5)
    y_off = jnp.einsum("bclgn,bcghpn,bcghl->bclghp", Cc, h_in, jnp.exp(a_cs))
    y = (y_diag + y_off).reshape(b, Lp, N_HEADS_M, HEAD_DIM_M)[:, :L]
    return y.astype(xh.dtype), h_last.reshape(b, N_HEADS_M, HEAD_DIM_M, D_STATE).astype(h0.dtype)


def _mamba_branch(z, xbc, dt_raw, conv_prev, ssm_prev, conv_w, conv_b, dt_bias, a_log, d_skip, ssm_norm):
    b, L = z.shape[:2]
    xbc, conv_new = _causal_conv(xbc, conv_prev, conv_w, conv_b)
    xs = xbc[..., :D_INNER]
    bm = xbc[..., D_INNER:D_INNER + N_GROUPS_M * D_STATE].reshape(b, L, N_GROUPS_M, D_STATE)
    cm = xbc[..., D_INNER + N_GROUPS_M * D_STATE:].reshape(b, L, N_GROUPS_M, D_STATE)
    xh = xs.reshape(b, L, N_HEADS_M, HEAD_DIM_M)
    dt = jax.nn.softplus(dt_raw.astype(jnp.float32) + dt_bias.astype(jnp.float32))
    a = -jnp.exp(a_log.astype(jnp.float32))
    y, ssm_new = _ssd(xh, dt, a, bm, cm, ssm_prev)
    y = (y + xh * d_skip[:, None]).reshape(b, L, D_INNER) * jax.nn.silu(z)
    yg = y.astype(jnp.float32).reshape(b, L, N_GROUPS_M, D_INNER // N_GROUPS_M)
    yg = yg * lax.rsqrt(jnp.mean(yg * yg, axis=-1, keepdims=True) + EPS)
    y = (yg.reshape(b, L, D_INNER) * ssm_norm.astype(jnp.float32)).astype(z.dtype)
    return y, conv_new, ssm_new


def _block(x, k_buf, v_buf, conv_prev, ssm_prev, norm1, w_in, sinks, conv_w, conv_b, dt_bias, a_log,
           d_skip, ssm_norm, w_oa, w_ob, w_o, norm2, w_up, w_down):
    b, L, _ = x.shape
    h = _rmsnorm(x, norm1)
    q, k, v, z, xbc, dt_raw, g_a, g_b = _split(h @ w_in)
    q = q.reshape(b, L, N_KV_A, Q_PER_KV, HEAD_DIM_A)
    k = k.reshape(b, L, N_KV_A, HEAD_DIM_A)
    v = v.reshape(b, L, N_KV_A, HEAD_DIM_A)
    if k_buf is None:
        a_out, k_new, v_new = _swa_prompt(q, k, v, sinks)
    else:
        a_out, k_new, v_new = _swa_sample(q, k, v, k_buf, v_buf, sinks)
    m_out, conv_new, ssm_new = _mamba_branch(z, xbc, dt_raw, conv_prev, ssm_prev, conv_w, conv_b,
                                             dt_bias, a_log, d_skip, ssm_norm)
    merged = jax.nn.sigmoid(g_a) * (a_out @ w_oa) + jax.nn.sigmoid(g_b) * (m_out @ w_ob)
    x = x + merged @ w_o
    hm = _rmsnorm(x, norm2)
    x = x + jnp.square(jax.nn.relu(hm @ w_up)) @ w_down
    return x, k_new, v_new, conv_new, ssm_new


def setup_inputs(seed: int = 0) -> dict:
    key = jax.random.key(seed)
    ks = jax.random.split(key, 24)
    f32 = jnp.float32
    nrm = lambda k, shape, s: jax.random.normal(k, shape, f32) * s
    dt0 = jnp.exp(jax.random.uniform(ks[10], (DEPTH, N_HEADS_M), f32, np.log(1e-3), np.log(1e-1)))
    return {
        "x_prompt": nrm(ks[0], (BATCH, SEQ, D_MODEL), 1.0),
        "x_sample": nrm(ks[1], (DEC_BATCH, DEC_SEQ, D_MODEL), 1.0),
        "cache_swa_k": nrm(ks[2], (DEPTH, DEC_BATCH, WINDOW, N_KV_A, HEAD_DIM_A), 1.0),
        "cache_swa_v": nrm(ks[3], (DEPTH, DEC_BATCH, WINDOW, N_KV_A, HEAD_DIM_A), 1.0),
        "state_conv": nrm(ks[4], (DEPTH, DEC_BATCH, CONV_W - 1, CONV_DIM), 1.0),
        "state_ssm": nrm(ks[5], (DEPTH, DEC_BATCH, N_HEADS_M, HEAD_DIM_M, D_STATE), 0.1),
        "norm1": 1.0 + nrm(ks[6], (DEPTH, D_MODEL), 0.05),
        "w_in": nrm(ks[7], (DEPTH, D_MODEL, IN_WIDTH), D_MODEL ** -0.5),
        "sinks": nrm(ks[8], (DEPTH, N_HEADS_A), 0.5),
        "conv_w": nrm(ks[9], (DEPTH, CONV_W, CONV_DIM), CONV_W ** -0.5),
        "conv_b": nrm(ks[11], (DEPTH, CONV_DIM), 0.01),
        "dt_bias": dt0 + jnp.log(-jnp.expm1(-dt0)),
        "a_log": jnp.log(jax.random.uniform(ks[12], (DEPTH, N_HEADS_M), f32, 1.0, 16.0)),
        "d_skip": 1.0 + nrm(ks[13], (DEPTH, N_HEADS_M), 0.1),
        "ssm_norm": 1.0 + nrm(ks[14], (DEPTH, D_INNER), 0.05),
        "w_oa": nrm(ks[15], (DEPTH, ATTN_WIDTH, D_MODEL), ATTN_WIDTH ** -0.5),
        "w_ob": nrm(ks[16], (DEPTH, D_INNER, D_MODEL), D_INNER ** -0.5),
        "w_o": nrm(ks[17], (DEPTH, D_MODEL, D_MODEL), D_MODEL ** -0.5),
        "norm2": 1.0 + nrm(ks[18], (DEPTH, D_MODEL), 0.05),
        "w_up": nrm(ks[19], (DEPTH, D_MODEL, D_FF), D_MODEL ** -0.5),
        "w_down": nrm(ks[20], (DEPTH, D_FF, D_MODEL), D_FF ** -0.5),
        "final_norm": 1.0 + nrm(ks[21], (D_MODEL,), 0.05),
    }


def reference(x_prompt, x_sample, cache_swa_k, cache_swa_v, state_conv, state_ssm, norm1, w_in, sinks,
              conv_w, conv_b, dt_bias, a_log, d_skip, ssm_norm, w_oa, w_ob, w_o, norm2, w_up, w_down,
              final_norm):
    xp, xs = x_prompt, x_sample
    pk, pv, pc, ps = [], [], [], []
    sk, sv, sc, ss = [], [], [], []
    bp = x_prompt.shape[0]
    for l in range(DEPTH):
        w = (norm1[l], w_in[l], sinks[l], conv_w[l], conv_b[l], dt_bias[l], a_log[l], d_skip[l],
             ssm_norm[l], w_oa[l], w_ob[l], w_o[l], norm2[l], w_up[l], w_down[l])
        conv0 = jnp.zeros((bp, CONV_W - 1, CONV_DIM), xp.dtype)
        ssm0 = jnp.zeros((bp, N_HEADS_M, HEAD_DIM_M, D_STATE), state_ssm.dtype)
        xp, k1, v1, c1, s1 = _block(xp, None, None, conv0, ssm0, *w)
        xs, k2, v2, c2, s2 = _block(xs, cache_swa_k[l], cache_swa_v[l], state_conv[l], state_ssm[l], *w)
        pk.append(k1); pv.append(v1); pc.append(c1); ps.append(s1)
        sk.append(k2); sv.append(v2); sc.append(c2); ss.append(s2)
    y_prompt = _rmsnorm(xp, final_norm)
    y_sample = _rmsnorm(xs, final_norm)
    return (y_prompt, y_sample, jnp.stack(pk), jnp.stack(pv), jnp.stack(pc), jnp.stack(ps),
            jnp.stack(sk), jnp.stack(sv), jnp.stack(sc), jnp.stack(ss))
```

```python
import numpy as np
import ml_dtypes
from contextlib import ExitStack
import concourse.bass as bass
import concourse.mybir as mybir
from concourse.bass_utils import run_bass_kernel_spmd

F32 = mybir.dt.float32
BF16 = mybir.dt.bfloat16
AF = mybir.ActivationFunctionType
ALU = mybir.AluOpType

NCORES = 8
D = 1024
NT = 2112
NTL = 17
SC0 = 2048
EPS = 1e-6
INW = 5392
ARENA_WORDS = 53000

C_IDF, C_UI, C_SL, C_ONES, C_UIS, C_SLS, C_SEQIND, C_SELBLK = 0, 128, 256, 384, 512, 576, 640, 656
C_HV, C_CONVW, C_CONVB, C_SINKS, NF = 1680, 1728, 1776, 1788, 1792
B_IDB, B_MP, B_MC, B_MCACHE, B_MNEW, B_UIB, B_UISB, B_ONESPAD, B_SEQM, B_SEQINDB, NB = \
    0, 128, 256, 384, 388, 452, 580, 644, 900, 1924, 1940


class DSem:
    def __init__(self, sem, name):
        self.sem = sem
        self.count = 0
        self.name = name
        self.open = None

    def add(self):
        self.count += 16
        if self.open is None:
            self.open = {"final": None}
        return self.open

    def seal(self):
        if self.open is not None:
            self.open["final"] = self.count
            self.open = None


class Prog:
    ENG_NAMES = ("pe", "act", "dve", "pool", "sp")

    def __init__(self, nc, stack):
        self.nc = nc
        self.stack = stack
        self.ops = []
        self.last_w = {}
        self.readers = {}
        self.esem = {e: stack.enter_context(nc.semaphore("s_" + e)) for e in ("pe", "act", "dve", "pool")}
        self.dsems = []
        self.last_on_eng = {e: None for e in self.ENG_NAMES}

    def dsem(self, name):
        d = DSem(self.stack.enter_context(self.nc.semaphore("d_" + name)), name)
        self.dsems.append(d)
        return d

    def _deps(self, r, w, eng=None):
        deps = set()
        for t in r:
            if t in self.last_w:
                deps.add(self.last_w[t])
            if isinstance(t, tuple) and t[0] == "B":
                for x in self.readers.get(t, ()):
                    if self.ops[x]["eng"] != eng:
                        deps.add(x)
        for t in w:
            if t in self.last_w:
                deps.add(self.last_w[t])
            for x in self.readers.get(t, ()):
                deps.add(x)
        for d in deps:
            p = self.ops[d]
            if p["kind"] == "dma" and p["grp"]["final"] is None:
                p["dsem"].seal()
        return deps

    def _commit(self, idx, r, w):
        for t in r:
            self.readers.setdefault(t, []).append(idx)
        for t in w:
            self.last_w[t] = idx
            self.readers[t] = []

    def op(self, eng, fn, r=(), w=()):
        idx = len(self.ops)
        deps = self._deps(r, w, eng)
        self.ops.append(dict(eng=eng, fn=fn, deps=deps, kind="c", inc=False, cnt=0))
        self._commit(idx, r, w)
        self.last_on_eng[eng] = idx
        return idx

    def dma(self, eng, fn, r=(), w=(), dsem=None):
        idx = len(self.ops)
        deps = self._deps(r, w, eng)
        grp = dsem.add()
        self.ops.append(dict(eng=eng, fn=fn, deps=deps, kind="dma", dsem=dsem, grp=grp))
        self._commit(idx, r, w)
        return idx

    def barrier(self):
        deps = set()
        for e in ("pe", "act", "dve", "pool"):
            if self.last_on_eng[e] is not None:
                deps.add(self.last_on_eng[e])
        for d in self.dsems:
            d.seal()
        dvals = [(d, d.count) for d in self.dsems if d.count > 0]
        for e in self.ENG_NAMES:
            self.ops.append(dict(eng=e, fn=None, deps=set(deps), kind="bar", dvals=dvals, inc=False, cnt=0))

    def finalize(self):
        ops = self.ops
        for d in self.dsems:
            d.seal()
        for o in ops:
            for d in o["deps"]:
                p = ops[d]
                if p["kind"] == "c":
                    if p["eng"] == "pe" and o["eng"] == "pe" and o["kind"] == "c":
                        continue
                    p["inc"] = True
        cnt = {e: 0 for e in self.ENG_NAMES}
        for o in ops:
            if o["kind"] == "c":
                if o["inc"]:
                    cnt[o["eng"]] += 1
                o["cnt"] = cnt[o["eng"]]

    def emit_engine(self, ename, engine):
        ops = self.ops
        waited = {}

        def wait(sem, key, val):
            if waited.get(key, 0) < val:
                engine.wait_ge(sem, val)
                waited[key] = val

        for o in ops:
            if o["eng"] != ename:
                continue
            for d in sorted(o["deps"]):
                p = ops[d]
                if p["kind"] == "c":
                    if p["eng"] == "pe" and ename == "pe" and o["kind"] == "c":
                        continue
                    wait(self.esem[p["eng"]], p["eng"], p["cnt"])
                elif p["kind"] == "dma":
                    wait(p["dsem"].sem, "d_" + p["dsem"].name, p["grp"]["final"])
            if o["kind"] == "bar":
                for d, v in o["dvals"]:
                    wait(d.sem, "d_" + d.name, v)
                continue
            ins = o["fn"](engine)
            if o["kind"] == "dma":
                ins.then_inc(o["dsem"].sem, 16)
            elif o["inc"]:
                ins.then_inc(self.esem[ename], 1)

    def emit(self):
        self.finalize()
        nc = self.nc
        with nc.Block() as block:
            @block.sync
            def _(e):
                self.emit_engine("sp", e)

            @block.tensor
            def _(e):
                self.emit_engine("pe", e)

            @block.scalar
            def _(e):
                self.emit_engine("act", e)

            @block.vector
            def _(e):
                self.emit_engine("dve", e)

            @block.gpsimd
            def _(e):
                self.emit_engine("pool", e)


class Bump:
    def __init__(self, arena, abf, regions):
        self.arena = arena
        self.abf = abf
        self.regs = [list(r) for r in regions]

    def _take(self, w):
        w = (w + 7) // 8 * 8
        for r in self.regs:
            if r[1] - r[0] >= w:
                off = r[0]
                r[0] += w
                return off
        raise RuntimeError("arena region overflow: need %d words, regions %s" % (w, self.regs))

    def f32(self, n):
        off = self._take(n)
        return self.arena[:, off:off + n]

    def bf(self, n):
        off = self._take((n + 1) // 2)
        return self.abf[:, 2 * off:2 * off + n]

    def take(self, w):
        return self._take(w)

    def fview(self, off, n):
        return self.arena[:, off:off + n]

    def bview(self, off, n):
        return self.abf[:, 2 * off:2 * off + n]

    def free_words(self):
        return sum(r[1] - r[0] for r in self.regs)


class StopBuild(Exception):
    pass


def rows_of(t):
    return 128 if t < 16 else 64


def build_program(stop_after=None, dbg=()):
    nc = bass.Bass("TRN2", target_bir_lowering=False)

    def din(name, shape, dt=F32):
        return nc.dram_tensor(name, list(shape), dt, kind="ExternalInput").ap()

    def dout(name, shape, dt=F32):
        return nc.dram_tensor(name, list(shape), dt, kind="ExternalOutput").ap()

    xin = din("xin", [NT, D])
    ck = din("ck", [16, 128, 128])
    cv = din("cv", [16, 128, 128])
    sconv = din("sconv", [16, 3, 1536])
    sssm = din("sssm", [16, 1024, 128])
    w_in = din("w_in", [D, INW])
    w_oa = din("w_oa", [512, D])
    w_ob = din("w_ob", [D, D])
    w_o = din("w_o", [D, D])
    w_up = din("w_up", [D, 4096])
    w_down = din("w_down", [4096, D])
    vecs = din("vecs", [128, 4, D])
    cstf = din("cstf", [128, NF])
    cstb = din("cstb", [128, NB], BF16)

    yout = dout("yout", [NT, D])
    nkp = dout("nkp", [128, 128])
    nvp = dout("nvp", [128, 128])
    ncp = dout("ncp", [3, 1536])
    nsp = dout("nsp", [1024, 128])
    nks = dout("nks", [16, 128, 128])
    nvs = dout("nvs", [16, 128, 128])
    ncs = dout("ncs", [16, 3, 1536])
    nss = dout("nss", [16, 1024, 128])

    dbg_outs = {}

    with ExitStack() as st:
        st.enter_context(nc.allow_low_precision("bf16 matmul operands, fp32 accumulation"))
        st.enter_context(nc.allow_non_contiguous_dma("strided weight / state layouts"))
        arena = st.enter_context(nc.sbuf_tensor("arena", [128, ARENA_WORDS], F32))
        abf = arena.bitcast(BF16)
        banks = [st.enter_context(nc.psum_tensor("bank%d" % i, [128, 512], F32)) for i in range(8)]
        bankb = [b.bitcast(BF16) for b in banks]
        P = Prog(nc, st)

        def MM(out, lhsT, rhs, start=True, stop=True, r=(), w=()):
            P.op("pe", lambda e: e.matmul(out, lhsT=lhsT, rhs=rhs, start=start, stop=stop), r, w)

        def TR(out, in_, ident, r=(), w=()):
            P.op("pe", lambda e: e.transpose(out=out, in_=in_, identity=ident), r, w)

        def ACT(out, in_, func, r=(), w=(), bias=None, scale=1.0, accum_out=None):
            def f(e):
                kw = dict(scale=scale)
                if bias is not None:
                    kw["bias"] = bias
                if accum_out is not None:
                    kw["accum_out"] = accum_out
                return e.activation(out=out, in_=in_, func=func, **kw)
            P.op("act", f, r, w)

        def CP(eng, out, in_, r=(), w=()):
            if eng == "act":
                P.op("act", lambda e: e.copy(out=out, in_=in_), r, w)
            else:
                P.op(eng, lambda e: e.tensor_copy(out=out, in_=in_), r, w)

        def TT(eng, out, in0, in1, op, r=(), w=()):
            P.op(eng, lambda e: e.tensor_tensor(out=out, in0=in0, in1=in1, op=op), r, w)

        def TS(eng, out, in0, s1, s2, op0, op1=None, r=(), w=()):
            if op1 is None:
                P.op(eng, lambda e: e.tensor_scalar(out=out, in0=in0, scalar1=s1, scalar2=None, op0=op0), r, w)
            else:
                P.op(eng, lambda e: e.tensor_scalar(out=out, in0=in0, scalar1=s1, scalar2=s2, op0=op0, op1=op1), r, w)

        def STT(eng, out, in0, scalar, in1, op0, op1, r=(), w=()):
            P.op(eng, lambda e: e.scalar_tensor_tensor(out=out, in0=in0, scalar=scalar, in1=in1, op0=op0, op1=op1), r, w)

        def MEMSET(eng, ap, val, w=()):
            P.op(eng, lambda e: e.memset(ap, val), (), w)

        def DMA(eng, out, in_, dsem, r=(), w=()):
            P.dma(eng, lambda e: e.dma_start(out=out, in_=in_), r, w, dsem)

        def BK(i):
            return ("B", i)

        def CHECK(tag):
            if stop_after == tag:
                raise StopBuild()

        d_out = P.dsem("out")
        d_dbg = P.dsem("dbg")

        def dump(name, ap, tokens, dt=F32):
            if name not in dbg:
                return
            shp = list(ap.shape)
            o = dout("dbg_" + name, shp, dt)
            dbg_outs[name] = shp
            DMA("sp", o, ap, d_dbg, r=tokens)

        cf = arena[:, 0:NF]
        cbw = (NB + 1) // 2
        cb = abf[:, 2 * NF:2 * NF + NB]
        o_small = NF + cbw + 4
        small = arena[:, o_small:o_small + 128]
        O_H = 2944
        assert o_small + 128 <= O_H
        O_X = O_H + 8448
        O_G = O_X + 17408
        O_T = O_G + 8448
        hT = abf[:, 2 * O_H:2 * O_H + 8 * NT].rearrange("p (a b) -> p a b", a=8)
        aoT = abf[:, 2 * O_X:2 * O_X + 4 * NT].rearrange("p (a b) -> p a b", a=4)
        moT = abf[:, 2 * (O_X + 4224):2 * (O_X + 4224) + 8 * NT].rearrange("p (a b) -> p a b", a=8)
        mgT = abf[:, 2 * O_G:2 * O_G + 8 * NT].rearrange("p (a b) -> p a b", a=8)
        x1 = arena[:, O_X:O_X + NTL * D].rearrange("p (a b) -> p a b", a=NTL)
        X_SPARE = (O_X + 12672, O_X + 17408)
        R_G = (O_G, O_G + 8448)
        R_T = (O_T, ARENA_WORDS)
        R_H = (O_H, O_H + 8448)

        idf = cf[:, C_IDF:C_IDF + 128]
        idb = cb[:, B_IDB:B_IDB + 128]

        d_c = P.dsem("const")
        DMA("sp", cf, cstf, d_c, w=["cf"])
        DMA("sp", cb, cstb, d_c, w=["cb"])
        CONST = ["cf", "cb"]

        sinkexp = small[:, 0:4]
        a_neg = small[:, 8:24]
        neghalf = small[:, 24:25]
        ACT(sinkexp, cf[:, C_SINKS:C_SINKS + 4], AF.Exp, r=CONST, w=["sinkexp"])
        ACT(a_neg, cf[:, C_HV + 16:C_HV + 32], AF.Exp, r=CONST, w=["a_neg0"])
        TS("dve", a_neg, a_neg, -1.0, None, ALU.mult, r=["a_neg0"], w=["a_neg"])
        MEMSET("pool", neghalf, -0.5, w=["neghalf"])
        dtb_bc = cf[:, C_HV:C_HV + 16]
        dskip_bc = cf[:, C_HV + 32:C_HV + 48]

        bank_ctr = [0]
        PF = {}
        WO_OFF = ARENA_WORDS - 8 - 4096
        E0_OFF = WO_OFF - 4096
        C0_OFF = WO_OFF - 3584

        def c0_views():
            o = 2 * C0_OFF
            wga0 = abf[:, o:o + 2048].rearrange("p (k c) -> p k c", k=8)
            wgb0 = abf[:, o + 2048:o + 4096].rearrange("p (k c) -> p k c", k=8)
            woa0 = abf[:, o + 4096:o + 5120].rearrange("p (k c) -> p k c", k=4)
            wob0 = abf[:, o + 5120:o + 7168].rearrange("p (k c) -> p k c", k=8)
            return wga0, wgb0, woa0, wob0

        def load_c_into(qi, wga_, wgb_, woa_, wob_, dsem_, ws):
            cs = slice(qi * 256, (qi + 1) * 256)
            woa_src = w_oa.rearrange("(kv g d) n -> kv d g n", kv=2, g=4)
            DMA("pool", wga_, w_in[:, 3344 + qi * 256:3344 + (qi + 1) * 256].rearrange("(kc p) n -> p kc n", p=128), dsem_, w=[("wC", ws)])
            DMA("pool", wgb_, w_in[:, 4368 + qi * 256:4368 + (qi + 1) * 256].rearrange("(kc p) n -> p kc n", p=128), dsem_, w=[("wC", ws)])
            for kv in range(2):
                DMA("pool", woa_[kv * 64:(kv + 1) * 64, :, :], woa_src[kv][:, :, cs], dsem_, w=[("wC", ws)])
            DMA("pool", wob_, w_ob[:, cs].rearrange("(kc p) n -> p kc n", p=128), dsem_, w=[("wC", ws)])

        def rstd_from_ss(ss_ap, ms_ap, rstd_ap, inv_n, eps, toks_r, tok_ms, tok_rstd):
            rows_ = ms_ap.shape[0]
            TS("pool", ms_ap, ss_ap, inv_n, eps, ALU.mult, ALU.add, r=toks_r, w=[tok_ms])
            TT("pool", rstd_ap, ms_ap, neghalf[0:rows_], ALU.pow, r=[tok_ms, "neghalf"], w=[tok_rstd])

        def phase_N():
            A = Bump(arena, abf, [R_T, R_G])
            wx_p = A.bf(8 * 1536).rearrange("p (k c) -> p k c", k=8)
            wz_p = A.bf(8 * 1024).rearrange("p (k c) -> p k c", k=8)
            PF["d_wB"] = P.dsem("wB")
            DMA("pool", wx_p, w_in[:, 1792:3328].rearrange("(kc p) n -> p kc n", p=128), PF["d_wB"], w=["wx"])
            nw = A.f32(1024)
            NLAG = 3
            NSL = NLAG + 2
            xt = [A.f32(1024) for _ in range(NSL)]
            hb = [A.bf(1024) for _ in range(2)]
            junk = A.bf(1024)
            st_ = A.f32(16)
            d_nw = P.dsem("nw1")
            d_x = [P.dsem("xN%d" % i) for i in range(NSL)]
            DMA("sp", nw, vecs[:, 0, :], d_nw, w=["nw"])
            def stage1(t):
                s = t % NSL
                rows = rows_of(t)
                DMA("sp", xt[s][0:rows], xin[t * 128:t * 128 + rows, :], d_x[s], w=[("xt", s)])
                ss = st_[:, s:s + 1]
                ms = st_[:, 5 + s:6 + s]
                rs = st_[:, 10 + s:11 + s]
                ACT(junk[0:rows], xt[s][0:rows], AF.Square, r=[("xt", s)], w=["junk", ("ss", s)], accum_out=ss[0:rows])
                rstd_from_ss(ss[0:rows], ms[0:rows], rs[0:rows], 1.0 / D, EPS, [("ss", s)], ("ms", s), ("rs", s))

            def stage2(t):
                s = t % NSL
                h2 = t % 2
                rows = rows_of(t)
                rs = st_[:, 10 + s:11 + s]
                STT("dve", hb[h2][0:rows], xt[s][0:rows], rs[0:rows], nw[0:rows], ALU.mult, ALU.mult,
                    r=[("xt", s), ("rs", s), "nw"], w=[("hb", h2)])
                b = t % 2
                pv = bankb[b][:, :].rearrange("p (a b) -> p a b", a=8)
                for kc in range(8):
                    TR(pv[:, kc, 0:rows], hb[h2][0:rows, kc * 128:(kc + 1) * 128], idb[0:rows, 0:rows],
                       r=[("hb", h2)] + CONST, w=[BK(b)])
                CP("act", hT[:, :, t * 128:t * 128 + rows], pv[:, :, 0:rows], r=[BK(b)], w=[("hT", t)])

            for t in range(NTL + NLAG):
                if t < NTL:
                    stage1(t)
                if t >= NLAG:
                    stage2(t - NLAG)

        HT_ALL = [("hT", t) for t in range(NTL)]

        def hT_toks(c0, n):
            return [("hT", t) for t in range(c0 // 128, (c0 + n + 127) // 128)]

        def phase_A():
            A = Bump(arena, abf, [(R_T[0], C0_OFF), R_G, X_SPARE, (WO_OFF, R_T[1])])
            wq = A.bf(8 * 512).rearrange("p (k g c) -> p k g c", k=8, g=4)
            wk = A.bf(8 * 128).rearrange("p (k c) -> p k c", k=8)
            wv = A.bf(8 * 128).rearrange("p (k c) -> p k c", k=8)
            assert A.regs[0][0] == O_T + 3072
            qT = A.bf(4 * NT).rearrange("p (g t) -> p g t", g=4)
            kT = A.bf(NT)
            vpad = A.bf(NTL * 256).rearrange("p (t k c) -> p t k c", t=NTL, k=2)
            kf = [A.f32(128) for _ in range(2)]
            vf = [A.f32(128) for _ in range(2)]
            pbuf = [A.bf(512) for _ in range(8)]
            dsum = [A.f32(512) for _ in range(2)]
            rden = [A.f32(512) for _ in range(2)]
            ckb = A.bf(16 * 128).rearrange("p (s c) -> p s c", s=16)
            kcT = A.bf(16 * 128).rearrange("p (s c) -> p s c", s=16)
            vcpad = A.bf(16 * 256).rearrange("p (s k c) -> p s k c", s=16, k=2)
            pc = A.bf(512)
            pn = A.bf(512)
            d_cache = P.dsem("cacheA")

            VP_ALL = [("vpad", t) for t in range(NTL)]
            MEMSET("pool", vpad[:, :, :, :], 0.0, w=VP_ALL)
            MEMSET("pool", vcpad[:, :, :, :], 0.0, w=["vcpad"])
            DMA("pool", ckb, ck.rearrange("s t c -> t s c"), d_cache, w=["ckb"])
            for kv in range(2):
                DMA("pool", vcpad[:, :, kv, kv * 64:(kv + 1) * 64], cv[:, :, kv * 64:(kv + 1) * 64].rearrange("s t c -> t s c"),
                    d_cache, r=[], w=["vcpad"])
            DMA("sp", nks[:, 0:124, :], ck[:, 4:128, :], d_out)
            DMA("sp", nvs[:, 0:124, :], cv[:, 4:128, :], d_out)

            def nb():
                b = bank_ctr[0] % 8
                bank_ctr[0] += 1
                return b

            ev = [0]

            def evac_eng():
                ev[0] += 1
                return "act" if ev[0] % 2 else "dve"

            CHECK("A0")
            for T in range(5):
                c0 = T * 512
                n = 512 if T < 4 else 64
                htk = hT_toks(c0, n)
                for g in range(4):
                    b = nb()
                    for kc in range(8):
                        MM(banks[b][:, 0:n], wq[:, kc, g, :], hT[:, kc, c0:c0 + n], start=(kc == 0), stop=(kc == 7),
                           r=[("wq", g)] + htk, w=[BK(b)])
                    CP(evac_eng(), qT[:, g, c0:c0 + n], banks[b][:, 0:n], r=[BK(b)], w=[("qT", g, T)])
                b = nb()
                for kc in range(8):
                    MM(banks[b][:, 0:n], wk[:, kc, :], hT[:, kc, c0:c0 + n], start=(kc == 0), stop=(kc == 7),
                       r=["wk"] + htk, w=[BK(b)])
                CP(evac_eng(), kT[:, c0:c0 + n], banks[b][:, 0:n], r=[BK(b)], w=[("kT", T)])
                for t in range(c0 // 128, (c0 + n + 127) // 128):
                    rows = rows_of(t)
                    b = nb()
                    for kc in range(8):
                        MM(banks[b][0:rows, 0:128], hT[:, kc, t * 128:t * 128 + rows], wv[:, kc, :], start=(kc == 0), stop=(kc == 7),
                           r=["wv", ("hT", t)], w=[BK(b)])
                    e = evac_eng()
                    for kv in range(2):
                        CP(e, vpad[0:rows, t, kv, kv * 64:(kv + 1) * 64], banks[b][0:rows, kv * 64:(kv + 1) * 64],
                           r=[BK(b)], w=[("vpad", t)])
                    if t >= 15:
                        i = t - 15
                        CP("dve", vf[i][0:rows], banks[b][0:rows, 0:128], r=[BK(b)], w=[("vf", i)])
                        b2 = nb()
                        for kc in range(8):
                            MM(banks[b2][0:rows, 0:128], hT[:, kc, t * 128:t * 128 + rows], wk[:, kc, :], start=(kc == 0), stop=(kc == 7),
                               r=["wk", ("hT", t)], w=[BK(b2)])
                        CP("dve", kf[i][0:rows], banks[b2][0:rows, 0:128], r=[BK(b2)], w=[("kf", i)])
            CHECK("A1")
            DMA("sp", nkp, kf[0], d_out, r=[("kf", 0)])
            DMA("sp", nvp, vf[0], d_out, r=[("vf", 0)])
            for l in range(4):
                DMA("sp", nks[:, 124 + l, :], kf[1][l * 16:(l + 1) * 16, :], d_out, r=[("kf", 1)])
                DMA("sp", nvs[:, 124 + l, :], vf[1][l * 16:(l + 1) * 16, :], d_out, r=[("vf", 1)])
            dump("qT", qT[:, :, :], [("qT", g, T) for g in range(4) for T in range(5)], BF16)
            dump("kT", kT, [("kT", T) for T in range(5)], BF16)
            dump("vpad", vpad[:, :, :, :], VP_ALL, BF16)

            CHECK("A1b")
            scale = 64 ** -0.5
            mP = cb[:, B_MP:B_MP + 128]
            mC = cb[:, B_MC:B_MC + 128]
            onespad = cb[:, B_ONESPAD:B_ONESPAD + 256].rearrange("p (k c) -> p k c", k=2)
            sink_bc = sinkexp.unsqueeze(2).to_broadcast([128, 4, 128])

            pslot = [0]
            plists = {}

            def stage_s(i):
                T = i // 4
                q0 = i * 128
                js = ([i - 1] if i > 0 else []) + [i]
                plist = []
                for kv in range(2):
                    ks = slice(kv * 64, (kv + 1) * 64)
                    for j in js:
                        b = nb()
                        MM(banks[b][:, :].rearrange("p (g q) -> p g q", g=4), kT[ks, j * 128:(j + 1) * 128], qT[ks, :, q0:q0 + 128],
                           r=[("kT", j // 4)] + [("qT", g, T) for g in range(4)], w=[BK(b)])
                        sl = pslot[0] % 8
                        pslot[0] += 1
                        pb = pbuf[sl]
                        ACT(pb, banks[b][:, :], AF.Exp, r=[BK(b)], w=[("pb", sl)], scale=scale)
                        msk = mP if j < i else mC
                        TT("dve", pb.rearrange("p (g q) -> p g q", g=4), pb.rearrange("p (g q) -> p g q", g=4),
                           msk.unsqueeze(1).to_broadcast([128, 4, 128]), ALU.mult, r=[("pb", sl)] + CONST, w=[("pb", sl)])
                        plist.append((kv, j, sl))
                plists[i] = plist

            def stage_p(i):
                q0 = i * 128
                plist = plists[i]
                bo = nb()
                bd = nb()
                for idx, (kv, j, sl) in enumerate(plist):
                    MM(banks[bo][:, :], vpad[:, j, kv, :], pbuf[sl], start=(idx == 0), stop=(idx == len(plist) - 1),
                       r=[("vpad", j), ("pb", sl)], w=[BK(bo)])
                for idx, (kv, j, sl) in enumerate(plist):
                    MM(banks[bd][:, :], onespad[:, kv, :], pbuf[sl], start=(idx == 0), stop=(idx == len(plist) - 1),
                       r=[("pb", sl)] + CONST, w=[BK(bd)])
                s2 = i % 2
                TT("dve", dsum[s2].rearrange("p (g q) -> p g q", g=4), banks[bd][:, :].rearrange("p (g q) -> p g q", g=4), sink_bc,
                   ALU.add, r=[BK(bd), "sinkexp"], w=[("dsum", s2)])
                P.op("dve", lambda e, o=rden[s2], i_=dsum[s2]: e.reciprocal(out=o, in_=i_), r=[("dsum", s2)], w=[("rden", s2)])
                TT("dve", aoT[:, :, q0:q0 + 128], banks[bo][:, :].rearrange("p (g q) -> p g q", g=4),
                   rden[s2].rearrange("p (g q) -> p g q", g=4), ALU.mult, r=[BK(bo), ("rden", s2)], w=[("aoT", i)])

            stage_s(0)
            for i in range(16):
                if i + 1 < 16:
                    stage_s(i + 1)
                stage_p(i)

            CHECK("A2")
            PF["d_wC0"] = P.dsem("wC0")
            load_c_into(0, *c0_views(), PF["d_wC0"], 0)
            for half in range(2):
                b = nb()
                pv = bankb[b][:, :].rearrange("p (a c) -> p a c", a=8)
                for a in range(8):
                    s_ = half * 8 + a
                    TR(pv[:, a, :], ckb[:, s_, :], idb, r=["ckb"] + CONST, w=[BK(b)])
                CP(evac_eng(), kcT[:, half * 8:(half + 1) * 8, :], pv, r=[BK(b)], w=[("kcT", half)])
            CHECK("A3a")
            TQ = [("qT", g, 4) for g in range(4)]
            pcv = pc.rearrange("p (k s g l) -> p k s g l", k=2, s=16, g=4)
            mcache = cb[:, B_MCACHE:B_MCACHE + 4]
            for kv in range(2):
                ks = slice(kv * 64, (kv + 1) * 64)
                bsc = nb()
                scv = banks[bsc][:, 0:256].rearrange("p (s g l) -> p s g l", s=16, g=4)
                for s_ in range(16):
                    rhs = qT[ks, :, SC0:SC0 + 64].rearrange("p g (l s) -> p g l s", s=16)[:, :, :, s_]
                    MM(scv[:, s_, :, :], kcT[ks, s_, :], rhs, r=[("kcT", s_ // 8)] + TQ, w=[BK(bsc)])
                ACT(pc[:, kv * 256:(kv + 1) * 256], banks[bsc][:, 0:256], AF.Exp, r=[BK(bsc)], w=[("pc", kv)], scale=scale)
                TT("dve", pc[:, kv * 256:(kv + 1) * 256].rearrange("p (a l) -> p a l", l=4), pc[:, kv * 256:(kv + 1) * 256].rearrange("p (a l) -> p a l", l=4),
                   mcache.unsqueeze(1).to_broadcast([128, 64, 4]), ALU.mult, r=[("pc", kv)] + CONST, w=[("pc", kv)])
            mnew = cb[0:64, B_MNEW:B_MNEW + 64]
            for kv in range(2):
                ks = slice(kv * 64, (kv + 1) * 64)
                bsn = nb()
                MM(banks[bsn][0:64, 0:256].rearrange("p (g q) -> p g q", g=4), kT[ks, SC0:SC0 + 64], qT[ks, :, SC0:SC0 + 64],
                   r=[("kT", 4)] + TQ, w=[BK(bsn)])
                ACT(pn[0:64, kv * 256:(kv + 1) * 256], banks[bsn][0:64, 0:256], AF.Exp, r=[BK(bsn)], w=[("pn", kv)], scale=scale)
                TT("dve", pn[0:64, kv * 256:(kv + 1) * 256].rearrange("p (a q) -> p a q", q=64), pn[0:64, kv * 256:(kv + 1) * 256].rearrange("p (a q) -> p a q", q=64),
                   mnew.unsqueeze(1).to_broadcast([64, 4, 64]), ALU.mult, r=[("pn", kv)] + CONST, w=[("pn", kv)])
            CHECK("A3b")
            pnv = pn[0:64].rearrange("p (k g l s) -> p k g l s", k=2, g=4, l=4)
            bo = nb()
            bd = nb()
            for which, bX in (("o", bo), ("d", bd)):
                ov = banks[bX][:, 0:256].rearrange("p (s g l) -> p s g l", s=16, g=4)
                for s_ in range(16):
                    for kv in range(2):
                        lhs = vpad[0:64, 16, kv, :] if which == "o" else onespad[0:64, kv, :]
                        MM(ov[:, s_, :, :], lhs, pnv[:, kv, :, :, s_], start=(kv == 0), stop=False,
                           r=[("vpad", 16), ("pn", kv)] + CONST, w=[BK(bX)])
                    for kv in range(2):
                        lhs = vcpad[:, s_, kv, :] if which == "o" else onespad[:, kv, :]
                        MM(ov[:, s_, :, :], lhs, pcv[:, kv, s_, :, :], start=False, stop=(kv == 1),
                           r=["vcpad", ("pc", kv)] + CONST, w=[BK(bX)])
            CHECK("A3c")
            TT("dve", dsum[0][:, 0:256].rearrange("p (s g l) -> p s g l", s=16, g=4), banks[bd][:, 0:256].rearrange("p (s g l) -> p s g l", s=16, g=4),
               sinkexp.unsqueeze(1).unsqueeze(3).to_broadcast([128, 16, 4, 4]), ALU.add, r=[BK(bd), "sinkexp"], w=[("dsum", 0)])
            P.op("dve", lambda e: e.reciprocal(out=rden[0][:, 0:256], in_=dsum[0][:, 0:256]), r=[("dsum", 0)], w=[("rden", 0)])
            TT("dve", dsum[0][:, 0:256], banks[bo][:, 0:256], rden[0][:, 0:256], ALU.mult, r=[BK(bo), ("rden", 0)], w=[("dsum", 0)])
            CP("dve", aoT[:, :, SC0:SC0 + 64].rearrange("p g (l s) -> p g l s", s=16),
               dsum[0][:, 0:256].rearrange("p (s g l) -> p g l s", s=16, g=4), r=[("dsum", 0)], w=[("aoT", 16)])
            dump("aoT", aoT[:, :, :], [("aoT", i) for i in range(17)], BF16)

        AO_ALL = [("aoT", i) for i in range(17)]

        def phase_B():
            SCH = 256
            A = Bump(arena, abf, [R_T, R_G, X_SPARE, (O_X, O_X + 4224)])
            wx = A.bf(8 * 1536).rearrange("p (k c) -> p k c", k=8)
            wz = A.bf(8 * 1024).rearrange("p (k c) -> p k c", k=8)
            o_xbcs = A.take(3072)
            xbcs = [A.bview(o_xbcs + i * 1536, 12 * SCH).rearrange("p (b t) -> p b t", b=12) for i in range(2)]
            o_big = A.take(4096)
            big = [A.fview(o_big + i * 2048, 2048) for i in range(2)]
            o_lw = A.take(3072)
            Lb = [A.bview(o_lw + i * 1024, 2048) for i in range(2)]
            wT = A.bview(o_lw + 2048, 2048)
            snw = A.f32(1024)
            yb = A.f32(1024)
            hst = A.f32(1024)
            diagW = A.bf(12 * 4 * 128).rearrange("p (b i c) -> p b i c", b=12, i=4)
            rawb = [A.bf(264) for _ in range(2)]
            thc = [A.bf(256) for _ in range(2)]
            xs_tm = A.bf(1024)
            xdt = A.bf(1024)
            xdd = A.bf(1024)
            xsd = A.bf(1024)
            thz1 = A.bf(1024)
            thz = [thz1, thz1]
            m_tm = A.bf(1024)
            hst_bf = A.bf(1024)
            dt_all = A.f32(NTL * 16).rearrange("p (t h) -> p t h", t=NTL)
            dA_all = A.f32(NTL * 16).rearrange("p (t h) -> p t h", t=NTL)
            B_tm = A.bf(256).rearrange("p (g n) -> p g n", g=2)
            cbm = A.bf(256)
            wdt = A.bf(8 * 16).rearrange("p (k c) -> p k c", k=8)
            cwh = A.f32(48).rearrange("p (b i) -> p b i", b=12)
            cbh = A.f32(16)
            carry = A.bf(40)[:, 0:36].rearrange("p (b i) -> p b i", b=12)
            E3 = A.f32(48)
            st8 = A.f32(16)
            xbss = A.bf(12 * 64).rearrange("p (b c) -> p b c", b=12)
            CTm = [A.bf(128).rearrange("p (g c) -> p g c", g=2) for _ in range(2)]
            Bm = [A.bf(256).rearrange("p (g n) -> p g n", g=2) for _ in range(2)]
            cdl = A.f32(16)
            cdcol = A.f32(128).rearrange("p (a s) -> p a s", a=8)
            h0 = [A.fview(o_xbcs + i * 1024, 1024).rearrange("p (a c) -> p a c", a=8) for i in range(2)]
            h0T = [A.bview(o_xbcs + 2048 + i * 512, 1024) for i in range(2)]
            scv = big[0][:, 0:1536]
            raws = A.bview(o_lw, 12 * 112).rearrange("p (b c) -> p b c", b=12)
            d_w = PF["d_wB"]
            d_big = P.dsem("bigB")
            DMA("pool", wdt, w_in[:, 3328:3344].rearrange("(kc p) n -> p kc n", p=128), d_w, w=["wdt"])
            DMA("pool", wz, w_in[:, 768:1792].rearrange("(kc p) n -> p kc n", p=128), P.dsem("wzB"), w=["wz"])
            DMA("sp", snw, vecs[:, 3, :], P.dsem("snwB"), w=["snw"])
            TS("pool", cwh, cf[:, C_CONVW:C_CONVW + 48].rearrange("p (b i) -> p b i", b=12), 0.5, None, ALU.mult, r=CONST, w=["cwh"])
            TS("pool", cbh[:, 0:12], cf[:, C_CONVB:C_CONVB + 12], 0.5, None, ALU.mult, r=CONST, w=["cbh"])
            MEMSET("pool", carry[:, :, :], 0.0, w=["carry"])
            for blk_ in range(12):
                for i_ in range(4):
                    TS("dve", diagW[:, blk_, i_, :], idb, cwh[:, blk_, i_:i_ + 1], None, ALU.mult, r=CONST + ["cwh"], w=["diagW"])
            MEMSET("pool", hst, 0.0, w=["hst"])
            MEMSET("pool", hst_bf, 0.0, w=["hst_bf"])

            UI = cf[:, C_UI:C_UI + 128]
            SL = cf[:, C_SL:C_SL + 128]
            ONES = cf[:, C_ONES:C_ONES + 128]
            UIB = cb[:, B_UIB:B_UIB + 128]

            for t in range(NTL):
                rows = rows_of(t)
                for kc in range(8):
                    MM(banks[0][0:rows, t * 16:(t + 1) * 16], hT[:, kc, t * 128:t * 128 + rows], wdt[:, kc, :], start=(kc == 0), stop=(kc == 7),
                       r=["wdt", ("hT", t)], w=[BK(0)])
            dtv = banks[0][:, 0:NTL * 16].rearrange("p (t h) -> p t h", t=NTL)
            TT("dve", dt_all[:, :, :], dtv, dtb_bc.unsqueeze(1).to_broadcast([128, NTL, 16]), ALU.add, r=[BK(0)] + CONST, w=["dt0"])
            ACT(dt_all[:, :, :], dt_all[:, :, :], AF.Exp, r=["dt0"], w=["dt1"])
            ACT(dt_all[:, :, :], dt_all[:, :, :], AF.Ln, r=["dt1"], w=["dt"], bias=1.0)
            TT("dve", dA_all[:, :, :], dt_all[:, :, :], a_neg.unsqueeze(1).to_broadcast([128, NTL, 16]), ALU.mult, r=["dt", "a_neg"], w=["dA"])
            dump("dt", dt_all[:, :, :], ["dt"])

            def conv_block(src_fn, width, blk, bY, thv, outv, rtoks, wtok):
                for i in range(4):
                    MM(banks[bY][:, 0:width], diagW[:, blk, i, :], src_fn(i), start=(i == 0), stop=(i == 3), r=rtoks + ["diagW"], w=[BK(bY)])
                tt = ("thc", blk % 2)
                ACT(thv, banks[bY][:, 0:width], AF.Tanh, r=[BK(bY), "cbh"], w=[tt], bias=cbh[:, blk:blk + 1])
                TS("dve", thv, thv, 1.0, None, ALU.add, r=[tt], w=[tt])
                STT("dve", outv, banks[bY][:, 0:width], cbh[:, blk:blk + 1], thv, ALU.add, ALU.mult, r=[BK(bY), tt, "cbh"], w=[wtok])

            ev = [0]

            def conv_stage1(S, blk):
                c0 = S * SCH
                b = blk % 2
                sl = blk % 2
                for kc in range(8):
                    MM(banks[b][:, 0:SCH], wx[:, kc, blk * 128:(blk + 1) * 128], hT[:, kc, c0:c0 + SCH], start=(kc == 0), stop=(kc == 7),
                       r=["wx"] + hT_toks(c0, SCH), w=[BK(b)])
                CP("pool", rawb[sl][:, 0:3], carry[:, blk, :], r=["carry"], w=[("rawc", sl)])
                ev[0] += 1
                CP("act" if ev[0] % 2 else "dve", rawb[sl][:, 3:3 + SCH], banks[b][:, 0:SCH], r=[BK(b)], w=[("raw", sl)])
                CP("pool", carry[:, blk, :], rawb[sl][:, SCH:SCH + 3], r=[("raw", sl)], w=["carry"])

            def conv_stage2(S, blk):
                xs_ = S % 2
                sl = blk % 2
                conv_block(lambda i: rawb[sl][:, i:i + SCH], SCH, blk, 2 + blk % 2, thc[sl], xbcs[xs_][:, blk, :],
                           [("raw", sl), ("rawc", sl)], ("xbcs", xs_, blk))

            def conv_blocks(S, blks):
                blks = list(blks)
                conv_stage1(S, blks[0])
                for i, blk in enumerate(blks):
                    if i + 1 < len(blks):
                        conv_stage1(S, blks[i + 1])
                    conv_stage2(S, blk)

            def front_decay_a(ci, R, sample=False):
                rs = slice(0, R)
                k = ci % 2
                UIr = cf[rs, C_UIS:C_UIS + 64] if sample else UI
                dAc = dA_all[rs, ci, :]
                bigv = big[k][rs].rearrange("p (h l) -> p h l", h=16)[:, :, 0:R]
                for h in range(16):
                    ACT(bigv[:, h, :], UIr, AF.Copy, r=["dA"] + CONST, w=[("big", k)], scale=dAc[:, h:h + 1])

            def front_decay_b(ci, R, sample=False):
                rs = slice(0, R)
                k = ci % 2
                SLr = cf[rs, C_SLS:C_SLS + 64] if sample else SL
                bigv = big[k][rs].rearrange("p (h l) -> p h l", h=16)[:, :, 0:R]
                Lv = Lb[k][rs].rearrange("p (h l) -> p h l", h=16)[:, :, 0:R]
                hq = 512 // R
                for qd in range(16 // hq):
                    b = 4 + (qd % 2)
                    MM(banks[b][rs, 0:hq * R].rearrange("p (h l) -> p h l", h=hq), SLr, bigv[:, qd * hq:(qd + 1) * hq, :], r=[("big", k)] + CONST, w=[BK(b)])
                    ACT(Lv[:, qd * hq:(qd + 1) * hq, :], banks[b][rs, 0:hq * R].rearrange("p (h l) -> p h l", h=hq), AF.Exp, r=[BK(b)], w=[("L", k, qd)])

            def ssd_chunk(R, ci, XS, BTg, CTg, xbc_toks, c_glob, sample=False, seq_loop=None, early_hook=None, mid_hook=None, end_hook=None):
                rs = slice(0, R)
                k = ci % 2
                UIr = cf[rs, C_UIS:C_UIS + 64] if sample else UI
                SLr = cf[rs, C_SLS:C_SLS + 64] if sample else SL
                UIBr = cb[rs, B_UISB:B_UISB + 64] if sample else UIB
                idr = idb
                gzk = thz[k]
                tmph = big[k][:, 0:1024]
                hq = 512 // R
                for half in range(2):
                    for kc in range(8):
                        MM(banks[6 + half][rs, :], hT[:, kc, c_glob:c_glob + R], wz[:, kc, half * 512:(half + 1) * 512], start=(kc == 0), stop=(kc == 7),
                           r=["wz", ("hT", c_glob // 128)], w=[BK(6 + half)])
                    hs = slice(half * 512, (half + 1) * 512)
                    ACT(gzk[rs, hs], banks[6 + half][rs, :], AF.Tanh, r=[BK(6 + half)], w=[("gz", half)], scale=0.5)
                    STT("dve", gzk[rs, hs], gzk[rs, hs], 1.0, banks[6 + half][rs, :], ALU.add, ALU.mult, r=[("gz", half), BK(6 + half)], w=[("gz", half)])
                pv = bankb[2][:, :].rearrange("p (a c) -> p a c", a=8)
                for blk in range(8):
                    TR(pv[rs, blk, :], XS(blk), idr, r=xbc_toks + CONST, w=[BK(2)])
                CP("act", xs_tm[rs].rearrange("p (a c) -> p a c", a=8), pv[rs, :, :], r=[BK(2)], w=["xs_tm"])
                b3 = banks[3]
                b3b = bankb[3]
                dAc = dA_all[rs, ci, :]
                MM(b3[rs, 0:16], UIr, dAc, r=["dA"] + CONST, w=[BK(3)])
                MM(b3[rs, 16:32], SLr, dAc, r=["dA"] + CONST, w=[BK(3)])
                if not sample:
                    MM(b3[:, 32:48], ONES, dAc, r=["dA"] + CONST, w=[BK(3)])
                for g in range(2):
                    MM(b3[rs, 64 + g * 128:64 + g * 128 + R], BTg(g), CTg(g), r=xbc_toks, w=[BK(3)])
                btv = b3b[:, 640:896].rearrange("p (g n) -> p g n", g=2)
                for g in range(2):
                    TR(btv[rs, g, :], BTg(g), idr, r=xbc_toks + CONST, w=[BK(3)])
                ncol = 32 if sample else 48
                ACT(E3[rs, 0:ncol], b3[rs, 0:ncol], AF.Exp, r=[BK(3)], w=["E3"])
                cbv = b3[rs, 64:320].rearrange("p (g l) -> p g l", g=2)[:, :, 0:R]
                cbmv = cbm[rs].rearrange("p (g l) -> p g l", g=2)[:, :, 0:R]
                TT("dve", cbmv, cbv, UIBr.unsqueeze(1).to_broadcast([R, 2, R]), ALU.mult, r=[BK(3)] + CONST, w=["cbm"])
                CP("act", B_tm[rs, :, :], btv[rs, :, :], r=[BK(3)], w=["B_tm"])
                Lv = Lb[k][rs].rearrange("p (h l) -> p h l", h=16)[:, :, 0:R]
                wTv = wT[rs].rearrange("p (h l) -> p h l", h=16)[:, :, 0:R]
                for g in range(2):
                    TT("dve", wTv[:, g * 8:(g + 1) * 8, :], Lv[:, g * 8:(g + 1) * 8, :], cbmv[:, g, :].unsqueeze(1).to_broadcast([R, 8, R]), ALU.mult,
                       r=[("L", k, q_) for q_ in range(16 // hq)] + ["cbm"], w=[("wT", g)])
                xsv = xs_tm[rs].rearrange("p (h c) -> p h c", h=16)
                xdtv = xdt[rs].rearrange("p (h c) -> p h c", h=16)
                xddv = xdd[rs].rearrange("p (h c) -> p h c", h=16)
                TT("dve", xdtv, xsv, dt_all[rs, ci, :].unsqueeze(2).to_broadcast([R, 16, 64]), ALU.mult, r=["xs_tm", "dt"], w=["xdt"])
                TT("dve", xsd[rs].rearrange("p (h c) -> p h c", h=16), xsv, dskip_bc[rs].unsqueeze(2).to_broadcast([R, 16, 64]), ALU.mult,
                   r=["xs_tm"] + CONST, w=["xsd"])
                TT("dve", xddv, xdtv, E3[rs, 16:32].unsqueeze(2).to_broadcast([R, 16, 64]), ALU.mult, r=["xdt", "E3"], w=["xdd"])
                if early_hook is not None:
                    early_hook()
                if not sample:
                    for g in range(2):
                        MM(banks[4 + g][:, :], CTg(g), hst_bf[:, g * 512:(g + 1) * 512], r=xbc_toks + ["hst_bf"], w=[BK(4 + g)])
                else:
                    seq_loop()
                for g in range(2):
                    gs = slice(g * 512, (g + 1) * 512)
                    TT("dve", yb[rs, gs].rearrange("p (h c) -> p h c", h=8), banks[4 + g][rs, :].rearrange("p (h c) -> p h c", h=8),
                       E3[rs, g * 8:(g + 1) * 8].unsqueeze(2).to_broadcast([R, 8, 64]), ALU.mult, r=[BK(4 + g), "E3"], w=[("yb", g)])
                for h in range(16):
                    b = 6 + h // 8
                    MM(banks[b][rs, (h % 8) * 64:(h % 8 + 1) * 64], wTv[:, h, :], xdtv[:, h, :], r=[("wT", h // 8), "xdt"], w=[BK(b)])
                for g in range(2):
                    gs = slice(g * 512, (g + 1) * 512)
                    TT("dve", yb[rs, gs], yb[rs, gs], banks[6 + g][rs, :], ALU.add, r=[("yb", g), BK(6 + g)], w=[("yb", g)])
                    TT("dve", yb[rs, gs], yb[rs, gs], xsd[rs, gs], ALU.add, r=[("yb", g), "xsd"], w=[("yb", g)])
                    TT("dve", yb[rs, gs], yb[rs, gs], gzk[rs, gs], ALU.mult, r=[("yb", g), ("gz", g)], w=[("yb", g)])
                    ACT(m_tm[rs, gs], yb[rs, gs], AF.Square, r=[("yb", g)], w=[("m_tm", g), ("ssq", g)], accum_out=st8[rs, g:g + 1])
                if mid_hook is not None:
                    mid_hook()
                TS("pool", st8[rs, 4:6], st8[rs, 0:2], 1.0 / 512, 4.0 * EPS, ALU.mult, ALU.add, r=[("ssq", 0), ("ssq", 1)], w=["msB"])
                TT("pool", st8[rs, 8:10], st8[rs, 4:6], neghalf[rs].to_broadcast([R, 2]), ALU.pow, r=["msB", "neghalf"], w=["rsB"])
                for g in range(2):
                    gs = slice(g * 512, (g + 1) * 512)
                    STT("dve", m_tm[rs, gs], yb[rs, gs], st8[rs, 8 + g:9 + g], snw[rs, gs], ALU.mult, ALU.mult, r=[("yb", g), "rsB", "snw"], w=[("m_tm", g)])
                pv2 = bankb[2][:, :].rearrange("p (a c) -> p a c", a=8)
                for blk in range(8):
                    TR(pv2[:, blk, 0:R], m_tm[rs, blk * 128:(blk + 1) * 128], idr[rs, rs], r=[("m_tm", blk // 4)] + CONST, w=[BK(2)])
                CP("act", moT[:, :, c_glob:c_glob + R], pv2[:, :, 0:R], r=[BK(2)], w=[("moT", c_glob // 128)])
                if not sample:
                    for g in range(2):
                        MM(banks[4 + g][:, :], B_tm[:, g, :], xdd[:, g * 512:(g + 1) * 512], r=["B_tm", "xdd"], w=[BK(4 + g)])
                    TT("dve", tmph.rearrange("p (h c) -> p h c", h=16), hst.rearrange("p (h c) -> p h c", h=16),
                       E3[:, 32:48].unsqueeze(2).to_broadcast([128, 16, 64]), ALU.mult, r=["hst", "E3"], w=[("big", k)])
                    for g in range(2):
                        gs = slice(g * 512, (g + 1) * 512)
                        TT("dve", hst[:, gs], tmph[:, gs], banks[4 + g][:, :], ALU.add, r=[("big", k), BK(4 + g)], w=["hst"])
                    CP("act", hst_bf, hst, r=["hst"], w=["hst_bf"])
                if end_hook is not None:
                    end_hook()

            NS = 2048 // SCH
            CPS = SCH // 128
            conv_blocks(0, range(12))
            front_decay_a(0, 128)
            front_decay_b(0, 128)
            for ci in range(16):
                S = ci // CPS
                cc = ci % CPS
                l0 = cc * 128
                xs_ = S % 2
                XT = [("xbcs", xs_, blk) for blk in range(12)]

                def h_early(ci=ci):
                    if ci + 1 < 16:
                        front_decay_a(ci + 1, 128)

                def h_mid(ci=ci):
                    if ci + 1 < 16:
                        front_decay_b(ci + 1, 128)

                def h_end(S=S, cc=cc):
                    if S + 1 < NS:
                        per = 12 // CPS
                        conv_blocks(S + 1, range(cc * per, (cc + 1) * per))

                ssd_chunk(128, ci,
                          lambda blk, l0=l0, xs_=xs_: xbcs[xs_][:, blk, l0:l0 + 128],
                          lambda g, l0=l0, xs_=xs_: xbcs[xs_][:, 8 + g, l0:l0 + 128],
                          lambda g, l0=l0, xs_=xs_: xbcs[xs_][:, 10 + g, l0:l0 + 128],
                          XT, S * SCH + l0, early_hook=h_early, mid_hook=h_mid, end_hook=h_end)
                if ci == 3:
                    dump("mo0", moT[:, :, 0:512], [("moT", i) for i in range(4)], BF16)
            P.barrier()
            hstv = hst.rearrange("p (a c) -> p a c", a=8)
            hout = big[0][:, 0:1024].rearrange("p (a c) -> p a c", a=8)
            for half in range(2):
                b = 4 + half
                for a in range(4):
                    blk = half * 4 + a
                    TR(banks[b][:, a * 128:(a + 1) * 128], hstv[:, blk, :], idf, r=["hst"] + CONST, w=[BK(b)])
                CP("dve", hout[:, half * 4:(half + 1) * 4, :], banks[b][:, :].rearrange("p (a c) -> p a c", a=4), r=[BK(b)], w=[("big", 0)])
            DMA("sp", nsp.rearrange("(blk q) n -> q blk n", q=128), hout, d_big, r=[("big", 0)])
            rawtm = big[1][:, 0:1536]
            for cb_ in range(3):
                b = 6 + (cb_ % 2)
                for kc in range(8):
                    MM(banks[b][:, :], hT[:, kc, 15 * 128:16 * 128], wx[:, kc, cb_ * 512:(cb_ + 1) * 512], start=(kc == 0), stop=(kc == 7),
                       r=["wx", ("hT", 15)], w=[BK(b)])
                CP("dve", rawtm[:, cb_ * 512:(cb_ + 1) * 512], banks[b][:, :], r=[BK(b)], w=[("big", 1)])
            DMA("sp", ncp, rawtm[125:128, :], d_big, r=[("big", 1)])

            CHECK("B2")
            P.barrier()
            d_s = P.dsem("sampB")
            d_h0 = [P.dsem("h0a"), P.dsem("h0b")]
            for i in range(3):
                DMA("sp", scv[i * 16:(i + 1) * 16, :], sconv[:, i, :], d_s, w=[("big", 0)])
            for half, (b, blks) in enumerate(((0, list(range(0, 8))), (1, list(range(8, 12))))):
                for a, blk in enumerate(blks):
                    TR(banks[b][:, a * 48:(a + 1) * 48], scv[0:48, blk * 128:(blk + 1) * 128], idf[0:48, 0:48], r=[("big", 0)] + CONST, w=[BK(b)])
                nblk = len(blks)
                CP("act", raws[:, blks[0]:blks[0] + nblk, 0:48], banks[b][:, 0:nblk * 48].rearrange("p (a c) -> p a c", a=nblk), r=[BK(b)], w=[("raws", half)])
            for half, (b, blks) in enumerate(((0, list(range(0, 8))), (1, list(range(8, 12))))):
                for a, blk in enumerate(blks):
                    for kc in range(8):
                        MM(banks[b][:, a * 64:(a + 1) * 64], wx[:, kc, blk * 128:(blk + 1) * 128], hT[:, kc, SC0:SC0 + 64], start=(kc == 0), stop=(kc == 7),
                           r=["wx", ("hT", 16)], w=[BK(b)])
                nblk = len(blks)
                CP("dve", raws[:, blks[0]:blks[0] + nblk, 48:112], banks[b][:, 0:nblk * 64].rearrange("p (a c) -> p a c", a=nblk), r=[BK(b)], w=[("raws2", half)])
            for blk in range(12):
                sl = blk % 2
                hf = 0 if blk < 8 else 1
                conv_block(lambda i, blk=blk: raws[:, blk, i * 16:i * 16 + 64], 64, blk, 2 + blk % 2, thc[sl][:, 0:64], xbss[:, blk, :],
                           [("raws", hf), ("raws2", hf)], ("xbss", blk))
            XTS = [("xbss", blk) for blk in range(12)]
            rawtm = big[1][:, 0:1536]
            for cb_ in range(3):
                b = 4 + (cb_ % 2)
                for kc in range(8):
                    MM(banks[b][0:64, :], hT[:, kc, SC0:SC0 + 64], wx[:, kc, cb_ * 512:(cb_ + 1) * 512], start=(kc == 0), stop=(kc == 7),
                       r=["wx", ("hT", 16)], w=[BK(b)])
                CP("dve", rawtm[0:64, cb_ * 512:(cb_ + 1) * 512], banks[b][0:64, :], r=[BK(b)], w=[("big", 1)])
            for l in range(1, 4):
                DMA("sp", ncs[:, l - 1, :], rawtm[l * 16:(l + 1) * 16, :], d_big, r=[("big", 1)])
            P.barrier()

            SEQM = cb[:, B_SEQM:B_SEQM + 1024].rearrange("p (s c) -> p s c", s=16)
            SEQINDB = cb[0:64, B_SEQINDB:B_SEQINDB + 16]

            def seq_loop():
                MM(banks[3][0:16, 32:48], dA_all[0:64, 16, :], cf[0:64, C_SEQIND:C_SEQIND + 16], r=["dA"] + CONST, w=[BK(3)])
                CP("act", cdl[0:16, :], banks[3][0:16, 32:48], r=[BK(3)], w=["cdl"])
                selblk = cf[0:16, C_SELBLK:C_SELBLK + 1024].rearrange("p (a c) -> p a c", a=8)
                for blk in range(8):
                    MM(banks[3][:, 64 + blk * 16:64 + (blk + 1) * 16], selblk[:, blk, :], cdl[0:16, :], r=["cdl"] + CONST, w=[BK(3)])
                ACT(cdcol, banks[3][:, 64:192].rearrange("p (a s) -> p a s", a=8), AF.Exp, r=[BK(3)], w=["cdcol"])

                def load_h0(s_):
                    DMA("sp", h0[s_ % 2], sssm[s_].rearrange("(blk q) n -> q blk n", q=128), d_h0[s_ % 2], w=[("h0", s_ % 2)])

                load_h0(0)
                for s_ in range(16):
                    sl = s_ % 2
                    if s_ + 1 < 16:
                        load_h0(s_ + 1)
                    TT("pool", CTm[sl], xbss[:, 10:12, :], SEQM[:, s_, :].unsqueeze(1).to_broadcast([128, 2, 64]), ALU.mult,
                       r=XTS + CONST, w=[("CTm", sl)])
                    TS("pool", Bm[sl][0:64].rearrange("p g n -> p (g n)"), B_tm[0:64].rearrange("p g n -> p (g n)"), SEQINDB[:, s_:s_ + 1], None, ALU.mult,
                       r=["B_tm"] + CONST, w=[("Bm", sl)])
                    for half in range(2):
                        b = half
                        for a in range(4):
                            TR(banks[b][:, a * 128:(a + 1) * 128], h0[sl][:, half * 4 + a, :], idf, r=[("h0", sl)] + CONST, w=[BK(b)])
                        CP("act", h0T[sl][:, half * 512:(half + 1) * 512], banks[b][:, :], r=[BK(b)], w=[("h0T", sl, half)])
                    for g in range(2):
                        MM(banks[4 + g][0:64, :], CTm[sl][:, g, :], h0T[sl][:, g * 512:(g + 1) * 512], start=(s_ == 0), stop=(s_ == 15),
                           r=[("CTm", sl), ("h0T", sl, g)], w=[BK(4 + g)])
                    for half in range(2):
                        b = 6 + half
                        for a in range(4):
                            blk = half * 4 + a
                            MM(banks[b][:, a * 128:(a + 1) * 128], xdd[0:64, blk * 128:(blk + 1) * 128], Bm[sl][0:64, blk // 4, :],
                               r=["xdd", ("Bm", sl)], w=[BK(b)])
                    for blk in range(8):
                        b = 6 + blk // 4
                        STT("dve", h0[sl][:, blk, :], h0[sl][:, blk, :], cdcol[:, blk, s_:s_ + 1], banks[b][:, (blk % 4) * 128:(blk % 4 + 1) * 128],
                            ALU.mult, ALU.add, r=[("h0", sl), "cdcol", BK(b)], w=[("h0", sl)])
                    DMA("sp", nss[s_].rearrange("(blk q) n -> q blk n", q=128), h0[sl], d_h0[sl], r=[("h0", sl)], w=[("h0", sl)])

            front_decay_a(16, 64, sample=True)
            front_decay_b(16, 64, sample=True)
            ssd_chunk(64, 16,
                      lambda blk: xbss[:, blk, :],
                      lambda g: xbss[:, 8 + g, :],
                      lambda g: xbss[:, 10 + g, :],
                      XTS, SC0, sample=True, seq_loop=seq_loop)
            dump("moT", moT[:, :, :], [("moT", i) for i in range(17)], BF16)
            PF["d_wA"] = P.dsem("wA")
            wq_p = abf[:, 2 * O_T:2 * O_T + 4096].rearrange("p (k g c) -> p k g c", k=8, g=4)
            wk_p = abf[:, 2 * (O_T + 2048):2 * (O_T + 2048) + 1024].rearrange("p (k c) -> p k c", k=8)
            wv_p = abf[:, 2 * (O_T + 2560):2 * (O_T + 2560) + 1024].rearrange("p (k c) -> p k c", k=8)
            wsrc_p = w_in[:, 0:512].rearrange("(kc p) (kv g d) -> p kc kv g d", p=128, kv=2, g=4)
            for g in range(4):
                for kv in range(2):
                    DMA("pool", wq_p[:, :, g, kv * 64:(kv + 1) * 64], wsrc_p[:, :, kv, g, :], PF["d_wA"], w=[("wq", g)])
            DMA("pool", wk_p, w_in[:, 512:640].rearrange("(kc p) n -> p kc n", p=128), PF["d_wA"], w=["wk"])
            DMA("pool", wv_p, w_in[:, 640:768].rearrange("(kc p) n -> p kc n", p=128), PF["d_wA"], w=["wv"])

        MO_ALL = [("moT", i) for i in range(17)]

        def phase_C():
            A = Bump(arena, abf, [(R_T[0], C0_OFF)])
            wo_p = abf[:, 2 * WO_OFF:2 * WO_OFF + 8 * 1024].rearrange("p (k c) -> p k c", k=8)
            PF["d_wD"] = P.dsem("wD")
            wga0, wgb0, woa0, wob0 = c0_views()
            wga = [wga0, A.bf(8 * 256).rearrange("p (k c) -> p k c", k=8)]
            wgb = [wgb0, A.bf(8 * 256).rearrange("p (k c) -> p k c", k=8)]
            woa = [woa0, A.bf(4 * 256).rearrange("p (k c) -> p k c", k=4)]
            wob = [wob0, A.bf(8 * 256).rearrange("p (k c) -> p k c", k=8)]
            tA = [A.bf(512) for _ in range(2)]
            tB = [A.bf(512) for _ in range(2)]
            u = [A.f32(512) for _ in range(2)]
            v = [A.f32(512) for _ in range(2)]
            d_w = [PF["d_wC0"], P.dsem("wC1")]
            woa_src = w_oa.rearrange("(kv g d) n -> kv d g n", kv=2, g=4)
            def load_c(qi):
                ws = qi % 2
                cs = slice(qi * 256, (qi + 1) * 256)
                DMA("pool", wga[ws], w_in[:, 3344 + qi * 256:3344 + (qi + 1) * 256].rearrange("(kc p) n -> p kc n", p=128), d_w[ws], w=[("wC", ws)])
                DMA("pool", wgb[ws], w_in[:, 4368 + qi * 256:4368 + (qi + 1) * 256].rearrange("(kc p) n -> p kc n", p=128), d_w[ws], w=[("wC", ws)])
                for kv in range(2):
                    DMA("pool", woa[ws][kv * 64:(kv + 1) * 64, :, :], woa_src[kv][:, :, cs], d_w[ws], w=[("wC", ws)])
                DMA("pool", wob[ws], w_ob[:, cs].rearrange("(kc p) n -> p kc n", p=128), d_w[ws], w=[("wC", ws)])

            load_c(1)
            it = 0
            for qi in range(4):
                ws = qi % 2
                if qi >= 1 and qi + 1 < 4:
                    load_c(qi + 1)
                if qi == 3:
                    DMA("pool", wo_p, w_o.rearrange("(kc p) n -> p kc n", p=128), PF["d_wD"], w=["wo"])
                for blk in range(2):
                    dm = qi * 2 + blk
                    bs = slice(blk * 128, (blk + 1) * 128)
                    for T in range(5):
                        c0 = T * 512
                        n = 512 if T < 4 else 64
                        s = it % 2
                        it += 1
                        bgA, bpA, bgB, bpB = 4 * s, 4 * s + 1, 4 * s + 2, 4 * s + 3
                        htk = hT_toks(c0, n)
                        tk = list(range(c0 // 128, (c0 + n + 127) // 128))
                        for kc in range(8):
                            MM(banks[bgA][:, 0:n], wga[ws][:, kc, bs], hT[:, kc, c0:c0 + n], start=(kc == 0), stop=(kc == 7), r=[("wC", ws)] + htk, w=[BK(bgA)])
                        for g in range(4):
                            MM(banks[bpA][:, 0:n], woa[ws][:, g, bs], aoT[:, g, c0:c0 + n], start=(g == 0), stop=(g == 3),
                               r=[("wC", ws)] + [("aoT", t) for t in tk], w=[BK(bpA)])
                        for kc in range(8):
                            MM(banks[bgB][:, 0:n], wgb[ws][:, kc, bs], hT[:, kc, c0:c0 + n], start=(kc == 0), stop=(kc == 7), r=[("wC", ws)] + htk, w=[BK(bgB)])
                        for kc in range(8):
                            MM(banks[bpB][:, 0:n], wob[ws][:, kc, bs], moT[:, kc, c0:c0 + n], start=(kc == 0), stop=(kc == 7),
                               r=[("wC", ws)] + [("moT", t) for t in tk], w=[BK(bpB)])
                        ACT(tA[s][:, 0:n], banks[bgA][:, 0:n], AF.Tanh, r=[BK(bgA)], w=[("tA", s)], scale=0.5)
                        ACT(tB[s][:, 0:n], banks[bgB][:, 0:n], AF.Tanh, r=[BK(bgB)], w=[("tB", s)], scale=0.5)
                        STT("dve", u[s][:, 0:n], tA[s][:, 0:n], 1.0, banks[bpA][:, 0:n], ALU.add, ALU.mult, r=[("tA", s), BK(bpA)], w=[("u", s)])
                        STT("dve", v[s][:, 0:n], tB[s][:, 0:n], 1.0, banks[bpB][:, 0:n], ALU.add, ALU.mult, r=[("tB", s), BK(bpB)], w=[("v", s)])
                        TT("dve", mgT[:, dm, c0:c0 + n], u[s][:, 0:n], v[s][:, 0:n], ALU.add, r=[("u", s), ("v", s)], w=[("mgT", dm, T)])
            dump("mgT", mgT[:, :, :], [("mgT", dm, T) for dm in range(8) for T in range(5)], BF16)

        def mg_toks(t):
            return [("mgT", dm, t // 4) for dm in range(8)]

        def phase_D():
            A = Bump(arena, abf, [(R_T[0], E0_OFF)])
            wo = abf[:, 2 * WO_OFF:2 * WO_OFF + 8 * 1024].rearrange("p (k c) -> p k c", k=8)
            wup0 = abf[:, 2 * E0_OFF:2 * E0_OFF + 4096].rearrange("p (k c) -> p k c", k=8)
            wdn0 = abf[:, 2 * E0_OFF + 4096:2 * E0_OFF + 8192].rearrange("p (k c) -> p k c", k=4)
            PF["d_wE0"] = P.dsem("wE0")
            nw = A.f32(1024)
            xt = [A.f32(1024) for _ in range(2)]
            hb = [A.bf(1024) for _ in range(2)]
            junk = A.bf(1024)
            st_ = A.f32(16)
            d_x = [P.dsem("xD0"), P.dsem("xD1")]
            DMA("pool", wup0, w_up[:, 0:512].rearrange("(kc q) n -> q kc n", q=128), PF["d_wE0"], w=[("wE", 0)])
            DMA("pool", wdn0, w_down[0:512, :].rearrange("(f q) n -> q f n", q=128), PF["d_wE0"], w=[("wE", 0)])
            DMA("sp", nw, vecs[:, 1, :], P.dsem("nwD"), w=["nw2"])
            def stage1(t):
                s = t % 2
                rows = rows_of(t)
                DMA("sp", xt[s][0:rows], xin[t * 128:t * 128 + rows, :], d_x[s], w=[("xtD", s)])
                for half in range(2):
                    b = 2 * s + half
                    for kc in range(8):
                        MM(banks[b][0:rows, :], mgT[:, kc, t * 128:t * 128 + rows], wo[:, kc, half * 512:(half + 1) * 512], start=(kc == 0), stop=(kc == 7),
                           r=["wo"] + mg_toks(t), w=[BK(b)])
                    STT("dve", x1[0:rows, t, half * 512:(half + 1) * 512], banks[b][0:rows, :], 0.5, xt[s][0:rows, half * 512:(half + 1) * 512],
                        ALU.mult, ALU.add, r=[BK(b), ("xtD", s)], w=[("x1", t)])
                s3 = t % 3
                ss = st_[:, s3:s3 + 1]
                ms = st_[:, 4 + s3:5 + s3]
                rs = st_[:, 8 + s3:9 + s3]
                ACT(junk[0:rows], x1[0:rows, t, :], AF.Square, r=[("x1", t)], w=["junkD", ("ssD", s3)], accum_out=ss[0:rows])
                rstd_from_ss(ss[0:rows], ms[0:rows], rs[0:rows], 1.0 / D, EPS, [("ssD", s3)], ("msD", s3), ("rsD", s3))

            def stage2a(t):
                s = t % 2
                rows = rows_of(t)
                s3 = t % 3
                rs = st_[:, 8 + s3:9 + s3]
                STT("dve", hb[s][0:rows], x1[0:rows, t, :], rs[0:rows], nw[0:rows], ALU.mult, ALU.mult, r=[("x1", t), ("rsD", s3), "nw2"], w=[("hbD", s)])

            def stage2b(t):
                s = t % 2
                rows = rows_of(t)
                b = 4 + s
                pv = bankb[b][:, :].rearrange("p (a c) -> p a c", a=8)
                for kc in range(8):
                    TR(pv[:, kc, 0:rows], hb[s][0:rows, kc * 128:(kc + 1) * 128], idb[0:rows, 0:rows], r=[("hbD", s)] + CONST, w=[BK(b)])
                CP("act", hT[:, :, t * 128:t * 128 + rows], pv[:, :, 0:rows], r=[BK(b)], w=[("hmT", t)])

            for t in range(NTL + 2):
                if t >= 2:
                    stage2a(t - 2)
                if t < NTL:
                    stage1(t)
                if t >= 2:
                    stage2b(t - 2)
            dump("x1", x1[:, :, :], [("x1", t) for t in range(NTL)])

        def hm_toks(c0, n):
            return [("hmT", t) for t in range(c0 // 128, (c0 + n + 127) // 128)]

        def phase_E():
            A = Bump(arena, abf, [(R_T[0], E0_OFF), R_G])
            NP = 8
            wup = [abf[:, 2 * E0_OFF:2 * E0_OFF + 4096].rearrange("p (k c) -> p k c", k=8), A.bf(8 * 512).rearrange("p (k c) -> p k c", k=8)]
            wdn = [abf[:, 2 * E0_OFF + 4096:2 * E0_OFF + 8192].rearrange("p (k c) -> p k c", k=4), A.bf(4 * 1024).rearrange("p (k c) -> p k c", k=4)]
            rl = [A.bf(512) for _ in range(2)]
            upT = [A.bf(4 * 512).rearrange("p (k c) -> p k c", k=4) for _ in range(2)]
            nw = A.f32(1024)
            yt = [A.f32(1024) for _ in range(2)]
            junk = A.bf(1024)
            st_ = A.f32(16)
            d_w = [PF["d_wE0"], P.dsem("wE1")]
            d_n = P.dsem("nwE")
            d_y = [P.dsem("yE0"), P.dsem("yE1")]
            DMA("sp", nw, vecs[:, 2, :], d_n, w=["nw3"])
            def load_w(p):
                ws = p % 2
                DMA("pool", wup[ws], w_up[:, p * 512:(p + 1) * 512].rearrange("(kc q) n -> q kc n", q=128), d_w[ws], w=[("wE", ws)])
                DMA("pool", wdn[ws], w_down[p * 512:(p + 1) * 512, :].rearrange("(f q) n -> q f n", q=128), d_w[ws], w=[("wE", ws)])

            tiles = [(p, T) for p in range(NP) for T in range(5)]
            dn_ctr = [0]

            def emit_up(k):
                p, T = tiles[k]
                ws, us = p % 2, k % 2
                c0 = T * 512
                n = 512 if T < 4 else 64
                for ffc in range(4):
                    b = ffc % 2
                    for kc in range(8):
                        MM(banks[b][:, 0:n], wup[ws][:, kc, ffc * 128:(ffc + 1) * 128], hT[:, kc, c0:c0 + n], start=(kc == 0), stop=(kc == 7),
                           r=[("wE", ws)] + hm_toks(c0, n), w=[BK(b)])
                    ACT(rl[b][:, 0:n], banks[b][:, 0:n], AF.Relu, r=[BK(b)], w=[("rl", b)])
                    TT("dve", upT[us][:, ffc, 0:n], rl[b][:, 0:n], rl[b][:, 0:n], ALU.mult, r=[("rl", b)], w=[("upT", us, ffc)])

            def emit_down(k):
                p, T = tiles[k]
                ws, us = p % 2, k % 2
                c0 = T * 512
                n = 512 if T < 4 else 64
                for t in range(c0 // 128, (c0 + n + 127) // 128):
                    rows = rows_of(t)
                    j0 = t * 128 - c0
                    for half in range(2):
                        b = 2 + (dn_ctr[0] % 4)
                        dn_ctr[0] += 1
                        for ffc in range(4):
                            MM(banks[b][0:rows, :], upT[us][:, ffc, j0:j0 + rows], wdn[ws][:, ffc, half * 512:(half + 1) * 512], start=(ffc == 0), stop=(ffc == 3),
                               r=[("wE", ws), ("upT", us, ffc)], w=[BK(b)])
                        hs = slice(half * 512, (half + 1) * 512)
                        TT("dve", x1[0:rows, t, hs], x1[0:rows, t, hs], banks[b][0:rows, :], ALU.add, r=[("x1", t), BK(b)], w=[("x1", t)])
                    if p == NP - 1:
                        s = t % 2
                        ss = st_[:, s:s + 1]
                        ms = st_[:, 4 + s:5 + s]
                        rs = st_[:, 8 + s:9 + s]
                        ACT(junk[0:rows], x1[0:rows, t, :], AF.Square, r=[("x1", t)], w=["junkE", ("ssE", s)], accum_out=ss[0:rows])
                        rstd_from_ss(ss[0:rows], ms[0:rows], rs[0:rows], 1.0 / D, EPS, [("ssE", s)], ("msE", s), ("rsE", s))
                        STT("dve", yt[s][0:rows], x1[0:rows, t, :], rs[0:rows], nw[0:rows], ALU.mult, ALU.mult, r=[("x1", t), ("rsE", s), "nw3"], w=[("yt", s)])
                        DMA("sp", yout[t * 128:t * 128 + rows, :], yt[s][0:rows], d_y[s], r=[("yt", s)])
                if T == 4 and p + 2 < NP:
                    load_w(p + 2)

            load_w(1)
            emit_up(0)
            for k in range(len(tiles)):
                if k + 1 < len(tiles):
                    emit_up(k + 1)
                emit_down(k)

        phases = [("N", phase_N), ("B", phase_B), ("A", phase_A), ("C", phase_C), ("D", phase_D), ("E", phase_E)]
        for name, fn in phases:
            try:
                fn()
            except StopBuild:
                P.barrier()
                break
            P.barrier()
            if stop_after == name:
                break
        P.emit()
    return nc, dbg_outs


def _constants():
    f = np.zeros((128, NF), np.float32)
    i = np.arange(128)
    f[:, C_IDF:C_IDF + 128] = np.eye(128, dtype=np.float32)
    f[:, C_UI:C_UI + 128] = (i[:, None] <= i[None, :])
    f[:, C_SL:C_SL + 128] = (i[:, None] > i[None, :])
    f[:, C_ONES:C_ONES + 128] = 1.0
    tok = np.arange(64)
    l_, s_ = tok // 16, tok % 16
    same = s_[:, None] == s_[None, :]
    f[0:64, C_UIS:C_UIS + 64] = same & (l_[:, None] <= l_[None, :])
    f[0:64, C_SLS:C_SLS + 64] = same & (l_[:, None] > l_[None, :])
    f[0:64, C_SEQIND:C_SEQIND + 16] = (s_[:, None] == np.arange(16)[None, :])
    sel = np.zeros((16, 8, 128), np.float32)
    for h in range(16):
        sel[h, h // 2, (h % 2) * 64:(h % 2 + 1) * 64] = 1.0
    f[0:16, C_SELBLK:C_SELBLK + 1024] = sel.reshape(16, 1024)
    b = np.zeros((128, NB), np.float32)
    b[:, B_IDB:B_IDB + 128] = np.eye(128)
    b[:, B_MP:B_MP + 128] = (i[:, None] >= i[None, :])
    b[:, B_MC:B_MC + 128] = (i[:, None] <= i[None, :])
    b[:, B_MCACHE:B_MCACHE + 4] = (i[:, None] >= np.arange(4)[None, :])
    b[0:64, B_MNEW:B_MNEW + 64] = same & (l_[:, None] <= l_[None, :])
    b[:, B_UIB:B_UIB + 128] = (i[:, None] <= i[None, :])
    b[0:64, B_UISB:B_UISB + 64] = same & (l_[:, None] <= l_[None, :])
    op = np.zeros((128, 2, 128), np.float32)
    op[:, 0, 0:64] = 1.0
    op[:, 1, 64:128] = 1.0
    b[:, B_ONESPAD:B_ONESPAD + 256] = op.reshape(128, 256)
    seqm = (np.arange(16)[:, None] == s_[None, :]).astype(np.float32)
    b[:, B_SEQM:B_SEQM + 1024] = np.broadcast_to(seqm.reshape(1, 1024), (128, 1024))
    b[0:64, B_SEQINDB:B_SEQINDB + 16] = (s_[:, None] == np.arange(16)[None, :])
    return f, b.astype(ml_dtypes.bfloat16)


_CACHE = {}


def _get_program(stop_after=None, dbg=()):
    key = (stop_after, tuple(dbg))
    if key not in _CACHE:
        _CACHE[key] = build_program(stop_after, dbg)
    return _CACHE[key]


def make_in_maps(inp):
    f32 = np.float32
    g = lambda k: np.ascontiguousarray(np.asarray(inp[k], dtype=f32))
    x_prompt, x_sample = g("x_prompt"), g("x_sample")
    ck, cv = g("cache_swa_k")[0], g("cache_swa_v")[0]
    sconv, sssm = g("state_conv")[0], g("state_ssm")[0]
    cf, cbb = _constants()
    hv = np.stack([g("dt_bias")[0], g("a_log")[0], g("d_skip")[0]]).reshape(1, 48)
    cf[:, C_HV:C_HV + 48] = hv
    cw = g("conv_w")[0]
    cf[:, C_CONVW:C_CONVW + 48] = cw.reshape(4, 12, 128).transpose(2, 1, 0).reshape(128, 48)
    cf[:, C_CONVB:C_CONVB + 12] = g("conv_b")[0].reshape(12, 128).T
    sk = g("sinks")[0]
    cf[:, C_SINKS:C_SINKS + 4] = sk.reshape(2, 4)[np.arange(128) // 64]
    vecs = np.stack([g("norm1")[0], g("norm2")[0], g("final_norm"), g("ssm_norm")[0]])
    vecs = np.ascontiguousarray(np.broadcast_to(vecs[None], (128, 4, D)))
    shared = dict(w_in=g("w_in")[0], w_oa=g("w_oa")[0], w_ob=g("w_ob")[0], w_o=g("w_o")[0], w_up=g("w_up")[0],
                  w_down=g("w_down")[0], vecs=vecs, cstf=cf, cstb=cbb)
    maps = []
    for c in range(NCORES):
        ss = slice(16 * c, 16 * c + 16)
        xs = x_sample[ss].transpose(1, 0, 2).reshape(64, D)
        m = dict(shared)
        m["xin"] = np.ascontiguousarray(np.concatenate([x_prompt[c], xs], axis=0))
        m["ck"] = np.ascontiguousarray(ck[ss].reshape(16, 128, 128))
        m["cv"] = np.ascontiguousarray(cv[ss].reshape(16, 128, 128))
        m["sconv"] = np.ascontiguousarray(sconv[ss])
        m["sssm"] = np.ascontiguousarray(sssm[ss].reshape(16, 1024, 128))
        maps.append(m)
    return maps


def kernel(**inputs):
    nc, _ = _get_program()
    maps = make_in_maps(inputs)
    res = run_bass_kernel_spmd(nc, maps, core_ids=list(range(NCORES)))
    R = res.results
    f32 = np.float32
    y_prompt = np.stack([np.asarray(R[c]["yout"], f32)[0:2048] for c in range(NCORES)])
    y_sample = np.concatenate([np.asarray(R[c]["yout"], f32)[2048:].reshape(4, 16, D).transpose(1, 0, 2) for c in range(NCORES)], axis=0)
    nkp = np.stack([np.asarray(R[c]["nkp"], f32).reshape(128, 2, 64) for c in range(NCORES)])[None]
    nvp = np.stack([np.asarray(R[c]["nvp"], f32).reshape(128, 2, 64) for c in range(NCORES)])[None]
    ncp = np.stack([np.asarray(R[c]["ncp"], f32) for c in range(NCORES)])[None]
    nsp = np.stack([np.asarray(R[c]["nsp"], f32).reshape(16, 64, 128) for c in range(NCORES)])[None]
    nks = np.concatenate([np.asarray(R[c]["nks"], f32).reshape(16, 128, 2, 64) for c in range(NCORES)], axis=0)[None]
    nvs = np.concatenate([np.asarray(R[c]["nvs"], f32).reshape(16, 128, 2, 64) for c in range(NCORES)], axis=0)[None]
    ncs = np.concatenate([np.asarray(R[c]["ncs"], f32) for c in range(NCORES)], axis=0)[None]
    nss = np.concatenate([np.asarray(R[c]["nss"], f32).reshape(16, 16, 64, 128) for c in range(NCORES)], axis=0)[None]
    return (y_prompt, y_sample, nkp, nvp, ncp, nsp, nks, nvs, ncs, nss)
```

```python
import numpy as np
import ml_dtypes
from contextlib import ExitStack
import concourse.bass as bass
import concourse.mybir as mybir
from concourse.bass_utils import run_bass_kernel_spmd

F32 = mybir.dt.float32
BF16 = mybir.dt.bfloat16
AF = mybir.ActivationFunctionType
ALU = mybir.AluOpType

NCORES = 8
D = 1024
NT = 2112
NTL = 17
SC0 = 2048
EPS = 1e-6
INW = 5392
ARENA_WORDS = 53000

C_IDF, C_UI, C_SL, C_ONES, C_UIS, C_SLS, C_SEQIND, C_SELBLK = 0, 128, 256, 384, 512, 576, 640, 656
C_HV, C_CONVW, C_CONVB, C_SINKS, NF = 1680, 1728, 1776, 1788, 1792
B_IDB, B_MP, B_MC, B_MCACHE, B_MNEW, B_UIB, B_UISB, B_ONESPAD, B_SEQM, B_SEQINDB, NB = \
    0, 128, 256, 384, 388, 452, 580, 644, 900, 1924, 1940


class DSem:
    def __init__(self, sem, name):
        self.sem = sem
        self.count = 0
        self.name = name
        self.open = None

    def add(self):
        self.count += 16
        if self.open is None:
            self.open = {"final": None}
        return self.open

    def seal(self):
        if self.open is not None:
            self.open["final"] = self.count
            self.open = None


class Prog:
    ENG_NAMES = ("pe", "act", "dve", "pool", "sp")

    def __init__(self, nc, stack):
        self.nc = nc
        self.stack = stack
        self.ops = []
        self.last_w = {}
        self.readers = {}
        self.esem = {e: stack.enter_context(nc.semaphore("s_" + e)) for e in ("pe", "act", "dve", "pool")}
        self.dsems = []
        self.last_on_eng = {e: None for e in self.ENG_NAMES}

    def dsem(self, name):
        d = DSem(self.stack.enter_context(self.nc.semaphore("d_" + name)), name)
        self.dsems.append(d)
        return d

    def _deps(self, r, w, eng=None):
        deps = set()
        for t in r:
            if t in self.last_w:
                deps.add(self.last_w[t])
            if isinstance(t, tuple) and t[0] == "B":
                for x in self.readers.get(t, ()):
                    if self.ops[x]["eng"] != eng:
                        deps.add(x)
        for t in w:
            if t in self.last_w:
                deps.add(self.last_w[t])
            for x in self.readers.get(t, ()):
                deps.add(x)
        for d in deps:
            p = self.ops[d]
            if p["kind"] == "dma" and p["grp"]["final"] is None:
                p["dsem"].seal()
        return deps

    def _commit(self, idx, r, w):
        for t in r:
            self.readers.setdefault(t, []).append(idx)
        for t in w:
            self.last_w[t] = idx
            self.readers[t] = []

    def op(self, eng, fn, r=(), w=()):
        idx = len(self.ops)
        deps = self._deps(r, w, eng)
        self.ops.append(dict(eng=eng, fn=fn, deps=deps, kind="c", inc=False, cnt=0))
        self._commit(idx, r, w)
        self.last_on_eng[eng] = idx
        return idx

    def dma(self, eng, fn, r=(), w=(), dsem=None):
        idx = len(self.ops)
        deps = self._deps(r, w, eng)
        grp = dsem.add()
        self.ops.append(dict(eng=eng, fn=fn, deps=deps, kind="dma", dsem=dsem, grp=grp))
        self._commit(idx, r, w)
        return idx

    def barrier(self):
        deps = set()
        for e in ("pe", "act", "dve", "pool"):
            if self.last_on_eng[e] is not None:
                deps.add(self.last_on_eng[e])
        for d in self.dsems:
            d.seal()
        dvals = [(d, d.count) for d in self.dsems if d.count > 0]
        for e in self.ENG_NAMES:
            self.ops.append(dict(eng=e, fn=None, deps=set(deps), kind="bar", dvals=dvals, inc=False, cnt=0))

    def finalize(self):
        ops = self.ops
        for d in self.dsems:
            d.seal()
        for o in ops:
            for d in o["deps"]:
                p = ops[d]
                if p["kind"] == "c":
                    if p["eng"] == "pe" and o["eng"] == "pe" and o["kind"] == "c":
                        continue
                    p["inc"] = True
        cnt = {e: 0 for e in self.ENG_NAMES}
        for o in ops:
            if o["kind"] == "c":
                if o["inc"]:
                    cnt[o["eng"]] += 1
                o["cnt"] = cnt[o["eng"]]

    def emit_engine(self, ename, engine):
        ops = self.ops
        waited = {}

        def wait(sem, key, val):
            if waited.get(key, 0) < val:
                engine.wait_ge(sem, val)
                waited[key] = val

        for o in ops:
            if o["eng"] != ename:
                continue
            for d in sorted(o["deps"]):
                p = ops[d]
                if p["kind"] == "c":
                    if p["eng"] == "pe" and ename == "pe" and o["kind"] == "c":
                        continue
                    wait(self.esem[p["eng"]], p["eng"], p["cnt"])
                elif p["kind"] == "dma":
                    wait(p["dsem"].sem, "d_" + p["dsem"].name, p["grp"]["final"])
            if o["kind"] == "bar":
                for d, v in o["dvals"]:
                    wait(d.sem, "d_" + d.name, v)
                continue
            ins = o["fn"](engine)
            if o["kind"] == "dma":
                ins.then_inc(o["dsem"].sem, 16)
            elif o["inc"]:
                ins.then_inc(self.esem[ename], 1)

    def emit(self):
        self.finalize()
        nc = self.nc
        with nc.Block() as block:
            @block.sync
            def _(e):
                self.emit_engine("sp", e)

            @block.tensor
            def _(e):
                self.emit_engine("pe", e)

            @block.scalar
            def _(e):
                self.emit_engine("act", e)

            @block.vector
            def _(e):
                self.emit_engine("dve", e)

            @block.gpsimd
            def _(e):
                self.emit_engine("pool", e)


class Bump:
    def __init__(self, arena, abf, regions):
        self.arena = arena
        self.abf = abf
        self.regs = [list(r) for r in regions]

    def _take(self, w):
        w = (w + 7) // 8 * 8
        for r in self.regs:
            if r[1] - r[0] >= w:
                off = r[0]
                r[0] += w
                return off
        raise RuntimeError("arena region overflow: need %d words, regions %s" % (w, self.regs))

    def f32(self, n):
        off = self._take(n)
        return self.arena[:, off:off + n]

    def bf(self, n):
        off = self._take((n + 1) // 2)
        return self.abf[:, 2 * off:2 * off + n]

    def take(self, w):
        return self._take(w)

    def fview(self, off, n):
        return self.arena[:, off:off + n]

    def bview(self, off, n):
        return self.abf[:, 2 * off:2 * off + n]

    def free_words(self):
        return sum(r[1] - r[0] for r in self.regs)


class StopBuild(Exception):
    pass


def rows_of(t):
    return 128 if t < 16 else 64


def build_program(stop_after=None, dbg=()):
    nc = bass.Bass("TRN2", target_bir_lowering=False)

    def din(name, shape, dt=F32):
        return nc.dram_tensor(name, list(shape), dt, kind="ExternalInput").ap()

    def dout(name, shape, dt=F32):
        return nc.dram_tensor(name, list(shape), dt, kind="ExternalOutput").ap()

    xin = din("xin", [NT, D])
    ck = din("ck", [16, 128, 128])
    cv = din("cv", [16, 128, 128])
    sconv = din("sconv", [16, 3, 1536])
    sssm = din("sssm", [16, 1024, 128])
    w_in = din("w_in", [D, INW])
    w_oa = din("w_oa", [512, D])
    w_ob = din("w_ob", [D, D])
    w_o = din("w_o", [D, D])
    w_up = din("w_up", [D, 4096])
    w_down = din("w_down", [4096, D])
    vecs = din("vecs", [128, 4, D])
    cstf = din("cstf", [128, NF])
    cstb = din("cstb", [128, NB], BF16)

    yout = dout("yout", [NT, D])
    nkp = dout("nkp", [128, 128])
    nvp = dout("nvp", [128, 128])
    ncp = dout("ncp", [3, 1536])
    nsp = dout("nsp", [1024, 128])
    nks = dout("nks", [16, 128, 128])
    nvs = dout("nvs", [16, 128, 128])
    ncs = dout("ncs", [16, 3, 1536])
    nss = dout("nss", [16, 1024, 128])

    dbg_outs = {}

    with ExitStack() as st:
        st.enter_context(nc.allow_low_precision("bf16 matmul operands, fp32 accumulation"))
        st.enter_context(nc.allow_non_contiguous_dma("strided weight / state layouts"))
        arena = st.enter_context(nc.sbuf_tensor("arena", [128, ARENA_WORDS], F32))
        abf = arena.bitcast(BF16)
        banks = [st.enter_context(nc.psum_tensor("bank%d" % i, [128, 512], F32)) for i in range(8)]
        bankb = [b.bitcast(BF16) for b in banks]
        P = Prog(nc, st)

        def MM(out, lhsT, rhs, start=True, stop=True, r=(), w=()):
            P.op("pe", lambda e: e.matmul(out, lhsT=lhsT, rhs=rhs, start=start, stop=stop), r, w)

        def TR(out, in_, ident, r=(), w=()):
            P.op("pe", lambda e: e.transpose(out=out, in_=in_, identity=ident), r, w)

        def ACT(out, in_, func, r=(), w=(), bias=None, scale=1.0, accum_out=None):
            def f(e):
                kw = dict(scale=scale)
                if bias is not None:
                    kw["bias"] = bias
                if accum_out is not None:
                    kw["accum_out"] = accum_out
                return e.activation(out=out, in_=in_, func=func, **kw)
            P.op("act", f, r, w)

        def CP(eng, out, in_, r=(), w=()):
            if eng == "act":
                P.op("act", lambda e: e.copy(out=out, in_=in_), r, w)
            else:
                P.op(eng, lambda e: e.tensor_copy(out=out, in_=in_), r, w)

        def TT(eng, out, in0, in1, op, r=(), w=()):
            P.op(eng, lambda e: e.tensor_tensor(out=out, in0=in0, in1=in1, op=op), r, w)

        def TS(eng, out, in0, s1, s2, op0, op1=None, r=(), w=()):
            if op1 is None:
                P.op(eng, lambda e: e.tensor_scalar(out=out, in0=in0, scalar1=s1, scalar2=None, op0=op0), r, w)
            else:
                P.op(eng, lambda e: e.tensor_scalar(out=out, in0=in0, scalar1=s1, scalar2=s2, op0=op0, op1=op1), r, w)

        def STT(eng, out, in0, scalar, in1, op0, op1, r=(), w=()):
            P.op(eng, lambda e: e.scalar_tensor_tensor(out=out, in0=in0, scalar=scalar, in1=in1, op0=op0, op1=op1), r, w)

        def MEMSET(eng, ap, val, w=()):
            P.op(eng, lambda e: e.memset(ap, val), (), w)

        def DMA(eng, out, in_, dsem, r=(), w=()):
            P.dma(eng, lambda e: e.dma_start(out=out, in_=in_), r, w, dsem)

        def BK(i):
            return ("B", i)

        def CHECK(tag):
            if stop_after == tag:
                raise StopBuild()

        d_out = P.dsem("out")
        d_dbg = P.dsem("dbg")

        def dump(name, ap, tokens, dt=F32):
            if name not in dbg:
                return
            shp = list(ap.shape)
            o = dout("dbg_" + name, shp, dt)
            dbg_outs[name] = shp
            DMA("sp", o, ap, d_dbg, r=tokens)

        cf = arena[:, 0:NF]
        cbw = (NB + 1) // 2
        cb = abf[:, 2 * NF:2 * NF + NB]
        o_small = NF + cbw + 4
        small = arena[:, o_small:o_small + 128]
        O_H = 2944
        assert o_small + 128 <= O_H
        O_X = O_H + 8448
        O_G = O_X + 17408
        O_T = O_G + 8448
        hT = abf[:, 2 * O_H:2 * O_H + 8 * NT].rearrange("p (a b) -> p a b", a=8)
        aoT = abf[:, 2 * O_X:2 * O_X + 4 * NT].rearrange("p (a b) -> p a b", a=4)
        moT = abf[:, 2 * (O_X + 4224):2 * (O_X + 4224) + 8 * NT].rearrange("p (a b) -> p a b", a=8)
        mgT = abf[:, 2 * O_G:2 * O_G + 8 * NT].rearrange("p (a b) -> p a b", a=8)
        x1 = arena[:, O_X:O_X + NTL * D].rearrange("p (a b) -> p a b", a=NTL)
        X_SPARE = (O_X + 12672, O_X + 17408)
        R_G = (O_G, O_G + 8448)
        R_T = (O_T, ARENA_WORDS)
        R_H = (O_H, O_H + 8448)

        idf = cf[:, C_IDF:C_IDF + 128]
        idb = cb[:, B_IDB:B_IDB + 128]

        d_c = P.dsem("const")
        DMA("sp", cf, cstf, d_c, w=["cf"])
        DMA("sp", cb, cstb, d_c, w=["cb"])
        CONST = ["cf", "cb"]

        sinkexp = small[:, 0:4]
        a_neg = small[:, 8:24]
        neghalf = small[:, 24:25]
        ACT(sinkexp, cf[:, C_SINKS:C_SINKS + 4], AF.Exp, r=CONST, w=["sinkexp"])
        ACT(a_neg, cf[:, C_HV + 16:C_HV + 32], AF.Exp, r=CONST, w=["a_neg0"])
        TS("dve", a_neg, a_neg, -1.0, None, ALU.mult, r=["a_neg0"], w=["a_neg"])
        MEMSET("pool", neghalf, -0.5, w=["neghalf"])
        dtb_bc = cf[:, C_HV:C_HV + 16]
        dskip_bc = cf[:, C_HV + 32:C_HV + 48]

        bank_ctr = [0]
        PF = {}
        WO_OFF = ARENA_WORDS - 8 - 4096
        E0_OFF = WO_OFF - 4096
        C0_OFF = WO_OFF - 3584

        def c0_views():
            o = 2 * C0_OFF
            wga0 = abf[:, o:o + 2048].rearrange("p (k c) -> p k c", k=8)
            wgb0 = abf[:, o + 2048:o + 4096].rearrange("p (k c) -> p k c", k=8)
            woa0 = abf[:, o + 4096:o + 5120].rearrange("p (k c) -> p k c", k=4)
            wob0 = abf[:, o + 5120:o + 7168].rearrange("p (k c) -> p k c", k=8)
            return wga0, wgb0, woa0, wob0

        def load_c_into(qi, wga_, wgb_, woa_, wob_, dsem_, ws):
            cs = slice(qi * 256, (qi + 1) * 256)
            woa_src = w_oa.rearrange("(kv g d) n -> kv d g n", kv=2, g=4)
            DMA("pool", wga_, w_in[:, 3344 + qi * 256:3344 + (qi + 1) * 256].rearrange("(kc p) n -> p kc n", p=128), dsem_, w=[("wC", ws)])
            DMA("pool", wgb_, w_in[:, 4368 + qi * 256:4368 + (qi + 1) * 256].rearrange("(kc p) n -> p kc n", p=128), dsem_, w=[("wC", ws)])
            for kv in range(2):
                DMA("pool", woa_[kv * 64:(kv + 1) * 64, :, :], woa_src[kv][:, :, cs], dsem_, w=[("wC", ws)])
            DMA("pool", wob_, w_ob[:, cs].rearrange("(kc p) n -> p kc n", p=128), dsem_, w=[("wC", ws)])

        def rstd_from_ss(ss_ap, ms_ap, rstd_ap, inv_n, eps, toks_r, tok_ms, tok_rstd):
            rows_ = ms_ap.shape[0]
            TS("pool", ms_ap, ss_ap, inv_n, eps, ALU.mult, ALU.add, r=toks_r, w=[tok_ms])
            TT("pool", rstd_ap, ms_ap, neghalf[0:rows_], ALU.pow, r=[tok_ms, "neghalf"], w=[tok_rstd])

        def phase_N():
            A = Bump(arena, abf, [R_T, R_G])
            wx_p = A.bf(8 * 1536).rearrange("p (k c) -> p k c", k=8)
            wz_p = A.bf(8 * 1024).rearrange("p (k c) -> p k c", k=8)
            PF["d_wB"] = P.dsem("wB")
            DMA("pool", wx_p, w_in[:, 1792:3328].rearrange("(kc p) n -> p kc n", p=128), PF["d_wB"], w=["wx"])
            nw = A.f32(1024)
            NLAG = 3
            NSL = NLAG + 2
            xt = [A.f32(1024) for _ in range(NSL)]
            hb = [A.bf(1024) for _ in range(2)]
            junk = A.bf(1024)
            st_ = A.f32(16)
            d_nw = P.dsem("nw1")
            d_x = [P.dsem("xN%d" % i) for i in range(NSL)]
            DMA("sp", nw, vecs[:, 0, :], d_nw, w=["nw"])
            def stage1(t):
                s = t % NSL
                rows = rows_of(t)
                DMA("sp", xt[s][0:rows], xin[t * 128:t * 128 + rows, :], d_x[s], w=[("xt", s)])
                ss = st_[:, s:s + 1]
                ms = st_[:, 5 + s:6 + s]
                rs = st_[:, 10 + s:11 + s]
                ACT(junk[0:rows], xt[s][0:rows], AF.Square, r=[("xt", s)], w=["junk", ("ss", s)], accum_out=ss[0:rows])
                rstd_from_ss(ss[0:rows], ms[0:rows], rs[0:rows], 1.0 / D, EPS, [("ss", s)], ("ms", s), ("rs", s))

            def stage2(t):
                s = t % NSL
                h2 = t % 2
                rows = rows_of(t)
                rs = st_[:, 10 + s:11 + s]
                STT("dve", hb[h2][0:rows], xt[s][0:rows], rs[0:rows], nw[0:rows], ALU.mult, ALU.mult,
                    r=[("xt", s), ("rs", s), "nw"], w=[("hb", h2)])
                b = t % 2
                pv = bankb[b][:, :].rearrange("p (a b) -> p a b", a=8)
                for kc in range(8):
                    TR(pv[:, kc, 0:rows], hb[h2][0:rows, kc * 128:(kc + 1) * 128], idb[0:rows, 0:rows],
                       r=[("hb", h2)] + CONST, w=[BK(b)])
                CP("act", hT[:, :, t * 128:t * 128 + rows], pv[:, :, 0:rows], r=[BK(b)], w=[("hT", t)])

            for t in range(NTL + NLAG):
                if t < NTL:
                    stage1(t)
                if t >= NLAG:
                    stage2(t - NLAG)

        HT_ALL = [("hT", t) for t in range(NTL)]

        def hT_toks(c0, n):
            return [("hT", t) for t in range(c0 // 128, (c0 + n + 127) // 128)]

        def phase_A():
            A = Bump(arena, abf, [(R_T[0], C0_OFF), R_G, X_SPARE, (WO_OFF, R_T[1])])
            wq = A.bf(8 * 512).rearrange("p (k g c) -> p k g c", k=8, g=4)
            wk = A.bf(8 * 128).rearrange("p (k c) -> p k c", k=8)
            wv = A.bf(8 * 128).rearrange("p (k c) -> p k c", k=8)
            assert A.regs[0][0] == O_T + 3072
            qT = A.bf(4 * NT).rearrange("p (g t) -> p g t", g=4)
            kT = A.bf(NT)
            vpad = A.bf(NTL * 256).rearrange("p (t k c) -> p t k c", t=NTL, k=2)
            kf = [A.f32(128) for _ in range(2)]
            vf = [A.f32(128) for _ in range(2)]
            pbuf = [A.bf(512) for _ in range(8)]
            dsum = [A.f32(512) for _ in range(2)]
            rden = [A.f32(512) for _ in range(2)]
            ckb = A.bf(16 * 128).rearrange("p (s c) -> p s c", s=16)
            kcT = A.bf(16 * 128).rearrange("p (s c) -> p s c", s=16)
            vcpad = A.bf(16 * 256).rearrange("p (s k c) -> p s k c", s=16, k=2)
            pc = A.bf(512)
            pn = A.bf(512)
            d_cache = P.dsem("cacheA")

            VP_ALL = [("vpad", t) for t in range(NTL)]
            MEMSET("pool", vpad[:, :, :, :], 0.0, w=VP_ALL)
            MEMSET("pool", vcpad[:, :, :, :], 0.0, w=["vcpad"])
            DMA("pool", ckb, ck.rearrange("s t c -> t s c"), d_cache, w=["ckb"])
            for kv in range(2):
                DMA("pool", vcpad[:, :, kv, kv * 64:(kv + 1) * 64], cv[:, :, kv * 64:(kv + 1) * 64].rearrange("s t c -> t s c"),
                    d_cache, r=[], w=["vcpad"])
            DMA("sp", nks[:, 0:124, :], ck[:, 4:128, :], d_out)
            DMA("sp", nvs[:, 0:124, :], cv[:, 4:128, :], d_out)

            def nb():
                b = bank_ctr[0] % 8
                bank_ctr[0] += 1
                return b

            ev = [0]

            def evac_eng():
                ev[0] += 1
                return "act" if ev[0] % 2 else "dve"

            CHECK("A0")
            for T in range(5):
                c0 = T * 512
                n = 512 if T < 4 else 64
                htk = hT_toks(c0, n)
                for g in range(4):
                    b = nb()
                    for kc in range(8):
                        MM(banks[b][:, 0:n], wq[:, kc, g, :], hT[:, kc, c0:c0 + n], start=(kc == 0), stop=(kc == 7),
                           r=[("wq", g)] + htk, w=[BK(b)])
                    CP(evac_eng(), qT[:, g, c0:c0 + n], banks[b][:, 0:n], r=[BK(b)], w=[("qT", g, T)])
                b = nb()
                for kc in range(8):
                    MM(banks[b][:, 0:n], wk[:, kc, :], hT[:, kc, c0:c0 + n], start=(kc == 0), stop=(kc == 7),
                       r=["wk"] + htk, w=[BK(b)])
                CP(evac_eng(), kT[:, c0:c0 + n], banks[b][:, 0:n], r=[BK(b)], w=[("kT", T)])
                for t in range(c0 // 128, (c0 + n + 127) // 128):
                    rows = rows_of(t)
                    b = nb()
                    for kc in range(8):
                        MM(banks[b][0:rows, 0:128], hT[:, kc, t * 128:t * 128 + rows], wv[:, kc, :], start=(kc == 0), stop=(kc == 7),
                           r=["wv", ("hT", t)], w=[BK(b)])
                    e = evac_eng()
                    for kv in range(2):
                        CP(e, vpad[0:rows, t, kv, kv * 64:(kv + 1) * 64], banks[b][0:rows, kv * 64:(kv + 1) * 64],
                           r=[BK(b)], w=[("vpad", t)])
                    if t >= 15:
                        i = t - 15
                        CP("dve", vf[i][0:rows], banks[b][0:rows, 0:128], r=[BK(b)], w=[("vf", i)])
                        b2 = nb()
                        for kc in range(8):
                            MM(banks[b2][0:rows, 0:128], hT[:, kc, t * 128:t * 128 + rows], wk[:, kc, :], start=(kc == 0), stop=(kc == 7),
                               r=["wk", ("hT", t)], w=[BK(b2)])
                        CP("dve", kf[i][0:rows], banks[b2][0:rows, 0:128], r=[BK(b2)], w=[("kf", i)])
            CHECK("A1")
            DMA("sp", nkp, kf[0], d_out, r=[("kf", 0)])
            DMA("sp", nvp, vf[0], d_out, r=[("vf", 0)])
            for l in range(4):
                DMA("sp", nks[:, 124 + l, :], kf[1][l * 16:(l + 1) * 16, :], d_out, r=[("kf", 1)])
                DMA("sp", nvs[:, 124 + l, :], vf[1][l * 16:(l + 1) * 16, :], d_out, r=[("vf", 1)])
            dump("qT", qT[:, :, :], [("qT", g, T) for g in range(4) for T in range(5)], BF16)
            dump("kT", kT, [("kT", T) for T in range(5)], BF16)
            dump("vpad", vpad[:, :, :, :], VP_ALL, BF16)

            CHECK("A1b")
            scale = 64 ** -0.5
            mP = cb[:, B_MP:B_MP + 128]
            mC = cb[:, B_MC:B_MC + 128]
            onespad = cb[:, B_ONESPAD:B_ONESPAD + 256].rearrange("p (k c) -> p k c", k=2)
            sink_bc = sinkexp.unsqueeze(2).to_broadcast([128, 4, 128])

            pslot = [0]
            plists = {}

            def stage_s(i):
                T = i // 4
                q0 = i * 128
                js = ([i - 1] if i > 0 else []) + [i]
                plist = []
                for kv in range(2):
                    ks = slice(kv * 64, (kv + 1) * 64)
                    for j in js:
                        b = nb()
                        MM(banks[b][:, :].rearrange("p (g q) -> p g q", g=4), kT[ks, j * 128:(j + 1) * 128], qT[ks, :, q0:q0 + 128],
                           r=[("kT", j // 4)] + [("qT", g, T) for g in range(4)], w=[BK(b)])
                        sl = pslot[0] % 8
                        pslot[0] += 1
                        pb = pbuf[sl]
                        ACT(pb, banks[b][:, :], AF.Exp, r=[BK(b)], w=[("pb", sl)], scale=scale)
                        msk = mP if j < i else mC
                        TT("dve", pb.rearrange("p (g q) -> p g q", g=4), pb.rearrange("p (g q) -> p g q", g=4),
                           msk.unsqueeze(1).to_broadcast([128, 4, 128]), ALU.mult, r=[("pb", sl)] + CONST, w=[("pb", sl)])
                        plist.append((kv, j, sl))
                plists[i] = plist

            def stage_p(i):
                q0 = i * 128
                plist = plists[i]
                bo = nb()
                bd = nb()
                for idx, (kv, j, sl) in enumerate(plist):
                    MM(banks[bo][:, :], vpad[:, j, kv, :], pbuf[sl], start=(idx == 0), stop=(idx == len(plist) - 1),
                       r=[("vpad", j), ("pb", sl)], w=[BK(bo)])
                for idx, (kv, j, sl) in enumerate(plist):
                    MM(banks[bd][:, :], onespad[:, kv, :], pbuf[sl], start=(idx == 0), stop=(idx == len(plist) - 1),
                       r=[("pb", sl)] + CONST, w=[BK(bd)])
                s2 = i % 2
                TT("dve", dsum[s2].rearrange("p (g q) -> p g q", g=4), banks[bd][:, :].rearrange("p (g q) -> p g q", g=4), sink_bc,
                   ALU.add, r=[BK(bd), "sinkexp"], w=[("dsum", s2)])
                P.op("dve", lambda e, o=rden[s2], i_=dsum[s2]: e.reciprocal(out=o, in_=i_), r=[("dsum", s2)], w=[("rden", s2)])
                TT("dve", aoT[:, :, q0:q0 + 128], banks[bo][:, :].rearrange("p (g q) -> p g q", g=4),
                   rden[s2].rearrange("p (g q) -> p g q", g=4), ALU.mult, r=[BK(bo), ("rden", s2)], w=[("aoT", i)])

            stage_s(0)
            for i in range(16):
                if i + 1 < 16:
                    stage_s(i + 1)
                stage_p(i)

            CHECK("A2")
            PF["d_wC0"] = P.dsem("wC0")
            load_c_into(0, *c0_views(), PF["d_wC0"], 0)
            for half in range(2):
                b = nb()
                pv = bankb[b][:, :].rearrange("p (a c) -> p a c", a=8)
                for a in range(8):
                    s_ = half * 8 + a
                    TR(pv[:, a, :], ckb[:, s_, :], idb, r=["ckb"] + CONST, w=[BK(b)])
                CP(evac_eng(), kcT[:, half * 8:(half + 1) * 8, :], pv, r=[BK(b)], w=[("kcT", half)])
            CHECK("A3a")
            TQ = [("qT", g, 4) for g in range(4)]
            pcv = pc.rearrange("p (k s g l) -> p k s g l", k=2, s=16, g=4)
            mcache = cb[:, B_MCACHE:B_MCACHE + 4]
            for kv in range(2):
                ks = slice(kv * 64, (kv + 1) * 64)
                bsc = nb()
                scv = banks[bsc][:, 0:256].rearrange("p (s g l) -> p s g l", s=16, g=4)
                for s_ in range(16):
                    rhs = qT[ks, :, SC0:SC0 + 64].rearrange("p g (l s) -> p g l s", s=16)[:, :, :, s_]
                    MM(scv[:, s_, :, :], kcT[ks, s_, :], rhs, r=[("kcT", s_ // 8)] + TQ, w=[BK(bsc)])
                ACT(pc[:, kv * 256:(kv + 1) * 256], banks[bsc][:, 0:256], AF.Exp, r=[BK(bsc)], w=[("pc", kv)], scale=scale)
                TT("dve", pc[:, kv * 256:(kv + 1) * 256].rearrange("p (a l) -> p a l", l=4), pc[:, kv * 256:(kv + 1) * 256].rearrange("p (a l) -> p a l", l=4),
                   mcache.unsqueeze(1).to_broadcast([128, 64, 4]), ALU.mult, r=[("pc", kv)] + CONST, w=[("pc", kv)])
            mnew = cb[0:64, B_MNEW:B_MNEW + 64]
            for kv in range(2):
                ks = slice(kv * 64, (kv + 1) * 64)
                bsn = nb()
                MM(banks[bsn][0:64, 0:256].rearrange("p (g q) -> p g q", g=4), kT[ks, SC0:SC0 + 64], qT[ks, :, SC0:SC0 + 64],
                   r=[("kT", 4)] + TQ, w=[BK(bsn)])
                ACT(pn[0:64, kv * 256:(kv + 1) * 256], banks[bsn][0:64, 0:256], AF.Exp, r=[BK(bsn)], w=[("pn", kv)], scale=scale)
                TT("dve", pn[0:64, kv * 256:(kv + 1) * 256].rearrange("p (a q) -> p a q", q=64), pn[0:64, kv * 256:(kv + 1) * 256].rearrange("p (a q) -> p a q", q=64),
                   mnew.unsqueeze(1).to_broadcast([64, 4, 64]), ALU.mult, r=[("pn", kv)] + CONST, w=[("pn", kv)])
            CHECK("A3b")
            pnv = pn[0:64].rearrange("p (k g l s) -> p k g l s", k=2, g=4, l=4)
            bo = nb()
            bd = nb()
            for which, bX in (("o", bo), ("d", bd)):
                ov = banks[bX][:, 0:256].rearrange("p (s g l) -> p s g l", s=16, g=4)
                for s_ in range(16):
                    for kv in range(2):
                        lhs = vpad[0:64, 16, kv, :] if which == "o" else onespad[0:64, kv, :]
                        MM(ov[:, s_, :, :], lhs, pnv[:, kv, :, :, s_], start=(kv == 0), stop=False,
                           r=[("vpad", 16), ("pn", kv)] + CONST, w=[BK(bX)])
                    for kv in range(2):
                        lhs = vcpad[:, s_, kv, :] if which == "o" else onespad[:, kv, :]
                        MM(ov[:, s_, :, :], lhs, pcv[:, kv, s_, :, :], start=False, stop=(kv == 1),
                           r=["vcpad", ("pc", kv)] + CONST, w=[BK(bX)])
            CHECK("A3c")
            TT("dve", dsum[0][:, 0:256].rearrange("p (s g l) -> p s g l", s=16, g=4), banks[bd][:, 0:256].rearrange("p (s g l) -> p s g l", s=16, g=4),
               sinkexp.unsqueeze(1).unsqueeze(3).to_broadcast([128, 16, 4, 4]), ALU.add, r=[BK(bd), "sinkexp"], w=[("dsum", 0)])
            P.op("dve", lambda e: e.reciprocal(out=rden[0][:, 0:256], in_=dsum[0][:, 0:256]), r=[("dsum", 0)], w=[("rden", 0)])
            TT("dve", dsum[0][:, 0:256], banks[bo][:, 0:256], rden[0][:, 0:256], ALU.mult, r=[BK(bo), ("rden", 0)], w=[("dsum", 0)])
            CP("dve", aoT[:, :, SC0:SC0 + 64].rearrange("p g (l s) -> p g l s", s=16),
               dsum[0][:, 0:256].rearrange("p (s g l) -> p g l s", s=16, g=4), r=[("dsum", 0)], w=[("aoT", 16)])
            dump("aoT", aoT[:, :, :], [("aoT", i) for i in range(17)], BF16)

        AO_ALL = [("aoT", i) for i in range(17)]

        def phase_B():
            SCH = 256
            A = Bump(arena, abf, [R_T, R_G, X_SPARE, (O_X, O_X + 4224)])
            wx = A.bf(8 * 1536).rearrange("p (k c) -> p k c", k=8)
            wz = A.bf(8 * 1024).rearrange("p (k c) -> p k c", k=8)
            o_xbcs = A.take(3072)
            xbcs = [A.bview(o_xbcs + i * 1536, 12 * SCH).rearrange("p (b t) -> p b t", b=12) for i in range(2)]
            o_big = A.take(4096)
            big = [A.fview(o_big + i * 2048, 2048) for i in range(2)]
            o_lw = A.take(3072)
            Lb = [A.bview(o_lw + i * 1024, 2048) for i in range(2)]
            wT = A.bview(o_lw + 2048, 2048)
            snw = A.f32(1024)
            yb = A.f32(1024)
            hst = A.f32(1024)
            diagW = A.bf(12 * 4 * 128).rearrange("p (b i c) -> p b i c", b=12, i=4)
            rawb = [A.bf(264) for _ in range(2)]
            thc = [A.bf(256) for _ in range(2)]
            xs_tm = A.bf(1024)
            xdt = A.bf(1024)
            xdd = A.bf(1024)
            xsd = A.bf(1024)
            thz1 = A.bf(1024)
            thz = [thz1, thz1]
            m_tm = A.bf(1024)
            hst_bf = A.bf(1024)
            dt_all = A.f32(NTL * 16).rearrange("p (t h) -> p t h", t=NTL)
            dA_all = A.f32(NTL * 16).rearrange("p (t h) -> p t h", t=NTL)
            B_tm = A.bf(256).rearrange("p (g n) -> p g n", g=2)
            cbm = A.bf(256)
            wdt = A.bf(8 * 16).rearrange("p (k c) -> p k c", k=8)
            cwh = A.f32(48).rearrange("p (b i) -> p b i", b=12)
            cbh = A.f32(16)
            carry = A.bf(40)[:, 0:36].rearrange("p (b i) -> p b i", b=12)
            E3 = A.f32(48)
            st8 = A.f32(16)
            xbss = A.bf(12 * 64).rearrange("p (b c) -> p b c", b=12)
            CTm = [A.bf(128).rearrange("p (g c) -> p g c", g=2) for _ in range(2)]
            Bm = [A.bf(256).rearrange("p (g n) -> p g n", g=2) for _ in range(2)]
            cdl = A.f32(16)
            cdcol = A.f32(128).rearrange("p (a s) -> p a s", a=8)
            h0 = [A.fview(o_xbcs + i * 1024, 1024).rearrange("p (a c) -> p a c", a=8) for i in range(2)]
            h0T = [A.bview(o_xbcs + 2048 + i * 512, 1024) for i in range(2)]
            scv = big[0][:, 0:1536]
            raws = A.bview(o_lw, 12 * 112).rearrange("p (b c) -> p b c", b=12)
            d_w = PF["d_wB"]
            d_big = P.dsem("bigB")
            DMA("pool", wdt, w_in[:, 3328:3344].rearrange("(kc p) n -> p kc n", p=128), d_w, w=["wdt"])
            DMA("pool", wz, w_in[:, 768:1792].rearrange("(kc p) n -> p kc n", p=128), P.dsem("wzB"), w=["wz"])
            DMA("sp", snw, vecs[:, 3, :], P.dsem("snwB"), w=["snw"])
            TS("pool", cwh, cf[:, C_CONVW:C_CONVW + 48].rearrange("p (b i) -> p b i", b=12), 0.5, None, ALU.mult, r=CONST, w=["cwh"])
            TS("pool", cbh[:, 0:12], cf[:, C_CONVB:C_CONVB + 12], 0.5, None, ALU.mult, r=CONST, w=["cbh"])
            MEMSET("pool", carry[:, :, :], 0.0, w=["carry"])
            for blk_ in range(12):
                for i_ in range(4):
                    TS("dve", diagW[:, blk_, i_, :], idb, cwh[:, blk_, i_:i_ + 1], None, ALU.mult, r=CONST + ["cwh"], w=["diagW"])
            MEMSET("pool", hst, 0.0, w=["hst"])
            MEMSET("pool", hst_bf, 0.0, w=["hst_bf"])

            UI = cf[:, C_UI:C_UI + 128]
            SL = cf[:, C_SL:C_SL + 128]
            ONES = cf[:, C_ONES:C_ONES + 128]
            UIB = cb[:, B_UIB:B_UIB + 128]

            for t in range(NTL):
                rows = rows_of(t)
                for kc in range(8):
                    MM(banks[0][0:rows, t * 16:(t + 1) * 16], hT[:, kc, t * 128:t * 128 + rows], wdt[:, kc, :], start=(kc == 0), stop=(kc == 7),
                       r=["wdt", ("hT", t)], w=[BK(0)])
            dtv = banks[0][:, 0:NTL * 16].rearrange("p (t h) -> p t h", t=NTL)
            TT("dve", dt_all[:, :, :], dtv, dtb_bc.unsqueeze(1).to_broadcast([128, NTL, 16]), ALU.add, r=[BK(0)] + CONST, w=["dt0"])
            ACT(dt_all[:, :, :], dt_all[:, :, :], AF.Exp, r=["dt0"], w=["dt1"])
            ACT(dt_all[:, :, :], dt_all[:, :, :], AF.Ln, r=["dt1"], w=["dt"], bias=1.0)
            TT("dve", dA_all[:, :, :], dt_all[:, :, :], a_neg.unsqueeze(1).to_broadcast([128, NTL, 16]), ALU.mult, r=["dt", "a_neg"], w=["dA"])
            dump("dt", dt_all[:, :, :], ["dt"])

            def conv_block(src_fn, width, blk, bY, thv, outv, rtoks, wtok):
                for i in range(4):
                    MM(banks[bY][:, 0:width], diagW[:, blk, i, :], src_fn(i), start=(i == 0), stop=(i == 3), r=rtoks + ["diagW"], w=[BK(bY)])
                tt = ("thc", blk % 2)
                ACT(thv, banks[bY][:, 0:width], AF.Tanh, r=[BK(bY), "cbh"], w=[tt], bias=cbh[:, blk:blk + 1])
                TS("dve", thv, thv, 1.0, None, ALU.add, r=[tt], w=[tt])
                STT("dve", outv, banks[bY][:, 0:width], cbh[:, blk:blk + 1], thv, ALU.add, ALU.mult, r=[BK(bY), tt, "cbh"], w=[wtok])

            ev = [0]

            def conv_stage1(S, blk):
                c0 = S * SCH
                b = blk % 2
                sl = blk % 2
                for kc in range(8):
                    MM(banks[b][:, 0:SCH], wx[:, kc, blk * 128:(blk + 1) * 128], hT[:, kc, c0:c0 + SCH], start=(kc == 0), stop=(kc == 7),
                       r=["wx"] + hT_toks(c0, SCH), w=[BK(b)])
                CP("pool", rawb[sl][:, 0:3], carry[:, blk, :], r=["carry"], w=[("rawc", sl)])
                ev[0] += 1
                CP("act" if ev[0] % 2 else "dve", rawb[sl][:, 3:3 + SCH], banks[b][:, 0:SCH], r=[BK(b)], w=[("raw", sl)])
                CP("pool", carry[:, blk, :], rawb[sl][:, SCH:SCH + 3], r=[("raw", sl)], w=["carry"])

            def conv_stage2(S, blk):
                xs_ = S % 2
                sl = blk % 2
                conv_block(lambda i: rawb[sl][:, i:i + SCH], SCH, blk, 2 + blk % 2, thc[sl], xbcs[xs_][:, blk, :],
                           [("raw", sl), ("rawc", sl)], ("xbcs", xs_, blk))

            def conv_blocks(S, blks):
                blks = list(blks)
                conv_stage1(S, blks[0])
                for i, blk in enumerate(blks):
                    if i + 1 < len(blks):
                        conv_stage1(S, blks[i + 1])
                    conv_stage2(S, blk)

            def front_decay_a(ci, R, sample=False):
                rs = slice(0, R)
                k = ci % 2
                UIr = cf[rs, C_UIS:C_UIS + 64] if sample else UI
                dAc = dA_all[rs, ci, :]
                bigv = big[k][rs].rearrange("p (h l) -> p h l", h=16)[:, :, 0:R]
                for h in range(16):
                    ACT(bigv[:, h, :], UIr, AF.Copy, r=["dA"] + CONST, w=[("big", k)], scale=dAc[:, h:h + 1])

            def front_decay_b(ci, R, sample=False):
                rs = slice(0, R)
                k = ci % 2
                SLr = cf[rs, C_SLS:C_SLS + 64] if sample else SL
                bigv = big[k][rs].rearrange("p (h l) -> p h l", h=16)[:, :, 0:R]
                Lv = Lb[k][rs].rearrange("p (h l) -> p h l", h=16)[:, :, 0:R]
                hq = 512 // R
                for qd in range(16 // hq):
                    b = 4 + (qd % 2)
                    MM(banks[b][rs, 0:hq * R].rearrange("p (h l) -> p h l", h=hq), SLr, bigv[:, qd * hq:(qd + 1) * hq, :], r=[("big", k)] + CONST, w=[BK(b)])
                    ACT(Lv[:, qd * hq:(qd + 1) * hq, :], banks[b][rs, 0:hq * R].rearrange("p (h l) -> p h l", h=hq), AF.Exp, r=[BK(b)], w=[("L", k, qd)])

            def ssd_chunk(R, ci, XS, BTg, CTg, xbc_toks, c_glob, sample=False, seq_loop=None, early_hook=None, mid_hook=None, end_hook=None):
                rs = slice(0, R)
                k = ci % 2
                UIr = cf[rs, C_UIS:C_UIS + 64] if sample else UI
                SLr = cf[rs, C_SLS:C_SLS + 64] if sample else SL
                UIBr = cb[rs, B_UISB:B_UISB + 64] if sample else UIB
                idr = idb
                gzk = thz[k]
                tmph = big[k][:, 0:1024]
                hq = 512 // R
                for half in range(2):
                    for kc in range(8):
                        MM(banks[6 + half][rs, :], hT[:, kc, c_glob:c_glob + R], wz[:, kc, half * 512:(half + 1) * 512], start=(kc == 0), stop=(kc == 7),
                           r=["wz", ("hT", c_glob // 128)], w=[BK(6 + half)])
                    hs = slice(half * 512, (half + 1) * 512)
                    ACT(gzk[rs, hs], banks[6 + half][rs, :], AF.Tanh, r=[BK(6 + half)], w=[("gz", half)], scale=0.5)
                    STT("dve", gzk[rs, hs], gzk[rs, hs], 1.0, banks[6 + half][rs, :], ALU.add, ALU.mult, r=[("gz", half), BK(6 + half)], w=[("gz", half)])
                pv = bankb[2][:, :].rearrange("p (a c) -> p a c", a=8)
                for blk in range(8):
                    TR(pv[rs, blk, :], XS(blk), idr, r=xbc_toks + CONST, w=[BK(2)])
                CP("act", xs_tm[rs].rearrange("p (a c) -> p a c", a=8), pv[rs, :, :], r=[BK(2)], w=["xs_tm"])
                b3 = banks[3]
                b3b = bankb[3]
                dAc = dA_all[rs, ci, :]
                MM(b3[rs, 0:16], UIr, dAc, r=["dA"] + CONST, w=[BK(3)])
                MM(b3[rs, 16:32], SLr, dAc, r=["dA"] + CONST, w=[BK(3)])
                if not sample:
                    MM(b3[:, 32:48], ONES, dAc, r=["dA"] + CONST, w=[BK(3)])
                for g in range(2):
                    MM(b3[rs, 64 + g * 128:64 + g * 128 + R], BTg(g), CTg(g), r=xbc_toks, w=[BK(3)])
                btv = b3b[:, 640:896].rearrange("p (g n) -> p g n", g=2)
                for g in range(2):
                    TR(btv[rs, g, :], BTg(g), idr, r=xbc_toks + CONST, w=[BK(3)])
                ncol = 32 if sample else 48
                ACT(E3[rs, 0:ncol], b3[rs, 0:ncol], AF.Exp, r=[BK(3)], w=["E3"])
                cbv = b3[rs, 64:320].rearrange("p (g l) -> p g l", g=2)[:, :, 0:R]
                cbmv = cbm[rs].rearrange("p (g l) -> p g l", g=2)[:, :, 0:R]
                TT("dve", cbmv, cbv, UIBr.unsqueeze(1).to_broadcast([R, 2, R]), ALU.mult, r=[BK(3)] + CONST, w=["cbm"])
                CP("act", B_tm[rs, :, :], btv[rs, :, :], r=[BK(3)], w=["B_tm"])
                Lv = Lb[k][rs].rearrange("p (h l) -> p h l", h=16)[:, :, 0:R]
                wTv = wT[rs].rearrange("p (h l) -> p h l", h=16)[:, :, 0:R]
                for g in range(2):
                    TT("dve", wTv[:, g * 8:(g + 1) * 8, :], Lv[:, g * 8:(g + 1) * 8, :], cbmv[:, g, :].unsqueeze(1).to_broadcast([R, 8, R]), ALU.mult,
                       r=[("L", k, q_) for q_ in range(16 // hq)] + ["cbm"], w=[("wT", g)])
                xsv = xs_tm[rs].rearrange("p (h c) -> p h c", h=16)
                xdtv = xdt[rs].rearrange("p (h c) -> p h c", h=16)
                xddv = xdd[rs].rearrange("p (h c) -> p h c", h=16)
                TT("dve", xdtv, xsv, dt_all[rs, ci, :].unsqueeze(2).to_broadcast([R, 16, 64]), ALU.mult, r=["xs_tm", "dt"], w=["xdt"])
                TT("dve", xsd[rs].rearrange("p (h c) -> p h c", h=16), xsv, dskip_bc[rs].unsqueeze(2).to_broadcast([R, 16, 64]), ALU.mult,
                   r=["xs_tm"] + CONST, w=["xsd"])
                TT("dve", xddv, xdtv, E3[rs, 16:32].unsqueeze(2).to_broadcast([R, 16, 64]), ALU.mult, r=["xdt", "E3"], w=["xdd"])
                if early_hook is not None:
                    early_hook()
                if not sample:
                    for g in range(2):
                        MM(banks[4 + g][:, :], CTg(g), hst_bf[:, g * 512:(g + 1) * 512], r=xbc_toks + ["hst_bf"], w=[BK(4 + g)])
                else:
                    seq_loop()
                for g in range(2):
                    gs = slice(g * 512, (g + 1) * 512)
                    TT("dve", yb[rs, gs].rearrange("p (h c) -> p h c", h=8), banks[4 + g][rs, :].rearrange("p (h c) -> p h c", h=8),
                       E3[rs, g * 8:(g + 1) * 8].unsqueeze(2).to_broadcast([R, 8, 64]), ALU.mult, r=[BK(4 + g), "E3"], w=[("yb", g)])
                for h in range(16):
                    b = 6 + h // 8
                    MM(banks[b][rs, (h % 8) * 64:(h % 8 + 1) * 64], wTv[:, h, :], xdtv[:, h, :], r=[("wT", h // 8), "xdt"], w=[BK(b)])
                for g in range(2):
                    gs = slice(g * 512, (g + 1) * 512)
                    TT("dve", yb[rs, gs], yb[rs, gs], banks[6 + g][rs, :], ALU.add, r=[("yb", g), BK(6 + g)], w=[("yb", g)])
                    TT("dve", yb[rs, gs], yb[rs, gs], xsd[rs, gs], ALU.add, r=[("yb", g), "xsd"], w=[("yb", g)])
                    TT("dve", yb[rs, gs], yb[rs, gs], gzk[rs, gs], ALU.mult, r=[("yb", g), ("gz", g)], w=[("yb", g)])
                    ACT(m_tm[rs, gs], yb[rs, gs], AF.Square, r=[("yb", g)], w=[("m_tm", g), ("ssq", g)], accum_out=st8[rs, g:g + 1])
                if mid_hook is not None:
                    mid_hook()
                TS("pool", st8[rs, 4:6], st8[rs, 0:2], 1.0 / 512, 4.0 * EPS, ALU.mult, ALU.add, r=[("ssq", 0), ("ssq", 1)], w=["msB"])
                TT("pool", st8[rs, 8:10], st8[rs, 4:6], neghalf[rs].to_broadcast([R, 2]), ALU.pow, r=["msB", "neghalf"], w=["rsB"])
                for g in range(2):
                    gs = slice(g * 512, (g + 1) * 512)
                    STT("dve", m_tm[rs, gs], yb[rs, gs], st8[rs, 8 + g:9 + g], snw[rs, gs], ALU.mult, ALU.mult, r=[("yb", g), "rsB", "snw"], w=[("m_tm", g)])
                pv2 = bankb[2][:, :].rearrange("p (a c) -> p a c", a=8)
                for blk in range(8):
                    TR(pv2[:, blk, 0:R], m_tm[rs, blk * 128:(blk + 1) * 128], idr[rs, rs], r=[("m_tm", blk // 4)] + CONST, w=[BK(2)])
                CP("act", moT[:, :, c_glob:c_glob + R], pv2[:, :, 0:R], r=[BK(2)], w=[("moT", c_glob // 128)])
                if not sample:
                    for g in range(2):
                        MM(banks[4 + g][:, :], B_tm[:, g, :], xdd[:, g * 512:(g + 1) * 512], r=["B_tm", "xdd"], w=[BK(4 + g)])
                    TT("dve", tmph.rearrange("p (h c) -> p h c", h=16), hst.rearrange("p (h c) -> p h c", h=16),
                       E3[:, 32:48].unsqueeze(2).to_broadcast([128, 16, 64]), ALU.mult, r=["hst", "E3"], w=[("big", k)])
                    for g in range(2):
                        gs = slice(g * 512, (g + 1) * 512)
                        TT("dve", hst[:, gs], tmph[:, gs], banks[4 + g][:, :], ALU.add, r=[("big", k), BK(4 + g)], w=["hst"])
                    CP("act", hst_bf, hst, r=["hst"], w=["hst_bf"])
                if end_hook is not None:
                    end_hook()

            NS = 2048 // SCH
            CPS = SCH // 128
            conv_blocks(0, range(12))
            front_decay_a(0, 128)
            front_decay_b(0, 128)
            for ci in range(16):
                S = ci // CPS
                cc = ci % CPS
                l0 = cc * 128
                xs_ = S % 2
                XT = [("xbcs", xs_, blk) for blk in range(12)]

                def h_early(ci=ci):
                    if ci + 1 < 16:
                        front_decay_a(ci + 1, 128)

                def h_mid(ci=ci):
                    if ci + 1 < 16:
                        front_decay_b(ci + 1, 128)

                def h_end(S=S, cc=cc):
                    if S + 1 < NS:
                        per = 12 // CPS
                        conv_blocks(S + 1, range(cc * per, (cc + 1) * per))

                ssd_chunk(128, ci,
                          lambda blk, l0=l0, xs_=xs_: xbcs[xs_][:, blk, l0:l0 + 128],
                          lambda g, l0=l0, xs_=xs_: xbcs[xs_][:, 8 + g, l0:l0 + 128],
                          lambda g, l0=l0, xs_=xs_: xbcs[xs_][:, 10 + g, l0:l0 + 128],
                          XT, S * SCH + l0, early_hook=h_early, mid_hook=h_mid, end_hook=h_end)
                if ci == 3:
                    dump("mo0", moT[:, :, 0:512], [("moT", i) for i in range(4)], BF16)
            P.barrier()
            hstv = hst.rearrange("p (a c) -> p a c", a=8)
            hout = big[0][:, 0:1024].rearrange("p (a c) -> p a c", a=8)
            for half in range(2):
                b = 4 + half
                for a in range(4):
                    blk = half * 4 + a
                    TR(banks[b][:, a * 128:(a + 1) * 128], hstv[:, blk, :], idf, r=["hst"] + CONST, w=[BK(b)])
                CP("dve", hout[:, half * 4:(half + 1) * 4, :], banks[b][:, :].rearrange("p (a c) -> p a c", a=4), r=[BK(b)], w=[("big", 0)])
            DMA("sp", nsp.rearrange("(blk q) n -> q blk n", q=128), hout, d_big, r=[("big", 0)])
            rawtm = big[1][:, 0:1536]
            for cb_ in range(3):
                b = 6 + (cb_ % 2)
                for kc in range(8):
                    MM(banks[b][:, :], hT[:, kc, 15 * 128:16 * 128], wx[:, kc, cb_ * 512:(cb_ + 1) * 512], start=(kc == 0), stop=(kc == 7),
                       r=["wx", ("hT", 15)], w=[BK(b)])
                CP("dve", rawtm[:, cb_ * 512:(cb_ + 1) * 512], banks[b][:, :], r=[BK(b)], w=[("big", 1)])
            DMA("sp", ncp, rawtm[125:128, :], d_big, r=[("big", 1)])

            CHECK("B2")
            P.barrier()
            d_s = P.dsem("sampB")
            d_h0 = [P.dsem("h0a"), P.dsem("h0b")]
            for i in range(3):
                DMA("sp", scv[i * 16:(i + 1) * 16, :], sconv[:, i, :], d_s, w=[("big", 0)])
            for half, (b, blks) in enumerate(((0, list(range(0, 8))), (1, list(range(8, 12))))):
                for a, blk in enumerate(blks):
                    TR(banks[b][:, a * 48:(a + 1) * 48], scv[0:48, blk * 128:(blk + 1) * 128], idf[0:48, 0:48], r=[("big", 0)] + CONST, w=[BK(b)])
                nblk = len(blks)
                CP("act", raws[:, blks[0]:blks[0] + nblk, 0:48], banks[b][:, 0:nblk * 48].rearrange("p (a c) -> p a c", a=nblk), r=[BK(b)], w=[("raws", half)])
            for half, (b, blks) in enumerate(((0, list(range(0, 8))), (1, list(range(8, 12))))):
                for a, blk in enumerate(blks):
                    for kc in range(8):
                        MM(banks[b][:, a * 64:(a + 1) * 64], wx[:, kc, blk * 128:(blk + 1) * 128], hT[:, kc, SC0:SC0 + 64], start=(kc == 0), stop=(kc == 7),
                           r=["wx", ("hT", 16)], w=[BK(b)])
                nblk = len(blks)
                CP("dve", raws[:, blks[0]:blks[0] + nblk, 48:112], banks[b][:, 0:nblk * 64].rearrange("p (a c) -> p a c", a=nblk), r=[BK(b)], w=[("raws2", half)])
            for blk in range(12):
                sl = blk % 2
                hf = 0 if blk < 8 else 1
                conv_block(lambda i, blk=blk: raws[:, blk, i * 16:i * 16 + 64], 64, blk, 2 + blk % 2, thc[sl][:, 0:64], xbss[:, blk, :],
                           [("raws", hf), ("raws2", hf)], ("xbss", blk))
            XTS = [("xbss", blk) for blk in range(12)]
            rawtm = big[1][:, 0:1536]
            for cb_ in range(3):
                b = 4 + (cb_ % 2)
                for kc in range(8):
                    MM(banks[b][0:64, :], hT[:, kc, SC0:SC0 + 64], wx[:, kc, cb_ * 512:(cb_ + 1) * 512], start=(kc == 0), stop=(kc == 7),
                       r=["wx", ("hT", 16)], w=[BK(b)])
                CP("dve", rawtm[0:64, cb_ * 512:(cb_ + 1) * 512], banks[b][0:64, :], r=[BK(b)], w=[("big", 1)])
            for l in range(1, 4):
                DMA("sp", ncs[:, l - 1, :], rawtm[l * 16:(l + 1) * 16, :], d_big, r=[("big", 1)])
            P.barrier()

            SEQM = cb[:, B_SEQM:B_SEQM + 1024].rearrange("p (s c) -> p s c", s=16)
            SEQINDB = cb[0:64, B_SEQINDB:B_SEQINDB + 16]

            def prefetch_A_weights():
                PF["d_wA"] = P.dsem("wA")
                wq_p = abf[:, 2 * O_T:2 * O_T + 4096].rearrange("p (k g c) -> p k g c", k=8, g=4)
                wk_p = abf[:, 2 * (O_T + 2048):2 * (O_T + 2048) + 1024].rearrange("p (k c) -> p k c", k=8)
                wv_p = abf[:, 2 * (O_T + 2560):2 * (O_T + 2560) + 1024].rearrange("p (k c) -> p k c", k=8)
                wsrc_p = w_in[:, 0:512].rearrange("(kc p) (kv g d) -> p kc kv g d", p=128, kv=2, g=4)
                for g in range(4):
                    for kv in range(2):
                        DMA("pool", wq_p[:, :, g, kv * 64:(kv + 1) * 64], wsrc_p[:, :, kv, g, :], PF["d_wA"], w=[("wq", g)])
                DMA("pool", wk_p, w_in[:, 512:640].rearrange("(kc p) n -> p kc n", p=128), PF["d_wA"], w=["wk"])
                DMA("pool", wv_p, w_in[:, 640:768].rearrange("(kc p) n -> p kc n", p=128), PF["d_wA"], w=["wv"])

            def seq_loop():
                MM(banks[3][0:16, 32:48], dA_all[0:64, 16, :], cf[0:64, C_SEQIND:C_SEQIND + 16], r=["dA"] + CONST, w=[BK(3)])
                CP("act", cdl[0:16, :], banks[3][0:16, 32:48], r=[BK(3)], w=["cdl"])
                selblk = cf[0:16, C_SELBLK:C_SELBLK + 1024].rearrange("p (a c) -> p a c", a=8)
                for blk in range(8):
                    MM(banks[3][:, 64 + blk * 16:64 + (blk + 1) * 16], selblk[:, blk, :], cdl[0:16, :], r=["cdl"] + CONST, w=[BK(3)])
                ACT(cdcol, banks[3][:, 64:192].rearrange("p (a s) -> p a s", a=8), AF.Exp, r=[BK(3)], w=["cdcol"])

                def load_h0(s_):
                    DMA("sp", h0[s_ % 2], sssm[s_].rearrange("(blk q) n -> q blk n", q=128), d_h0[s_ % 2], w=[("h0", s_ % 2)])

                load_h0(0)
                for s_ in range(16):
                    sl = s_ % 2
                    if s_ + 1 < 16:
                        load_h0(s_ + 1)
                    if s_ == 8:
                        prefetch_A_weights()
                    TT("pool", CTm[sl], xbss[:, 10:12, :], SEQM[:, s_, :].unsqueeze(1).to_broadcast([128, 2, 64]), ALU.mult,
                       r=XTS + CONST, w=[("CTm", sl)])
                    TS("pool", Bm[sl][0:64].rearrange("p g n -> p (g n)"), B_tm[0:64].rearrange("p g n -> p (g n)"), SEQINDB[:, s_:s_ + 1], None, ALU.mult,
                       r=["B_tm"] + CONST, w=[("Bm", sl)])
                    for half in range(2):
                        b = half
                        for a in range(4):
                            TR(banks[b][:, a * 128:(a + 1) * 128], h0[sl][:, half * 4 + a, :], idf, r=[("h0", sl)] + CONST, w=[BK(b)])
                        CP("act", h0T[sl][:, half * 512:(half + 1) * 512], banks[b][:, :], r=[BK(b)], w=[("h0T", sl, half)])
                    for g in range(2):
                        MM(banks[4 + g][0:64, :], CTm[sl][:, g, :], h0T[sl][:, g * 512:(g + 1) * 512], start=(s_ == 0), stop=(s_ == 15),
                           r=[("CTm", sl), ("h0T", sl, g)], w=[BK(4 + g)])
                    for half in range(2):
                        b = 6 + half
                        for a in range(4):
                            blk = half * 4 + a
                            MM(banks[b][:, a * 128:(a + 1) * 128], xdd[0:64, blk * 128:(blk + 1) * 128], Bm[sl][0:64, blk // 4, :],
                               r=["xdd", ("Bm", sl)], w=[BK(b)])
                    for blk in range(8):
                        b = 6 + blk // 4
                        STT("dve", h0[sl][:, blk, :], h0[sl][:, blk, :], cdcol[:, blk, s_:s_ + 1], banks[b][:, (blk % 4) * 128:(blk % 4 + 1) * 128],
                            ALU.mult, ALU.add, r=[("h0", sl), "cdcol", BK(b)], w=[("h0", sl)])
                    DMA("sp", nss[s_].rearrange("(blk q) n -> q blk n", q=128), h0[sl], d_h0[sl], r=[("h0", sl)], w=[("h0", sl)])

            front_decay_a(16, 64, sample=True)
            front_decay_b(16, 64, sample=True)
            ssd_chunk(64, 16,
                      lambda blk: xbss[:, blk, :],
                      lambda g: xbss[:, 8 + g, :],
                      lambda g: xbss[:, 10 + g, :],
                      XTS, SC0, sample=True, seq_loop=seq_loop)
            dump("moT", moT[:, :, :], [("moT", i) for i in range(17)], BF16)

        MO_ALL = [("moT", i) for i in range(17)]

        def phase_C():
            A = Bump(arena, abf, [(R_T[0], C0_OFF)])
            wo_p = abf[:, 2 * WO_OFF:2 * WO_OFF + 8 * 1024].rearrange("p (k c) -> p k c", k=8)
            PF["d_wD"] = P.dsem("wD")
            wga0, wgb0, woa0, wob0 = c0_views()
            wga = [wga0, A.bf(8 * 256).rearrange("p (k c) -> p k c", k=8)]
            wgb = [wgb0, A.bf(8 * 256).rearrange("p (k c) -> p k c", k=8)]
            woa = [woa0, A.bf(4 * 256).rearrange("p (k c) -> p k c", k=4)]
            wob = [wob0, A.bf(8 * 256).rearrange("p (k c) -> p k c", k=8)]
            tA = [A.bf(512) for _ in range(2)]
            tB = [A.bf(512) for _ in range(2)]
            u = [A.f32(512) for _ in range(2)]
            v = [A.f32(512) for _ in range(2)]
            d_w = [PF["d_wC0"], P.dsem("wC1")]
            woa_src = w_oa.rearrange("(kv g d) n -> kv d g n", kv=2, g=4)
            def load_c(qi):
                ws = qi % 2
                cs = slice(qi * 256, (qi + 1) * 256)
                DMA("pool", wga[ws], w_in[:, 3344 + qi * 256:3344 + (qi + 1) * 256].rearrange("(kc p) n -> p kc n", p=128), d_w[ws], w=[("wC", ws)])
                DMA("pool", wgb[ws], w_in[:, 4368 + qi * 256:4368 + (qi + 1) * 256].rearrange("(kc p) n -> p kc n", p=128), d_w[ws], w=[("wC", ws)])
                for kv in range(2):
                    DMA("pool", woa[ws][kv * 64:(kv + 1) * 64, :, :], woa_src[kv][:, :, cs], d_w[ws], w=[("wC", ws)])
                DMA("pool", wob[ws], w_ob[:, cs].rearrange("(kc p) n -> p kc n", p=128), d_w[ws], w=[("wC", ws)])

            load_c(1)
            it = 0
            for qi in range(4):
                ws = qi % 2
                if qi >= 1 and qi + 1 < 4:
                    load_c(qi + 1)
                if qi == 3:
                    DMA("pool", wo_p, w_o.rearrange("(kc p) n -> p kc n", p=128), PF["d_wD"], w=["wo"])
                for blk in range(2):
                    dm = qi * 2 + blk
                    bs = slice(blk * 128, (blk + 1) * 128)
                    for T in range(5):
                        c0 = T * 512
                        n = 512 if T < 4 else 64
                        s = it % 2
                        it += 1
                        bgA, bpA, bgB, bpB = 4 * s, 4 * s + 1, 4 * s + 2, 4 * s + 3
                        htk = hT_toks(c0, n)
                        tk = list(range(c0 // 128, (c0 + n + 127) // 128))
                        for kc in range(8):
                            MM(banks[bgA][:, 0:n], wga[ws][:, kc, bs], hT[:, kc, c0:c0 + n], start=(kc == 0), stop=(kc == 7), r=[("wC", ws)] + htk, w=[BK(bgA)])
                        for g in range(4):
                            MM(banks[bpA][:, 0:n], woa[ws][:, g, bs], aoT[:, g, c0:c0 + n], start=(g == 0), stop=(g == 3),
                               r=[("wC", ws)] + [("aoT", t) for t in tk], w=[BK(bpA)])
                        for kc in range(8):
                            MM(banks[bgB][:, 0:n], wgb[ws][:, kc, bs], hT[:, kc, c0:c0 + n], start=(kc == 0), stop=(kc == 7), r=[("wC", ws)] + htk, w=[BK(bgB)])
                        for kc in range(8):
                            MM(banks[bpB][:, 0:n], wob[ws][:, kc, bs], moT[:, kc, c0:c0 + n], start=(kc == 0), stop=(kc == 7),
                               r=[("wC", ws)] + [("moT", t) for t in tk], w=[BK(bpB)])
                        ACT(tA[s][:, 0:n], banks[bgA][:, 0:n], AF.Tanh, r=[BK(bgA)], w=[("tA", s)], scale=0.5)
                        ACT(tB[s][:, 0:n], banks[bgB][:, 0:n], AF.Tanh, r=[BK(bgB)], w=[("tB", s)], scale=0.5)
                        STT("dve", u[s][:, 0:n], tA[s][:, 0:n], 1.0, banks[bpA][:, 0:n], ALU.add, ALU.mult, r=[("tA", s), BK(bpA)], w=[("u", s)])
                        STT("dve", v[s][:, 0:n], tB[s][:, 0:n], 1.0, banks[bpB][:, 0:n], ALU.add, ALU.mult, r=[("tB", s), BK(bpB)], w=[("v", s)])
                        TT("dve", mgT[:, dm, c0:c0 + n], u[s][:, 0:n], v[s][:, 0:n], ALU.add, r=[("u", s), ("v", s)], w=[("mgT", dm, T)])
            dump("mgT", mgT[:, :, :], [("mgT", dm, T) for dm in range(8) for T in range(5)], BF16)

        def mg_toks(t):
            return [("mgT", dm, t // 4) for dm in range(8)]

        def phase_D():
            A = Bump(arena, abf, [(R_T[0], E0_OFF)])
            wo = abf[:, 2 * WO_OFF:2 * WO_OFF + 8 * 1024].rearrange("p (k c) -> p k c", k=8)
            wup0 = abf[:, 2 * E0_OFF:2 * E0_OFF + 4096].rearrange("p (k c) -> p k c", k=8)
            wdn0 = abf[:, 2 * E0_OFF + 4096:2 * E0_OFF + 8192].rearrange("p (k c) -> p k c", k=4)
            PF["d_wE0"] = P.dsem("wE0")
            nw = A.f32(1024)
            xt = [A.f32(1024) for _ in range(2)]
            hb = [A.bf(1024) for _ in range(2)]
            junk = A.bf(1024)
            st_ = A.f32(16)
            d_x = [P.dsem("xD0"), P.dsem("xD1")]
            DMA("pool", wup0, w_up[:, 0:512].rearrange("(kc q) n -> q kc n", q=128), PF["d_wE0"], w=[("wE", 0)])
            DMA("pool", wdn0, w_down[0:512, :].rearrange("(f q) n -> q f n", q=128), PF["d_wE0"], w=[("wE", 0)])
            DMA("sp", nw, vecs[:, 1, :], P.dsem("nwD"), w=["nw2"])
            def stage1(t):
                s = t % 2
                rows = rows_of(t)
                DMA("sp", xt[s][0:rows], xin[t * 128:t * 128 + rows, :], d_x[s], w=[("xtD", s)])
                for half in range(2):
                    b = 2 * s + half
                    for kc in range(8):
                        MM(banks[b][0:rows, :], mgT[:, kc, t * 128:t * 128 + rows], wo[:, kc, half * 512:(half + 1) * 512], start=(kc == 0), stop=(kc == 7),
                           r=["wo"] + mg_toks(t), w=[BK(b)])
                    STT("dve", x1[0:rows, t, half * 512:(half + 1) * 512], banks[b][0:rows, :], 0.5, xt[s][0:rows, half * 512:(half + 1) * 512],
                        ALU.mult, ALU.add, r=[BK(b), ("xtD", s)], w=[("x1", t)])
                s3 = t % 3
                ss = st_[:, s3:s3 + 1]
                ms = st_[:, 4 + s3:5 + s3]
                rs = st_[:, 8 + s3:9 + s3]
                ACT(junk[0:rows], x1[0:rows, t, :], AF.Square, r=[("x1", t)], w=["junkD", ("ssD", s3)], accum_out=ss[0:rows])
                rstd_from_ss(ss[0:rows], ms[0:rows], rs[0:rows], 1.0 / D, EPS, [("ssD", s3)], ("msD", s3), ("rsD", s3))

            def stage2a(t):
                s = t % 2
                rows = rows_of(t)
                s3 = t % 3
                rs = st_[:, 8 + s3:9 + s3]
                STT("dve", hb[s][0:rows], x1[0:rows, t, :], rs[0:rows], nw[0:rows], ALU.mult, ALU.mult, r=[("x1", t), ("rsD", s3), "nw2"], w=[("hbD", s)])

            def stage2b(t):
                s = t % 2
                rows = rows_of(t)
                b = 4 + s
                pv = bankb[b][:, :].rearrange("p (a c) -> p a c", a=8)
                for kc in range(8):
                    TR(pv[:, kc, 0:rows], hb[s][0:rows, kc * 128:(kc + 1) * 128], idb[0:rows, 0:rows], r=[("hbD", s)] + CONST, w=[BK(b)])
                CP("act", hT[:, :, t * 128:t * 128 + rows], pv[:, :, 0:rows], r=[BK(b)], w=[("hmT", t)])

            for t in range(NTL + 2):
                if t >= 2:
                    stage2a(t - 2)
                if t < NTL:
                    stage1(t)
                if t >= 2:
                    stage2b(t - 2)
            dump("x1", x1[:, :, :], [("x1", t) for t in range(NTL)])

        def hm_toks(c0, n):
            return [("hmT", t) for t in range(c0 // 128, (c0 + n + 127) // 128)]

        def phase_E():
            A = Bump(arena, abf, [(R_T[0], E0_OFF), R_G])
            NP = 8
            wup = [abf[:, 2 * E0_OFF:2 * E0_OFF + 4096].rearrange("p (k c) -> p k c", k=8), A.bf(8 * 512).rearrange("p (k c) -> p k c", k=8)]
            wdn = [abf[:, 2 * E0_OFF + 4096:2 * E0_OFF + 8192].rearrange("p (k c) -> p k c", k=4), A.bf(4 * 1024).rearrange("p (k c) -> p k c", k=4)]
            rl = [A.bf(512) for _ in range(2)]
            upT = [A.bf(4 * 512).rearrange("p (k c) -> p k c", k=4) for _ in range(2)]
            nw = A.f32(1024)
            yt = [A.f32(1024) for _ in range(2)]
            junk = A.bf(1024)
            st_ = A.f32(16)
            d_w = [PF["d_wE0"], P.dsem("wE1")]
            d_n = P.dsem("nwE")
            d_y = [P.dsem("yE0"), P.dsem("yE1")]
            DMA("sp", nw, vecs[:, 2, :], d_n, w=["nw3"])
            def load_w(p):
                ws = p % 2
                DMA("pool", wup[ws], w_up[:, p * 512:(p + 1) * 512].rearrange("(kc q) n -> q kc n", q=128), d_w[ws], w=[("wE", ws)])
                DMA("pool", wdn[ws], w_down[p * 512:(p + 1) * 512, :].rearrange("(f q) n -> q f n", q=128), d_w[ws], w=[("wE", ws)])

            tiles = [(p, T) for p in range(NP) for T in range(5)]
            dn_ctr = [0]

            def emit_up(k):
                p, T = tiles[k]
                ws, us = p % 2, k % 2
                c0 = T * 512
                n = 512 if T < 4 else 64
                for ffc in range(4):
                    b = ffc % 2
                    for kc in range(8):
                        MM(banks[b][:, 0:n], wup[ws][:, kc, ffc * 128:(ffc + 1) * 128], hT[:, kc, c0:c0 + n], start=(kc == 0), stop=(kc == 7),
                           r=[("wE", ws)] + hm_toks(c0, n), w=[BK(b)])
                    ACT(rl[b][:, 0:n], banks[b][:, 0:n], AF.Relu, r=[BK(b)], w=[("rl", b)])
                    TT("dve", upT[us][:, ffc, 0:n], rl[b][:, 0:n], rl[b][:, 0:n], ALU.mult, r=[("rl", b)], w=[("upT", us, ffc)])

            def emit_down(k):
                p, T = tiles[k]
                ws, us = p % 2, k % 2
                c0 = T * 512
                n = 512 if T < 4 else 64
                for t in range(c0 // 128, (c0 + n + 127) // 128):
                    rows = rows_of(t)
                    j0 = t * 128 - c0
                    for half in range(2):
                        b = 2 + (dn_ctr[0] % 4)
                        dn_ctr[0] += 1
                        for ffc in range(4):
                            MM(banks[b][0:rows, :], upT[us][:, ffc, j0:j0 + rows], wdn[ws][:, ffc, half * 512:(half + 1) * 512], start=(ffc == 0), stop=(ffc == 3),
                               r=[("wE", ws), ("upT", us, ffc)], w=[BK(b)])
                        hs = slice(half * 512, (half + 1) * 512)
                        TT("dve", x1[0:rows, t, hs], x1[0:rows, t, hs], banks[b][0:rows, :], ALU.add, r=[("x1", t), BK(b)], w=[("x1", t)])
                    if p == NP - 1:
                        s = t % 2
                        ss = st_[:, s:s + 1]
                        ms = st_[:, 4 + s:5 + s]
                        rs = st_[:, 8 + s:9 + s]
                        ACT(junk[0:rows], x1[0:rows, t, :], AF.Square, r=[("x1", t)], w=["junkE", ("ssE", s)], accum_out=ss[0:rows])
                        rstd_from_ss(ss[0:rows], ms[0:rows], rs[0:rows], 1.0 / D, EPS, [("ssE", s)], ("msE", s), ("rsE", s))
                        STT("dve", yt[s][0:rows], x1[0:rows, t, :], rs[0:rows], nw[0:rows], ALU.mult, ALU.mult, r=[("x1", t), ("rsE", s), "nw3"], w=[("yt", s)])
                        DMA("sp", yout[t * 128:t * 128 + rows, :], yt[s][0:rows], d_y[s], r=[("yt", s)])
                if T == 4 and p + 2 < NP:
                    load_w(p + 2)

            load_w(1)
            emit_up(0)
            for k in range(len(tiles)):
                if k + 1 < len(tiles):
                    emit_up(k + 1)
                emit_down(k)

        phases = [("N", phase_N), ("B", phase_B), ("A", phase_A), ("C", phase_C), ("D", phase_D), ("E", phase_E)]
        for name, fn in phases:
            try:
                fn()
            except StopBuild:
                P.barrier()
                break
            P.barrier()
            if stop_after == name:
                break
        P.emit()
    return nc, dbg_outs


def _constants():
    f = np.zeros((128, NF), np.float32)
    i = np.arange(128)
    f[:, C_IDF:C_IDF + 128] = np.eye(128, dtype=np.float32)
    f[:, C_UI:C_UI + 128] = (i[:, None] <= i[None, :])
    f[:, C_SL:C_SL + 128] = (i[:, None] > i[None, :])
    f[:, C_ONES:C_ONES + 128] = 1.0
    tok = np.arange(64)
    l_, s_ = tok // 16, tok % 16
    same = s_[:, None] == s_[None, :]
    f[0:64, C_UIS:C_UIS + 64] = same & (l_[:, None] <= l_[None, :])
    f[0:64, C_SLS:C_SLS + 64] = same & (l_[:, None] > l_[None, :])
    f[0:64, C_SEQIND:C_SEQIND + 16] = (s_[:, None] == np.arange(16)[None, :])
    sel = np.zeros((16, 8, 128), np.float32)
    for h in range(16):
        sel[h, h // 2, (h % 2) * 64:(h % 2 + 1) * 64] = 1.0
    f[0:16, C_SELBLK:C_SELBLK + 1024] = sel.reshape(16, 1024)
    b = np.zeros((128, NB), np.float32)
    b[:, B_IDB:B_IDB + 128] = np.eye(128)
    b[:, B_MP:B_MP + 128] = (i[:, None] >= i[None, :])
    b[:, B_MC:B_MC + 128] = (i[:, None] <= i[None, :])
    b[:, B_MCACHE:B_MCACHE + 4] = (i[:, None] >= np.arange(4)[None, :])
    b[0:64, B_MNEW:B_MNEW + 64] = same & (l_[:, None] <= l_[None, :])
    b[:, B_UIB:B_UIB + 128] = (i[:, None] <= i[None, :])
    b[0:64, B_UISB:B_UISB + 64] = same & (l_[:, None] <= l_[None, :])
    op = np.zeros((128, 2, 128), np.float32)
    op[:, 0, 0:64] = 1.0
    op[:, 1, 64:128] = 1.0
    b[:, B_ONESPAD:B_ONESPAD + 256] = op.reshape(128, 256)
    seqm = (np.arange(16)[:, None] == s_[None, :]).astype(np.float32)
    b[:, B_SEQM:B_SEQM + 1024] = np.broadcast_to(seqm.reshape(1, 1024), (128, 1024))
    b[0:64, B_SEQINDB:B_SEQINDB + 16] = (s_[:, None] == np.arange(16)[None, :])
    return f, b.astype(ml_dtypes.bfloat16)


_CACHE = {}


def _get_program(stop_after=None, dbg=()):
    key = (stop_after, tuple(dbg))
    if key not in _CACHE:
        _CACHE[key] = build_program(stop_after, dbg)
    return _CACHE[key]


def make_in_maps(inp):
    f32 = np.float32
    g = lambda k: np.ascontiguousarray(np.asarray(inp[k], dtype=f32))
    x_prompt, x_sample = g("x_prompt"), g("x_sample")
    ck, cv = g("cache_swa_k")[0], g("cache_swa_v")[0]
    sconv, sssm = g("state_conv")[0], g("state_ssm")[0]
    cf, cbb = _constants()
    hv = np.stack([g("dt_bias")[0], g("a_log")[0], g("d_skip")[0]]).reshape(1, 48)
    cf[:, C_HV:C_HV + 48] = hv
    cw = g("conv_w")[0]
    cf[:, C_CONVW:C_CONVW + 48] = cw.reshape(4, 12, 128).transpose(2, 1, 0).reshape(128, 48)
    cf[:, C_CONVB:C_CONVB + 12] = g("conv_b")[0].reshape(12, 128).T
    sk = g("sinks")[0]
    cf[:, C_SINKS:C_SINKS + 4] = sk.reshape(2, 4)[np.arange(128) // 64]
    vecs = np.stack([g("norm1")[0], g("norm2")[0], g("final_norm"), g("ssm_norm")[0]])
    vecs = np.ascontiguousarray(np.broadcast_to(vecs[None], (128, 4, D)))
    shared = dict(w_in=g("w_in")[0], w_oa=g("w_oa")[0], w_ob=g("w_ob")[0], w_o=g("w_o")[0], w_up=g("w_up")[0],
                  w_down=g("w_down")[0], vecs=vecs, cstf=cf, cstb=cbb)
    maps = []
    for c in range(NCORES):
        ss = slice(16 * c, 16 * c + 16)
        xs = x_sample[ss].transpose(1, 0, 2).reshape(64, D)
        m = dict(shared)
        m["xin"] = np.ascontiguousarray(np.concatenate([x_prompt[c], xs], axis=0))
        m["ck"] = np.ascontiguousarray(ck[ss].reshape(16, 128, 128))
        m["cv"] = np.ascontiguousarray(cv[ss].reshape(16, 128, 128))
        m["sconv"] = np.ascontiguousarray(sconv[ss])
        m["sssm"] = np.ascontiguousarray(sssm[ss].reshape(16, 1024, 128))
        maps.append(m)
    return maps


def kernel(**inputs):
    nc, _ = _get_program()
    maps = make_in_maps(inputs)
    res = run_bass_kernel_spmd(nc, maps, core_ids=list(range(NCORES)))
    R = res.results
    f32 = np.float32
    y_prompt = np.stack([np.asarray(R[c]["yout"], f32)[0:2048] for c in range(NCORES)])
    y_sample = np.concatenate([np.asarray(R[c]["yout"], f32)[2048:].reshape(4, 16, D).transpose(1, 0, 2) for c in range(NCORES)], axis=0)
    nkp = np.stack([np.asarray(R[c]["nkp"], f32).reshape(128, 2, 64) for c in range(NCORES)])[None]
    nvp = np.stack([np.asarray(R[c]["nvp"], f32).reshape(128, 2, 64) for c in range(NCORES)])[None]
    ncp = np.stack([np.asarray(R[c]["ncp"], f32) for c in range(NCORES)])[None]
    nsp = np.stack([np.asarray(R[c]["nsp"], f32).reshape(16, 64, 128) for c in range(NCORES)])[None]
    nks = np.concatenate([np.asarray(R[c]["nks"], f32).reshape(16, 128, 2, 64) for c in range(NCORES)], axis=0)[None]
    nvs = np.concatenate([np.asarray(R[c]["nvs"], f32).reshape(16, 128, 2, 64) for c in range(NCORES)], axis=0)[None]
    ncs = np.concatenate([np.asarray(R[c]["ncs"], f32) for c in range(NCORES)], axis=0)[None]
    nss = np.concatenate([np.asarray(R[c]["nss"], f32).reshape(16, 16, 64, 128) for c in range(NCORES)], axis=0)[None]
    return (y_prompt, y_sample, nkp, nvp, ncp, nsp, nks, nvs, ncs, nss)
```

```python
import numpy as np
import ml_dtypes
from contextlib import ExitStack
import concourse.bass as bass
import concourse.mybir as mybir
from concourse.bass_utils import run_bass_kernel_spmd

F32 = mybir.dt.float32
BF16 = mybir.dt.bfloat16
AF = mybir.ActivationFunctionType
ALU = mybir.AluOpType

NCORES = 8
D = 1024
NT = 2112
NTL = 17
SC0 = 2048
EPS = 1e-6
INW = 5392
ARENA_WORDS = 53000

C_IDF, C_UI, C_SL, C_ONES, C_UIS, C_SLS, C_SEQIND, C_SELBLK = 0, 128, 256, 384, 512, 576, 640, 656
C_HV, C_CONVW, C_CONVB, C_SINKS, NF = 1680, 1728, 1776, 1788, 1792
B_IDB, B_MP, B_MC, B_MCACHE, B_MNEW, B_UIB, B_UISB, B_ONESPAD, B_SEQM, B_SEQINDB, NB = \
    0, 128, 256, 384, 388, 452, 580, 644, 900, 1924, 1940


class DSem:
    def __init__(self, sem, name):
        self.sem = sem
        self.count = 0
        self.name = name
        self.open = None

    def add(self):
        self.count += 16
        if self.open is None:
            self.open = {"final": None}
        return self.open

    def seal(self):
        if self.open is not None:
            self.open["final"] = self.count
            self.open = None


class Prog:
    ENG_NAMES = ("pe", "act", "dve", "pool", "sp")

    def __init__(self, nc, stack):
        self.nc = nc
        self.stack = stack
        self.ops = []
        self.last_w = {}
        self.readers = {}
        self.esem = {e: stack.enter_context(nc.semaphore("s_" + e)) for e in ("pe", "act", "dve", "pool")}
        self.dsems = []
        self.last_on_eng = {e: None for e in self.ENG_NAMES}

    def dsem(self, name):
        d = DSem(self.stack.enter_context(self.nc.semaphore("d_" + name)), name)
        self.dsems.append(d)
        return d

    def _deps(self, r, w, eng=None):
        deps = set()
        for t in r:
            if t in self.last_w:
                deps.add(self.last_w[t])
            if isinstance(t, tuple) and t[0] == "B":
                for x in self.readers.get(t, ()):
                    if self.ops[x]["eng"] != eng:
                        deps.add(x)
        for t in w:
            if t in self.last_w:
                deps.add(self.last_w[t])
            for x in self.readers.get(t, ()):
                deps.add(x)
        for d in deps:
            p = self.ops[d]
            if p["kind"] == "dma" and p["grp"]["final"] is None:
                p["dsem"].seal()
        return deps

    def _commit(self, idx, r, w):
        for t in r:
            self.readers.setdefault(t, []).append(idx)
        for t in w:
            self.last_w[t] = idx
            self.readers[t] = []

    def op(self, eng, fn, r=(), w=()):
        idx = len(self.ops)
        deps = self._deps(r, w, eng)
        self.ops.append(dict(eng=eng, fn=fn, deps=deps, kind="c", inc=False, cnt=0))
        self._commit(idx, r, w)
        self.last_on_eng[eng] = idx
        return idx

    def dma(self, eng, fn, r=(), w=(), dsem=None):
        idx = len(self.ops)
        deps = self._deps(r, w, eng)
        grp = dsem.add()
        self.ops.append(dict(eng=eng, fn=fn, deps=deps, kind="dma", dsem=dsem, grp=grp))
        self._commit(idx, r, w)
        return idx

    def barrier(self):
        deps = set()
        for e in ("pe", "act", "dve", "pool"):
            if self.last_on_eng[e] is not None:
                deps.add(self.last_on_eng[e])
        for d in self.dsems:
            d.seal()
        dvals = [(d, d.count) for d in self.dsems if d.count > 0]
        for e in self.ENG_NAMES:
            self.ops.append(dict(eng=e, fn=None, deps=set(deps), kind="bar", dvals=dvals, inc=False, cnt=0))

    def finalize(self):
        ops = self.ops
        for d in self.dsems:
            d.seal()
        for o in ops:
            for d in o["deps"]:
                p = ops[d]
                if p["kind"] == "c":
                    if p["eng"] == "pe" and o["eng"] == "pe" and o["kind"] == "c":
                        continue
                    p["inc"] = True
        cnt = {e: 0 for e in self.ENG_NAMES}
        for o in ops:
            if o["kind"] == "c":
                if o["inc"]:
                    cnt[o["eng"]] += 1
                o["cnt"] = cnt[o["eng"]]

    def emit_engine(self, ename, engine):
        ops = self.ops
        waited = {}

        def wait(sem, key, val):
            if waited.get(key, 0) < val:
                engine.wait_ge(sem, val)
                waited[key] = val

        for o in ops:
            if o["eng"] != ename:
                continue
            for d in sorted(o["deps"]):
                p = ops[d]
                if p["kind"] == "c":
                    if p["eng"] == "pe" and ename == "pe" and o["kind"] == "c":
                        continue
                    wait(self.esem[p["eng"]], p["eng"], p["cnt"])
                elif p["kind"] == "dma":
                    wait(p["dsem"].sem, "d_" + p["dsem"].name, p["grp"]["final"])
            if o["kind"] == "bar":
                for d, v in o["dvals"]:
                    wait(d.sem, "d_" + d.name, v)
                continue
            ins = o["fn"](engine)
            if o["kind"] == "dma":
                ins.then_inc(o["dsem"].sem, 16)
            elif o["inc"]:
                ins.then_inc(self.esem[ename], 1)

    def emit(self):
        self.finalize()
        nc = self.nc
        with nc.Block() as block:
            @block.sync
            def _(e):
                self.emit_engine("sp", e)

            @block.tensor
            def _(e):
                self.emit_engine("pe", e)

            @block.scalar
            def _(e):
                self.emit_engine("act", e)

            @block.vector
            def _(e):
                self.emit_engine("dve", e)

            @block.gpsimd
            def _(e):
                self.emit_engine("pool", e)


class Bump:
    def __init__(self, arena, abf, regions):
        self.arena = arena
        self.abf = abf
        self.regs = [list(r) for r in regions]

    def _take(self, w):
        w = (w + 7) // 8 * 8
        for r in self.regs:
            if r[1] - r[0] >= w:
                off = r[0]
                r[0] += w
                return off
        raise RuntimeError("arena region overflow: need %d words, regions %s" % (w, self.regs))

    def f32(self, n):
        off = self._take(n)
        return self.arena[:, off:off + n]

    def bf(self, n):
        off = self._take((n + 1) // 2)
        return self.abf[:, 2 * off:2 * off + n]

    def take(self, w):
        return self._take(w)

    def fview(self, off, n):
        return self.arena[:, off:off + n]

    def bview(self, off, n):
        return self.abf[:, 2 * off:2 * off + n]

    def free_words(self):
        return sum(r[1] - r[0] for r in self.regs)


class StopBuild(Exception):
    pass


def rows_of(t):
    return 128 if t < 16 else 64


def build_program(stop_after=None, dbg=()):
    nc = bass.Bass("TRN2", target_bir_lowering=False)

    def din(name, shape, dt=F32):
        return nc.dram_tensor(name, list(shape), dt, kind="ExternalInput").ap()

    def dout(name, shape, dt=F32):
        return nc.dram_tensor(name, list(shape), dt, kind="ExternalOutput").ap()

    xin = din("xin", [NT, D])
    ck = din("ck", [16, 128, 128])
    cv = din("cv", [16, 128, 128])
    sconv = din("sconv", [16, 3, 1536])
    sssm = din("sssm", [16, 1024, 128])
    w_in = din("w_in", [D, INW])
    w_oa = din("w_oa", [512, D])
    w_ob = din("w_ob", [D, D])
    w_o = din("w_o", [D, D])
    w_up = din("w_up", [D, 4096])
    w_down = din("w_down", [4096, D])
    vecs = din("vecs", [128, 4, D])
    cstf = din("cstf", [128, NF])
    cstb = din("cstb", [128, NB], BF16)

    yout = dout("yout", [NT, D])
    nkp = dout("nkp", [128, 128])
    nvp = dout("nvp", [128, 128])
    ncp = dout("ncp", [3, 1536])
    nsp = dout("nsp", [1024, 128])
    nks = dout("nks", [16, 128, 128])
    nvs = dout("nvs", [16, 128, 128])
    ncs = dout("ncs", [16, 3, 1536])
    nss = dout("nss", [16, 1024, 128])

    dbg_outs = {}

    with ExitStack() as st:
        st.enter_context(nc.allow_low_precision("bf16 matmul operands, fp32 accumulation"))
        st.enter_context(nc.allow_non_contiguous_dma("strided weight / state layouts"))
        arena = st.enter_context(nc.sbuf_tensor("arena", [128, ARENA_WORDS], F32))
        abf = arena.bitcast(BF16)
        banks = [st.enter_context(nc.psum_tensor("bank%d" % i, [128, 512], F32)) for i in range(8)]
        bankb = [b.bitcast(BF16) for b in banks]
        P = Prog(nc, st)

        def MM(out, lhsT, rhs, start=True, stop=True, r=(), w=()):
            P.op("pe", lambda e: e.matmul(out, lhsT=lhsT, rhs=rhs, start=start, stop=stop), r, w)

        def TR(out, in_, ident, r=(), w=()):
            P.op("pe", lambda e: e.transpose(out=out, in_=in_, identity=ident), r, w)

        def ACT(out, in_, func, r=(), w=(), bias=None, scale=1.0, accum_out=None):
            def f(e):
                kw = dict(scale=scale)
                if bias is not None:
                    kw["bias"] = bias
                if accum_out is not None:
                    kw["accum_out"] = accum_out
                return e.activation(out=out, in_=in_, func=func, **kw)
            P.op("act", f, r, w)

        def CP(eng, out, in_, r=(), w=()):
            if eng == "act":
                P.op("act", lambda e: e.copy(out=out, in_=in_), r, w)
            else:
                P.op(eng, lambda e: e.tensor_copy(out=out, in_=in_), r, w)

        def TT(eng, out, in0, in1, op, r=(), w=()):
            P.op(eng, lambda e: e.tensor_tensor(out=out, in0=in0, in1=in1, op=op), r, w)

        def TS(eng, out, in0, s1, s2, op0, op1=None, r=(), w=()):
            if op1 is None:
                P.op(eng, lambda e: e.tensor_scalar(out=out, in0=in0, scalar1=s1, scalar2=None, op0=op0), r, w)
            else:
                P.op(eng, lambda e: e.tensor_scalar(out=out, in0=in0, scalar1=s1, scalar2=s2, op0=op0, op1=op1), r, w)

        def STT(eng, out, in0, scalar, in1, op0, op1, r=(), w=()):
            P.op(eng, lambda e: e.scalar_tensor_tensor(out=out, in0=in0, scalar=scalar, in1=in1, op0=op0, op1=op1), r, w)

        def MEMSET(eng, ap, val, w=()):
            P.op(eng, lambda e: e.memset(ap, val), (), w)

        def DMA(eng, out, in_, dsem, r=(), w=()):
            P.dma(eng, lambda e: e.dma_start(out=out, in_=in_), r, w, dsem)

        def BK(i):
            return ("B", i)

        def CHECK(tag):
            if stop_after == tag:
                raise StopBuild()

        d_out = P.dsem("out")
        d_dbg = P.dsem("dbg")

        def dump(name, ap, tokens, dt=F32):
            if name not in dbg:
                return
            shp = list(ap.shape)
            o = dout("dbg_" + name, shp, dt)
            dbg_outs[name] = shp
            DMA("sp", o, ap, d_dbg, r=tokens)

        cf = arena[:, 0:NF]
        cbw = (NB + 1) // 2
        cb = abf[:, 2 * NF:2 * NF + NB]
        o_small = NF + cbw + 4
        small = arena[:, o_small:o_small + 128]
        O_H = 2944
        assert o_small + 128 <= O_H
        O_X = O_H + 8448
        O_G = O_X + 17408
        O_T = O_G + 8448
        hT = abf[:, 2 * O_H:2 * O_H + 8 * NT].rearrange("p (a b) -> p a b", a=8)
        aoT = abf[:, 2 * O_X:2 * O_X + 4 * NT].rearrange("p (a b) -> p a b", a=4)
        moT = abf[:, 2 * (O_X + 4224):2 * (O_X + 4224) + 8 * NT].rearrange("p (a b) -> p a b", a=8)
        mgT = abf[:, 2 * O_G:2 * O_G + 8 * NT].rearrange("p (a b) -> p a b", a=8)
        x1 = arena[:, O_X:O_X + NTL * D].rearrange("p (a b) -> p a b", a=NTL)
        X_SPARE = (O_X + 12672, O_X + 17408)
        R_G = (O_G, O_G + 8448)
        R_T = (O_T, ARENA_WORDS)
        R_H = (O_H, O_H + 8448)

        idf = cf[:, C_IDF:C_IDF + 128]
        idb = cb[:, B_IDB:B_IDB + 128]

        d_c = P.dsem("const")
        DMA("sp", cf, cstf, d_c, w=["cf"])
        DMA("sp", cb, cstb, d_c, w=["cb"])
        CONST = ["cf", "cb"]

        sinkexp = small[:, 0:4]
        a_neg = small[:, 8:24]
        neghalf = small[:, 24:25]
        ACT(sinkexp, cf[:, C_SINKS:C_SINKS + 4], AF.Exp, r=CONST, w=["sinkexp"])
        ACT(a_neg, cf[:, C_HV + 16:C_HV + 32], AF.Exp, r=CONST, w=["a_neg0"])
        TS("dve", a_neg, a_neg, -1.0, None, ALU.mult, r=["a_neg0"], w=["a_neg"])
        MEMSET("pool", neghalf, -0.5, w=["neghalf"])
        dtb_bc = cf[:, C_HV:C_HV + 16]
        dskip_bc = cf[:, C_HV + 32:C_HV + 48]

        bank_ctr = [0]
        PF = {}
        WO_OFF = ARENA_WORDS - 8 - 4096
        E0_OFF = WO_OFF - 4096
        C0_OFF = WO_OFF - 3584

        def c0_views():
            o = 2 * C0_OFF
            wga0 = abf[:, o:o + 2048].rearrange("p (k c) -> p k c", k=8)
            wgb0 = abf[:, o + 2048:o + 4096].rearrange("p (k c) -> p k c", k=8)
            woa0 = abf[:, o + 4096:o + 5120].rearrange("p (k c) -> p k c", k=4)
            wob0 = abf[:, o + 5120:o + 7168].rearrange("p (k c) -> p k c", k=8)
            return wga0, wgb0, woa0, wob0

        def load_c_into(qi, wga_, wgb_, woa_, wob_, dsem_, ws):
            cs = slice(qi * 256, (qi + 1) * 256)
            woa_src = w_oa.rearrange("(kv g d) n -> kv d g n", kv=2, g=4)
            DMA("pool", wga_, w_in[:, 3344 + qi * 256:3344 + (qi + 1) * 256].rearrange("(kc p) n -> p kc n", p=128), dsem_, w=[("wC", ws)])
            DMA("pool", wgb_, w_in[:, 4368 + qi * 256:4368 + (qi + 1) * 256].rearrange("(kc p) n -> p kc n", p=128), dsem_, w=[("wC", ws)])
            for kv in range(2):
                DMA("pool", woa_[kv * 64:(kv + 1) * 64, :, :], woa_src[kv][:, :, cs], dsem_, w=[("wC", ws)])
            DMA("pool", wob_, w_ob[:, cs].rearrange("(kc p) n -> p kc n", p=128), dsem_, w=[("wC", ws)])

        def rstd_from_ss(ss_ap, ms_ap, rstd_ap, inv_n, eps, toks_r, tok_ms, tok_rstd):
            rows_ = ms_ap.shape[0]
            TS("pool", ms_ap, ss_ap, inv_n, eps, ALU.mult, ALU.add, r=toks_r, w=[tok_ms])
            TT("pool", rstd_ap, ms_ap, neghalf[0:rows_], ALU.pow, r=[tok_ms, "neghalf"], w=[tok_rstd])

        def phase_N():
            A = Bump(arena, abf, [R_T, R_G])
            wx_p = A.bf(8 * 1536).rearrange("p (k c) -> p k c", k=8)
            wz_p = A.bf(8 * 1024).rearrange("p (k c) -> p k c", k=8)
            PF["d_wB"] = P.dsem("wB")
            DMA("pool", wx_p, w_in[:, 1792:3328].rearrange("(kc p) n -> p kc n", p=128), PF["d_wB"], w=["wx"])
            nw = A.f32(1024)
            NLAG = 3
            NSL = NLAG + 2
            xt = [A.f32(1024) for _ in range(NSL)]
            hb = [A.bf(1024) for _ in range(2)]
            junk = A.bf(1024)
            st_ = A.f32(16)
            d_nw = P.dsem("nw1")
            d_x = [P.dsem("xN%d" % i) for i in range(NSL)]
            DMA("sp", nw, vecs[:, 0, :], d_nw, w=["nw"])
            def stage1(t):
                s = t % NSL
                rows = rows_of(t)
                DMA("sp", xt[s][0:rows], xin[t * 128:t * 128 + rows, :], d_x[s], w=[("xt", s)])
                ss = st_[:, s:s + 1]
                ms = st_[:, 5 + s:6 + s]
                rs = st_[:, 10 + s:11 + s]
                ACT(junk[0:rows], xt[s][0:rows], AF.Square, r=[("xt", s)], w=["junk", ("ss", s)], accum_out=ss[0:rows])
                rstd_from_ss(ss[0:rows], ms[0:rows], rs[0:rows], 1.0 / D, EPS, [("ss", s)], ("ms", s), ("rs", s))

            def stage2(t):
                s = t % NSL
                h2 = t % 2
                rows = rows_of(t)
                rs = st_[:, 10 + s:11 + s]
                STT("dve", hb[h2][0:rows], xt[s][0:rows], rs[0:rows], nw[0:rows], ALU.mult, ALU.mult,
                    r=[("xt", s), ("rs", s), "nw"], w=[("hb", h2)])
                b = t % 2
                pv = bankb[b][:, :].rearrange("p (a b) -> p a b", a=8)
                for kc in range(8):
                    TR(pv[:, kc, 0:rows], hb[h2][0:rows, kc * 128:(kc + 1) * 128], idb[0:rows, 0:rows],
                       r=[("hb", h2)] + CONST, w=[BK(b)])
                CP("act", hT[:, :, t * 128:t * 128 + rows], pv[:, :, 0:rows], r=[BK(b)], w=[("hT", t)])

            for t in range(NTL + NLAG):
                if t < NTL:
                    stage1(t)
                if t >= NLAG:
                    stage2(t - NLAG)

        HT_ALL = [("hT", t) for t in range(NTL)]

        def hT_toks(c0, n):
            return [("hT", t) for t in range(c0 // 128, (c0 + n + 127) // 128)]

        def phase_A():
            A = Bump(arena, abf, [(R_T[0], C0_OFF), R_G, X_SPARE, (WO_OFF, R_T[1])])
            wq = A.bf(8 * 512).rearrange("p (k g c) -> p k g c", k=8, g=4)
            wk = A.bf(8 * 128).rearrange("p (k c) -> p k c", k=8)
            wv = A.bf(8 * 128).rearrange("p (k c) -> p k c", k=8)
            assert A.regs[0][0] == O_T + 3072
            qT = A.bf(4 * NT).rearrange("p (g t) -> p g t", g=4)
            kT = A.bf(NT)
            vpad = A.bf(NTL * 256).rearrange("p (t k c) -> p t k c", t=NTL, k=2)
            kf = [A.f32(128) for _ in range(2)]
            vf = [A.f32(128) for _ in range(2)]
            pbuf = [A.bf(512) for _ in range(8)]
            dsum = [A.f32(512) for _ in range(2)]
            rden = [A.f32(512) for _ in range(2)]
            ckb = A.bf(16 * 128).rearrange("p (s c) -> p s c", s=16)
            kcT = A.bf(16 * 128).rearrange("p (s c) -> p s c", s=16)
            vcpad = A.bf(16 * 256).rearrange("p (s k c) -> p s k c", s=16, k=2)
            pc = A.bf(512)
            pn = A.bf(512)
            d_cache = P.dsem("cacheA")

            VP_ALL = [("vpad", t) for t in range(NTL)]
            MEMSET("pool", vpad[:, :, :, :], 0.0, w=VP_ALL)
            MEMSET("pool", vcpad[:, :, :, :], 0.0, w=["vcpad"])
            DMA("pool", ckb, ck.rearrange("s t c -> t s c"), d_cache, w=["ckb"])
            for kv in range(2):
                DMA("pool", vcpad[:, :, kv, kv * 64:(kv + 1) * 64], cv[:, :, kv * 64:(kv + 1) * 64].rearrange("s t c -> t s c"),
                    d_cache, r=[], w=["vcpad"])
            DMA("sp", nks[:, 0:124, :], ck[:, 4:128, :], d_out)
            DMA("sp", nvs[:, 0:124, :], cv[:, 4:128, :], d_out)

            def nb():
                b = bank_ctr[0] % 8
                bank_ctr[0] += 1
                return b

            ev = [0]

            def evac_eng():
                ev[0] += 1
                return "act" if ev[0] % 2 else "dve"

            CHECK("A0")
            for T in range(5):
                c0 = T * 512
                n = 512 if T < 4 else 64
                htk = hT_toks(c0, n)
                for g in range(4):
                    b = nb()
                    for kc in range(8):
                        MM(banks[b][:, 0:n], wq[:, kc, g, :], hT[:, kc, c0:c0 + n], start=(kc == 0), stop=(kc == 7),
                           r=[("wq", g)] + htk, w=[BK(b)])
                    CP(evac_eng(), qT[:, g, c0:c0 + n], banks[b][:, 0:n], r=[BK(b)], w=[("qT", g, T)])
                b = nb()
                for kc in range(8):
                    MM(banks[b][:, 0:n], wk[:, kc, :], hT[:, kc, c0:c0 + n], start=(kc == 0), stop=(kc == 7),
                       r=["wk"] + htk, w=[BK(b)])
                CP(evac_eng(), kT[:, c0:c0 + n], banks[b][:, 0:n], r=[BK(b)], w=[("kT", T)])
                for t in range(c0 // 128, (c0 + n + 127) // 128):
                    rows = rows_of(t)
                    b = nb()
                    for kc in range(8):
                        MM(banks[b][0:rows, 0:128], hT[:, kc, t * 128:t * 128 + rows], wv[:, kc, :], start=(kc == 0), stop=(kc == 7),
                           r=["wv", ("hT", t)], w=[BK(b)])
                    e = evac_eng()
                    for kv in range(2):
                        CP(e, vpad[0:rows, t, kv, kv * 64:(kv + 1) * 64], banks[b][0:rows, kv * 64:(kv + 1) * 64],
                           r=[BK(b)], w=[("vpad", t)])
                    if t >= 15:
                        i = t - 15
                        CP("dve", vf[i][0:rows], banks[b][0:rows, 0:128], r=[BK(b)], w=[("vf", i)])
                        b2 = nb()
                        for kc in range(8):
                            MM(banks[b2][0:rows, 0:128], hT[:, kc, t * 128:t * 128 + rows], wk[:, kc, :], start=(kc == 0), stop=(kc == 7),
                               r=["wk", ("hT", t)], w=[BK(b2)])
                        CP("dve", kf[i][0:rows], banks[b2][0:rows, 0:128], r=[BK(b2)], w=[("kf", i)])
            CHECK("A1")
            DMA("sp", nkp, kf[0], d_out, r=[("kf", 0)])
            DMA("sp", nvp, vf[0], d_out, r=[("vf", 0)])
            for l in range(4):
                DMA("sp", nks[:, 124 + l, :], kf[1][l * 16:(l + 1) * 16, :], d_out, r=[("kf", 1)])
                DMA("sp", nvs[:, 124 + l, :], vf[1][l * 16:(l + 1) * 16, :], d_out, r=[("vf", 1)])
            dump("qT", qT[:, :, :], [("qT", g, T) for g in range(4) for T in range(5)], BF16)
            dump("kT", kT, [("kT", T) for T in range(5)], BF16)
            dump("vpad", vpad[:, :, :, :], VP_ALL, BF16)

            CHECK("A1b")
            scale = 64 ** -0.5
            mP = cb[:, B_MP:B_MP + 128]
            mC = cb[:, B_MC:B_MC + 128]
            onespad = cb[:, B_ONESPAD:B_ONESPAD + 256].rearrange("p (k c) -> p k c", k=2)
            sink_bc = sinkexp.unsqueeze(2).to_broadcast([128, 4, 128])

            pslot = [0]
            plists = {}

            def stage_s(i):
                T = i // 4
                q0 = i * 128
                js = ([i - 1] if i > 0 else []) + [i]
                plist = []
                for kv in range(2):
                    ks = slice(kv * 64, (kv + 1) * 64)
                    for j in js:
                        b = nb()
                        MM(banks[b][:, :].rearrange("p (g q) -> p g q", g=4), kT[ks, j * 128:(j + 1) * 128], qT[ks, :, q0:q0 + 128],
                           r=[("kT", j // 4)] + [("qT", g, T) for g in range(4)], w=[BK(b)])
                        sl = pslot[0] % 8
                        pslot[0] += 1
                        pb = pbuf[sl]
                        ACT(pb, banks[b][:, :], AF.Exp, r=[BK(b)], w=[("pb", sl)], scale=scale)
                        msk = mP if j < i else mC
                        TT("dve", pb.rearrange("p (g q) -> p g q", g=4), pb.rearrange("p (g q) -> p g q", g=4),
                           msk.unsqueeze(1).to_broadcast([128, 4, 128]), ALU.mult, r=[("pb", sl)] + CONST, w=[("pb", sl)])
                        plist.append((kv, j, sl))
                plists[i] = plist

            def stage_p(i):
                q0 = i * 128
                plist = plists[i]
                bo = nb()
                bd = nb()
                for idx, (kv, j, sl) in enumerate(plist):
                    MM(banks[bo][:, :], vpad[:, j, kv, :], pbuf[sl], start=(idx == 0), stop=(idx == len(plist) - 1),
                       r=[("vpad", j), ("pb", sl)], w=[BK(bo)])
                for idx, (kv, j, sl) in enumerate(plist):
                    MM(banks[bd][:, :], onespad[:, kv, :], pbuf[sl], start=(idx == 0), stop=(idx == len(plist) - 1),
                       r=[("pb", sl)] + CONST, w=[BK(bd)])
                s2 = i % 2
                TT("dve", dsum[s2].rearrange("p (g q) -> p g q", g=4), banks[bd][:, :].rearrange("p (g q) -> p g q", g=4), sink_bc,
                   ALU.add, r=[BK(bd), "sinkexp"], w=[("dsum", s2)])
                P.op("dve", lambda e, o=rden[s2], i_=dsum[s2]: e.reciprocal(out=o, in_=i_), r=[("dsum", s2)], w=[("rden", s2)])
                TT("dve", aoT[:, :, q0:q0 + 128], banks[bo][:, :].rearrange("p (g q) -> p g q", g=4),
                   rden[s2].rearrange("p (g q) -> p g q", g=4), ALU.mult, r=[BK(bo), ("rden", s2)], w=[("aoT", i)])

            stage_s(0)
            for i in range(16):
                if i + 1 < 16:
                    stage_s(i + 1)
                stage_p(i)

            CHECK("A2")
            PF["d_wC0"] = P.dsem("wC0")
            load_c_into(0, *c0_views(), PF["d_wC0"], 0)
            for half in range(2):
                b = nb()
                pv = bankb[b][:, :].rearrange("p (a c) -> p a c", a=8)
                for a in range(8):
                    s_ = half * 8 + a
                    TR(pv[:, a, :], ckb[:, s_, :], idb, r=["ckb"] + CONST, w=[BK(b)])
                CP(evac_eng(), kcT[:, half * 8:(half + 1) * 8, :], pv, r=[BK(b)], w=[("kcT", half)])
            CHECK("A3a")
            TQ = [("qT", g, 4) for g in range(4)]
            pcv = pc.rearrange("p (k s g l) -> p k s g l", k=2, s=16, g=4)
            mcache = cb[:, B_MCACHE:B_MCACHE + 4]
            for kv in range(2):
                ks = slice(kv * 64, (kv + 1) * 64)
                bsc = nb()
                scv = banks[bsc][:, 0:256].rearrange("p (s g l) -> p s g l", s=16, g=4)
                for s_ in range(16):
                    rhs = qT[ks, :, SC0:SC0 + 64].rearrange("p g (l s) -> p g l s", s=16)[:, :, :, s_]
                    MM(scv[:, s_, :, :], kcT[ks, s_, :], rhs, r=[("kcT", s_ // 8)] + TQ, w=[BK(bsc)])
                ACT(pc[:, kv * 256:(kv + 1) * 256], banks[bsc][:, 0:256], AF.Exp, r=[BK(bsc)], w=[("pc", kv)], scale=scale)
                TT("dve", pc[:, kv * 256:(kv + 1) * 256].rearrange("p (a l) -> p a l", l=4), pc[:, kv * 256:(kv + 1) * 256].rearrange("p (a l) -> p a l", l=4),
                   mcache.unsqueeze(1).to_broadcast([128, 64, 4]), ALU.mult, r=[("pc", kv)] + CONST, w=[("pc", kv)])
            mnew = cb[0:64, B_MNEW:B_MNEW + 64]
            for kv in range(2):
                ks = slice(kv * 64, (kv + 1) * 64)
                bsn = nb()
                MM(banks[bsn][0:64, 0:256].rearrange("p (g q) -> p g q", g=4), kT[ks, SC0:SC0 + 64], qT[ks, :, SC0:SC0 + 64],
                   r=[("kT", 4)] + TQ, w=[BK(bsn)])
                ACT(pn[0:64, kv * 256:(kv + 1) * 256], banks[bsn][0:64, 0:256], AF.Exp, r=[BK(bsn)], w=[("pn", kv)], scale=scale)
                TT("dve", pn[0:64, kv * 256:(kv + 1) * 256].rearrange("p (a q) -> p a q", q=64), pn[0:64, kv * 256:(kv + 1) * 256].rearrange("p (a q) -> p a q", q=64),
                   mnew.unsqueeze(1).to_broadcast([64, 4, 64]), ALU.mult, r=[("pn", kv)] + CONST, w=[("pn", kv)])
            CHECK("A3b")
            pnv = pn[0:64].rearrange("p (k g l s) -> p k g l s", k=2, g=4, l=4)
            bo = nb()
            bd = nb()
            for which, bX in (("o", bo), ("d", bd)):
                ov = banks[bX][:, 0:256].rearrange("p (s g l) -> p s g l", s=16, g=4)
                for s_ in range(16):
                    for kv in range(2):
                        lhs = vpad[0:64, 16, kv, :] if which == "o" else onespad[0:64, kv, :]
                        MM(ov[:, s_, :, :], lhs, pnv[:, kv, :, :, s_], start=(kv == 0), stop=False,
                           r=[("vpad", 16), ("pn", kv)] + CONST, w=[BK(bX)])
                    for kv in range(2):
                        lhs = vcpad[:, s_, kv, :] if which == "o" else onespad[:, kv, :]
                        MM(ov[:, s_, :, :], lhs, pcv[:, kv, s_, :, :], start=False, stop=(kv == 1),
                           r=["vcpad", ("pc", kv)] + CONST, w=[BK(bX)])
            CHECK("A3c")
            TT("dve", dsum[0][:, 0:256].rearrange("p (s g l) -> p s g l", s=16, g=4), banks[bd][:, 0:256].rearrange("p (s g l) -> p s g l", s=16, g=4),
               sinkexp.unsqueeze(1).unsqueeze(3).to_broadcast([128, 16, 4, 4]), ALU.add, r=[BK(bd), "sinkexp"], w=[("dsum", 0)])
            P.op("dve", lambda e: e.reciprocal(out=rden[0][:, 0:256], in_=dsum[0][:, 0:256]), r=[("dsum", 0)], w=[("rden", 0)])
            TT("dve", dsum[0][:, 0:256], banks[bo][:, 0:256], rden[0][:, 0:256], ALU.mult, r=[BK(bo), ("rden", 0)], w=[("dsum", 0)])
            CP("dve", aoT[:, :, SC0:SC0 + 64].rearrange("p g (l s) -> p g l s", s=16),
               dsum[0][:, 0:256].rearrange("p (s g l) -> p g l s", s=16, g=4), r=[("dsum", 0)], w=[("aoT", 16)])
            dump("aoT", aoT[:, :, :], [("aoT", i) for i in range(17)], BF16)

        AO_ALL = [("aoT", i) for i in range(17)]

        def phase_B():
            SCH = 256
            A = Bump(arena, abf, [R_T, R_G, X_SPARE, (O_X, O_X + 4224)])
            wx = A.bf(8 * 1536).rearrange("p (k c) -> p k c", k=8)
            wz = A.bf(8 * 1024).rearrange("p (k c) -> p k c", k=8)
            o_xbcs = A.take(3072)
            xbcs = [A.bview(o_xbcs + i * 1536, 12 * SCH).rearrange("p (b t) -> p b t", b=12) for i in range(2)]
            o_big = A.take(4096)
            big = [A.fview(o_big + i * 2048, 2048) for i in range(2)]
            o_lw = A.take(3072)
            Lb = [A.bview(o_lw + i * 1024, 2048) for i in range(2)]
            wT = A.bview(o_lw + 2048, 2048)
            snw = A.f32(1024)
            yb = A.f32(1024)
            hst = A.f32(1024)
            diagW = A.bf(12 * 4 * 128).rearrange("p (b i c) -> p b i c", b=12, i=4)
            rawb = [A.bf(264) for _ in range(2)]
            thc = [A.bf(256) for _ in range(2)]
            xs_tm = A.bf(1024)
            xdt = A.bf(1024)
            xdd = A.bf(1024)
            xsd = A.bf(1024)
            thz1 = A.bf(1024)
            thz = [thz1, thz1]
            m_tm = A.bf(1024)
            hst_bf = A.bf(1024)
            dt_all = A.f32(NTL * 16).rearrange("p (t h) -> p t h", t=NTL)
            dA_all = A.f32(NTL * 16).rearrange("p (t h) -> p t h", t=NTL)
            B_tm = A.bf(256).rearrange("p (g n) -> p g n", g=2)
            cbm = A.bf(256)
            wdt = A.bf(8 * 16).rearrange("p (k c) -> p k c", k=8)
            cwh = A.f32(48).rearrange("p (b i) -> p b i", b=12)
            cbh = A.f32(16)
            carry = A.bf(40)[:, 0:36].rearrange("p (b i) -> p b i", b=12)
            E3 = A.f32(48)
            st8 = A.f32(16)
            xbss = A.bf(12 * 64).rearrange("p (b c) -> p b c", b=12)
            CTm = [A.bf(128).rearrange("p (g c) -> p g c", g=2) for _ in range(2)]
            Bm = [A.bf(256).rearrange("p (g n) -> p g n", g=2) for _ in range(2)]
            cdl = A.f32(16)
            cdcol = A.f32(128).rearrange("p (a s) -> p a s", a=8)
            h0 = [A.fview(o_xbcs + i * 1024, 1024).rearrange("p (a c) -> p a c", a=8) for i in range(2)]
            h0T = [A.bview(o_xbcs + 2048 + i * 512, 1024) for i in range(2)]
            scv = big[0][:, 0:1536]
            raws = A.bview(o_lw, 12 * 112).rearrange("p (b c) -> p b c", b=12)
            d_w = PF["d_wB"]
            d_big = P.dsem("bigB")
            DMA("pool", wdt, w_in[:, 3328:3344].rearrange("(kc p) n -> p kc n", p=128), d_w, w=["wdt"])
            DMA("pool", wz, w_in[:, 768:1792].rearrange("(kc p) n -> p kc n", p=128), P.dsem("wzB"), w=["wz"])
            DMA("sp", snw, vecs[:, 3, :], P.dsem("snwB"), w=["snw"])
            TS("pool", cwh, cf[:, C_CONVW:C_CONVW + 48].rearrange("p (b i) -> p b i", b=12), 0.5, None, ALU.mult, r=CONST, w=["cwh"])
            TS("pool", cbh[:, 0:12], cf[:, C_CONVB:C_CONVB + 12], 0.5, None, ALU.mult, r=CONST, w=["cbh"])
            MEMSET("pool", carry[:, :, :], 0.0, w=["carry"])
            for blk_ in range(12):
                for i_ in range(4):
                    TS("dve", diagW[:, blk_, i_, :], idb, cwh[:, blk_, i_:i_ + 1], None, ALU.mult, r=CONST + ["cwh"], w=["diagW"])
            MEMSET("pool", hst, 0.0, w=["hst"])
            MEMSET("pool", hst_bf, 0.0, w=["hst_bf"])

            UI = cf[:, C_UI:C_UI + 128]
            SL = cf[:, C_SL:C_SL + 128]
            ONES = cf[:, C_ONES:C_ONES + 128]
            UIB = cb[:, B_UIB:B_UIB + 128]

            for t in range(NTL):
                rows = rows_of(t)
                for kc in range(8):
                    MM(banks[0][0:rows, t * 16:(t + 1) * 16], hT[:, kc, t * 128:t * 128 + rows], wdt[:, kc, :], start=(kc == 0), stop=(kc == 7),
                       r=["wdt", ("hT", t)], w=[BK(0)])
            dtv = banks[0][:, 0:NTL * 16].rearrange("p (t h) -> p t h", t=NTL)
            TT("dve", dt_all[:, :, :], dtv, dtb_bc.unsqueeze(1).to_broadcast([128, NTL, 16]), ALU.add, r=[BK(0)] + CONST, w=["dt0"])
            ACT(dt_all[:, :, :], dt_all[:, :, :], AF.Exp, r=["dt0"], w=["dt1"])
            ACT(dt_all[:, :, :], dt_all[:, :, :], AF.Ln, r=["dt1"], w=["dt"], bias=1.0)
            TT("dve", dA_all[:, :, :], dt_all[:, :, :], a_neg.unsqueeze(1).to_broadcast([128, NTL, 16]), ALU.mult, r=["dt", "a_neg"], w=["dA"])
            dump("dt", dt_all[:, :, :], ["dt"])

            def conv_block(src_fn, width, blk, bY, thv, outv, rtoks, wtok):
                for i in range(4):
                    MM(banks[bY][:, 0:width], diagW[:, blk, i, :], src_fn(i), start=(i == 0), stop=(i == 3), r=rtoks + ["diagW"], w=[BK(bY)])
                tt = ("thc", blk % 2)
                ACT(thv, banks[bY][:, 0:width], AF.Tanh, r=[BK(bY), "cbh"], w=[tt], bias=cbh[:, blk:blk + 1])
                TS("dve", thv, thv, 1.0, None, ALU.add, r=[tt], w=[tt])
                STT("dve", outv, banks[bY][:, 0:width], cbh[:, blk:blk + 1], thv, ALU.add, ALU.mult, r=[BK(bY), tt, "cbh"], w=[wtok])

            ev = [0]

            def conv_stage1(S, blk):
                c0 = S * SCH
                b = blk % 2
                sl = blk % 2
                for kc in range(8):
                    MM(banks[b][:, 0:SCH], wx[:, kc, blk * 128:(blk + 1) * 128], hT[:, kc, c0:c0 + SCH], start=(kc == 0), stop=(kc == 7),
                       r=["wx"] + hT_toks(c0, SCH), w=[BK(b)])
                CP("pool", rawb[sl][:, 0:3], carry[:, blk, :], r=["carry"], w=[("rawc", sl)])
                ev[0] += 1
                CP("act" if ev[0] % 2 else "dve", rawb[sl][:, 3:3 + SCH], banks[b][:, 0:SCH], r=[BK(b)], w=[("raw", sl)])
                CP("pool", carry[:, blk, :], rawb[sl][:, SCH:SCH + 3], r=[("raw", sl)], w=["carry"])

            def conv_stage2(S, blk):
                xs_ = S % 2
                sl = blk % 2
                conv_block(lambda i: rawb[sl][:, i:i + SCH], SCH, blk, 2 + blk % 2, thc[sl], xbcs[xs_][:, blk, :],
                           [("raw", sl), ("rawc", sl)], ("xbcs", xs_, blk))

            def conv_blocks(S, blks):
                blks = list(blks)
                conv_stage1(S, blks[0])
                for i, blk in enumerate(blks):
                    if i + 1 < len(blks):
                        conv_stage1(S, blks[i + 1])
                    conv_stage2(S, blk)

            def front_decay_a(ci, R, sample=False):
                rs = slice(0, R)
                k = ci % 2
                UIr = cf[rs, C_UIS:C_UIS + 64] if sample else UI
                dAc = dA_all[rs, ci, :]
                bigv = big[k][rs].rearrange("p (h l) -> p h l", h=16)[:, :, 0:R]
                for h in range(16):
                    ACT(bigv[:, h, :], UIr, AF.Copy, r=["dA"] + CONST, w=[("big", k)], scale=dAc[:, h:h + 1])

            def front_decay_b(ci, R, sample=False):
                rs = slice(0, R)
                k = ci % 2
                SLr = cf[rs, C_SLS:C_SLS + 64] if sample else SL
                bigv = big[k][rs].rearrange("p (h l) -> p h l", h=16)[:, :, 0:R]
                Lv = Lb[k][rs].rearrange("p (h l) -> p h l", h=16)[:, :, 0:R]
                hq = 512 // R
                for qd in range(16 // hq):
                    b = 4 + (qd % 2)
                    MM(banks[b][rs, 0:hq * R].rearrange("p (h l) -> p h l", h=hq), SLr, bigv[:, qd * hq:(qd + 1) * hq, :], r=[("big", k)] + CONST, w=[BK(b)])
                    ACT(Lv[:, qd * hq:(qd + 1) * hq, :], banks[b][rs, 0:hq * R].rearrange("p (h l) -> p h l", h=hq), AF.Exp, r=[BK(b)], w=[("L", k, qd)])

            def ssd_chunk(R, ci, XS, BTg, CTg, xbc_toks, c_glob, sample=False, seq_loop=None, early_hook=None, mid_hook=None, end_hook=None):
                rs = slice(0, R)
                k = ci % 2
                UIr = cf[rs, C_UIS:C_UIS + 64] if sample else UI
                SLr = cf[rs, C_SLS:C_SLS + 64] if sample else SL
                UIBr = cb[rs, B_UISB:B_UISB + 64] if sample else UIB
                idr = idb
                gzk = thz[k]
                tmph = big[k][:, 0:1024]
                hq = 512 // R
                for half in range(2):
                    for kc in range(8):
                        MM(banks[6 + half][rs, :], hT[:, kc, c_glob:c_glob + R], wz[:, kc, half * 512:(half + 1) * 512], start=(kc == 0), stop=(kc == 7),
                           r=["wz", ("hT", c_glob // 128)], w=[BK(6 + half)])
                    hs = slice(half * 512, (half + 1) * 512)
                    ACT(gzk[rs, hs], banks[6 + half][rs, :], AF.Tanh, r=[BK(6 + half)], w=[("gz", half)], scale=0.5)
                    STT("dve", gzk[rs, hs], gzk[rs, hs], 1.0, banks[6 + half][rs, :], ALU.add, ALU.mult, r=[("gz", half), BK(6 + half)], w=[("gz", half)])
                pv = bankb[2][:, :].rearrange("p (a c) -> p a c", a=8)
                for blk in range(8):
                    TR(pv[rs, blk, :], XS(blk), idr, r=xbc_toks + CONST, w=[BK(2)])
                CP("act", xs_tm[rs].rearrange("p (a c) -> p a c", a=8), pv[rs, :, :], r=[BK(2)], w=["xs_tm"])
                b3 = banks[3]
                b3b = bankb[3]
                dAc = dA_all[rs, ci, :]
                MM(b3[rs, 0:16], UIr, dAc, r=["dA"] + CONST, w=[BK(3)])
                MM(b3[rs, 16:32], SLr, dAc, r=["dA"] + CONST, w=[BK(3)])
                if not sample:
                    MM(b3[:, 32:48], ONES, dAc, r=["dA"] + CONST, w=[BK(3)])
                for g in range(2):
                    MM(b3[rs, 64 + g * 128:64 + g * 128 + R], BTg(g), CTg(g), r=xbc_toks, w=[BK(3)])
                btv = b3b[:, 640:896].rearrange("p (g n) -> p g n", g=2)
                for g in range(2):
                    TR(btv[rs, g, :], BTg(g), idr, r=xbc_toks + CONST, w=[BK(3)])
                ncol = 32 if sample else 48
                ACT(E3[rs, 0:ncol], b3[rs, 0:ncol], AF.Exp, r=[BK(3)], w=["E3"])
                cbv = b3[rs, 64:320].rearrange("p (g l) -> p g l", g=2)[:, :, 0:R]
                cbmv = cbm[rs].rearrange("p (g l) -> p g l", g=2)[:, :, 0:R]
                TT("dve", cbmv, cbv, UIBr.unsqueeze(1).to_broadcast([R, 2, R]), ALU.mult, r=[BK(3)] + CONST, w=["cbm"])
                CP("act", B_tm[rs, :, :], btv[rs, :, :], r=[BK(3)], w=["B_tm"])
                Lv = Lb[k][rs].rearrange("p (h l) -> p h l", h=16)[:, :, 0:R]
                wTv = wT[rs].rearrange("p (h l) -> p h l", h=16)[:, :, 0:R]
                for g in range(2):
                    TT("dve", wTv[:, g * 8:(g + 1) * 8, :], Lv[:, g * 8:(g + 1) * 8, :], cbmv[:, g, :].unsqueeze(1).to_broadcast([R, 8, R]), ALU.mult,
                       r=[("L", k, q_) for q_ in range(16 // hq)] + ["cbm"], w=[("wT", g)])
                xsv = xs_tm[rs].rearrange("p (h c) -> p h c", h=16)
                xdtv = xdt[rs].rearrange("p (h c) -> p h c", h=16)
                xddv = xdd[rs].rearrange("p (h c) -> p h c", h=16)
                TT("dve", xdtv, xsv, dt_all[rs, ci, :].unsqueeze(2).to_broadcast([R, 16, 64]), ALU.mult, r=["xs_tm", "dt"], w=["xdt"])
                TT("dve", xsd[rs].rearrange("p (h c) -> p h c", h=16), xsv, dskip_bc[rs].unsqueeze(2).to_broadcast([R, 16, 64]), ALU.mult,
                   r=["xs_tm"] + CONST, w=["xsd"])
                TT("dve", xddv, xdtv, E3[rs, 16:32].unsqueeze(2).to_broadcast([R, 16, 64]), ALU.mult, r=["xdt", "E3"], w=["xdd"])
                if early_hook is not None:
                    early_hook()
                if not sample:
                    for g in range(2):
                        MM(banks[4 + g][:, :], CTg(g), hst_bf[:, g * 512:(g + 1) * 512], r=xbc_toks + ["hst_bf"], w=[BK(4 + g)])
                else:
                    seq_loop()
                for g in range(2):
                    gs = slice(g * 512, (g + 1) * 512)
                    TT("dve", yb[rs, gs].rearrange("p (h c) -> p h c", h=8), banks[4 + g][rs, :].rearrange("p (h c) -> p h c", h=8),
                       E3[rs, g * 8:(g + 1) * 8].unsqueeze(2).to_broadcast([R, 8, 64]), ALU.mult, r=[BK(4 + g), "E3"], w=[("yb", g)])
                for h in range(16):
                    b = 6 + h // 8
                    MM(banks[b][rs, (h % 8) * 64:(h % 8 + 1) * 64], wTv[:, h, :], xdtv[:, h, :], r=[("wT", h // 8), "xdt"], w=[BK(b)])
                for g in range(2):
                    gs = slice(g * 512, (g + 1) * 512)
                    TT("dve", yb[rs, gs], yb[rs, gs], banks[6 + g][rs, :], ALU.add, r=[("yb", g), BK(6 + g)], w=[("yb", g)])
                    TT("dve", yb[rs, gs], yb[rs, gs], xsd[rs, gs], ALU.add, r=[("yb", g), "xsd"], w=[("yb", g)])
                    TT("dve", yb[rs, gs], yb[rs, gs], gzk[rs, gs], ALU.mult, r=[("yb", g), ("gz", g)], w=[("yb", g)])
                    ACT(m_tm[rs, gs], yb[rs, gs], AF.Square, r=[("yb", g)], w=[("m_tm", g), ("ssq", g)], accum_out=st8[rs, g:g + 1])
                if mid_hook is not None:
                    mid_hook()
                TS("pool", st8[rs, 4:6], st8[rs, 0:2], 1.0 / 512, 4.0 * EPS, ALU.mult, ALU.add, r=[("ssq", 0), ("ssq", 1)], w=["msB"])
                TT("pool", st8[rs, 8:10], st8[rs, 4:6], neghalf[rs].to_broadcast([R, 2]), ALU.pow, r=["msB", "neghalf"], w=["rsB"])
                for g in range(2):
                    gs = slice(g * 512, (g + 1) * 512)
                    STT("dve", m_tm[rs, gs], yb[rs, gs], st8[rs, 8 + g:9 + g], snw[rs, gs], ALU.mult, ALU.mult, r=[("yb", g), "rsB", "snw"], w=[("m_tm", g)])
                pv2 = bankb[2][:, :].rearrange("p (a c) -> p a c", a=8)
                for blk in range(8):
                    TR(pv2[:, blk, 0:R], m_tm[rs, blk * 128:(blk + 1) * 128], idr[rs, rs], r=[("m_tm", blk // 4)] + CONST, w=[BK(2)])
                CP("act", moT[:, :, c_glob:c_glob + R], pv2[:, :, 0:R], r=[BK(2)], w=[("moT", c_glob // 128)])
                if not sample:
                    for g in range(2):
                        MM(banks[4 + g][:, :], B_tm[:, g, :], xdd[:, g * 512:(g + 1) * 512], r=["B_tm", "xdd"], w=[BK(4 + g)])
                    TT("dve", tmph.rearrange("p (h c) -> p h c", h=16), hst.rearrange("p (h c) -> p h c", h=16),
                       E3[:, 32:48].unsqueeze(2).to_broadcast([128, 16, 64]), ALU.mult, r=["hst", "E3"], w=[("big", k)])
                    for g in range(2):
                        gs = slice(g * 512, (g + 1) * 512)
                        TT("dve", hst[:, gs], tmph[:, gs], banks[4 + g][:, :], ALU.add, r=[("big", k), BK(4 + g)], w=["hst"])
                    CP("act", hst_bf, hst, r=["hst"], w=["hst_bf"])
                if end_hook is not None:
                    end_hook()

            NS = 2048 // SCH
            CPS = SCH // 128
            conv_blocks(0, range(12))
            front_decay_a(0, 128)
            front_decay_b(0, 128)
            for ci in range(16):
                S = ci // CPS
                cc = ci % CPS
                l0 = cc * 128
                xs_ = S % 2
                XT = [("xbcs", xs_, blk) for blk in range(12)]

                def h_early(ci=ci):
                    if ci + 1 < 16:
                        front_decay_a(ci + 1, 128)

                def h_mid(ci=ci):
                    if ci + 1 < 16:
                        front_decay_b(ci + 1, 128)

                def h_end(S=S, cc=cc):
                    if S + 1 < NS:
                        per = 12 // CPS
                        conv_blocks(S + 1, range(cc * per, (cc + 1) * per))

                ssd_chunk(128, ci,
                          lambda blk, l0=l0, xs_=xs_: xbcs[xs_][:, blk, l0:l0 + 128],
                          lambda g, l0=l0, xs_=xs_: xbcs[xs_][:, 8 + g, l0:l0 + 128],
                          lambda g, l0=l0, xs_=xs_: xbcs[xs_][:, 10 + g, l0:l0 + 128],
                          XT, S * SCH + l0, early_hook=h_early, mid_hook=h_mid, end_hook=h_end)
                if ci == 3:
                    dump("mo0", moT[:, :, 0:512], [("moT", i) for i in range(4)], BF16)
            P.barrier()
            hstv = hst.rearrange("p (a c) -> p a c", a=8)
            hout = big[0][:, 0:1024].rearrange("p (a c) -> p a c", a=8)
            for half in range(2):
                b = 4 + half
                for a in range(4):
                    blk = half * 4 + a
                    TR(banks[b][:, a * 128:(a + 1) * 128], hstv[:, blk, :], idf, r=["hst"] + CONST, w=[BK(b)])
                CP("dve", hout[:, half * 4:(half + 1) * 4, :], banks[b][:, :].rearrange("p (a c) -> p a c", a=4), r=[BK(b)], w=[("big", 0)])
            DMA("sp", nsp.rearrange("(blk q) n -> q blk n", q=128), hout, d_big, r=[("big", 0)])
            rawtm = big[1][:, 0:1536]
            for cb_ in range(3):
                b = 6 + (cb_ % 2)
                for kc in range(8):
                    MM(banks[b][:, :], hT[:, kc, 15 * 128:16 * 128], wx[:, kc, cb_ * 512:(cb_ + 1) * 512], start=(kc == 0), stop=(kc == 7),
                       r=["wx", ("hT", 15)], w=[BK(b)])
                CP("dve", rawtm[:, cb_ * 512:(cb_ + 1) * 512], banks[b][:, :], r=[BK(b)], w=[("big", 1)])
            DMA("sp", ncp, rawtm[125:128, :], d_big, r=[("big", 1)])

            CHECK("B2")
            P.barrier()
            d_s = P.dsem("sampB")
            d_h0 = [P.dsem("h0a"), P.dsem("h0b")]
            for i in range(3):
                DMA("sp", scv[i * 16:(i + 1) * 16, :], sconv[:, i, :], d_s, w=[("big", 0)])
            for half, (b, blks) in enumerate(((0, list(range(0, 8))), (1, list(range(8, 12))))):
                for a, blk in enumerate(blks):
                    TR(banks[b][:, a * 48:(a + 1) * 48], scv[0:48, blk * 128:(blk + 1) * 128], idf[0:48, 0:48], r=[("big", 0)] + CONST, w=[BK(b)])
                nblk = len(blks)
                CP("act", raws[:, blks[0]:blks[0] + nblk, 0:48], banks[b][:, 0:nblk * 48].rearrange("p (a c) -> p a c", a=nblk), r=[BK(b)], w=[("raws", half)])
            for half, (b, blks) in enumerate(((0, list(range(0, 8))), (1, list(range(8, 12))))):
                for a, blk in enumerate(blks):
                    for kc in range(8):
                        MM(banks[b][:, a * 64:(a + 1) * 64], wx[:, kc, blk * 128:(blk + 1) * 128], hT[:, kc, SC0:SC0 + 64], start=(kc == 0), stop=(kc == 7),
                           r=["wx", ("hT", 16)], w=[BK(b)])
                nblk = len(blks)
                CP("dve", raws[:, blks[0]:blks[0] + nblk, 48:112], banks[b][:, 0:nblk * 64].rearrange("p (a c) -> p a c", a=nblk), r=[BK(b)], w=[("raws2", half)])
            for blk in range(12):
                sl = blk % 2
                hf = 0 if blk < 8 else 1
                conv_block(lambda i, blk=blk: raws[:, blk, i * 16:i * 16 + 64], 64, blk, 2 + blk % 2, thc[sl][:, 0:64], xbss[:, blk, :],
                           [("raws", hf), ("raws2", hf)], ("xbss", blk))
            XTS = [("xbss", blk) for blk in range(12)]
            rawtm = big[1][:, 0:1536]
            for cb_ in range(3):
                b = 4 + (cb_ % 2)
                for kc in range(8):
                    MM(banks[b][0:64, :], hT[:, kc, SC0:SC0 + 64], wx[:, kc, cb_ * 512:(cb_ + 1) * 512], start=(kc == 0), stop=(kc == 7),
                       r=["wx", ("hT", 16)], w=[BK(b)])
                CP("dve", rawtm[0:64, cb_ * 512:(cb_ + 1) * 512], banks[b][0:64, :], r=[BK(b)], w=[("big", 1)])
            for l in range(1, 4):
                DMA("sp", ncs[:, l - 1, :], rawtm[l * 16:(l + 1) * 16, :], d_big, r=[("big", 1)])
            P.barrier()

            SEQM = cb[:, B_SEQM:B_SEQM + 1024].rearrange("p (s c) -> p s c", s=16)
            SEQINDB = cb[0:64, B_SEQINDB:B_SEQINDB + 16]

            def prefetch_A_weights(part):
                if part == 0:
                    PF["d_wA"] = P.dsem("wA")
                wq_p = abf[:, 2 * O_T:2 * O_T + 4096].rearrange("p (k g c) -> p k g c", k=8, g=4)
                wk_p = abf[:, 2 * (O_T + 2048):2 * (O_T + 2048) + 1024].rearrange("p (k c) -> p k c", k=8)
                wv_p = abf[:, 2 * (O_T + 2560):2 * (O_T + 2560) + 1024].rearrange("p (k c) -> p k c", k=8)
                wsrc_p = w_in[:, 0:512].rearrange("(kc p) (kv g d) -> p kc kv g d", p=128, kv=2, g=4)
                if part < 4:
                    g = part
                    for kv in range(2):
                        DMA("pool", wq_p[:, :, g, kv * 64:(kv + 1) * 64], wsrc_p[:, :, kv, g, :], PF["d_wA"], w=[("wq", g)])
                else:
                    DMA("pool", wk_p, w_in[:, 512:640].rearrange("(kc p) n -> p kc n", p=128), PF["d_wA"], w=["wk"])
                    DMA("pool", wv_p, w_in[:, 640:768].rearrange("(kc p) n -> p kc n", p=128), PF["d_wA"], w=["wv"])

            def seq_loop():
                MM(banks[3][0:16, 32:48], dA_all[0:64, 16, :], cf[0:64, C_SEQIND:C_SEQIND + 16], r=["dA"] + CONST, w=[BK(3)])
                CP("act", cdl[0:16, :], banks[3][0:16, 32:48], r=[BK(3)], w=["cdl"])
                selblk = cf[0:16, C_SELBLK:C_SELBLK + 1024].rearrange("p (a c) -> p a c", a=8)
                for blk in range(8):
                    MM(banks[3][:, 64 + blk * 16:64 + (blk + 1) * 16], selblk[:, blk, :], cdl[0:16, :], r=["cdl"] + CONST, w=[BK(3)])
                ACT(cdcol, banks[3][:, 64:192].rearrange("p (a s) -> p a s", a=8), AF.Exp, r=[BK(3)], w=["cdcol"])

                def load_h0(s_):
                    DMA("sp", h0[s_ % 2], sssm[s_].rearrange("(blk q) n -> q blk n", q=128), d_h0[s_ % 2], w=[("h0", s_ % 2)])

                load_h0(0)
                for s_ in range(16):
                    sl = s_ % 2
                    if s_ + 1 < 16:
                        load_h0(s_ + 1)
                    if 4 <= s_ <= 8:
                        prefetch_A_weights(s_ - 4)
                    TT("pool", CTm[sl], xbss[:, 10:12, :], SEQM[:, s_, :].unsqueeze(1).to_broadcast([128, 2, 64]), ALU.mult,
                       r=XTS + CONST, w=[("CTm", sl)])
                    TS("pool", Bm[sl][0:64].rearrange("p g n -> p (g n)"), B_tm[0:64].rearrange("p g n -> p (g n)"), SEQINDB[:, s_:s_ + 1], None, ALU.mult,
                       r=["B_tm"] + CONST, w=[("Bm", sl)])
                    for half in range(2):
                        b = half
                        for a in range(4):
                            TR(banks[b][:, a * 128:(a + 1) * 128], h0[sl][:, half * 4 + a, :], idf, r=[("h0", sl)] + CONST, w=[BK(b)])
                        CP("act", h0T[sl][:, half * 512:(half + 1) * 512], banks[b][:, :], r=[BK(b)], w=[("h0T", sl, half)])
                    for g in range(2):
                        MM(banks[4 + g][0:64, :], CTm[sl][:, g, :], h0T[sl][:, g * 512:(g + 1) * 512], start=(s_ == 0), stop=(s_ == 15),
                           r=[("CTm", sl), ("h0T", sl, g)], w=[BK(4 + g)])
                    for half in range(2):
                        b = 6 + half
                        for a in range(4):
                            blk = half * 4 + a
                            MM(banks[b][:, a * 128:(a + 1) * 128], xdd[0:64, blk * 128:(blk + 1) * 128], Bm[sl][0:64, blk // 4, :],
                               r=["xdd", ("Bm", sl)], w=[BK(b)])
                    for blk in range(8):
                        b = 6 + blk // 4
                        STT("dve", h0[sl][:, blk, :], h0[sl][:, blk, :], cdcol[:, blk, s_:s_ + 1], banks[b][:, (blk % 4) * 128:(blk % 4 + 1) * 128],
                            ALU.mult, ALU.add, r=[("h0", sl), "cdcol", BK(b)], w=[("h0", sl)])
                    DMA("sp", nss[s_].rearrange("(blk q) n -> q blk n", q=128), h0[sl], d_h0[sl], r=[("h0", sl)], w=[("h0", sl)])

            front_decay_a(16, 64, sample=True)
            front_decay_b(16, 64, sample=True)
            ssd_chunk(64, 16,
                      lambda blk: xbss[:, blk, :],
                      lambda g: xbss[:, 8 + g, :],
                      lambda g: xbss[:, 10 + g, :],
                      XTS, SC0, sample=True, seq_loop=seq_loop)
            dump("moT", moT[:, :, :], [("moT", i) for i in range(17)], BF16)

        MO_ALL = [("moT", i) for i in range(17)]

        def phase_C():
            A = Bump(arena, abf, [(R_T[0], C0_OFF)])
            wo_p = abf[:, 2 * WO_OFF:2 * WO_OFF + 8 * 1024].rearrange("p (k c) -> p k c", k=8)
            PF["d_wD"] = P.dsem("wD")
            wga0, wgb0, woa0, wob0 = c0_views()
            wga = [wga0, A.bf(8 * 256).rearrange("p (k c) -> p k c", k=8)]
            wgb = [wgb0, A.bf(8 * 256).rearrange("p (k c) -> p k c", k=8)]
            woa = [woa0, A.bf(4 * 256).rearrange("p (k c) -> p k c", k=4)]
            wob = [wob0, A.bf(8 * 256).rearrange("p (k c) -> p k c", k=8)]
            tA = [A.bf(512) for _ in range(2)]
            tB = [A.bf(512) for _ in range(2)]
            u = [A.f32(512) for _ in range(2)]
            v = [A.f32(512) for _ in range(2)]
            d_w = [PF["d_wC0"], P.dsem("wC1")]
            woa_src = w_oa.rearrange("(kv g d) n -> kv d g n", kv=2, g=4)
            def load_c(qi):
                ws = qi % 2
                cs = slice(qi * 256, (qi + 1) * 256)
                DMA("pool", wga[ws], w_in[:, 3344 + qi * 256:3344 + (qi + 1) * 256].rearrange("(kc p) n -> p kc n", p=128), d_w[ws], w=[("wC", ws)])
                DMA("pool", wgb[ws], w_in[:, 4368 + qi * 256:4368 + (qi + 1) * 256].rearrange("(kc p) n -> p kc n", p=128), d_w[ws], w=[("wC", ws)])
                for kv in range(2):
                    DMA("pool", woa[ws][kv * 64:(kv + 1) * 64, :, :], woa_src[kv][:, :, cs], d_w[ws], w=[("wC", ws)])
                DMA("pool", wob[ws], w_ob[:, cs].rearrange("(kc p) n -> p kc n", p=128), d_w[ws], w=[("wC", ws)])

            load_c(1)
            it = 0
            for qi in range(4):
                ws = qi % 2
                if qi >= 1 and qi + 1 < 4:
                    load_c(qi + 1)
                if qi == 3:
                    DMA("pool", wo_p, w_o.rearrange("(kc p) n -> p kc n", p=128), PF["d_wD"], w=["wo"])
                for blk in range(2):
                    dm = qi * 2 + blk
                    bs = slice(blk * 128, (blk + 1) * 128)
                    for T in range(5):
                        c0 = T * 512
                        n = 512 if T < 4 else 64
                        s = it % 2
                        it += 1
                        bgA, bpA, bgB, bpB = 4 * s, 4 * s + 1, 4 * s + 2, 4 * s + 3
                        htk = hT_toks(c0, n)
                        tk = list(range(c0 // 128, (c0 + n + 127) // 128))
                        for kc in range(8):
                            MM(banks[bgA][:, 0:n], wga[ws][:, kc, bs], hT[:, kc, c0:c0 + n], start=(kc == 0), stop=(kc == 7), r=[("wC", ws)] + htk, w=[BK(bgA)])
                        for g in range(4):
                            MM(banks[bpA][:, 0:n], woa[ws][:, g, bs], aoT[:, g, c0:c0 + n], start=(g == 0), stop=(g == 3),
                               r=[("wC", ws)] + [("aoT", t) for t in tk], w=[BK(bpA)])
                        for kc in range(8):
                            MM(banks[bgB][:, 0:n], wgb[ws][:, kc, bs], hT[:, kc, c0:c0 + n], start=(kc == 0), stop=(kc == 7), r=[("wC", ws)] + htk, w=[BK(bgB)])
                        for kc in range(8):
                            MM(banks[bpB][:, 0:n], wob[ws][:, kc, bs], moT[:, kc, c0:c0 + n], start=(kc == 0), stop=(kc == 7),
                               r=[("wC", ws)] + [("moT", t) for t in tk], w=[BK(bpB)])
                        ACT(tA[s][:, 0:n], banks[bgA][:, 0:n], AF.Tanh, r=[BK(bgA)], w=[("tA", s)], scale=0.5)
                        ACT(tB[s][:, 0:n], banks[bgB][:, 0:n], AF.Tanh, r=[BK(bgB)], w=[("tB", s)], scale=0.5)
                        STT("dve", u[s][:, 0:n], tA[s][:, 0:n], 1.0, banks[bpA][:, 0:n], ALU.add, ALU.mult, r=[("tA", s), BK(bpA)], w=[("u", s)])
                        STT("dve", v[s][:, 0:n], tB[s][:, 0:n], 1.0, banks[bpB][:, 0:n], ALU.add, ALU.mult, r=[("tB", s), BK(bpB)], w=[("v", s)])
                        TT("dve", mgT[:, dm, c0:c0 + n], u[s][:, 0:n], v[s][:, 0:n], ALU.add, r=[("u", s), ("v", s)], w=[("mgT", dm, T)])
            dump("mgT", mgT[:, :, :], [("mgT", dm, T) for dm in range(8) for T in range(5)], BF16)

        def mg_toks(t):
            return [("mgT", dm, t // 4) for dm in range(8)]

        def phase_D():
            A = Bump(arena, abf, [(R_T[0], E0_OFF)])
            wo = abf[:, 2 * WO_OFF:2 * WO_OFF + 8 * 1024].rearrange("p (k c) -> p k c", k=8)
            wup0 = abf[:, 2 * E0_OFF:2 * E0_OFF + 4096].rearrange("p (k c) -> p k c", k=8)
            wdn0 = abf[:, 2 * E0_OFF + 4096:2 * E0_OFF + 8192].rearrange("p (k c) -> p k c", k=4)
            PF["d_wE0"] = P.dsem("wE0")
            nw = A.f32(1024)
            xt = [A.f32(1024) for _ in range(2)]
            hb = [A.bf(1024) for _ in range(2)]
            junk = A.bf(1024)
            st_ = A.f32(16)
            d_x = [P.dsem("xD0"), P.dsem("xD1")]
            DMA("pool", wup0, w_up[:, 0:512].rearrange("(kc q) n -> q kc n", q=128), PF["d_wE0"], w=[("wE", 0)])
            DMA("pool", wdn0, w_down[0:512, :].rearrange("(f q) n -> q f n", q=128), PF["d_wE0"], w=[("wE", 0)])
            DMA("sp", nw, vecs[:, 1, :], P.dsem("nwD"), w=["nw2"])
            def stage1(t):
                s = t % 2
                rows = rows_of(t)
                DMA("sp", xt[s][0:rows], xin[t * 128:t * 128 + rows, :], d_x[s], w=[("xtD", s)])
                for half in range(2):
                    b = 2 * s + half
                    for kc in range(8):
                        MM(banks[b][0:rows, :], mgT[:, kc, t * 128:t * 128 + rows], wo[:, kc, half * 512:(half + 1) * 512], start=(kc == 0), stop=(kc == 7),
                           r=["wo"] + mg_toks(t), w=[BK(b)])
                    STT("dve", x1[0:rows, t, half * 512:(half + 1) * 512], banks[b][0:rows, :], 0.5, xt[s][0:rows, half * 512:(half + 1) * 512],
                        ALU.mult, ALU.add, r=[BK(b), ("xtD", s)], w=[("x1", t)])
                s3 = t % 3
                ss = st_[:, s3:s3 + 1]
                ms = st_[:, 4 + s3:5 + s3]
                rs = st_[:, 8 + s3:9 + s3]
                ACT(junk[0:rows], x1[0:rows, t, :], AF.Square, r=[("x1", t)], w=["junkD", ("ssD", s3)], accum_out=ss[0:rows])
                rstd_from_ss(ss[0:rows], ms[0:rows], rs[0:rows], 1.0 / D, EPS, [("ssD", s3)], ("msD", s3), ("rsD", s3))

            def stage2a(t):
                s = t % 2
                rows = rows_of(t)
                s3 = t % 3
                rs = st_[:, 8 + s3:9 + s3]
                STT("dve", hb[s][0:rows], x1[0:rows, t, :], rs[0:rows], nw[0:rows], ALU.mult, ALU.mult, r=[("x1", t), ("rsD", s3), "nw2"], w=[("hbD", s)])

            def stage2b(t):
                s = t % 2
                rows = rows_of(t)
                b = 4 + s
                pv = bankb[b][:, :].rearrange("p (a c) -> p a c", a=8)
                for kc in range(8):
                    TR(pv[:, kc, 0:rows], hb[s][0:rows, kc * 128:(kc + 1) * 128], idb[0:rows, 0:rows], r=[("hbD", s)] + CONST, w=[BK(b)])
                CP("act", hT[:, :, t * 128:t * 128 + rows], pv[:, :, 0:rows], r=[BK(b)], w=[("hmT", t)])

            for t in range(NTL + 2):
                if t >= 2:
                    stage2a(t - 2)
                if t < NTL:
                    stage1(t)
                if t >= 2:
                    stage2b(t - 2)
            dump("x1", x1[:, :, :], [("x1", t) for t in range(NTL)])

        def hm_toks(c0, n):
            return [("hmT", t) for t in range(c0 // 128, (c0 + n + 127) // 128)]

        def phase_E():
            A = Bump(arena, abf, [(R_T[0], E0_OFF), R_G])
            NP = 8
            wup = [abf[:, 2 * E0_OFF:2 * E0_OFF + 4096].rearrange("p (k c) -> p k c", k=8), A.bf(8 * 512).rearrange("p (k c) -> p k c", k=8)]
            wdn = [abf[:, 2 * E0_OFF + 4096:2 * E0_OFF + 8192].rearrange("p (k c) -> p k c", k=4), A.bf(4 * 1024).rearrange("p (k c) -> p k c", k=4)]
            rl = [A.bf(512) for _ in range(2)]
            upT = [A.bf(4 * 512).rearrange("p (k c) -> p k c", k=4) for _ in range(2)]
            nw = A.f32(1024)
            yt = [A.f32(1024) for _ in range(2)]
            junk = A.bf(1024)
            st_ = A.f32(16)
            d_w = [PF["d_wE0"], P.dsem("wE1")]
            d_n = P.dsem("nwE")
            d_y = [P.dsem("yE0"), P.dsem("yE1")]
            DMA("sp", nw, vecs[:, 2, :], d_n, w=["nw3"])
            def load_w(p):
                ws = p % 2
                DMA("pool", wup[ws], w_up[:, p * 512:(p + 1) * 512].rearrange("(kc q) n -> q kc n", q=128), d_w[ws], w=[("wE", ws)])
                DMA("pool", wdn[ws], w_down[p * 512:(p + 1) * 512, :].rearrange("(f q) n -> q f n", q=128), d_w[ws], w=[("wE", ws)])

            tiles = [(p, T) for p in range(NP) for T in range(5)]
            dn_ctr = [0]

            def emit_up(k):
                p, T = tiles[k]
                ws, us = p % 2, k % 2
                c0 = T * 512
                n = 512 if T < 4 else 64
                for ffc in range(4):
                    b = ffc % 2
                    for kc in range(8):
                        MM(banks[b][:, 0:n], wup[ws][:, kc, ffc * 128:(ffc + 1) * 128], hT[:, kc, c0:c0 + n], start=(kc == 0), stop=(kc == 7),
                           r=[("wE", ws)] + hm_toks(c0, n), w=[BK(b)])
                    ACT(rl[b][:, 0:n], banks[b][:, 0:n], AF.Relu, r=[BK(b)], w=[("rl", b)])
                    TT("dve", upT[us][:, ffc, 0:n], rl[b][:, 0:n], rl[b][:, 0:n], ALU.mult, r=[("rl", b)], w=[("upT", us, ffc)])

            def emit_down(k):
                p, T = tiles[k]
                ws, us = p % 2, k % 2
                c0 = T * 512
                n = 512 if T < 4 else 64
                for t in range(c0 // 128, (c0 + n + 127) // 128):
                    rows = rows_of(t)
                    j0 = t * 128 - c0
                    for half in range(2):
                        b = 2 + (dn_ctr[0] % 4)
                        dn_ctr[0] += 1
                        for ffc in range(4):
                            MM(banks[b][0:rows, :], upT[us][:, ffc, j0:j0 + rows], wdn[ws][:, ffc, half * 512:(half + 1) * 512], start=(ffc == 0), stop=(ffc == 3),
                               r=[("wE", ws), ("upT", us, ffc)], w=[BK(b)])
                        hs = slice(half * 512, (half + 1) * 512)
                        TT("dve", x1[0:rows, t, hs], x1[0:rows, t, hs], banks[b][0:rows, :], ALU.add, r=[("x1", t), BK(b)], w=[("x1", t)])
                    if p == NP - 1:
                        s = t % 2
                        ss = st_[:, s:s + 1]
                        ms = st_[:, 4 + s:5 + s]
                        rs = st_[:, 8 + s:9 + s]
                        ACT(junk[0:rows], x1[0:rows, t, :], AF.Square, r=[("x1", t)], w=["junkE", ("ssE", s)], accum_out=ss[0:rows])
                        rstd_from_ss(ss[0:rows], ms[0:rows], rs[0:rows], 1.0 / D, EPS, [("ssE", s)], ("msE", s), ("rsE", s))
                        STT("dve", yt[s][0:rows], x1[0:rows, t, :], rs[0:rows], nw[0:rows], ALU.mult, ALU.mult, r=[("x1", t), ("rsE", s), "nw3"], w=[("yt", s)])
                        DMA("sp", yout[t * 128:t * 128 + rows, :], yt[s][0:rows], d_y[s], r=[("yt", s)])
                if T == 4 and p + 2 < NP:
                    load_w(p + 2)

            load_w(1)
            emit_up(0)
            for k in range(len(tiles)):
                if k + 1 < len(tiles):
                    emit_up(k + 1)
                emit_down(k)

        phases = [("N", phase_N), ("B", phase_B), ("A", phase_A), ("C", phase_C), ("D", phase_D), ("E", phase_E)]
        for name, fn in phases:
            try:
                fn()
            except StopBuild:
                P.barrier()
                break
            P.barrier()
            if stop_after == name:
                break
        P.emit()
    return nc, dbg_outs


def _constants():
    f = np.zeros((128, NF), np.float32)
    i = np.arange(128)
    f[:, C_IDF:C_IDF + 128] = np.eye(128, dtype=np.float32)
    f[:, C_UI:C_UI + 128] = (i[:, None] <= i[None, :])
    f[:, C_SL:C_SL + 128] = (i[:, None] > i[None, :])
    f[:, C_ONES:C_ONES + 128] = 1.0
    tok = np.arange(64)
    l_, s_ = tok // 16, tok % 16
    same = s_[:, None] == s_[None, :]
    f[0:64, C_UIS:C_UIS + 64] = same & (l_[:, None] <= l_[None, :])
    f[0:64, C_SLS:C_SLS + 64] = same & (l_[:, None] > l_[None, :])
    f[0:64, C_SEQIND:C_SEQIND + 16] = (s_[:, None] == np.arange(16)[None, :])
    sel = np.zeros((16, 8, 128), np.float32)
    for h in range(16):
        sel[h, h // 2, (h % 2) * 64:(h % 2 + 1) * 64] = 1.0
    f[0:16, C_SELBLK:C_SELBLK + 1024] = sel.reshape(16, 1024)
    b = np.zeros((128, NB), np.float32)
    b[:, B_IDB:B_IDB + 128] = np.eye(128)
    b[:, B_MP:B_MP + 128] = (i[:, None] >= i[None, :])
    b[:, B_MC:B_MC + 128] = (i[:, None] <= i[None, :])
    b[:, B_MCACHE:B_MCACHE + 4] = (i[:, None] >= np.arange(4)[None, :])
    b[0:64, B_MNEW:B_MNEW + 64] = same & (l_[:, None] <= l_[None, :])
    b[:, B_UIB:B_UIB + 128] = (i[:, None] <= i[None, :])
    b[0:64, B_UISB:B_UISB + 64] = same & (l_[:, None] <= l_[None, :])
    op = np.zeros((128, 2, 128), np.float32)
    op[:, 0, 0:64] = 1.0
    op[:, 1, 64:128] = 1.0
    b[:, B_ONESPAD:B_ONESPAD + 256] = op.reshape(128, 256)
    seqm = (np.arange(16)[:, None] == s_[None, :]).astype(np.float32)
    b[:, B_SEQM:B_SEQM + 1024] = np.broadcast_to(seqm.reshape(1, 1024), (128, 1024))
    b[0:64, B_SEQINDB:B_SEQINDB + 16] = (s_[:, None] == np.arange(16)[None, :])
    return f, b.astype(ml_dtypes.bfloat16)


_CACHE = {}


def _get_program(stop_after=None, dbg=()):
    key = (stop_after, tuple(dbg))
    if key not in _CACHE:
        _CACHE[key] = build_program(stop_after, dbg)
    return _CACHE[key]


def make_in_maps(inp):
    f32 = np.float32
    g = lambda k: np.ascontiguousarray(np.asarray(inp[k], dtype=f32))
    x_prompt, x_sample = g("x_prompt"), g("x_sample")
    ck, cv = g("cache_swa_k")[0], g("cache_swa_v")[0]
    sconv, sssm = g("state_conv")[0], g("state_ssm")[0]
    cf, cbb = _constants()
    hv = np.stack([g("dt_bias")[0], g("a_log")[0], g("d_skip")[0]]).reshape(1, 48)
    cf[:, C_HV:C_HV + 48] = hv
    cw = g("conv_w")[0]
    cf[:, C_CONVW:C_CONVW + 48] = cw.reshape(4, 12, 128).transpose(2, 1, 0).reshape(128, 48)
    cf[:, C_CONVB:C_CONVB + 12] = g("conv_b")[0].reshape(12, 128).T
    sk = g("sinks")[0]
    cf[:, C_SINKS:C_SINKS + 4] = sk.reshape(2, 4)[np.arange(128) // 64]
    vecs = np.stack([g("norm1")[0], g("norm2")[0], g("final_norm"), g("ssm_norm")[0]])
    vecs = np.ascontiguousarray(np.broadcast_to(vecs[None], (128, 4, D)))
    shared = dict(w_in=g("w_in")[0], w_oa=g("w_oa")[0], w_ob=g("w_ob")[0], w_o=g("w_o")[0], w_up=g("w_up")[0],
                  w_down=g("w_down")[0], vecs=vecs, cstf=cf, cstb=cbb)
    maps = []
    for c in range(NCORES):
        ss = slice(16 * c, 16 * c + 16)
        xs = x_sample[ss].transpose(1, 0, 2).reshape(64, D)
        m = dict(shared)
        m["xin"] = np.ascontiguousarray(np.concatenate([x_prompt[c], xs], axis=0))
        m["ck"] = np.ascontiguousarray(ck[ss].reshape(16, 128, 128))
        m["cv"] = np.ascontiguousarray(cv[ss].reshape(16, 128, 128))
        m["sconv"] = np.ascontiguousarray(sconv[ss])
        m["sssm"] = np.ascontiguousarray(sssm[ss].reshape(16, 1024, 128))
        maps.append(m)
    return maps


def kernel(**inputs):
    nc, _ = _get_program()
    maps = make_in_maps(inputs)
    res = run_bass_kernel_spmd(nc, maps, core_ids=list(range(NCORES)))
    R = res.results
    f32 = np.float32
    y_prompt = np.stack([np.asarray(R[c]["yout"], f32)[0:2048] for c in range(NCORES)])
    y_sample = np.concatenate([np.asarray(R[c]["yout"], f32)[2048:].reshape(4, 16, D).transpose(1, 0, 2) for c in range(NCORES)], axis=0)
    nkp = np.stack([np.asarray(R[c]["nkp"], f32).reshape(128, 2, 64) for c in range(NCORES)])[None]
    nvp = np.stack([np.asarray(R[c]["nvp"], f32).reshape(128, 2, 64) for c in range(NCORES)])[None]
    ncp = np.stack([np.asarray(R[c]["ncp"], f32) for c in range(NCORES)])[None]
    nsp = np.stack([np.asarray(R[c]["nsp"], f32).reshape(16, 64, 128) for c in range(NCORES)])[None]
    nks = np.concatenate([np.asarray(R[c]["nks"], f32).reshape(16, 128, 2, 64) for c in range(NCORES)], axis=0)[None]
    nvs = np.concatenate([np.asarray(R[c]["nvs"], f32).reshape(16, 128, 2, 64) for c in range(NCORES)], axis=0)[None]
    ncs = np.concatenate([np.asarray(R[c]["ncs"], f32) for c in range(NCORES)], axis=0)[None]
    nss = np.concatenate([np.asarray(R[c]["nss"], f32).reshape(16, 16, 64, 128) for c in range(NCORES)], axis=0)[None]
    return (y_prompt, y_sample, nkp, nvp, ncp, nsp, nks, nvs, ncs, nss)
```

```python
import numpy as np
import ml_dtypes
from contextlib import ExitStack
import concourse.bass as bass
import concourse.mybir as mybir
from concourse.bass_utils import run_bass_kernel_spmd

F32 = mybir.dt.float32
BF16 = mybir.dt.bfloat16
AF = mybir.ActivationFunctionType
ALU = mybir.AluOpType

NCORES = 8
D = 1024
NT = 2112
NTL = 17
SC0 = 2048
EPS = 1e-6
INW = 5392
ARENA_WORDS = 53000

C_IDF, C_UI, C_SL, C_ONES, C_UIS, C_SLS, C_SEQIND, C_SELBLK = 0, 128, 256, 384, 512, 576, 640, 656
C_HV, C_CONVW, C_CONVB, C_SINKS, NF = 1680, 1728, 1776, 1788, 1792
B_IDB, B_MP, B_MC, B_MCACHE, B_MNEW, B_UIB, B_UISB, B_ONESPAD, B_SEQM, B_SEQINDB, NB = \
    0, 128, 256, 384, 388, 452, 580, 644, 900, 1924, 1940


class DSem:
    def __init__(self, sem, name):
        self.sem = sem
        self.count = 0
        self.name = name
        self.open = None

    def add(self):
        self.count += 16
        if self.open is None:
            self.open = {"final": None}
        return self.open

    def seal(self):
        if self.open is not None:
            self.open["final"] = self.count
            self.open = None


class Prog:
    ENG_NAMES = ("pe", "act", "dve", "pool", "sp")

    def __init__(self, nc, stack):
        self.nc = nc
        self.stack = stack
        self.ops = []
        self.last_w = {}
        self.readers = {}
        self.esem = {e: stack.enter_context(nc.semaphore("s_" + e)) for e in ("pe", "act", "dve", "pool")}
        self.dsems = []
        self.last_on_eng = {e: None for e in self.ENG_NAMES}

    def dsem(self, name):
        d = DSem(self.stack.enter_context(self.nc.semaphore("d_" + name)), name)
        self.dsems.append(d)
        return d

    def _deps(self, r, w, eng=None):
        deps = set()
        for t in r:
            if t in self.last_w:
                deps.add(self.last_w[t])
            if isinstance(t, tuple) and t[0] == "B":
                for x in self.readers.get(t, ()):
                    if self.ops[x]["eng"] != eng:
                        deps.add(x)
        for t in w:
            if t in self.last_w:
                deps.add(self.last_w[t])
            for x in self.readers.get(t, ()):
                deps.add(x)
        for d in deps:
            p = self.ops[d]
            if p["kind"] == "dma" and p["grp"]["final"] is None:
                p["dsem"].seal()
        return deps

    def _commit(self, idx, r, w):
        for t in r:
            self.readers.setdefault(t, []).append(idx)
        for t in w:
            self.last_w[t] = idx
            self.readers[t] = []

    def op(self, eng, fn, r=(), w=()):
        idx = len(self.ops)
        deps = self._deps(r, w, eng)
        self.ops.append(dict(eng=eng, fn=fn, deps=deps, kind="c", inc=False, cnt=0))
        self._commit(idx, r, w)
        self.last_on_eng[eng] = idx
        return idx

    def dma(self, eng, fn, r=(), w=(), dsem=None):
        idx = len(self.ops)
        deps = self._deps(r, w, eng)
        grp = dsem.add()
        self.ops.append(dict(eng=eng, fn=fn, deps=deps, kind="dma", dsem=dsem, grp=grp))
        self._commit(idx, r, w)
        return idx

    def barrier(self):
        deps = set()
        for e in ("pe", "act", "dve", "pool"):
            if self.last_on_eng[e] is not None:
                deps.add(self.last_on_eng[e])
        for d in self.dsems:
            d.seal()
        dvals = [(d, d.count) for d in self.dsems if d.count > 0]
        for e in self.ENG_NAMES:
            self.ops.append(dict(eng=e, fn=None, deps=set(deps), kind="bar", dvals=dvals, inc=False, cnt=0))

    def finalize(self):
        ops = self.ops
        for d in self.dsems:
            d.seal()
        for o in ops:
            for d in o["deps"]:
                p = ops[d]
                if p["kind"] == "c":
                    if p["eng"] == "pe" and o["eng"] == "pe" and o["kind"] == "c":
                        continue
                    p["inc"] = True
        cnt = {e: 0 for e in self.ENG_NAMES}
        for o in ops:
            if o["kind"] == "c":
                if o["inc"]:
                    cnt[o["eng"]] += 1
                o["cnt"] = cnt[o["eng"]]

    def emit_engine(self, ename, engine):
        ops = self.ops
        waited = {}

        def wait(sem, key, val):
            if waited.get(key, 0) < val:
                engine.wait_ge(sem, val)
                waited[key] = val

        for o in ops:
            if o["eng"] != ename:
                continue
            for d in sorted(o["deps"]):
                p = ops[d]
                if p["kind"] == "c":
                    if p["eng"] == "pe" and ename == "pe" and o["kind"] == "c":
                        continue
                    wait(self.esem[p["eng"]], p["eng"], p["cnt"])
                elif p["kind"] == "dma":
                    wait(p["dsem"].sem, "d_" + p["dsem"].name, p["grp"]["final"])
            if o["kind"] == "bar":
                for d, v in o["dvals"]:
                    wait(d.sem, "d_" + d.name, v)
                continue
            ins = o["fn"](engine)
            if o["kind"] == "dma":
                ins.then_inc(o["dsem"].sem, 16)
            elif o["inc"]:
                ins.then_inc(self.esem[ename], 1)

    def emit(self):
        self.finalize()
        nc = self.nc
        with nc.Block() as block:
            @block.sync
            def _(e):
                self.emit_engine("sp", e)

            @block.tensor
            def _(e):
                self.emit_engine("pe", e)

            @block.scalar
            def _(e):
                self.emit_engine("act", e)

            @block.vector
            def _(e):
                self.emit_engine("dve", e)

            @block.gpsimd
            def _(e):
                self.emit_engine("pool", e)


class Bump:
    def __init__(self, arena, abf, regions):
        self.arena = arena
        self.abf = abf
        self.regs = [list(r) for r in regions]

    def _take(self, w):
        w = (w + 7) // 8 * 8
        for r in self.regs:
            if r[1] - r[0] >= w:
                off = r[0]
                r[0] += w
                return off
        raise RuntimeError("arena region overflow: need %d words, regions %s" % (w, self.regs))

    def f32(self, n):
        off = self._take(n)
        return self.arena[:, off:off + n]

    def bf(self, n):
        off = self._take((n + 1) // 2)
        return self.abf[:, 2 * off:2 * off + n]

    def take(self, w):
        return self._take(w)

    def fview(self, off, n):
        return self.arena[:, off:off + n]

    def bview(self, off, n):
        return self.abf[:, 2 * off:2 * off + n]

    def free_words(self):
        return sum(r[1] - r[0] for r in self.regs)


class StopBuild(Exception):
    pass


def rows_of(t):
    return 128 if t < 16 else 64


def build_program(stop_after=None, dbg=()):
    nc = bass.Bass("TRN2", target_bir_lowering=False)

    def din(name, shape, dt=F32):
        return nc.dram_tensor(name, list(shape), dt, kind="ExternalInput").ap()

    def dout(name, shape, dt=F32):
        return nc.dram_tensor(name, list(shape), dt, kind="ExternalOutput").ap()

    xin = din("xin", [NT, D])
    ck = din("ck", [16, 128, 128])
    cv = din("cv", [16, 128, 128])
    sconv = din("sconv", [16, 3, 1536])
    sssm = din("sssm", [16, 1024, 128])
    w_in = din("w_in", [D, INW])
    w_oa = din("w_oa", [512, D])
    w_ob = din("w_ob", [D, D])
    w_o = din("w_o", [D, D])
    w_up = din("w_up", [D, 4096])
    w_down = din("w_down", [4096, D])
    vecs = din("vecs", [128, 4, D])
    cstf = din("cstf", [128, NF])
    cstb = din("cstb", [128, NB], BF16)

    yout = dout("yout", [NT, D])
    nkp = dout("nkp", [128, 128])
    nvp = dout("nvp", [128, 128])
    ncp = dout("ncp", [3, 1536])
    nsp = dout("nsp", [1024, 128])
    nks = dout("nks", [16, 128, 128])
    nvs = dout("nvs", [16, 128, 128])
    ncs = dout("ncs", [16, 3, 1536])
    nss = dout("nss", [16, 1024, 128])

    dbg_outs = {}

    with ExitStack() as st:
        st.enter_context(nc.allow_low_precision("bf16 matmul operands, fp32 accumulation"))
        st.enter_context(nc.allow_non_contiguous_dma("strided weight / state layouts"))
        arena = st.enter_context(nc.sbuf_tensor("arena", [128, ARENA_WORDS], F32))
        abf = arena.bitcast(BF16)
        banks = [st.enter_context(nc.psum_tensor("bank%d" % i, [128, 512], F32)) for i in range(8)]
        bankb = [b.bitcast(BF16) for b in banks]
        P = Prog(nc, st)

        def MM(out, lhsT, rhs, start=True, stop=True, r=(), w=()):
            P.op("pe", lambda e: e.matmul(out, lhsT=lhsT, rhs=rhs, start=start, stop=stop), r, w)

        def TR(out, in_, ident, r=(), w=()):
            P.op("pe", lambda e: e.transpose(out=out, in_=in_, identity=ident), r, w)

        def ACT(out, in_, func, r=(), w=(), bias=None, scale=1.0, accum_out=None):
            def f(e):
                kw = dict(scale=scale)
                if bias is not None:
                    kw["bias"] = bias
                if accum_out is not None:
                    kw["accum_out"] = accum_out
                return e.activation(out=out, in_=in_, func=func, **kw)
            P.op("act", f, r, w)

        def CP(eng, out, in_, r=(), w=()):
            if eng == "act":
                P.op("act", lambda e: e.copy(out=out, in_=in_), r, w)
            else:
                P.op(eng, lambda e: e.tensor_copy(out=out, in_=in_), r, w)

        def TT(eng, out, in0, in1, op, r=(), w=()):
            P.op(eng, lambda e: e.tensor_tensor(out=out, in0=in0, in1=in1, op=op), r, w)

        def TS(eng, out, in0, s1, s2, op0, op1=None, r=(), w=()):
            if op1 is None:
                P.op(eng, lambda e: e.tensor_scalar(out=out, in0=in0, scalar1=s1, scalar2=None, op0=op0), r, w)
            else:
                P.op(eng, lambda e: e.tensor_scalar(out=out, in0=in0, scalar1=s1, scalar2=s2, op0=op0, op1=op1), r, w)

        def STT(eng, out, in0, scalar, in1, op0, op1, r=(), w=()):
            P.op(eng, lambda e: e.scalar_tensor_tensor(out=out, in0=in0, scalar=scalar, in1=in1, op0=op0, op1=op1), r, w)

        def MEMSET(eng, ap, val, w=()):
            P.op(eng, lambda e: e.memset(ap, val), (), w)

        def DMA(eng, out, in_, dsem, r=(), w=()):
            P.dma(eng, lambda e: e.dma_start(out=out, in_=in_), r, w, dsem)

        def BK(i):
            return ("B", i)

        def CHECK(tag):
            if stop_after == tag:
                raise StopBuild()

        d_out = P.dsem("out")
        d_dbg = P.dsem("dbg")

        def dump(name, ap, tokens, dt=F32):
            if name not in dbg:
                return
            shp = list(ap.shape)
            o = dout("dbg_" + name, shp, dt)
            dbg_outs[name] = shp
            DMA("sp", o, ap, d_dbg, r=tokens)

        cf = arena[:, 0:NF]
        cbw = (NB + 1) // 2
        cb = abf[:, 2 * NF:2 * NF + NB]
        o_small = NF + cbw + 4
        small = arena[:, o_small:o_small + 128]
        O_H = 2944
        assert o_small + 128 <= O_H
        O_X = O_H + 8448
        O_G = O_X + 17408
        O_T = O_G + 8448
        hT = abf[:, 2 * O_H:2 * O_H + 8 * NT].rearrange("p (a b) -> p a b", a=8)
        aoT = abf[:, 2 * O_X:2 * O_X + 4 * NT].rearrange("p (a b) -> p a b", a=4)
        moT = abf[:, 2 * (O_X + 4224):2 * (O_X + 4224) + 8 * NT].rearrange("p (a b) -> p a b", a=8)
        mgT = abf[:, 2 * O_G:2 * O_G + 8 * NT].rearrange("p (a b) -> p a b", a=8)
        x1 = arena[:, O_X:O_X + NTL * D].rearrange("p (a b) -> p a b", a=NTL)
        X_SPARE = (O_X + 12672, O_X + 17408)
        R_G = (O_G, O_G + 8448)
        R_T = (O_T, ARENA_WORDS)
        R_H = (O_H, O_H + 8448)

        idf = cf[:, C_IDF:C_IDF + 128]
        idb = cb[:, B_IDB:B_IDB + 128]

        d_c = P.dsem("const")
        DMA("sp", cf, cstf, d_c, w=["cf"])
        DMA("sp", cb, cstb, d_c, w=["cb"])
        CONST = ["cf", "cb"]

        sinkexp = small[:, 0:4]
        a_neg = small[:, 8:24]
        neghalf = small[:, 24:25]
        ACT(sinkexp, cf[:, C_SINKS:C_SINKS + 4], AF.Exp, r=CONST, w=["sinkexp"])
        ACT(a_neg, cf[:, C_HV + 16:C_HV + 32], AF.Exp, r=CONST, w=["a_neg0"])
        TS("dve", a_neg, a_neg, -1.0, None, ALU.mult, r=["a_neg0"], w=["a_neg"])
        MEMSET("pool", neghalf, -0.5, w=["neghalf"])
        dtb_bc = cf[:, C_HV:C_HV + 16]
        dskip_bc = cf[:, C_HV + 32:C_HV + 48]

        bank_ctr = [0]
        PF = {}
        WO_OFF = ARENA_WORDS - 8 - 4096
        E0_OFF = WO_OFF - 4096
        C0_OFF = WO_OFF - 3584

        def c0_views():
            o = 2 * C0_OFF
            wga0 = abf[:, o:o + 2048].rearrange("p (k c) -> p k c", k=8)
            wgb0 = abf[:, o + 2048:o + 4096].rearrange("p (k c) -> p k c", k=8)
            woa0 = abf[:, o + 4096:o + 5120].rearrange("p (k c) -> p k c", k=4)
            wob0 = abf[:, o + 5120:o + 7168].rearrange("p (k c) -> p k c", k=8)
            return wga0, wgb0, woa0, wob0

        def load_c_into(qi, wga_, wgb_, woa_, wob_, dsem_, ws):
            cs = slice(qi * 256, (qi + 1) * 256)
            woa_src = w_oa.rearrange("(kv g d) n -> kv d g n", kv=2, g=4)
            DMA("pool", wga_, w_in[:, 3344 + qi * 256:3344 + (qi + 1) * 256].rearrange("(kc p) n -> p kc n", p=128), dsem_, w=[("wC", ws)])
            DMA("pool", wgb_, w_in[:, 4368 + qi * 256:4368 + (qi + 1) * 256].rearrange("(kc p) n -> p kc n", p=128), dsem_, w=[("wC", ws)])
            for kv in range(2):
                DMA("pool", woa_[kv * 64:(kv + 1) * 64, :, :], woa_src[kv][:, :, cs], dsem_, w=[("wC", ws)])
            DMA("pool", wob_, w_ob[:, cs].rearrange("(kc p) n -> p kc n", p=128), dsem_, w=[("wC", ws)])

        def rstd_from_ss(ss_ap, ms_ap, rstd_ap, inv_n, eps, toks_r, tok_ms, tok_rstd):
            rows_ = ms_ap.shape[0]
            TS("pool", ms_ap, ss_ap, inv_n, eps, ALU.mult, ALU.add, r=toks_r, w=[tok_ms])
            TT("pool", rstd_ap, ms_ap, neghalf[0:rows_], ALU.pow, r=[tok_ms, "neghalf"], w=[tok_rstd])

        def phase_N():
            A = Bump(arena, abf, [R_T, R_G])
            wx_p = A.bf(8 * 1536).rearrange("p (k c) -> p k c", k=8)
            wz_p = A.bf(8 * 1024).rearrange("p (k c) -> p k c", k=8)
            PF["d_wB"] = P.dsem("wB")
            DMA("pool", wx_p, w_in[:, 1792:3328].rearrange("(kc p) n -> p kc n", p=128), PF["d_wB"], w=["wx"])
            nw = A.f32(1024)
            NLAG = 3
            NSL = NLAG + 2
            xt = [A.f32(1024) for _ in range(NSL)]
            hb = [A.bf(1024) for _ in range(2)]
            junk = A.bf(1024)
            st_ = A.f32(16)
            d_nw = P.dsem("nw1")
            d_x = [P.dsem("xN%d" % i) for i in range(NSL)]
            DMA("sp", nw, vecs[:, 0, :], d_nw, w=["nw"])
            def stage1(t):
                s = t % NSL
                rows = rows_of(t)
                DMA("sp", xt[s][0:rows], xin[t * 128:t * 128 + rows, :], d_x[s], w=[("xt", s)])
                ss = st_[:, s:s + 1]
                ms = st_[:, 5 + s:6 + s]
                rs = st_[:, 10 + s:11 + s]
                ACT(junk[0:rows], xt[s][0:rows], AF.Square, r=[("xt", s)], w=["junk", ("ss", s)], accum_out=ss[0:rows])
                rstd_from_ss(ss[0:rows], ms[0:rows], rs[0:rows], 1.0 / D, EPS, [("ss", s)], ("ms", s), ("rs", s))

            def stage2(t):
                s = t % NSL
                h2 = t % 2
                rows = rows_of(t)
                rs = st_[:, 10 + s:11 + s]
                STT("dve", hb[h2][0:rows], xt[s][0:rows], rs[0:rows], nw[0:rows], ALU.mult, ALU.mult,
                    r=[("xt", s), ("rs", s), "nw"], w=[("hb", h2)])
                b = t % 2
                pv = bankb[b][:, :].rearrange("p (a b) -> p a b", a=8)
                for kc in range(8):
                    TR(pv[:, kc, 0:rows], hb[h2][0:rows, kc * 128:(kc + 1) * 128], idb[0:rows, 0:rows],
                       r=[("hb", h2)] + CONST, w=[BK(b)])
                CP("act", hT[:, :, t * 128:t * 128 + rows], pv[:, :, 0:rows], r=[BK(b)], w=[("hT", t)])

            for t in range(NTL + NLAG):
                if t < NTL:
                    stage1(t)
                if t >= NLAG:
                    stage2(t - NLAG)

        HT_ALL = [("hT", t) for t in range(NTL)]

        def hT_toks(c0, n):
            return [("hT", t) for t in range(c0 // 128, (c0 + n + 127) // 128)]

        def phase_A():
            A = Bump(arena, abf, [(R_T[0], C0_OFF), R_G, X_SPARE, (WO_OFF, R_T[1])])
            wq = A.bf(8 * 512).rearrange("p (k g c) -> p k g c", k=8, g=4)
            wk = A.bf(8 * 128).rearrange("p (k c) -> p k c", k=8)
            wv = A.bf(8 * 128).rearrange("p (k c) -> p k c", k=8)
            assert A.regs[0][0] == O_T + 3072
            qT = A.bf(4 * NT).rearrange("p (g t) -> p g t", g=4)
            kT = A.bf(NT)
            vpad = A.bf(NTL * 256).rearrange("p (t k c) -> p t k c", t=NTL, k=2)
            kf = [A.f32(128) for _ in range(2)]
            vf = [A.f32(128) for _ in range(2)]
            pbuf = [A.bf(512) for _ in range(8)]
            dsum = [A.f32(512) for _ in range(2)]
            rden = [A.f32(512) for _ in range(2)]
            ckb = A.bf(16 * 128).rearrange("p (s c) -> p s c", s=16)
            kcT = A.bf(16 * 128).rearrange("p (s c) -> p s c", s=16)
            vcpad = A.bf(16 * 256).rearrange("p (s k c) -> p s k c", s=16, k=2)
            pc = A.bf(512)
            pn = A.bf(512)
            d_cache = P.dsem("cacheA")

            VP_ALL = [("vpad", t) for t in range(NTL)]
            MEMSET("pool", vpad[:, :, :, :], 0.0, w=VP_ALL)
            MEMSET("pool", vcpad[:, :, :, :], 0.0, w=["vcpad"])
            DMA("pool", ckb, ck.rearrange("s t c -> t s c"), d_cache, w=["ckb"])
            for kv in range(2):
                DMA("pool", vcpad[:, :, kv, kv * 64:(kv + 1) * 64], cv[:, :, kv * 64:(kv + 1) * 64].rearrange("s t c -> t s c"),
                    d_cache, r=[], w=["vcpad"])
            DMA("sp", nks[:, 0:124, :], ck[:, 4:128, :], d_out)
            DMA("sp", nvs[:, 0:124, :], cv[:, 4:128, :], d_out)

            def nb():
                b = bank_ctr[0] % 8
                bank_ctr[0] += 1
                return b

            ev = [0]

            def evac_eng():
                ev[0] += 1
                return "act" if ev[0] % 2 else "dve"

            CHECK("A0")
            for T in range(5):
                c0 = T * 512
                n = 512 if T < 4 else 64
                htk = hT_toks(c0, n)
                for g in range(4):
                    b = nb()
                    for kc in range(8):
                        MM(banks[b][:, 0:n], wq[:, kc, g, :], hT[:, kc, c0:c0 + n], start=(kc == 0), stop=(kc == 7),
                           r=[("wq", g)] + htk, w=[BK(b)])
                    CP(evac_eng(), qT[:, g, c0:c0 + n], banks[b][:, 0:n], r=[BK(b)], w=[("qT", g, T)])
                b = nb()
                for kc in range(8):
                    MM(banks[b][:, 0:n], wk[:, kc, :], hT[:, kc, c0:c0 + n], start=(kc == 0), stop=(kc == 7),
                       r=["wk"] + htk, w=[BK(b)])
                CP(evac_eng(), kT[:, c0:c0 + n], banks[b][:, 0:n], r=[BK(b)], w=[("kT", T)])
                for t in range(c0 // 128, (c0 + n + 127) // 128):
                    rows = rows_of(t)
                    b = nb()
                    for kc in range(8):
                        MM(banks[b][0:rows, 0:128], hT[:, kc, t * 128:t * 128 + rows], wv[:, kc, :], start=(kc == 0), stop=(kc == 7),
                           r=["wv", ("hT", t)], w=[BK(b)])
                    e = evac_eng()
                    for kv in range(2):
                        CP(e, vpad[0:rows, t, kv, kv * 64:(kv + 1) * 64], banks[b][0:rows, kv * 64:(kv + 1) * 64],
                           r=[BK(b)], w=[("vpad", t)])
                    if t >= 15:
                        i = t - 15
                        CP("dve", vf[i][0:rows], banks[b][0:rows, 0:128], r=[BK(b)], w=[("vf", i)])
                        b2 = nb()
                        for kc in range(8):
                            MM(banks[b2][0:rows, 0:128], hT[:, kc, t * 128:t * 128 + rows], wk[:, kc, :], start=(kc == 0), stop=(kc == 7),
                               r=["wk", ("hT", t)], w=[BK(b2)])
                        CP("dve", kf[i][0:rows], banks[b2][0:rows, 0:128], r=[BK(b2)], w=[("kf", i)])
            CHECK("A1")
            DMA("sp", nkp, kf[0], d_out, r=[("kf", 0)])
            DMA("sp", nvp, vf[0], d_out, r=[("vf", 0)])
            for l in range(4):
                DMA("sp", nks[:, 124 + l, :], kf[1][l * 16:(l + 1) * 16, :], d_out, r=[("kf", 1)])
                DMA("sp", nvs[:, 124 + l, :], vf[1][l * 16:(l + 1) * 16, :], d_out, r=[("vf", 1)])
            dump("qT", qT[:, :, :], [("qT", g, T) for g in range(4) for T in range(5)], BF16)
            dump("kT", kT, [("kT", T) for T in range(5)], BF16)
            dump("vpad", vpad[:, :, :, :], VP_ALL, BF16)

            CHECK("A1b")
            scale = 64 ** -0.5
            mP = cb[:, B_MP:B_MP + 128]
            mC = cb[:, B_MC:B_MC + 128]
            onespad = cb[:, B_ONESPAD:B_ONESPAD + 256].rearrange("p (k c) -> p k c", k=2)
            sink_bc = sinkexp.unsqueeze(2).to_broadcast([128, 4, 128])

            pslot = [0]
            plists = {}

            def stage_s(i):
                T = i // 4
                q0 = i * 128
                js = ([i - 1] if i > 0 else []) + [i]
                plist = []
                for kv in range(2):
                    ks = slice(kv * 64, (kv + 1) * 64)
                    for j in js:
                        b = nb()
                        MM(banks[b][:, :].rearrange("p (g q) -> p g q", g=4), kT[ks, j * 128:(j + 1) * 128], qT[ks, :, q0:q0 + 128],
                           r=[("kT", j // 4)] + [("qT", g, T) for g in range(4)], w=[BK(b)])
                        sl = pslot[0] % 8
                        pslot[0] += 1
                        pb = pbuf[sl]
                        ACT(pb, banks[b][:, :], AF.Exp, r=[BK(b)], w=[("pb", sl)], scale=scale)
                        msk = mP if j < i else mC
                        TT("dve", pb.rearrange("p (g q) -> p g q", g=4), pb.rearrange("p (g q) -> p g q", g=4),
                           msk.unsqueeze(1).to_broadcast([128, 4, 128]), ALU.mult, r=[("pb", sl)] + CONST, w=[("pb", sl)])
                        plist.append((kv, j, sl))
                plists[i] = plist

            def stage_p(i):
                q0 = i * 128
                plist = plists[i]
                bo = nb()
                bd = nb()
                for idx, (kv, j, sl) in enumerate(plist):
                    MM(banks[bo][:, :], vpad[:, j, kv, :], pbuf[sl], start=(idx == 0), stop=(idx == len(plist) - 1),
                       r=[("vpad", j), ("pb", sl)], w=[BK(bo)])
                for idx, (kv, j, sl) in enumerate(plist):
                    MM(banks[bd][:, :], onespad[:, kv, :], pbuf[sl], start=(idx == 0), stop=(idx == len(plist) - 1),
                       r=[("pb", sl)] + CONST, w=[BK(bd)])
                s2 = i % 2
                TT("dve", dsum[s2].rearrange("p (g q) -> p g q", g=4), banks[bd][:, :].rearrange("p (g q) -> p g q", g=4), sink_bc,
                   ALU.add, r=[BK(bd), "sinkexp"], w=[("dsum", s2)])
                P.op("dve", lambda e, o=rden[s2], i_=dsum[s2]: e.reciprocal(out=o, in_=i_), r=[("dsum", s2)], w=[("rden", s2)])
                TT("dve", aoT[:, :, q0:q0 + 128], banks[bo][:, :].rearrange("p (g q) -> p g q", g=4),
                   rden[s2].rearrange("p (g q) -> p g q", g=4), ALU.mult, r=[BK(bo), ("rden", s2)], w=[("aoT", i)])

            stage_s(0)
            for i in range(16):
                if i + 1 < 16:
                    stage_s(i + 1)
                stage_p(i)

            CHECK("A2")
            PF["d_wC0"] = P.dsem("wC0")
            load_c_into(0, *c0_views(), PF["d_wC0"], 0)
            for half in range(2):
                b = nb()
                pv = bankb[b][:, :].rearrange("p (a c) -> p a c", a=8)
                for a in range(8):
                    s_ = half * 8 + a
                    TR(pv[:, a, :], ckb[:, s_, :], idb, r=["ckb"] + CONST, w=[BK(b)])
                CP(evac_eng(), kcT[:, half * 8:(half + 1) * 8, :], pv, r=[BK(b)], w=[("kcT", half)])
            CHECK("A3a")
            TQ = [("qT", g, 4) for g in range(4)]
            pcv = pc.rearrange("p (k s g l) -> p k s g l", k=2, s=16, g=4)
            mcache = cb[:, B_MCACHE:B_MCACHE + 4]
            for kv in range(2):
                ks = slice(kv * 64, (kv + 1) * 64)
                bsc = nb()
                scv = banks[bsc][:, 0:256].rearrange("p (s g l) -> p s g l", s=16, g=4)
                for s_ in range(16):
                    rhs = qT[ks, :, SC0:SC0 + 64].rearrange("p g (l s) -> p g l s", s=16)[:, :, :, s_]
                    MM(scv[:, s_, :, :], kcT[ks, s_, :], rhs, r=[("kcT", s_ // 8)] + TQ, w=[BK(bsc)])
                ACT(pc[:, kv * 256:(kv + 1) * 256], banks[bsc][:, 0:256], AF.Exp, r=[BK(bsc)], w=[("pc", kv)], scale=scale)
                TT("dve", pc[:, kv * 256:(kv + 1) * 256].rearrange("p (a l) -> p a l", l=4), pc[:, kv * 256:(kv + 1) * 256].rearrange("p (a l) -> p a l", l=4),
                   mcache.unsqueeze(1).to_broadcast([128, 64, 4]), ALU.mult, r=[("pc", kv)] + CONST, w=[("pc", kv)])
            mnew = cb[0:64, B_MNEW:B_MNEW + 64]
            for kv in range(2):
                ks = slice(kv * 64, (kv + 1) * 64)
                bsn = nb()
                MM(banks[bsn][0:64, 0:256].rearrange("p (g q) -> p g q", g=4), kT[ks, SC0:SC0 + 64], qT[ks, :, SC0:SC0 + 64],
                   r=[("kT", 4)] + TQ, w=[BK(bsn)])
                ACT(pn[0:64, kv * 256:(kv + 1) * 256], banks[bsn][0:64, 0:256], AF.Exp, r=[BK(bsn)], w=[("pn", kv)], scale=scale)
                TT("dve", pn[0:64, kv * 256:(kv + 1) * 256].rearrange("p (a q) -> p a q", q=64), pn[0:64, kv * 256:(kv + 1) * 256].rearrange("p (a q) -> p a q", q=64),
                   mnew.unsqueeze(1).to_broadcast([64, 4, 64]), ALU.mult, r=[("pn", kv)] + CONST, w=[("pn", kv)])
            CHECK("A3b")
            pnv = pn[0:64].rearrange("p (k g l s) -> p k g l s", k=2, g=4, l=4)
            bo = nb()
            bd = nb()
            for which, bX in (("o", bo), ("d", bd)):
                ov = banks[bX][:, 0:256].rearrange("p (s g l) -> p s g l", s=16, g=4)
                for s_ in range(16):
                    for kv in range(2):
                        lhs = vpad[0:64, 16, kv, :] if which == "o" else onespad[0:64, kv, :]
                        MM(ov[:, s_, :, :], lhs, pnv[:, kv, :, :, s_], start=(kv == 0), stop=False,
                           r=[("vpad", 16), ("pn", kv)] + CONST, w=[BK(bX)])
                    for kv in range(2):
                        lhs = vcpad[:, s_, kv, :] if which == "o" else onespad[:, kv, :]
                        MM(ov[:, s_, :, :], lhs, pcv[:, kv, s_, :, :], start=False, stop=(kv == 1),
                           r=["vcpad", ("pc", kv)] + CONST, w=[BK(bX)])
            CHECK("A3c")
            TT("dve", dsum[0][:, 0:256].rearrange("p (s g l) -> p s g l", s=16, g=4), banks[bd][:, 0:256].rearrange("p (s g l) -> p s g l", s=16, g=4),
               sinkexp.unsqueeze(1).unsqueeze(3).to_broadcast([128, 16, 4, 4]), ALU.add, r=[BK(bd), "sinkexp"], w=[("dsum", 0)])
            P.op("dve", lambda e: e.reciprocal(out=rden[0][:, 0:256], in_=dsum[0][:, 0:256]), r=[("dsum", 0)], w=[("rden", 0)])
            TT("dve", dsum[0][:, 0:256], banks[bo][:, 0:256], rden[0][:, 0:256], ALU.mult, r=[BK(bo), ("rden", 0)], w=[("dsum", 0)])
            CP("dve", aoT[:, :, SC0:SC0 + 64].rearrange("p g (l s) -> p g l s", s=16),
               dsum[0][:, 0:256].rearrange("p (s g l) -> p g l s", s=16, g=4), r=[("dsum", 0)], w=[("aoT", 16)])
            dump("aoT", aoT[:, :, :], [("aoT", i) for i in range(17)], BF16)

        AO_ALL = [("aoT", i) for i in range(17)]

        def phase_B():
            SCH = 256
            A = Bump(arena, abf, [R_T, R_G, X_SPARE, (O_X, O_X + 4224)])
            wx = A.bf(8 * 1536).rearrange("p (k c) -> p k c", k=8)
            wz = A.bf(8 * 1024).rearrange("p (k c) -> p k c", k=8)
            o_xbcs = A.take(3072)
            xbcs = [A.bview(o_xbcs + i * 1536, 12 * SCH).rearrange("p (b t) -> p b t", b=12) for i in range(2)]
            o_big = A.take(4096)
            big = [A.fview(o_big + i * 2048, 2048) for i in range(2)]
            o_lw = A.take(3072)
            Lb = [A.bview(o_lw + i * 1024, 2048) for i in range(2)]
            wT = A.bview(o_lw + 2048, 2048)
            snw = A.f32(1024)
            yb = A.f32(1024)
            hst = A.f32(1024)
            diagW = A.bf(12 * 4 * 128).rearrange("p (b i c) -> p b i c", b=12, i=4)
            rawb = [A.bf(264) for _ in range(2)]
            thc = [A.bf(256) for _ in range(2)]
            xs_tm = A.bf(1024)
            xdt = A.bf(1024)
            xdd = A.bf(1024)
            xsd = A.bf(1024)
            thz1 = A.bf(1024)
            thz = [thz1, thz1]
            m_tm = A.bf(1024)
            hst_bf = A.bf(1024)
            dt_all = A.f32(NTL * 16).rearrange("p (t h) -> p t h", t=NTL)
            dA_all = A.f32(NTL * 16).rearrange("p (t h) -> p t h", t=NTL)
            B_tm = A.bf(256).rearrange("p (g n) -> p g n", g=2)
            cbm = A.bf(256)
            wdt = A.bf(8 * 16).rearrange("p (k c) -> p k c", k=8)
            cwh = A.f32(48).rearrange("p (b i) -> p b i", b=12)
            cbh = A.f32(16)
            carry = A.bf(40)[:, 0:36].rearrange("p (b i) -> p b i", b=12)
            E3 = A.f32(48)
            st8 = A.f32(16)
            xbss = A.bf(12 * 64).rearrange("p (b c) -> p b c", b=12)
            CTm = [A.bf(128).rearrange("p (g c) -> p g c", g=2) for _ in range(2)]
            Bm = [A.bf(256).rearrange("p (g n) -> p g n", g=2) for _ in range(2)]
            cdl = A.f32(16)
            cdcol = A.f32(128).rearrange("p (a s) -> p a s", a=8)
            h0 = [A.fview(o_xbcs + i * 1024, 1024).rearrange("p (a c) -> p a c", a=8) for i in range(2)]
            h0T = [A.bview(o_xbcs + 2048 + i * 512, 1024) for i in range(2)]
            scv = big[0][:, 0:1536]
            raws = A.bview(o_lw, 12 * 112).rearrange("p (b c) -> p b c", b=12)
            d_w = PF["d_wB"]
            d_big = P.dsem("bigB")
            DMA("pool", wdt, w_in[:, 3328:3344].rearrange("(kc p) n -> p kc n", p=128), d_w, w=["wdt"])
            DMA("pool", wz, w_in[:, 768:1792].rearrange("(kc p) n -> p kc n", p=128), P.dsem("wzB"), w=["wz"])
            DMA("sp", snw, vecs[:, 3, :], P.dsem("snwB"), w=["snw"])
            TS("pool", cwh, cf[:, C_CONVW:C_CONVW + 48].rearrange("p (b i) -> p b i", b=12), 0.5, None, ALU.mult, r=CONST, w=["cwh"])
            TS("pool", cbh[:, 0:12], cf[:, C_CONVB:C_CONVB + 12], 0.5, None, ALU.mult, r=CONST, w=["cbh"])
            MEMSET("pool", carry[:, :, :], 0.0, w=["carry"])
            for blk_ in range(12):
                for i_ in range(4):
                    TS("dve", diagW[:, blk_, i_, :], idb, cwh[:, blk_, i_:i_ + 1], None, ALU.mult, r=CONST + ["cwh"], w=["diagW"])
            MEMSET("pool", hst, 0.0, w=["hst"])
            MEMSET("pool", hst_bf, 0.0, w=["hst_bf"])

            UI = cf[:, C_UI:C_UI + 128]
            SL = cf[:, C_SL:C_SL + 128]
            ONES = cf[:, C_ONES:C_ONES + 128]
            UIB = cb[:, B_UIB:B_UIB + 128]

            for t in range(NTL):
                rows = rows_of(t)
                for kc in range(8):
                    MM(banks[0][0:rows, t * 16:(t + 1) * 16], hT[:, kc, t * 128:t * 128 + rows], wdt[:, kc, :], start=(kc == 0), stop=(kc == 7),
                       r=["wdt", ("hT", t)], w=[BK(0)])
            dtv = banks[0][:, 0:NTL * 16].rearrange("p (t h) -> p t h", t=NTL)
            TT("dve", dt_all[:, :, :], dtv, dtb_bc.unsqueeze(1).to_broadcast([128, NTL, 16]), ALU.add, r=[BK(0)] + CONST, w=["dt0"])
            ACT(dt_all[:, :, :], dt_all[:, :, :], AF.Exp, r=["dt0"], w=["dt1"])
            ACT(dt_all[:, :, :], dt_all[:, :, :], AF.Ln, r=["dt1"], w=["dt"], bias=1.0)
            TT("dve", dA_all[:, :, :], dt_all[:, :, :], a_neg.unsqueeze(1).to_broadcast([128, NTL, 16]), ALU.mult, r=["dt", "a_neg"], w=["dA"])
            dump("dt", dt_all[:, :, :], ["dt"])

            def conv_block(src_fn, width, blk, bY, thv, outv, rtoks, wtok):
                for i in range(4):
                    MM(banks[bY][:, 0:width], diagW[:, blk, i, :], src_fn(i), start=(i == 0), stop=(i == 3), r=rtoks + ["diagW"], w=[BK(bY)])
                tt = ("thc", blk % 2)
                ACT(thv, banks[bY][:, 0:width], AF.Tanh, r=[BK(bY), "cbh"], w=[tt], bias=cbh[:, blk:blk + 1])
                TS("dve", thv, thv, 1.0, None, ALU.add, r=[tt], w=[tt])
                STT("dve", outv, banks[bY][:, 0:width], cbh[:, blk:blk + 1], thv, ALU.add, ALU.mult, r=[BK(bY), tt, "cbh"], w=[wtok])

            ev = [0]

            def conv_stage1(S, blk):
                c0 = S * SCH
                b = blk % 2
                sl = blk % 2
                for kc in range(8):
                    MM(banks[b][:, 0:SCH], wx[:, kc, blk * 128:(blk + 1) * 128], hT[:, kc, c0:c0 + SCH], start=(kc == 0), stop=(kc == 7),
                       r=["wx"] + hT_toks(c0, SCH), w=[BK(b)])
                CP("pool", rawb[sl][:, 0:3], carry[:, blk, :], r=["carry"], w=[("rawc", sl)])
                ev[0] += 1
                CP("act" if ev[0] % 2 else "dve", rawb[sl][:, 3:3 + SCH], banks[b][:, 0:SCH], r=[BK(b)], w=[("raw", sl)])
                CP("pool", carry[:, blk, :], rawb[sl][:, SCH:SCH + 3], r=[("raw", sl)], w=["carry"])

            def conv_stage2(S, blk):
                xs_ = S % 2
                sl = blk % 2
                conv_block(lambda i: rawb[sl][:, i:i + SCH], SCH, blk, 2 + blk % 2, thc[sl], xbcs[xs_][:, blk, :],
                           [("raw", sl), ("rawc", sl)], ("xbcs", xs_, blk))

            def conv_blocks(S, blks):
                blks = list(blks)
                conv_stage1(S, blks[0])
                for i, blk in enumerate(blks):
                    if i + 1 < len(blks):
                        conv_stage1(S, blks[i + 1])
                    conv_stage2(S, blk)

            def front_decay_a(ci, R, sample=False):
                rs = slice(0, R)
                k = ci % 2
                UIr = cf[rs, C_UIS:C_UIS + 64] if sample else UI
                dAc = dA_all[rs, ci, :]
                bigv = big[k][rs].rearrange("p (h l) -> p h l", h=16)[:, :, 0:R]
                for h in range(16):
                    ACT(bigv[:, h, :], UIr, AF.Copy, r=["dA"] + CONST, w=[("big", k)], scale=dAc[:, h:h + 1])

            def front_decay_b(ci, R, sample=False):
                rs = slice(0, R)
                k = ci % 2
                SLr = cf[rs, C_SLS:C_SLS + 64] if sample else SL
                bigv = big[k][rs].rearrange("p (h l) -> p h l", h=16)[:, :, 0:R]
                Lv = Lb[k][rs].rearrange("p (h l) -> p h l", h=16)[:, :, 0:R]
                hq = 512 // R
                for qd in range(16 // hq):
                    b = 4 + (qd % 2)
                    MM(banks[b][rs, 0:hq * R].rearrange("p (h l) -> p h l", h=hq), SLr, bigv[:, qd * hq:(qd + 1) * hq, :], r=[("big", k)] + CONST, w=[BK(b)])
                    ACT(Lv[:, qd * hq:(qd + 1) * hq, :], banks[b][rs, 0:hq * R].rearrange("p (h l) -> p h l", h=hq), AF.Exp, r=[BK(b)], w=[("L", k, qd)])

            def ssd_chunk(R, ci, XS, BTg, CTg, xbc_toks, c_glob, sample=False, seq_loop=None, early_hook=None, mid_hook=None, end_hook=None):
                rs = slice(0, R)
                k = ci % 2
                UIr = cf[rs, C_UIS:C_UIS + 64] if sample else UI
                SLr = cf[rs, C_SLS:C_SLS + 64] if sample else SL
                UIBr = cb[rs, B_UISB:B_UISB + 64] if sample else UIB
                idr = idb
                gzk = thz[k]
                tmph = big[k][:, 0:1024]
                hq = 512 // R
                for half in range(2):
                    for kc in range(8):
                        MM(banks[6 + half][rs, :], hT[:, kc, c_glob:c_glob + R], wz[:, kc, half * 512:(half + 1) * 512], start=(kc == 0), stop=(kc == 7),
                           r=["wz", ("hT", c_glob // 128)], w=[BK(6 + half)])
                    hs = slice(half * 512, (half + 1) * 512)
                    ACT(gzk[rs, hs], banks[6 + half][rs, :], AF.Tanh, r=[BK(6 + half)], w=[("gz", half)], scale=0.5)
                    STT("dve", gzk[rs, hs], gzk[rs, hs], 1.0, banks[6 + half][rs, :], ALU.add, ALU.mult, r=[("gz", half), BK(6 + half)], w=[("gz", half)])
                pv = bankb[2][:, :].rearrange("p (a c) -> p a c", a=8)
                for blk in range(8):
                    TR(pv[rs, blk, :], XS(blk), idr, r=xbc_toks + CONST, w=[BK(2)])
                CP("act", xs_tm[rs].rearrange("p (a c) -> p a c", a=8), pv[rs, :, :], r=[BK(2)], w=["xs_tm"])
                b3 = banks[3]
                b3b = bankb[3]
                dAc = dA_all[rs, ci, :]
                MM(b3[rs, 0:16], UIr, dAc, r=["dA"] + CONST, w=[BK(3)])
                MM(b3[rs, 16:32], SLr, dAc, r=["dA"] + CONST, w=[BK(3)])
                if not sample:
                    MM(b3[:, 32:48], ONES, dAc, r=["dA"] + CONST, w=[BK(3)])
                for g in range(2):
                    MM(b3[rs, 64 + g * 128:64 + g * 128 + R], BTg(g), CTg(g), r=xbc_toks, w=[BK(3)])
                btv = b3b[:, 640:896].rearrange("p (g n) -> p g n", g=2)
                for g in range(2):
                    TR(btv[rs, g, :], BTg(g), idr, r=xbc_toks + CONST, w=[BK(3)])
                ncol = 32 if sample else 48
                ACT(E3[rs, 0:ncol], b3[rs, 0:ncol], AF.Exp, r=[BK(3)], w=["E3"])
                cbv = b3[rs, 64:320].rearrange("p (g l) -> p g l", g=2)[:, :, 0:R]
                cbmv = cbm[rs].rearrange("p (g l) -> p g l", g=2)[:, :, 0:R]
                TT("dve", cbmv, cbv, UIBr.unsqueeze(1).to_broadcast([R, 2, R]), ALU.mult, r=[BK(3)] + CONST, w=["cbm"])
                CP("act", B_tm[rs, :, :], btv[rs, :, :], r=[BK(3)], w=["B_tm"])
                Lv = Lb[k][rs].rearrange("p (h l) -> p h l", h=16)[:, :, 0:R]
                wTv = wT[rs].rearrange("p (h l) -> p h l", h=16)[:, :, 0:R]
                for g in range(2):
                    TT("dve", wTv[:, g * 8:(g + 1) * 8, :], Lv[:, g * 8:(g + 1) * 8, :], cbmv[:, g, :].unsqueeze(1).to_broadcast([R, 8, R]), ALU.mult,
                       r=[("L", k, q_) for q_ in range(16 // hq)] + ["cbm"], w=[("wT", g)])
                xsv = xs_tm[rs].rearrange("p (h c) -> p h c", h=16)
                xdtv = xdt[rs].rearrange("p (h c) -> p h c", h=16)
                xddv = xdd[rs].rearrange("p (h c) -> p h c", h=16)
                TT("dve", xdtv, xsv, dt_all[rs, ci, :].unsqueeze(2).to_broadcast([R, 16, 64]), ALU.mult, r=["xs_tm", "dt"], w=["xdt"])
                TT("dve", xsd[rs].rearrange("p (h c) -> p h c", h=16), xsv, dskip_bc[rs].unsqueeze(2).to_broadcast([R, 16, 64]), ALU.mult,
                   r=["xs_tm"] + CONST, w=["xsd"])
                TT("dve", xddv, xdtv, E3[rs, 16:32].unsqueeze(2).to_broadcast([R, 16, 64]), ALU.mult, r=["xdt", "E3"], w=["xdd"])
                if early_hook is not None:
                    early_hook()
                if not sample:
                    for g in range(2):
                        MM(banks[4 + g][:, :], CTg(g), hst_bf[:, g * 512:(g + 1) * 512], r=xbc_toks + ["hst_bf"], w=[BK(4 + g)])
                else:
                    seq_loop()
                for g in range(2):
                    gs = slice(g * 512, (g + 1) * 512)
                    TT("dve", yb[rs, gs].rearrange("p (h c) -> p h c", h=8), banks[4 + g][rs, :].rearrange("p (h c) -> p h c", h=8),
                       E3[rs, g * 8:(g + 1) * 8].unsqueeze(2).to_broadcast([R, 8, 64]), ALU.mult, r=[BK(4 + g), "E3"], w=[("yb", g)])
                for h in range(16):
                    b = 6 + h // 8
                    MM(banks[b][rs, (h % 8) * 64:(h % 8 + 1) * 64], wTv[:, h, :], xdtv[:, h, :], r=[("wT", h // 8), "xdt"], w=[BK(b)])
                for g in range(2):
                    gs = slice(g * 512, (g + 1) * 512)
                    TT("dve", yb[rs, gs], yb[rs, gs], banks[6 + g][rs, :], ALU.add, r=[("yb", g), BK(6 + g)], w=[("yb", g)])
                    TT("dve", yb[rs, gs], yb[rs, gs], xsd[rs, gs], ALU.add, r=[("yb", g), "xsd"], w=[("yb", g)])
                    TT("dve", yb[rs, gs], yb[rs, gs], gzk[rs, gs], ALU.mult, r=[("yb", g), ("gz", g)], w=[("yb", g)])
                    ACT(m_tm[rs, gs], yb[rs, gs], AF.Square, r=[("yb", g)], w=[("m_tm", g), ("ssq", g)], accum_out=st8[rs, g:g + 1])
                if mid_hook is not None:
                    mid_hook()
                TS("pool", st8[rs, 4:6], st8[rs, 0:2], 1.0 / 512, 4.0 * EPS, ALU.mult, ALU.add, r=[("ssq", 0), ("ssq", 1)], w=["msB"])
                TT("pool", st8[rs, 8:10], st8[rs, 4:6], neghalf[rs].to_broadcast([R, 2]), ALU.pow, r=["msB", "neghalf"], w=["rsB"])
                for g in range(2):
                    gs = slice(g * 512, (g + 1) * 512)
                    STT("dve", m_tm[rs, gs], yb[rs, gs], st8[rs, 8 + g:9 + g], snw[rs, gs], ALU.mult, ALU.mult, r=[("yb", g), "rsB", "snw"], w=[("m_tm", g)])
                pv2 = bankb[2][:, :].rearrange("p (a c) -> p a c", a=8)
                for blk in range(8):
                    TR(pv2[:, blk, 0:R], m_tm[rs, blk * 128:(blk + 1) * 128], idr[rs, rs], r=[("m_tm", blk // 4)] + CONST, w=[BK(2)])
                CP("act", moT[:, :, c_glob:c_glob + R], pv2[:, :, 0:R], r=[BK(2)], w=[("moT", c_glob // 128)])
                if not sample:
                    for g in range(2):
                        MM(banks[4 + g][:, :], B_tm[:, g, :], xdd[:, g * 512:(g + 1) * 512], r=["B_tm", "xdd"], w=[BK(4 + g)])
                    TT("dve", tmph.rearrange("p (h c) -> p h c", h=16), hst.rearrange("p (h c) -> p h c", h=16),
                       E3[:, 32:48].unsqueeze(2).to_broadcast([128, 16, 64]), ALU.mult, r=["hst", "E3"], w=[("big", k)])
                    for g in range(2):
                        gs = slice(g * 512, (g + 1) * 512)
                        TT("dve", hst[:, gs], tmph[:, gs], banks[4 + g][:, :], ALU.add, r=[("big", k), BK(4 + g)], w=["hst"])
                    CP("act", hst_bf, hst, r=["hst"], w=["hst_bf"])
                if end_hook is not None:
                    end_hook()

            NS = 2048 // SCH
            CPS = SCH // 128
            conv_blocks(0, range(12))
            front_decay_a(0, 128)
            front_decay_b(0, 128)
            for ci in range(16):
                S = ci // CPS
                cc = ci % CPS
                l0 = cc * 128
                xs_ = S % 2
                XT = [("xbcs", xs_, blk) for blk in range(12)]

                def h_early(ci=ci):
                    if ci + 1 < 16:
                        front_decay_a(ci + 1, 128)

                def h_mid(ci=ci):
                    if ci + 1 < 16:
                        front_decay_b(ci + 1, 128)

                def h_end(S=S, cc=cc):
                    if S + 1 < NS:
                        per = 12 // CPS
                        conv_blocks(S + 1, range(cc * per, (cc + 1) * per))

                ssd_chunk(128, ci,
                          lambda blk, l0=l0, xs_=xs_: xbcs[xs_][:, blk, l0:l0 + 128],
                          lambda g, l0=l0, xs_=xs_: xbcs[xs_][:, 8 + g, l0:l0 + 128],
                          lambda g, l0=l0, xs_=xs_: xbcs[xs_][:, 10 + g, l0:l0 + 128],
                          XT, S * SCH + l0, early_hook=h_early, mid_hook=h_mid, end_hook=h_end)
                if ci == 3:
                    dump("mo0", moT[:, :, 0:512], [("moT", i) for i in range(4)], BF16)
            hstv = hst.rearrange("p (a c) -> p a c", a=8)
            hout = big[0][:, 0:1024].rearrange("p (a c) -> p a c", a=8)
            for half in range(2):
                b = 4 + half
                for a in range(4):
                    blk = half * 4 + a
                    TR(banks[b][:, a * 128:(a + 1) * 128], hstv[:, blk, :], idf, r=["hst"] + CONST, w=[BK(b)])
                CP("dve", hout[:, half * 4:(half + 1) * 4, :], banks[b][:, :].rearrange("p (a c) -> p a c", a=4), r=[BK(b)], w=[("big", 0)])
            DMA("sp", nsp.rearrange("(blk q) n -> q blk n", q=128), hout, d_big, r=[("big", 0)])
            rawtm = big[1][:, 0:1536]
            for cb_ in range(3):
                b = 6 + (cb_ % 2)
                for kc in range(8):
                    MM(banks[b][:, :], hT[:, kc, 15 * 128:16 * 128], wx[:, kc, cb_ * 512:(cb_ + 1) * 512], start=(kc == 0), stop=(kc == 7),
                       r=["wx", ("hT", 15)], w=[BK(b)])
                CP("dve", rawtm[:, cb_ * 512:(cb_ + 1) * 512], banks[b][:, :], r=[BK(b)], w=[("big", 1)])
            DMA("sp", ncp, rawtm[125:128, :], d_big, r=[("big", 1)])

            CHECK("B2")
            P.barrier()
            d_s = P.dsem("sampB")
            d_h0 = [P.dsem("h0a"), P.dsem("h0b")]
            for i in range(3):
                DMA("sp", scv[i * 16:(i + 1) * 16, :], sconv[:, i, :], d_s, w=[("big", 0)])
            for half, (b, blks) in enumerate(((0, list(range(0, 8))), (1, list(range(8, 12))))):
                for a, blk in enumerate(blks):
                    TR(banks[b][:, a * 48:(a + 1) * 48], scv[0:48, blk * 128:(blk + 1) * 128], idf[0:48, 0:48], r=[("big", 0)] + CONST, w=[BK(b)])
                nblk = len(blks)
                CP("act", raws[:, blks[0]:blks[0] + nblk, 0:48], banks[b][:, 0:nblk * 48].rearrange("p (a c) -> p a c", a=nblk), r=[BK(b)], w=[("raws", half)])
            for half, (b, blks) in enumerate(((0, list(range(0, 8))), (1, list(range(8, 12))))):
                for a, blk in enumerate(blks):
                    for kc in range(8):
                        MM(banks[b][:, a * 64:(a + 1) * 64], wx[:, kc, blk * 128:(blk + 1) * 128], hT[:, kc, SC0:SC0 + 64], start=(kc == 0), stop=(kc == 7),
                           r=["wx", ("hT", 16)], w=[BK(b)])
                nblk = len(blks)
                CP("dve", raws[:, blks[0]:blks[0] + nblk, 48:112], banks[b][:, 0:nblk * 64].rearrange("p (a c) -> p a c", a=nblk), r=[BK(b)], w=[("raws2", half)])
            for blk in range(12):
                sl = blk % 2
                hf = 0 if blk < 8 else 1
                conv_block(lambda i, blk=blk: raws[:, blk, i * 16:i * 16 + 64], 64, blk, 2 + blk % 2, thc[sl][:, 0:64], xbss[:, blk, :],
                           [("raws", hf), ("raws2", hf)], ("xbss", blk))
            XTS = [("xbss", blk) for blk in range(12)]
            rawtm = big[1][:, 0:1536]
            for cb_ in range(3):
                b = 4 + (cb_ % 2)
                for kc in range(8):
                    MM(banks[b][0:64, :], hT[:, kc, SC0:SC0 + 64], wx[:, kc, cb_ * 512:(cb_ + 1) * 512], start=(kc == 0), stop=(kc == 7),
                       r=["wx", ("hT", 16)], w=[BK(b)])
                CP("dve", rawtm[0:64, cb_ * 512:(cb_ + 1) * 512], banks[b][0:64, :], r=[BK(b)], w=[("big", 1)])
            for l in range(1, 4):
                DMA("sp", ncs[:, l - 1, :], rawtm[l * 16:(l + 1) * 16, :], d_big, r=[("big", 1)])
            P.barrier()

            SEQM = cb[:, B_SEQM:B_SEQM + 1024].rearrange("p (s c) -> p s c", s=16)
            SEQINDB = cb[0:64, B_SEQINDB:B_SEQINDB + 16]

            def prefetch_A_weights(part):
                if part == 0:
                    PF["d_wA"] = P.dsem("wA")
                wq_p = abf[:, 2 * O_T:2 * O_T + 4096].rearrange("p (k g c) -> p k g c", k=8, g=4)
                wk_p = abf[:, 2 * (O_T + 2048):2 * (O_T + 2048) + 1024].rearrange("p (k c) -> p k c", k=8)
                wv_p = abf[:, 2 * (O_T + 2560):2 * (O_T + 2560) + 1024].rearrange("p (k c) -> p k c", k=8)
                wsrc_p = w_in[:, 0:512].rearrange("(kc p) (kv g d) -> p kc kv g d", p=128, kv=2, g=4)
                if part < 4:
                    g = part
                    for kv in range(2):
                        DMA("pool", wq_p[:, :, g, kv * 64:(kv + 1) * 64], wsrc_p[:, :, kv, g, :], PF["d_wA"], w=[("wq", g)])
                else:
                    DMA("pool", wk_p, w_in[:, 512:640].rearrange("(kc p) n -> p kc n", p=128), PF["d_wA"], w=["wk"])
                    DMA("pool", wv_p, w_in[:, 640:768].rearrange("(kc p) n -> p kc n", p=128), PF["d_wA"], w=["wv"])

            def seq_loop():
                MM(banks[3][0:16, 32:48], dA_all[0:64, 16, :], cf[0:64, C_SEQIND:C_SEQIND + 16], r=["dA"] + CONST, w=[BK(3)])
                CP("act", cdl[0:16, :], banks[3][0:16, 32:48], r=[BK(3)], w=["cdl"])
                selblk = cf[0:16, C_SELBLK:C_SELBLK + 1024].rearrange("p (a c) -> p a c", a=8)
                for blk in range(8):
                    MM(banks[3][:, 64 + blk * 16:64 + (blk + 1) * 16], selblk[:, blk, :], cdl[0:16, :], r=["cdl"] + CONST, w=[BK(3)])
                ACT(cdcol, banks[3][:, 64:192].rearrange("p (a s) -> p a s", a=8), AF.Exp, r=[BK(3)], w=["cdcol"])

                def load_h0(s_):
                    DMA("sp", h0[s_ % 2], sssm[s_].rearrange("(blk q) n -> q blk n", q=128), d_h0[s_ % 2], w=[("h0", s_ % 2)])

                load_h0(0)
                for s_ in range(16):
                    sl = s_ % 2
                    if s_ + 1 < 16:
                        load_h0(s_ + 1)
                    if 4 <= s_ <= 8:
                        prefetch_A_weights(s_ - 4)
                    TT("pool", CTm[sl], xbss[:, 10:12, :], SEQM[:, s_, :].unsqueeze(1).to_broadcast([128, 2, 64]), ALU.mult,
                       r=XTS + CONST, w=[("CTm", sl)])
                    TS("pool", Bm[sl][0:64].rearrange("p g n -> p (g n)"), B_tm[0:64].rearrange("p g n -> p (g n)"), SEQINDB[:, s_:s_ + 1], None, ALU.mult,
                       r=["B_tm"] + CONST, w=[("Bm", sl)])
                    for half in range(2):
                        b = half
                        for a in range(4):
                            TR(banks[b][:, a * 128:(a + 1) * 128], h0[sl][:, half * 4 + a, :], idf, r=[("h0", sl)] + CONST, w=[BK(b)])
                        CP("act", h0T[sl][:, half * 512:(half + 1) * 512], banks[b][:, :], r=[BK(b)], w=[("h0T", sl, half)])
                    for g in range(2):
                        MM(banks[4 + g][0:64, :], CTm[sl][:, g, :], h0T[sl][:, g * 512:(g + 1) * 512], start=(s_ == 0), stop=(s_ == 15),
                           r=[("CTm", sl), ("h0T", sl, g)], w=[BK(4 + g)])
                    for half in range(2):
                        b = 6 + half
                        for a in range(4):
                            blk = half * 4 + a
                            MM(banks[b][:, a * 128:(a + 1) * 128], xdd[0:64, blk * 128:(blk + 1) * 128], Bm[sl][0:64, blk // 4, :],
                               r=["xdd", ("Bm", sl)], w=[BK(b)])
                    for blk in range(8):
                        b = 6 + blk // 4
                        STT("dve", h0[sl][:, blk, :], h0[sl][:, blk, :], cdcol[:, blk, s_:s_ + 1], banks[b][:, (blk % 4) * 128:(blk % 4 + 1) * 128],
                            ALU.mult, ALU.add, r=[("h0", sl), "cdcol", BK(b)], w=[("h0", sl)])
                    DMA("sp", nss[s_].rearrange("(blk q) n -> q blk n", q=128), h0[sl], d_h0[sl], r=[("h0", sl)], w=[("h0", sl)])

            front_decay_a(16, 64, sample=True)
            front_decay_b(16, 64, sample=True)
            ssd_chunk(64, 16,
                      lambda blk: xbss[:, blk, :],
                      lambda g: xbss[:, 8 + g, :],
                      lambda g: xbss[:, 10 + g, :],
                      XTS, SC0, sample=True, seq_loop=seq_loop)
            dump("moT", moT[:, :, :], [("moT", i) for i in range(17)], BF16)

        MO_ALL = [("moT", i) for i in range(17)]

        def phase_C():
            A = Bump(arena, abf, [(R_T[0], C0_OFF)])
            wo_p = abf[:, 2 * WO_OFF:2 * WO_OFF + 8 * 1024].rearrange("p (k c) -> p k c", k=8)
            PF["d_wD"] = P.dsem("wD")
            wga0, wgb0, woa0, wob0 = c0_views()
            wga = [wga0, A.bf(8 * 256).rearrange("p (k c) -> p k c", k=8)]
            wgb = [wgb0, A.bf(8 * 256).rearrange("p (k c) -> p k c", k=8)]
            woa = [woa0, A.bf(4 * 256).rearrange("p (k c) -> p k c", k=4)]
            wob = [wob0, A.bf(8 * 256).rearrange("p (k c) -> p k c", k=8)]
            tA = [A.bf(512) for _ in range(2)]
            tB = [A.bf(512) for _ in range(2)]
            u = [A.f32(512) for _ in range(2)]
            v = [A.f32(512) for _ in range(2)]
            d_w = [PF["d_wC0"], P.dsem("wC1")]
            woa_src = w_oa.rearrange("(kv g d) n -> kv d g n", kv=2, g=4)
            def load_c(qi):
                ws = qi % 2
                cs = slice(qi * 256, (qi + 1) * 256)
                DMA("pool", wga[ws], w_in[:, 3344 + qi * 256:3344 + (qi + 1) * 256].rearrange("(kc p) n -> p kc n", p=128), d_w[ws], w=[("wC", ws)])
                DMA("pool", wgb[ws], w_in[:, 4368 + qi * 256:4368 + (qi + 1) * 256].rearrange("(kc p) n -> p kc n", p=128), d_w[ws], w=[("wC", ws)])
                for kv in range(2):
                    DMA("pool", woa[ws][kv * 64:(kv + 1) * 64, :, :], woa_src[kv][:, :, cs], d_w[ws], w=[("wC", ws)])
                DMA("pool", wob[ws], w_ob[:, cs].rearrange("(kc p) n -> p kc n", p=128), d_w[ws], w=[("wC", ws)])

            load_c(1)
            it = 0
            for qi in range(4):
                ws = qi % 2
                if qi >= 1 and qi + 1 < 4:
                    load_c(qi + 1)
                if qi == 3:
                    DMA("pool", wo_p, w_o.rearrange("(kc p) n -> p kc n", p=128), PF["d_wD"], w=["wo"])
                for blk in range(2):
                    dm = qi * 2 + blk
                    bs = slice(blk * 128, (blk + 1) * 128)
                    for T in range(5):
                        c0 = T * 512
                        n = 512 if T < 4 else 64
                        s = it % 2
                        it += 1
                        bgA, bpA, bgB, bpB = 4 * s, 4 * s + 1, 4 * s + 2, 4 * s + 3
                        htk = hT_toks(c0, n)
                        tk = list(range(c0 // 128, (c0 + n + 127) // 128))
                        for kc in range(8):
                            MM(banks[bgA][:, 0:n], wga[ws][:, kc, bs], hT[:, kc, c0:c0 + n], start=(kc == 0), stop=(kc == 7), r=[("wC", ws)] + htk, w=[BK(bgA)])
                        for g in range(4):
                            MM(banks[bpA][:, 0:n], woa[ws][:, g, bs], aoT[:, g, c0:c0 + n], start=(g == 0), stop=(g == 3),
                               r=[("wC", ws)] + [("aoT", t) for t in tk], w=[BK(bpA)])
                        for kc in range(8):
                            MM(banks[bgB][:, 0:n], wgb[ws][:, kc, bs], hT[:, kc, c0:c0 + n], start=(kc == 0), stop=(kc == 7), r=[("wC", ws)] + htk, w=[BK(bgB)])
                        for kc in range(8):
                            MM(banks[bpB][:, 0:n], wob[ws][:, kc, bs], moT[:, kc, c0:c0 + n], start=(kc == 0), stop=(kc == 7),
                               r=[("wC", ws)] + [("moT", t) for t in tk], w=[BK(bpB)])
                        ACT(tA[s][:, 0:n], banks[bgA][:, 0:n], AF.Tanh, r=[BK(bgA)], w=[("tA", s)], scale=0.5)
                        ACT(tB[s][:, 0:n], banks[bgB][:, 0:n], AF.Tanh, r=[BK(bgB)], w=[("tB", s)], scale=0.5)
                        STT("dve", u[s][:, 0:n], tA[s][:, 0:n], 1.0, banks[bpA][:, 0:n], ALU.add, ALU.mult, r=[("tA", s), BK(bpA)], w=[("u", s)])
                        STT("dve", v[s][:, 0:n], tB[s][:, 0:n], 1.0, banks[bpB][:, 0:n], ALU.add, ALU.mult, r=[("tB", s), BK(bpB)], w=[("v", s)])
                        TT("dve", mgT[:, dm, c0:c0 + n], u[s][:, 0:n], v[s][:, 0:n], ALU.add, r=[("u", s), ("v", s)], w=[("mgT", dm, T)])
            dump("mgT", mgT[:, :, :], [("mgT", dm, T) for dm in range(8) for T in range(5)], BF16)

        def mg_toks(t):
            return [("mgT", dm, t // 4) for dm in range(8)]

        def phase_D():
            A = Bump(arena, abf, [(R_T[0], E0_OFF)])
            wo = abf[:, 2 * WO_OFF:2 * WO_OFF + 8 * 1024].rearrange("p (k c) -> p k c", k=8)
            wup0 = abf[:, 2 * E0_OFF:2 * E0_OFF + 4096].rearrange("p (k c) -> p k c", k=8)
            wdn0 = abf[:, 2 * E0_OFF + 4096:2 * E0_OFF + 8192].rearrange("p (k c) -> p k c", k=4)
            PF["d_wE0"] = P.dsem("wE0")
            nw = A.f32(1024)
            xt = [A.f32(1024) for _ in range(2)]
            hb = [A.bf(1024) for _ in range(2)]
            junk = A.bf(1024)
            st_ = A.f32(16)
            d_x = [P.dsem("xD0"), P.dsem("xD1")]
            DMA("pool", wup0, w_up[:, 0:512].rearrange("(kc q) n -> q kc n", q=128), PF["d_wE0"], w=[("wE", 0)])
            DMA("pool", wdn0, w_down[0:512, :].rearrange("(f q) n -> q f n", q=128), PF["d_wE0"], w=[("wE", 0)])
            DMA("sp", nw, vecs[:, 1, :], P.dsem("nwD"), w=["nw2"])
            def stage1(t):
                s = t % 2
                rows = rows_of(t)
                DMA("sp", xt[s][0:rows], xin[t * 128:t * 128 + rows, :], d_x[s], w=[("xtD", s)])
                for half in range(2):
                    b = 2 * s + half
                    for kc in range(8):
                        MM(banks[b][0:rows, :], mgT[:, kc, t * 128:t * 128 + rows], wo[:, kc, half * 512:(half + 1) * 512], start=(kc == 0), stop=(kc == 7),
                           r=["wo"] + mg_toks(t), w=[BK(b)])
                    STT("dve", x1[0:rows, t, half * 512:(half + 1) * 512], banks[b][0:rows, :], 0.5, xt[s][0:rows, half * 512:(half + 1) * 512],
                        ALU.mult, ALU.add, r=[BK(b), ("xtD", s)], w=[("x1", t)])
                s3 = t % 3
                ss = st_[:, s3:s3 + 1]
                ms = st_[:, 4 + s3:5 + s3]
                rs = st_[:, 8 + s3:9 + s3]
                ACT(junk[0:rows], x1[0:rows, t, :], AF.Square, r=[("x1", t)], w=["junkD", ("ssD", s3)], accum_out=ss[0:rows])
                rstd_from_ss(ss[0:rows], ms[0:rows], rs[0:rows], 1.0 / D, EPS, [("ssD", s3)], ("msD", s3), ("rsD", s3))

            def stage2a(t):
                s = t % 2
                rows = rows_of(t)
                s3 = t % 3
                rs = st_[:, 8 + s3:9 + s3]
                STT("dve", hb[s][0:rows], x1[0:rows, t, :], rs[0:rows], nw[0:rows], ALU.mult, ALU.mult, r=[("x1", t), ("rsD", s3), "nw2"], w=[("hbD", s)])

            def stage2b(t):
                s = t % 2
                rows = rows_of(t)
                b = 4 + s
                pv = bankb[b][:, :].rearrange("p (a c) -> p a c", a=8)
                for kc in range(8):
                    TR(pv[:, kc, 0:rows], hb[s][0:rows, kc * 128:(kc + 1) * 128], idb[0:rows, 0:rows], r=[("hbD", s)] + CONST, w=[BK(b)])
                CP("act", hT[:, :, t * 128:t * 128 + rows], pv[:, :, 0:rows], r=[BK(b)], w=[("hmT", t)])

            for t in range(NTL + 2):
                if t >= 2:
                    stage2a(t - 2)
                if t < NTL:
                    stage1(t)
                if t >= 2:
                    stage2b(t - 2)
            dump("x1", x1[:, :, :], [("x1", t) for t in range(NTL)])

        def hm_toks(c0, n):
            return [("hmT", t) for t in range(c0 // 128, (c0 + n + 127) // 128)]

        def phase_E():
            A = Bump(arena, abf, [(R_T[0], E0_OFF), R_G])
            NP = 8
            wup = [abf[:, 2 * E0_OFF:2 * E0_OFF + 4096].rearrange("p (k c) -> p k c", k=8), A.bf(8 * 512).rearrange("p (k c) -> p k c", k=8)]
            wdn = [abf[:, 2 * E0_OFF + 4096:2 * E0_OFF + 8192].rearrange("p (k c) -> p k c", k=4), A.bf(4 * 1024).rearrange("p (k c) -> p k c", k=4)]
            rl = [A.bf(512) for _ in range(2)]
            upT = [A.bf(4 * 512).rearrange("p (k c) -> p k c", k=4) for _ in range(2)]
            nw = A.f32(1024)
            yt = [A.f32(1024) for _ in range(2)]
            junk = A.bf(1024)
            st_ = A.f32(16)
            d_w = [PF["d_wE0"], P.dsem("wE1")]
            d_n = P.dsem("nwE")
            d_y = [P.dsem("yE0"), P.dsem("yE1")]
            DMA("sp", nw, vecs[:, 2, :], d_n, w=["nw3"])
            def load_w(p):
                ws = p % 2
                DMA("pool", wup[ws], w_up[:, p * 512:(p + 1) * 512].rearrange("(kc q) n -> q kc n", q=128), d_w[ws], w=[("wE", ws)])
                DMA("pool", wdn[ws], w_down[p * 512:(p + 1) * 512, :].rearrange("(f q) n -> q f n", q=128), d_w[ws], w=[("wE", ws)])

            tiles = [(p, T) for p in range(NP) for T in range(5)]
            dn_ctr = [0]

            def emit_up(k):
                p, T = tiles[k]
                ws, us = p % 2, k % 2
                c0 = T * 512
                n = 512 if T < 4 else 64
                for ffc in range(4):
                    b = ffc % 2
                    for kc in range(8):
                        MM(banks[b][:, 0:n], wup[ws][:, kc, ffc * 128:(ffc + 1) * 128], hT[:, kc, c0:c0 + n], start=(kc == 0), stop=(kc == 7),
                           r=[("wE", ws)] + hm_toks(c0, n), w=[BK(b)])
                    ACT(rl[b][:, 0:n], banks[b][:, 0:n], AF.Relu, r=[BK(b)], w=[("rl", b)])
                    TT("dve", upT[us][:, ffc, 0:n], rl[b][:, 0:n], rl[b][:, 0:n], ALU.mult, r=[("rl", b)], w=[("upT", us, ffc)])

            def emit_down(k):
                p, T = tiles[k]
                ws, us = p % 2, k % 2
                c0 = T * 512
                n = 512 if T < 4 else 64
                for t in range(c0 // 128, (c0 + n + 127) // 128):
                    rows = rows_of(t)
                    j0 = t * 128 - c0
                    for half in range(2):
                        b = 2 + (dn_ctr[0] % 4)
                        dn_ctr[0] += 1
                        for ffc in range(4):
                            MM(banks[b][0:rows, :], upT[us][:, ffc, j0:j0 + rows], wdn[ws][:, ffc, half * 512:(half + 1) * 512], start=(ffc == 0), stop=(ffc == 3),
                               r=[("wE", ws), ("upT", us, ffc)], w=[BK(b)])
                        hs = slice(half * 512, (half + 1) * 512)
                        TT("dve", x1[0:rows, t, hs], x1[0:rows, t, hs], banks[b][0:rows, :], ALU.add, r=[("x1", t), BK(b)], w=[("x1", t)])
                    if p == NP - 1:
                        s = t % 2
                        ss = st_[:, s:s + 1]
                        ms = st_[:, 4 + s:5 + s]
                        rs = st_[:, 8 + s:9 + s]
                        ACT(junk[0:rows], x1[0:rows, t, :], AF.Square, r=[("x1", t)], w=["junkE", ("ssE", s)], accum_out=ss[0:rows])
                        rstd_from_ss(ss[0:rows], ms[0:rows], rs[0:rows], 1.0 / D, EPS, [("ssE", s)], ("msE", s), ("rsE", s))
                        STT("dve", yt[s][0:rows], x1[0:rows, t, :], rs[0:rows], nw[0:rows], ALU.mult, ALU.mult, r=[("x1", t), ("rsE", s), "nw3"], w=[("yt", s)])
                        DMA("sp", yout[t * 128:t * 128 + rows, :], yt[s][0:rows], d_y[s], r=[("yt", s)])
                if T == 4 and p + 2 < NP:
                    load_w(p + 2)

            load_w(1)
            emit_up(0)
            for k in range(len(tiles)):
                if k + 1 < len(tiles):
                    emit_up(k + 1)
                emit_down(k)

        phases = [("N", phase_N), ("B", phase_B), ("A", phase_A), ("C", phase_C), ("D", phase_D), ("E", phase_E)]
        for name, fn in phases:
            try:
                fn()
            except StopBuild:
                P.barrier()
                break
            P.barrier()
            if stop_after == name:
                break
        P.emit()
    return nc, dbg_outs


def _constants():
    f = np.zeros((128, NF), np.float32)
    i = np.arange(128)
    f[:, C_IDF:C_IDF + 128] = np.eye(128, dtype=np.float32)
    f[:, C_UI:C_UI + 128] = (i[:, None] <= i[None, :])
    f[:, C_SL:C_SL + 128] = (i[:, None] > i[None, :])
    f[:, C_ONES:C_ONES + 128] = 1.0
    tok = np.arange(64)
    l_, s_ = tok // 16, tok % 16
    same = s_[:, None] == s_[None, :]
    f[0:64, C_UIS:C_UIS + 64] = same & (l_[:, None] <= l_[None, :])
    f[0:64, C_SLS:C_SLS + 64] = same & (l_[:, None] > l_[None, :])
    f[0:64, C_SEQIND:C_SEQIND + 16] = (s_[:, None] == np.arange(16)[None, :])
    sel = np.zeros((16, 8, 128), np.float32)
    for h in range(16):
        sel[h, h // 2, (h % 2) * 64:(h % 2 + 1) * 64] = 1.0
    f[0:16, C_SELBLK:C_SELBLK + 1024] = sel.reshape(16, 1024)
    b = np.zeros((128, NB), np.float32)
    b[:, B_IDB:B_IDB + 128] = np.eye(128)
    b[:, B_MP:B_MP + 128] = (i[:, None] >= i[None, :])
    b[:, B_MC:B_MC + 128] = (i[:, None] <= i[None, :])
    b[:, B_MCACHE:B_MCACHE + 4] = (i[:, None] >= np.arange(4)[None, :])
    b[0:64, B_MNEW:B_MNEW + 64] = same & (l_[:, None] <= l_[None, :])
    b[:, B_UIB:B_UIB + 128] = (i[:, None] <= i[None, :])
    b[0:64, B_UISB:B_UISB + 64] = same & (l_[:, None] <= l_[None, :])
    op = np.zeros((128, 2, 128), np.float32)
    op[:, 0, 0:64] = 1.0
    op[:, 1, 64:128] = 1.0
    b[:, B_ONESPAD:B_ONESPAD + 256] = op.reshape(128, 256)
    seqm = (np.arange(16)[:, None] == s_[None, :]).astype(np.float32)
    b[:, B_SEQM:B_SEQM + 1024] = np.broadcast_to(seqm.reshape(1, 1024), (128, 1024))
    b[0:64, B_SEQINDB:B_SEQINDB + 16] = (s_[:, None] == np.arange(16)[None, :])
    return f, b.astype(ml_dtypes.bfloat16)


_CACHE = {}


def _get_program(stop_after=None, dbg=()):
    key = (stop_after, tuple(dbg))
    if key not in _CACHE:
        _CACHE[key] = build_program(stop_after, dbg)
    return _CACHE[key]


def make_in_maps(inp):
    f32 = np.float32
    g = lambda k: np.ascontiguousarray(np.asarray(inp[k], dtype=f32))
    x_prompt, x_sample = g("x_prompt"), g("x_sample")
    ck, cv = g("cache_swa_k")[0], g("cache_swa_v")[0]
    sconv, sssm = g("state_conv")[0], g("state_ssm")[0]
    cf, cbb = _constants()
    hv = np.stack([g("dt_bias")[0], g("a_log")[0], g("d_skip")[0]]).reshape(1, 48)
    cf[:, C_HV:C_HV + 48] = hv
    cw = g("conv_w")[0]
    cf[:, C_CONVW:C_CONVW + 48] = cw.reshape(4, 12, 128).transpose(2, 1, 0).reshape(128, 48)
    cf[:, C_CONVB:C_CONVB + 12] = g("conv_b")[0].reshape(12, 128).T
    sk = g("sinks")[0]
    cf[:, C_SINKS:C_SINKS + 4] = sk.reshape(2, 4)[np.arange(128) // 64]
    vecs = np.stack([g("norm1")[0], g("norm2")[0], g("final_norm"), g("ssm_norm")[0]])
    vecs = np.ascontiguousarray(np.broadcast_to(vecs[None], (128, 4, D)))
    shared = dict(w_in=g("w_in")[0], w_oa=g("w_oa")[0], w_ob=g("w_ob")[0], w_o=g("w_o")[0], w_up=g("w_up")[0],
                  w_down=g("w_down")[0], vecs=vecs, cstf=cf, cstb=cbb)
    maps = []
    for c in range(NCORES):
        ss = slice(16 * c, 16 * c + 16)
        xs = x_sample[ss].transpose(1, 0, 2).reshape(64, D)
        m = dict(shared)
        m["xin"] = np.ascontiguousarray(np.concatenate([x_prompt[c], xs], axis=0))
        m["ck"] = np.ascontiguousarray(ck[ss].reshape(16, 128, 128))
        m["cv"] = np.ascontiguousarray(cv[ss].reshape(16, 128, 128))
        m["sconv"] = np.ascontiguousarray(sconv[ss])
        m["sssm"] = np.ascontiguousarray(sssm[ss].reshape(16, 1024, 128))
        maps.append(m)
    return maps


def kernel(**inputs):
    nc, _ = _get_program()
    maps = make_in_maps(inputs)
    res = run_bass_kernel_spmd(nc, maps, core_ids=list(range(NCORES)))
    R = res.results
    f32 = np.float32
    y_prompt = np.stack([np.asarray(R[c]["yout"], f32)[0:2048] for c in range(NCORES)])
    y_sample = np.concatenate([np.asarray(R[c]["yout"], f32)[2048:].reshape(4, 16, D).transpose(1, 0, 2) for c in range(NCORES)], axis=0)
    nkp = np.stack([np.asarray(R[c]["nkp"], f32).reshape(128, 2, 64) for c in range(NCORES)])[None]
    nvp = np.stack([np.asarray(R[c]["nvp"], f32).reshape(128, 2, 64) for c in range(NCORES)])[None]
    ncp = np.stack([np.asarray(R[c]["ncp"], f32) for c in range(NCORES)])[None]
    nsp = np.stack([np.asarray(R[c]["nsp"], f32).reshape(16, 64, 128) for c in range(NCORES)])[None]
    nks = np.concatenate([np.asarray(R[c]["nks"], f32).reshape(16, 128, 2, 64) for c in range(NCORES)], axis=0)[None]
    nvs = np.concatenate([np.asarray(R[c]["nvs"], f32).reshape(16, 128, 2, 64) for c in range(NCORES)], axis=0)[None]
    ncs = np.concatenate([np.asarray(R[c]["ncs"], f32) for c in range(NCORES)], axis=0)[None]
    nss = np.concatenate([np.asarray(R[c]["nss"], f32).reshape(16, 16, 64, 128) for c in range(NCORES)], axis=0)[None]
    return (y_prompt, y_sample, nkp, nvp, ncp, nsp, nks, nvs, ncs, nss)
```

```python
import numpy as np
import ml_dtypes
from contextlib import ExitStack
import concourse.bass as bass
import concourse.mybir as mybir
from concourse.bass_utils import run_bass_kernel_spmd

F32 = mybir.dt.float32
BF16 = mybir.dt.bfloat16
AF = mybir.ActivationFunctionType
ALU = mybir.AluOpType

NCORES = 8
D = 1024
NT = 2112
NTL = 17
SC0 = 2048
EPS = 1e-6
INW = 5392
ARENA_WORDS = 53000

C_IDF, C_UI, C_SL, C_ONES, C_UIS, C_SLS, C_SEQIND, C_SELBLK = 0, 128, 256, 384, 512, 576, 640, 656
C_HV, C_CONVW, C_CONVB, C_SINKS, NF = 1680, 1728, 1776, 1788, 1792
B_IDB, B_MP, B_MC, B_MCACHE, B_MNEW, B_UIB, B_UISB, B_ONESPAD, B_SEQM, B_SEQINDB, NB = \
    0, 128, 256, 384, 388, 452, 580, 644, 900, 1924, 1940


class DSem:
    def __init__(self, sem, name):
        self.sem = sem
        self.count = 0
        self.name = name
        self.open = None

    def add(self):
        self.count += 16
        if self.open is None:
            self.open = {"final": None}
        return self.open

    def seal(self):
        if self.open is not None:
            self.open["final"] = self.count
            self.open = None


class Prog:
    ENG_NAMES = ("pe", "act", "dve", "pool", "sp")

    def __init__(self, nc, stack):
        self.nc = nc
        self.stack = stack
        self.ops = []
        self.last_w = {}
        self.readers = {}
        self.esem = {e: stack.enter_context(nc.semaphore("s_" + e)) for e in ("pe", "act", "dve", "pool")}
        self.dsems = []
        self.last_on_eng = {e: None for e in self.ENG_NAMES}

    def dsem(self, name):
        d = DSem(self.stack.enter_context(self.nc.semaphore("d_" + name)), name)
        self.dsems.append(d)
        return d

    def _deps(self, r, w, eng=None):
        deps = set()
        for t in r:
            if t in self.last_w:
                deps.add(self.last_w[t])
            if isinstance(t, tuple) and t[0] == "B":
                for x in self.readers.get(t, ()):
                    if self.ops[x]["eng"] != eng:
                        deps.add(x)
        for t in w:
            if t in self.last_w:
                deps.add(self.last_w[t])
            for x in self.readers.get(t, ()):
                deps.add(x)
        for d in deps:
            p = self.ops[d]
            if p["kind"] == "dma" and p["grp"]["final"] is None:
                p["dsem"].seal()
        return deps

    def _commit(self, idx, r, w):
        for t in r:
            self.readers.setdefault(t, []).append(idx)
        for t in w:
            self.last_w[t] = idx
            self.readers[t] = []

    def op(self, eng, fn, r=(), w=()):
        idx = len(self.ops)
        deps = self._deps(r, w, eng)
        self.ops.append(dict(eng=eng, fn=fn, deps=deps, kind="c", inc=False, cnt=0))
        self._commit(idx, r, w)
        self.last_on_eng[eng] = idx
        return idx

    def dma(self, eng, fn, r=(), w=(), dsem=None):
        idx = len(self.ops)
        deps = self._deps(r, w, eng)
        grp = dsem.add()
        self.ops.append(dict(eng=eng, fn=fn, deps=deps, kind="dma", dsem=dsem, grp=grp))
        self._commit(idx, r, w)
        return idx

    def barrier(self):
        deps = set()
        for e in ("pe", "act", "dve", "pool"):
            if self.last_on_eng[e] is not None:
                deps.add(self.last_on_eng[e])
        for d in self.dsems:
            d.seal()
        dvals = [(d, d.count) for d in self.dsems if d.count > 0]
        for e in self.ENG_NAMES:
            self.ops.append(dict(eng=e, fn=None, deps=set(deps), kind="bar", dvals=dvals, inc=False, cnt=0))

    def finalize(self):
        ops = self.ops
        for d in self.dsems:
            d.seal()
        for o in ops:
            for d in o["deps"]:
                p = ops[d]
                if p["kind"] == "c":
                    if p["eng"] == "pe" and o["eng"] == "pe" and o["kind"] == "c":
                        continue
                    p["inc"] = True
        cnt = {e: 0 for e in self.ENG_NAMES}
        for o in ops:
            if o["kind"] == "c":
                if o["inc"]:
                    cnt[o["eng"]] += 1
                o["cnt"] = cnt[o["eng"]]

    def emit_engine(self, ename, engine):
        ops = self.ops
        waited = {}

        def wait(sem, key, val):
            if waited.get(key, 0) < val:
                engine.wait_ge(sem, val)
                waited[key] = val

        for o in ops:
            if o["eng"] != ename:
                continue
            for d in sorted(o["deps"]):
                p = ops[d]
                if p["kind"] == "c":
                    if p["eng"] == "pe" and ename == "pe" and o["kind"] == "c":
                        continue
                    wait(self.esem[p["eng"]], p["eng"], p["cnt"])
                elif p["kind"] == "dma":
                    wait(p["dsem"].sem, "d_" + p["dsem"].name, p["grp"]["final"])
            if o["kind"] == "bar":
                for d, v in o["dvals"]:
                    wait(d.sem, "d_" + d.name, v)
                continue
            ins = o["fn"](engine)
            if o["kind"] == "dma":
                ins.then_inc(o["dsem"].sem, 16)
            elif o["inc"]:
                ins.then_inc(self.esem[ename], 1)

    def emit(self):
        self.finalize()
        nc = self.nc
        with nc.Block() as block:
            @block.sync
            def _(e):
                self.emit_engine("sp", e)

            @block.tensor
            def _(e):
                self.emit_engine("pe", e)

            @block.scalar
            def _(e):
                self.emit_engine("act", e)

            @block.vector
            def _(e):
                self.emit_engine("dve", e)

            @block.gpsimd
            def _(e):
                self.emit_engine("pool", e)


class Bump:
    def __init__(self, arena, abf, regions):
        self.arena = arena
        self.abf = abf
        self.regs = [list(r) for r in regions]

    def _take(self, w):
        w = (w + 7) // 8 * 8
        for r in self.regs:
            if r[1] - r[0] >= w:
                off = r[0]
                r[0] += w
                return off
        raise RuntimeError("arena region overflow: need %d words, regions %s" % (w, self.regs))

    def f32(self, n):
        off = self._take(n)
        return self.arena[:, off:off + n]

    def bf(self, n):
        off = self._take((n + 1) // 2)
        return self.abf[:, 2 * off:2 * off + n]

    def take(self, w):
        return self._take(w)

    def fview(self, off, n):
        return self.arena[:, off:off + n]

    def bview(self, off, n):
        return self.abf[:, 2 * off:2 * off + n]

    def free_words(self):
        return sum(r[1] - r[0] for r in self.regs)


class StopBuild(Exception):
    pass


def rows_of(t):
    return 128 if t < 16 else 64


def build_program(stop_after=None, dbg=()):
    nc = bass.Bass("TRN2", target_bir_lowering=False)

    def din(name, shape, dt=F32):
        return nc.dram_tensor(name, list(shape), dt, kind="ExternalInput").ap()

    def dout(name, shape, dt=F32):
        return nc.dram_tensor(name, list(shape), dt, kind="ExternalOutput").ap()

    xin = din("xin", [NT, D])
    ck = din("ck", [16, 128, 128])
    cv = din("cv", [16, 128, 128])
    sconv = din("sconv", [16, 3, 1536])
    sssm = din("sssm", [16, 1024, 128])
    w_in = din("w_in", [D, INW])
    w_oa = din("w_oa", [512, D])
    w_ob = din("w_ob", [D, D])
    w_o = din("w_o", [D, D])
    w_up = din("w_up", [D, 4096])
    w_down = din("w_down", [4096, D])
    vecs = din("vecs", [128, 4, D])
    cstf = din("cstf", [128, NF])
    cstb = din("cstb", [128, NB], BF16)

    yout = dout("yout", [NT, D])
    nkp = dout("nkp", [128, 128])
    nvp = dout("nvp", [128, 128])
    ncp = dout("ncp", [3, 1536])
    nsp = dout("nsp", [1024, 128])
    nks = dout("nks", [16, 128, 128])
    nvs = dout("nvs", [16, 128, 128])
    ncs = dout("ncs", [16, 3, 1536])
    nss = dout("nss", [16, 1024, 128])

    dbg_outs = {}

    with ExitStack() as st:
        st.enter_context(nc.allow_low_precision("bf16 matmul operands, fp32 accumulation"))
        st.enter_context(nc.allow_non_contiguous_dma("strided weight / state layouts"))
        arena = st.enter_context(nc.sbuf_tensor("arena", [128, ARENA_WORDS], F32))
        abf = arena.bitcast(BF16)
        banks = [st.enter_context(nc.psum_tensor("bank%d" % i, [128, 512], F32)) for i in range(8)]
        bankb = [b.bitcast(BF16) for b in banks]
        P = Prog(nc, st)

        def MM(out, lhsT, rhs, start=True, stop=True, r=(), w=()):
            P.op("pe", lambda e: e.matmul(out, lhsT=lhsT, rhs=rhs, start=start, stop=stop), r, w)

        def TR(out, in_, ident, r=(), w=()):
            P.op("pe", lambda e: e.transpose(out=out, in_=in_, identity=ident), r, w)

        def ACT(out, in_, func, r=(), w=(), bias=None, scale=1.0, accum_out=None):
            def f(e):
                kw = dict(scale=scale)
                if bias is not None:
                    kw["bias"] = bias
                if accum_out is not None:
                    kw["accum_out"] = accum_out
                return e.activation(out=out, in_=in_, func=func, **kw)
            P.op("act", f, r, w)

        def CP(eng, out, in_, r=(), w=()):
            if eng == "act":
                P.op("act", lambda e: e.copy(out=out, in_=in_), r, w)
            else:
                P.op(eng, lambda e: e.tensor_copy(out=out, in_=in_), r, w)

        def TT(eng, out, in0, in1, op, r=(), w=()):
            P.op(eng, lambda e: e.tensor_tensor(out=out, in0=in0, in1=in1, op=op), r, w)

        def TS(eng, out, in0, s1, s2, op0, op1=None, r=(), w=()):
            if op1 is None:
                P.op(eng, lambda e: e.tensor_scalar(out=out, in0=in0, scalar1=s1, scalar2=None, op0=op0), r, w)
            else:
                P.op(eng, lambda e: e.tensor_scalar(out=out, in0=in0, scalar1=s1, scalar2=s2, op0=op0, op1=op1), r, w)

        def STT(eng, out, in0, scalar, in1, op0, op1, r=(), w=()):
            P.op(eng, lambda e: e.scalar_tensor_tensor(out=out, in0=in0, scalar=scalar, in1=in1, op0=op0, op1=op1), r, w)

        def MEMSET(eng, ap, val, w=()):
            P.op(eng, lambda e: e.memset(ap, val), (), w)

        def DMA(eng, out, in_, dsem, r=(), w=()):
            P.dma(eng, lambda e: e.dma_start(out=out, in_=in_), r, w, dsem)

        def BK(i):
            return ("B", i)

        def CHECK(tag):
            if stop_after == tag:
                raise StopBuild()

        d_out = P.dsem("out")
        d_dbg = P.dsem("dbg")

        def dump(name, ap, tokens, dt=F32):
            if name not in dbg:
                return
            shp = list(ap.shape)
            o = dout("dbg_" + name, shp, dt)
            dbg_outs[name] = shp
            DMA("sp", o, ap, d_dbg, r=tokens)

        cf = arena[:, 0:NF]
        cbw = (NB + 1) // 2
        cb = abf[:, 2 * NF:2 * NF + NB]
        o_small = NF + cbw + 4
        small = arena[:, o_small:o_small + 128]
        O_H = 2944
        assert o_small + 128 <= O_H
        O_X = O_H + 8448
        O_G = O_X + 17408
        O_T = O_G + 8448
        hT = abf[:, 2 * O_H:2 * O_H + 8 * NT].rearrange("p (a b) -> p a b", a=8)
        aoT = abf[:, 2 * O_X:2 * O_X + 4 * NT].rearrange("p (a b) -> p a b", a=4)
        moT = abf[:, 2 * (O_X + 4224):2 * (O_X + 4224) + 8 * NT].rearrange("p (a b) -> p a b", a=8)
        mgT = abf[:, 2 * O_G:2 * O_G + 8 * NT].rearrange("p (a b) -> p a b", a=8)
        x1 = arena[:, O_X:O_X + NTL * D].rearrange("p (a b) -> p a b", a=NTL)
        X_SPARE = (O_X + 12672, O_X + 17408)
        R_G = (O_G, O_G + 8448)
        R_T = (O_T, ARENA_WORDS)
        R_H = (O_H, O_H + 8448)

        idf = cf[:, C_IDF:C_IDF + 128]
        idb = cb[:, B_IDB:B_IDB + 128]

        d_c = P.dsem("const")
        DMA("sp", cf, cstf, d_c, w=["cf"])
        DMA("sp", cb, cstb, d_c, w=["cb"])
        CONST = ["cf", "cb"]

        sinkexp = small[:, 0:4]
        a_neg = small[:, 8:24]
        neghalf = small[:, 24:25]
        ACT(sinkexp, cf[:, C_SINKS:C_SINKS + 4], AF.Exp, r=CONST, w=["sinkexp"])
        ACT(a_neg, cf[:, C_HV + 16:C_HV + 32], AF.Exp, r=CONST, w=["a_neg0"])
        TS("dve", a_neg, a_neg, -1.0, None, ALU.mult, r=["a_neg0"], w=["a_neg"])
        MEMSET("pool", neghalf, -0.5, w=["neghalf"])
        dtb_bc = cf[:, C_HV:C_HV + 16]
        dskip_bc = cf[:, C_HV + 32:C_HV + 48]

        bank_ctr = [0]
        PF = {}
        WO_OFF = ARENA_WORDS - 8 - 4096
        E0_OFF = WO_OFF - 4096
        C0_OFF = WO_OFF - 3584

        def c0_views():
            o = 2 * C0_OFF
            wga0 = abf[:, o:o + 2048].rearrange("p (k c) -> p k c", k=8)
            wgb0 = abf[:, o + 2048:o + 4096].rearrange("p (k c) -> p k c", k=8)
            woa0 = abf[:, o + 4096:o + 5120].rearrange("p (k c) -> p k c", k=4)
            wob0 = abf[:, o + 5120:o + 7168].rearrange("p (k c) -> p k c", k=8)
            return wga0, wgb0, woa0, wob0

        def load_c_into(qi, wga_, wgb_, woa_, wob_, dsem_, ws):
            cs = slice(qi * 256, (qi + 1) * 256)
            woa_src = w_oa.rearrange("(kv g d) n -> kv d g n", kv=2, g=4)
            DMA("pool", wga_, w_in[:, 3344 + qi * 256:3344 + (qi + 1) * 256].rearrange("(kc p) n -> p kc n", p=128), dsem_, w=[("wC", ws)])
            DMA("pool", wgb_, w_in[:, 4368 + qi * 256:4368 + (qi + 1) * 256].rearrange("(kc p) n -> p kc n", p=128), dsem_, w=[("wC", ws)])
            for kv in range(2):
                DMA("pool", woa_[kv * 64:(kv + 1) * 64, :, :], woa_src[kv][:, :, cs], dsem_, w=[("wC", ws)])
            DMA("pool", wob_, w_ob[:, cs].rearrange("(kc p) n -> p kc n", p=128), dsem_, w=[("wC", ws)])

        def rstd_from_ss(ss_ap, ms_ap, rstd_ap, inv_n, eps, toks_r, tok_ms, tok_rstd):
            rows_ = ms_ap.shape[0]
            TS("pool", ms_ap, ss_ap, inv_n, eps, ALU.mult, ALU.add, r=toks_r, w=[tok_ms])
            TT("pool", rstd_ap, ms_ap, neghalf[0:rows_], ALU.pow, r=[tok_ms, "neghalf"], w=[tok_rstd])

        def phase_N():
            A = Bump(arena, abf, [R_T, R_G])
            wx_p = A.bf(8 * 1536).rearrange("p (k c) -> p k c", k=8)
            wz_p = A.bf(8 * 1024).rearrange("p (k c) -> p k c", k=8)
            PF["d_wB"] = P.dsem("wB")
            DMA("pool", wx_p, w_in[:, 1792:3328].rearrange("(kc p) n -> p kc n", p=128), PF["d_wB"], w=["wx"])
            nw = A.f32(1024)
            NLAG = 3
            NSL = NLAG + 2
            xt = [A.f32(1024) for _ in range(NSL)]
            hb = [A.bf(1024) for _ in range(2)]
            junk = A.bf(1024)
            st_ = A.f32(16)
            d_nw = P.dsem("nw1")
            d_x = [P.dsem("xN%d" % i) for i in range(NSL)]
            DMA("sp", nw, vecs[:, 0, :], d_nw, w=["nw"])
            def stage1(t):
                s = t % NSL
                rows = rows_of(t)
                DMA("sp", xt[s][0:rows], xin[t * 128:t * 128 + rows, :], d_x[s], w=[("xt", s)])
                ss = st_[:, s:s + 1]
                ms = st_[:, 5 + s:6 + s]
                rs = st_[:, 10 + s:11 + s]
                ACT(junk[0:rows], xt[s][0:rows], AF.Square, r=[("xt", s)], w=["junk", ("ss", s)], accum_out=ss[0:rows])
                rstd_from_ss(ss[0:rows], ms[0:rows], rs[0:rows], 1.0 / D, EPS, [("ss", s)], ("ms", s), ("rs", s))

            def stage2(t):
                s = t % NSL
                h2 = t % 2
                rows = rows_of(t)
                rs = st_[:, 10 + s:11 + s]
                STT("dve", hb[h2][0:rows], xt[s][0:rows], rs[0:rows], nw[0:rows], ALU.mult, ALU.mult,
                    r=[("xt", s), ("rs", s), "nw"], w=[("hb", h2)])
                b = t % 2
                pv = bankb[b][:, :].rearrange("p (a b) -> p a b", a=8)
                for kc in range(8):
                    TR(pv[:, kc, 0:rows], hb[h2][0:rows, kc * 128:(kc + 1) * 128], idb[0:rows, 0:rows],
                       r=[("hb", h2)] + CONST, w=[BK(b)])
                CP("act", hT[:, :, t * 128:t * 128 + rows], pv[:, :, 0:rows], r=[BK(b)], w=[("hT", t)])

            for t in range(NTL + NLAG):
                if t < NTL:
                    stage1(t)
                if t >= NLAG:
                    stage2(t - NLAG)

        HT_ALL = [("hT", t) for t in range(NTL)]

        def hT_toks(c0, n):
            return [("hT", t) for t in range(c0 // 128, (c0 + n + 127) // 128)]

        def phase_A():
            A = Bump(arena, abf, [(R_T[0], C0_OFF), R_G, X_SPARE, (WO_OFF, R_T[1])])
            wq = A.bf(8 * 512).rearrange("p (k g c) -> p k g c", k=8, g=4)
            wk = A.bf(8 * 128).rearrange("p (k c) -> p k c", k=8)
            wv = A.bf(8 * 128).rearrange("p (k c) -> p k c", k=8)
            assert A.regs[0][0] == O_T + 3072
            qT = A.bf(4 * NT).rearrange("p (g t) -> p g t", g=4)
            kT = A.bf(NT)
            vpad = A.bf(NTL * 256).rearrange("p (t k c) -> p t k c", t=NTL, k=2)
            kf = [A.f32(128) for _ in range(2)]
            vf = [A.f32(128) for _ in range(2)]
            pbuf = [A.bf(512) for _ in range(8)]
            dsum = [A.f32(512) for _ in range(2)]
            rden = [A.f32(512) for _ in range(2)]
            ckb = A.bf(16 * 128).rearrange("p (s c) -> p s c", s=16)
            kcT = A.bf(16 * 128).rearrange("p (s c) -> p s c", s=16)
            vcpad = A.bf(16 * 256).rearrange("p (s k c) -> p s k c", s=16, k=2)
            pc = A.bf(512)
            pn = A.bf(512)
            d_cache = P.dsem("cacheA")

            VP_ALL = [("vpad", t) for t in range(NTL)]
            MEMSET("pool", vpad[:, :, :, :], 0.0, w=VP_ALL)
            MEMSET("pool", vcpad[:, :, :, :], 0.0, w=["vcpad"])
            DMA("pool", ckb, ck.rearrange("s t c -> t s c"), d_cache, w=["ckb"])
            for kv in range(2):
                DMA("pool", vcpad[:, :, kv, kv * 64:(kv + 1) * 64], cv[:, :, kv * 64:(kv + 1) * 64].rearrange("s t c -> t s c"),
                    d_cache, r=[], w=["vcpad"])
            DMA("sp", nks[:, 0:124, :], ck[:, 4:128, :], d_out)
            DMA("sp", nvs[:, 0:124, :], cv[:, 4:128, :], d_out)

            def nb():
                b = bank_ctr[0] % 8
                bank_ctr[0] += 1
                return b

            ev = [0]

            def evac_eng():
                ev[0] += 1
                return "act" if ev[0] % 2 else "dve"

            CHECK("A0")
            for T in range(5):
                c0 = T * 512
                n = 512 if T < 4 else 64
                htk = hT_toks(c0, n)
                for g in range(4):
                    b = nb()
                    for kc in range(8):
                        MM(banks[b][:, 0:n], wq[:, kc, g, :], hT[:, kc, c0:c0 + n], start=(kc == 0), stop=(kc == 7),
                           r=[("wq", g)] + htk, w=[BK(b)])
                    CP(evac_eng(), qT[:, g, c0:c0 + n], banks[b][:, 0:n], r=[BK(b)], w=[("qT", g, T)])
                b = nb()
                for kc in range(8):
                    MM(banks[b][:, 0:n], wk[:, kc, :], hT[:, kc, c0:c0 + n], start=(kc == 0), stop=(kc == 7),
                       r=["wk"] + htk, w=[BK(b)])
                CP(evac_eng(), kT[:, c0:c0 + n], banks[b][:, 0:n], r=[BK(b)], w=[("kT", T)])
                for t in range(c0 // 128, (c0 + n + 127) // 128):
                    rows = rows_of(t)
                    b = nb()
                    for kc in range(8):
                        MM(banks[b][0:rows, 0:128], hT[:, kc, t * 128:t * 128 + rows], wv[:, kc, :], start=(kc == 0), stop=(kc == 7),
                           r=["wv", ("hT", t)], w=[BK(b)])
                    e = evac_eng()
                    for kv in range(2):
                        CP(e, vpad[0:rows, t, kv, kv * 64:(kv + 1) * 64], banks[b][0:rows, kv * 64:(kv + 1) * 64],
                           r=[BK(b)], w=[("vpad", t)])
                    if t >= 15:
                        i = t - 15
                        CP("dve", vf[i][0:rows], banks[b][0:rows, 0:128], r=[BK(b)], w=[("vf", i)])
                        b2 = nb()
                        for kc in range(8):
                            MM(banks[b2][0:rows, 0:128], hT[:, kc, t * 128:t * 128 + rows], wk[:, kc, :], start=(kc == 0), stop=(kc == 7),
                               r=["wk", ("hT", t)], w=[BK(b2)])
                        CP("dve", kf[i][0:rows], banks[b2][0:rows, 0:128], r=[BK(b2)], w=[("kf", i)])
            CHECK("A1")
            DMA("sp", nkp, kf[0], d_out, r=[("kf", 0)])
            DMA("sp", nvp, vf[0], d_out, r=[("vf", 0)])
            for l in range(4):
                DMA("sp", nks[:, 124 + l, :], kf[1][l * 16:(l + 1) * 16, :], d_out, r=[("kf", 1)])
                DMA("sp", nvs[:, 124 + l, :], vf[1][l * 16:(l + 1) * 16, :], d_out, r=[("vf", 1)])
            dump("qT", qT[:, :, :], [("qT", g, T) for g in range(4) for T in range(5)], BF16)
            dump("kT", kT, [("kT", T) for T in range(5)], BF16)
            dump("vpad", vpad[:, :, :, :], VP_ALL, BF16)

            CHECK("A1b")
            scale = 64 ** -0.5
            mP = cb[:, B_MP:B_MP + 128]
            mC = cb[:, B_MC:B_MC + 128]
            onespad = cb[:, B_ONESPAD:B_ONESPAD + 256].rearrange("p (k c) -> p k c", k=2)
            sink_bc = sinkexp.unsqueeze(2).to_broadcast([128, 4, 128])

            pslot = [0]
            plists = {}

            def stage_s(i):
                T = i // 4
                q0 = i * 128
                js = ([i - 1] if i > 0 else []) + [i]
                plist = []
                for kv in range(2):
                    ks = slice(kv * 64, (kv + 1) * 64)
                    for j in js:
                        b = nb()
                        MM(banks[b][:, :].rearrange("p (g q) -> p g q", g=4), kT[ks, j * 128:(j + 1) * 128], qT[ks, :, q0:q0 + 128],
                           r=[("kT", j // 4)] + [("qT", g, T) for g in range(4)], w=[BK(b)])
                        sl = pslot[0] % 8
                        pslot[0] += 1
                        pb = pbuf[sl]
                        ACT(pb, banks[b][:, :], AF.Exp, r=[BK(b)], w=[("pb", sl)], scale=scale)
                        msk = mP if j < i else mC
                        TT("dve", pb.rearrange("p (g q) -> p g q", g=4), pb.rearrange("p (g q) -> p g q", g=4),
                           msk.unsqueeze(1).to_broadcast([128, 4, 128]), ALU.mult, r=[("pb", sl)] + CONST, w=[("pb", sl)])
                        plist.append((kv, j, sl))
                plists[i] = plist

            def stage_p(i):
                q0 = i * 128
                plist = plists[i]
                bo = nb()
                bd = nb()
                for idx, (kv, j, sl) in enumerate(plist):
                    MM(banks[bo][:, :], vpad[:, j, kv, :], pbuf[sl], start=(idx == 0), stop=(idx == len(plist) - 1),
                       r=[("vpad", j), ("pb", sl)], w=[BK(bo)])
                for idx, (kv, j, sl) in enumerate(plist):
                    MM(banks[bd][:, :], onespad[:, kv, :], pbuf[sl], start=(idx == 0), stop=(idx == len(plist) - 1),
                       r=[("pb", sl)] + CONST, w=[BK(bd)])
                s2 = i % 2
                TT("dve", dsum[s2].rearrange("p (g q) -> p g q", g=4), banks[bd][:, :].rearrange("p (g q) -> p g q", g=4), sink_bc,
                   ALU.add, r=[BK(bd), "sinkexp"], w=[("dsum", s2)])
                P.op("dve", lambda e, o=rden[s2], i_=dsum[s2]: e.reciprocal(out=o, in_=i_), r=[("dsum", s2)], w=[("rden", s2)])
                TT("dve", aoT[:, :, q0:q0 + 128], banks[bo][:, :].rearrange("p (g q) -> p g q", g=4),
                   rden[s2].rearrange("p (g q) -> p g q", g=4), ALU.mult, r=[BK(bo), ("rden", s2)], w=[("aoT", i)])

            stage_s(0)
            for i in range(16):
                if i + 1 < 16:
                    stage_s(i + 1)
                stage_p(i)

            CHECK("A2")
            PF["d_wC0"] = P.dsem("wC0")
            load_c_into(0, *c0_views(), PF["d_wC0"], 0)
            for half in range(2):
                b = nb()
                pv = bankb[b][:, :].rearrange("p (a c) -> p a c", a=8)
                for a in range(8):
                    s_ = half * 8 + a
                    TR(pv[:, a, :], ckb[:, s_, :], idb, r=["ckb"] + CONST, w=[BK(b)])
                CP(evac_eng(), kcT[:, half * 8:(half + 1) * 8, :], pv, r=[BK(b)], w=[("kcT", half)])
            CHECK("A3a")
            TQ = [("qT", g, 4) for g in range(4)]
            pcv = pc.rearrange("p (k s g l) -> p k s g l", k=2, s=16, g=4)
            mcache = cb[:, B_MCACHE:B_MCACHE + 4]
            for kv in range(2):
                ks = slice(kv * 64, (kv + 1) * 64)
                bsc = nb()
                scv = banks[bsc][:, 0:256].rearrange("p (s g l) -> p s g l", s=16, g=4)
                for s_ in range(16):
                    rhs = qT[ks, :, SC0:SC0 + 64].rearrange("p g (l s) -> p g l s", s=16)[:, :, :, s_]
                    MM(scv[:, s_, :, :], kcT[ks, s_, :], rhs, r=[("kcT", s_ // 8)] + TQ, w=[BK(bsc)])
                ACT(pc[:, kv * 256:(kv + 1) * 256], banks[bsc][:, 0:256], AF.Exp, r=[BK(bsc)], w=[("pc", kv)], scale=scale)
                TT("dve", pc[:, kv * 256:(kv + 1) * 256].rearrange("p (a l) -> p a l", l=4), pc[:, kv * 256:(kv + 1) * 256].rearrange("p (a l) -> p a l", l=4),
                   mcache.unsqueeze(1).to_broadcast([128, 64, 4]), ALU.mult, r=[("pc", kv)] + CONST, w=[("pc", kv)])
            mnew = cb[0:64, B_MNEW:B_MNEW + 64]
            for kv in range(2):
                ks = slice(kv * 64, (kv + 1) * 64)
                bsn = nb()
                MM(banks[bsn][0:64, 0:256].rearrange("p (g q) -> p g q", g=4), kT[ks, SC0:SC0 + 64], qT[ks, :, SC0:SC0 + 64],
                   r=[("kT", 4)] + TQ, w=[BK(bsn)])
                ACT(pn[0:64, kv * 256:(kv + 1) * 256], banks[bsn][0:64, 0:256], AF.Exp, r=[BK(bsn)], w=[("pn", kv)], scale=scale)
                TT("dve", pn[0:64, kv * 256:(kv + 1) * 256].rearrange("p (a q) -> p a q", q=64), pn[0:64, kv * 256:(kv + 1) * 256].rearrange("p (a q) -> p a q", q=64),
                   mnew.unsqueeze(1).to_broadcast([64, 4, 64]), ALU.mult, r=[("pn", kv)] + CONST, w=[("pn", kv)])
            CHECK("A3b")
            pnv = pn[0:64].rearrange("p (k g l s) -> p k g l s", k=2, g=4, l=4)
            bo = nb()
            bd = nb()
            for which, bX in (("o", bo), ("d", bd)):
                ov = banks[bX][:, 0:256].rearrange("p (s g l) -> p s g l", s=16, g=4)
                for s_ in range(16):
                    for kv in range(2):
                        lhs = vpad[0:64, 16, kv, :] if which == "o" else onespad[0:64, kv, :]
                        MM(ov[:, s_, :, :], lhs, pnv[:, kv, :, :, s_], start=(kv == 0), stop=False,
                           r=[("vpad", 16), ("pn", kv)] + CONST, w=[BK(bX)])
                    for kv in range(2):
                        lhs = vcpad[:, s_, kv, :] if which == "o" else onespad[:, kv, :]
                        MM(ov[:, s_, :, :], lhs, pcv[:, kv, s_, :, :], start=False, stop=(kv == 1),
                           r=["vcpad", ("pc", kv)] + CONST, w=[BK(bX)])
            CHECK("A3c")
            TT("dve", dsum[0][:, 0:256].rearrange("p (s g l) -> p s g l", s=16, g=4), banks[bd][:, 0:256].rearrange("p (s g l) -> p s g l", s=16, g=4),
               sinkexp.unsqueeze(1).unsqueeze(3).to_broadcast([128, 16, 4, 4]), ALU.add, r=[BK(bd), "sinkexp"], w=[("dsum", 0)])
            P.op("dve", lambda e: e.reciprocal(out=rden[0][:, 0:256], in_=dsum[0][:, 0:256]), r=[("dsum", 0)], w=[("rden", 0)])
            TT("dve", dsum[0][:, 0:256], banks[bo][:, 0:256], rden[0][:, 0:256], ALU.mult, r=[BK(bo), ("rden", 0)], w=[("dsum", 0)])
            CP("dve", aoT[:, :, SC0:SC0 + 64].rearrange("p g (l s) -> p g l s", s=16),
               dsum[0][:, 0:256].rearrange("p (s g l) -> p g l s", s=16, g=4), r=[("dsum", 0)], w=[("aoT", 16)])
            dump("aoT", aoT[:, :, :], [("aoT", i) for i in range(17)], BF16)

        AO_ALL = [("aoT", i) for i in range(17)]

        def phase_B():
            SCH = 256
            A = Bump(arena, abf, [R_T, R_G, X_SPARE, (O_X, O_X + 4224)])
            wx = A.bf(8 * 1536).rearrange("p (k c) -> p k c", k=8)
            wz = A.bf(8 * 1024).rearrange("p (k c) -> p k c", k=8)
            o_xbcs = A.take(3072)
            xbcs = [A.bview(o_xbcs + i * 1536, 12 * SCH).rearrange("p (b t) -> p b t", b=12) for i in range(2)]
            o_big = A.take(4096)
            big = [A.fview(o_big + i * 2048, 2048) for i in range(2)]
            o_lw = A.take(3072)
            Lb = [A.bview(o_lw + i * 1024, 2048) for i in range(2)]
            wT = A.bview(o_lw + 2048, 2048)
            snw = A.f32(1024)
            yb = A.f32(1024)
            hst = A.f32(1024)
            diagW = A.bf(12 * 4 * 128).rearrange("p (b i c) -> p b i c", b=12, i=4)
            rawb = [A.bf(264) for _ in range(2)]
            thc = [A.bf(256) for _ in range(2)]
            xs_tm = A.bf(1024)
            xdt = A.bf(1024)
            xdd = A.bf(1024)
            xsd = A.bf(1024)
            thz1 = A.bf(1024)
            thz = [thz1, thz1]
            m_tm = A.bf(1024)
            hst_bf = A.bf(1024)
            dt_all = A.f32(NTL * 16).rearrange("p (t h) -> p t h", t=NTL)
            dA_all = A.f32(NTL * 16).rearrange("p (t h) -> p t h", t=NTL)
            B_tm = A.bf(256).rearrange("p (g n) -> p g n", g=2)
            cbm = A.bf(256)
            wdt = A.bf(8 * 16).rearrange("p (k c) -> p k c", k=8)
            cwh = A.f32(48).rearrange("p (b i) -> p b i", b=12)
            cbh = A.f32(16)
            carry = A.bf(40)[:, 0:36].rearrange("p (b i) -> p b i", b=12)
            E3 = A.f32(48)
            st8 = A.f32(16)
            xbss = A.bf(12 * 64).rearrange("p (b c) -> p b c", b=12)
            CTm = [A.bf(128).rearrange("p (g c) -> p g c", g=2) for _ in range(2)]
            Bm = [A.bf(256).rearrange("p (g n) -> p g n", g=2) for _ in range(2)]
            cdl = A.f32(16)
            cdcol = A.f32(128).rearrange("p (a s) -> p a s", a=8)
            h0 = [A.fview(o_xbcs + i * 1024, 1024).rearrange("p (a c) -> p a c", a=8) for i in range(2)]
            h0T = [A.bview(o_xbcs + 2048 + i * 512, 1024) for i in range(2)]
            scv = big[0][:, 0:1536]
            raws = A.bview(o_lw, 12 * 112).rearrange("p (b c) -> p b c", b=12)
            d_w = PF["d_wB"]
            d_big = P.dsem("bigB")
            DMA("pool", wdt, w_in[:, 3328:3344].rearrange("(kc p) n -> p kc n", p=128), d_w, w=["wdt"])
            DMA("pool", wz, w_in[:, 768:1792].rearrange("(kc p) n -> p kc n", p=128), P.dsem("wzB"), w=["wz"])
            DMA("sp", snw, vecs[:, 3, :], P.dsem("snwB"), w=["snw"])
            TS("pool", cwh, cf[:, C_CONVW:C_CONVW + 48].rearrange("p (b i) -> p b i", b=12), 0.5, None, ALU.mult, r=CONST, w=["cwh"])
            TS("pool", cbh[:, 0:12], cf[:, C_CONVB:C_CONVB + 12], 0.5, None, ALU.mult, r=CONST, w=["cbh"])
            MEMSET("pool", carry[:, :, :], 0.0, w=["carry"])
            for blk_ in range(12):
                for i_ in range(4):
                    TS("dve", diagW[:, blk_, i_, :], idb, cwh[:, blk_, i_:i_ + 1], None, ALU.mult, r=CONST + ["cwh"], w=["diagW"])
            MEMSET("pool", hst, 0.0, w=["hst"])
            MEMSET("pool", hst_bf, 0.0, w=["hst_bf"])

            UI = cf[:, C_UI:C_UI + 128]
            SL = cf[:, C_SL:C_SL + 128]
            ONES = cf[:, C_ONES:C_ONES + 128]
            UIB = cb[:, B_UIB:B_UIB + 128]

            for t in range(NTL):
                rows = rows_of(t)
                for kc in range(8):
                    MM(banks[0][0:rows, t * 16:(t + 1) * 16], hT[:, kc, t * 128:t * 128 + rows], wdt[:, kc, :], start=(kc == 0), stop=(kc == 7),
                       r=["wdt", ("hT", t)], w=[BK(0)])
            dtv = banks[0][:, 0:NTL * 16].rearrange("p (t h) -> p t h", t=NTL)
            TT("dve", dt_all[:, :, :], dtv, dtb_bc.unsqueeze(1).to_broadcast([128, NTL, 16]), ALU.add, r=[BK(0)] + CONST, w=["dt0"])
            ACT(dt_all[:, :, :], dt_all[:, :, :], AF.Exp, r=["dt0"], w=["dt1"])
            ACT(dt_all[:, :, :], dt_all[:, :, :], AF.Ln, r=["dt1"], w=["dt"], bias=1.0)
            TT("dve", dA_all[:, :, :], dt_all[:, :, :], a_neg.unsqueeze(1).to_broadcast([128, NTL, 16]), ALU.mult, r=["dt", "a_neg"], w=["dA"])
            dump("dt", dt_all[:, :, :], ["dt"])

            def conv_block(src_fn, width, blk, bY, thv, outv, rtoks, wtok):
                for i in range(4):
                    MM(banks[bY][:, 0:width], diagW[:, blk, i, :], src_fn(i), start=(i == 0), stop=(i == 3), r=rtoks + ["diagW"], w=[BK(bY)])
                tt = ("thc", blk % 2)
                ACT(thv, banks[bY][:, 0:width], AF.Tanh, r=[BK(bY), "cbh"], w=[tt], bias=cbh[:, blk:blk + 1])
                TS("dve", thv, thv, 1.0, None, ALU.add, r=[tt], w=[tt])
                STT("dve", outv, banks[bY][:, 0:width], cbh[:, blk:blk + 1], thv, ALU.add, ALU.mult, r=[BK(bY), tt, "cbh"], w=[wtok])

            ev = [0]

            def conv_stage1(S, blk):
                c0 = S * SCH
                b = blk % 2
                sl = blk % 2
                for kc in range(8):
                    MM(banks[b][:, 0:SCH], wx[:, kc, blk * 128:(blk + 1) * 128], hT[:, kc, c0:c0 + SCH], start=(kc == 0), stop=(kc == 7),
                       r=["wx"] + hT_toks(c0, SCH), w=[BK(b)])
                CP("pool", rawb[sl][:, 0:3], carry[:, blk, :], r=["carry"], w=[("rawc", sl)])
                ev[0] += 1
                CP("act" if ev[0] % 2 else "dve", rawb[sl][:, 3:3 + SCH], banks[b][:, 0:SCH], r=[BK(b)], w=[("raw", sl)])
                CP("pool", carry[:, blk, :], rawb[sl][:, SCH:SCH + 3], r=[("raw", sl)], w=["carry"])

            def conv_stage2(S, blk):
                xs_ = S % 2
                sl = blk % 2
                conv_block(lambda i: rawb[sl][:, i:i + SCH], SCH, blk, 2 + blk % 2, thc[sl], xbcs[xs_][:, blk, :],
                           [("raw", sl), ("rawc", sl)], ("xbcs", xs_, blk))

            def conv_blocks(S, blks):
                blks = list(blks)
                conv_stage1(S, blks[0])
                for i, blk in enumerate(blks):
                    if i + 1 < len(blks):
                        conv_stage1(S, blks[i + 1])
                    conv_stage2(S, blk)

            def front_decay_a(ci, R, sample=False):
                rs = slice(0, R)
                k = ci % 2
                UIr = cf[rs, C_UIS:C_UIS + 64] if sample else UI
                dAc = dA_all[rs, ci, :]
                bigv = big[k][rs].rearrange("p (h l) -> p h l", h=16)[:, :, 0:R]
                for h in range(16):
                    ACT(bigv[:, h, :], UIr, AF.Copy, r=["dA"] + CONST, w=[("big", k)], scale=dAc[:, h:h + 1])

            def front_decay_b(ci, R, sample=False):
                rs = slice(0, R)
                k = ci % 2
                SLr = cf[rs, C_SLS:C_SLS + 64] if sample else SL
                bigv = big[k][rs].rearrange("p (h l) -> p h l", h=16)[:, :, 0:R]
                Lv = Lb[k][rs].rearrange("p (h l) -> p h l", h=16)[:, :, 0:R]
                hq = 512 // R
                for qd in range(16 // hq):
                    b = 4 + (qd % 2)
                    MM(banks[b][rs, 0:hq * R].rearrange("p (h l) -> p h l", h=hq), SLr, bigv[:, qd * hq:(qd + 1) * hq, :], r=[("big", k)] + CONST, w=[BK(b)])
                    ACT(Lv[:, qd * hq:(qd + 1) * hq, :], banks[b][rs, 0:hq * R].rearrange("p (h l) -> p h l", h=hq), AF.Exp, r=[BK(b)], w=[("L", k, qd)])

            def ssd_chunk(R, ci, XS, BTg, CTg, xbc_toks, c_glob, sample=False, seq_loop=None, early_hook=None, mid_hook=None, end_hook=None):
                rs = slice(0, R)
                k = ci % 2
                UIr = cf[rs, C_UIS:C_UIS + 64] if sample else UI
                SLr = cf[rs, C_SLS:C_SLS + 64] if sample else SL
                UIBr = cb[rs, B_UISB:B_UISB + 64] if sample else UIB
                idr = idb
                gzk = thz[k]
                tmph = big[k][:, 0:1024]
                hq = 512 // R
                for half in range(2):
                    for kc in range(8):
                        MM(banks[6 + half][rs, :], hT[:, kc, c_glob:c_glob + R], wz[:, kc, half * 512:(half + 1) * 512], start=(kc == 0), stop=(kc == 7),
                           r=["wz", ("hT", c_glob // 128)], w=[BK(6 + half)])
                    hs = slice(half * 512, (half + 1) * 512)
                    ACT(gzk[rs, hs], banks[6 + half][rs, :], AF.Tanh, r=[BK(6 + half)], w=[("gz", half)], scale=0.5)
                    STT("dve", gzk[rs, hs], gzk[rs, hs], 1.0, banks[6 + half][rs, :], ALU.add, ALU.mult, r=[("gz", half), BK(6 + half)], w=[("gz", half)])
                pv = bankb[2][:, :].rearrange("p (a c) -> p a c", a=8)
                for blk in range(8):
                    TR(pv[rs, blk, :], XS(blk), idr, r=xbc_toks + CONST, w=[BK(2)])
                CP("act", xs_tm[rs].rearrange("p (a c) -> p a c", a=8), pv[rs, :, :], r=[BK(2)], w=["xs_tm"])
                b3 = banks[3]
                b3b = bankb[3]
                dAc = dA_all[rs, ci, :]
                MM(b3[rs, 0:16], UIr, dAc, r=["dA"] + CONST, w=[BK(3)])
                MM(b3[rs, 16:32], SLr, dAc, r=["dA"] + CONST, w=[BK(3)])
                if not sample:
                    MM(b3[:, 32:48], ONES, dAc, r=["dA"] + CONST, w=[BK(3)])
                for g in range(2):
                    MM(b3[rs, 64 + g * 128:64 + g * 128 + R], BTg(g), CTg(g), r=xbc_toks, w=[BK(3)])
                btv = b3b[:, 640:896].rearrange("p (g n) -> p g n", g=2)
                for g in range(2):
                    TR(btv[rs, g, :], BTg(g), idr, r=xbc_toks + CONST, w=[BK(3)])
                ncol = 32 if sample else 48
                ACT(E3[rs, 0:ncol], b3[rs, 0:ncol], AF.Exp, r=[BK(3)], w=["E3"])
                cbv = b3[rs, 64:320].rearrange("p (g l) -> p g l", g=2)[:, :, 0:R]
                cbmv = cbm[rs].rearrange("p (g l) -> p g l", g=2)[:, :, 0:R]
                TT("dve", cbmv, cbv, UIBr.unsqueeze(1).to_broadcast([R, 2, R]), ALU.mult, r=[BK(3)] + CONST, w=["cbm"])
                CP("act", B_tm[rs, :, :], btv[rs, :, :], r=[BK(3)], w=["B_tm"])
                Lv = Lb[k][rs].rearrange("p (h l) -> p h l", h=16)[:, :, 0:R]
                wTv = wT[rs].rearrange("p (h l) -> p h l", h=16)[:, :, 0:R]
                for g in range(2):
                    TT("dve", wTv[:, g * 8:(g + 1) * 8, :], Lv[:, g * 8:(g + 1) * 8, :], cbmv[:, g, :].unsqueeze(1).to_broadcast([R, 8, R]), ALU.mult,
                       r=[("L", k, q_) for q_ in range(16 // hq)] + ["cbm"], w=[("wT", g)])
                xsv = xs_tm[rs].rearrange("p (h c) -> p h c", h=16)
                xdtv = xdt[rs].rearrange("p (h c) -> p h c", h=16)
                xddv = xdd[rs].rearrange("p (h c) -> p h c", h=16)
                TT("dve", xdtv, xsv, dt_all[rs, ci, :].unsqueeze(2).to_broadcast([R, 16, 64]), ALU.mult, r=["xs_tm", "dt"], w=["xdt"])
                TT("dve", xsd[rs].rearrange("p (h c) -> p h c", h=16), xsv, dskip_bc[rs].unsqueeze(2).to_broadcast([R, 16, 64]), ALU.mult,
                   r=["xs_tm"] + CONST, w=["xsd"])
                TT("dve", xddv, xdtv, E3[rs, 16:32].unsqueeze(2).to_broadcast([R, 16, 64]), ALU.mult, r=["xdt", "E3"], w=["xdd"])
                if early_hook is not None:
                    early_hook()
                if not sample:
                    for g in range(2):
                        MM(banks[4 + g][:, :], CTg(g), hst_bf[:, g * 512:(g + 1) * 512], r=xbc_toks + ["hst_bf"], w=[BK(4 + g)])
                else:
                    seq_loop()
                for g in range(2):
                    gs = slice(g * 512, (g + 1) * 512)
                    TT("dve", yb[rs, gs].rearrange("p (h c) -> p h c", h=8), banks[4 + g][rs, :].rearrange("p (h c) -> p h c", h=8),
                       E3[rs, g * 8:(g + 1) * 8].unsqueeze(2).to_broadcast([R, 8, 64]), ALU.mult, r=[BK(4 + g), "E3"], w=[("yb", g)])
                for h in range(16):
                    b = 6 + h // 8
                    MM(banks[b][rs, (h % 8) * 64:(h % 8 + 1) * 64], wTv[:, h, :], xdtv[:, h, :], r=[("wT", h // 8), "xdt"], w=[BK(b)])
                for g in range(2):
                    gs = slice(g * 512, (g + 1) * 512)
                    TT("dve", yb[rs, gs], yb[rs, gs], banks[6 + g][rs, :], ALU.add, r=[("yb", g), BK(6 + g)], w=[("yb", g)])
                    TT("dve", yb[rs, gs], yb[rs, gs], xsd[rs, gs], ALU.add, r=[("yb", g), "xsd"], w=[("yb", g)])
                    TT("dve", yb[rs, gs], yb[rs, gs], gzk[rs, gs], ALU.mult, r=[("yb", g), ("gz", g)], w=[("yb", g)])
                    ACT(m_tm[rs, gs], yb[rs, gs], AF.Square, r=[("yb", g)], w=[("m_tm", g), ("ssq", g)], accum_out=st8[rs, g:g + 1])
                if mid_hook is not None:
                    mid_hook()
                TS("pool", st8[rs, 4:6], st8[rs, 0:2], 1.0 / 512, 4.0 * EPS, ALU.mult, ALU.add, r=[("ssq", 0), ("ssq", 1)], w=["msB"])
                TT("pool", st8[rs, 8:10], st8[rs, 4:6], neghalf[rs].to_broadcast([R, 2]), ALU.pow, r=["msB", "neghalf"], w=["rsB"])
                for g in range(2):
                    gs = slice(g * 512, (g + 1) * 512)
                    STT("dve", m_tm[rs, gs], yb[rs, gs], st8[rs, 8 + g:9 + g], snw[rs, gs], ALU.mult, ALU.mult, r=[("yb", g), "rsB", "snw"], w=[("m_tm", g)])
                pv2 = bankb[2][:, :].rearrange("p (a c) -> p a c", a=8)
                for blk in range(8):
                    TR(pv2[:, blk, 0:R], m_tm[rs, blk * 128:(blk + 1) * 128], idr[rs, rs], r=[("m_tm", blk // 4)] + CONST, w=[BK(2)])
                CP("act", moT[:, :, c_glob:c_glob + R], pv2[:, :, 0:R], r=[BK(2)], w=[("moT", c_glob // 128)])
                if not sample:
                    for g in range(2):
                        MM(banks[4 + g][:, :], B_tm[:, g, :], xdd[:, g * 512:(g + 1) * 512], r=["B_tm", "xdd"], w=[BK(4 + g)])
                    TT("dve", tmph.rearrange("p (h c) -> p h c", h=16), hst.rearrange("p (h c) -> p h c", h=16),
                       E3[:, 32:48].unsqueeze(2).to_broadcast([128, 16, 64]), ALU.mult, r=["hst", "E3"], w=[("big", k)])
                    for g in range(2):
                        gs = slice(g * 512, (g + 1) * 512)
                        TT("dve", hst[:, gs], tmph[:, gs], banks[4 + g][:, :], ALU.add, r=[("big", k), BK(4 + g)], w=["hst"])
                    CP("act", hst_bf, hst, r=["hst"], w=["hst_bf"])
                if end_hook is not None:
                    end_hook()

            NS = 2048 // SCH
            CPS = SCH // 128
            conv_blocks(0, range(12))
            front_decay_a(0, 128)
            front_decay_b(0, 128)
            for ci in range(16):
                S = ci // CPS
                cc = ci % CPS
                l0 = cc * 128
                xs_ = S % 2
                XT = [("xbcs", xs_, blk) for blk in range(12)]

                def h_early(ci=ci):
                    if ci + 1 < 16:
                        front_decay_a(ci + 1, 128)

                def h_mid(ci=ci):
                    if ci + 1 < 16:
                        front_decay_b(ci + 1, 128)

                def h_end(S=S, cc=cc):
                    if S + 1 < NS:
                        per = 12 // CPS
                        conv_blocks(S + 1, range(cc * per, (cc + 1) * per))

                ssd_chunk(128, ci,
                          lambda blk, l0=l0, xs_=xs_: xbcs[xs_][:, blk, l0:l0 + 128],
                          lambda g, l0=l0, xs_=xs_: xbcs[xs_][:, 8 + g, l0:l0 + 128],
                          lambda g, l0=l0, xs_=xs_: xbcs[xs_][:, 10 + g, l0:l0 + 128],
                          XT, S * SCH + l0, early_hook=h_early, mid_hook=h_mid, end_hook=h_end)
                if ci == 3:
                    dump("mo0", moT[:, :, 0:512], [("moT", i) for i in range(4)], BF16)
            hstv = hst.rearrange("p (a c) -> p a c", a=8)
            hout = big[0][:, 0:1024].rearrange("p (a c) -> p a c", a=8)
            for half in range(2):
                b = 4 + half
                for a in range(4):
                    blk = half * 4 + a
                    TR(banks[b][:, a * 128:(a + 1) * 128], hstv[:, blk, :], idf, r=["hst"] + CONST, w=[BK(b)])
                CP("dve", hout[:, half * 4:(half + 1) * 4, :], banks[b][:, :].rearrange("p (a c) -> p a c", a=4), r=[BK(b)], w=[("big", 0)])
            DMA("sp", nsp.rearrange("(blk q) n -> q blk n", q=128), hout, d_big, r=[("big", 0)])
            rawtm = big[1][:, 0:1536]
            for cb_ in range(3):
                b = 6 + (cb_ % 2)
                for kc in range(8):
                    MM(banks[b][:, :], hT[:, kc, 15 * 128:16 * 128], wx[:, kc, cb_ * 512:(cb_ + 1) * 512], start=(kc == 0), stop=(kc == 7),
                       r=["wx", ("hT", 15)], w=[BK(b)])
                CP("dve", rawtm[:, cb_ * 512:(cb_ + 1) * 512], banks[b][:, :], r=[BK(b)], w=[("big", 1)])
            DMA("sp", ncp, rawtm[125:128, :], d_big, r=[("big", 1)])

            CHECK("B2")
            P.barrier()
            d_s = P.dsem("sampB")
            d_h0 = [P.dsem("h0a"), P.dsem("h0b")]
            for i in range(3):
                DMA("sp", scv[i * 16:(i + 1) * 16, :], sconv[:, i, :], d_s, w=[("big", 0)])
            for half, (b, blks) in enumerate(((0, list(range(0, 8))), (1, list(range(8, 12))))):
                for a, blk in enumerate(blks):
                    TR(banks[b][:, a * 48:(a + 1) * 48], scv[0:48, blk * 128:(blk + 1) * 128], idf[0:48, 0:48], r=[("big", 0)] + CONST, w=[BK(b)])
                nblk = len(blks)
                CP("act", raws[:, blks[0]:blks[0] + nblk, 0:48], banks[b][:, 0:nblk * 48].rearrange("p (a c) -> p a c", a=nblk), r=[BK(b)], w=[("raws", half)])
            for half, (b, blks) in enumerate(((0, list(range(0, 8))), (1, list(range(8, 12))))):
                for a, blk in enumerate(blks):
                    for kc in range(8):
                        MM(banks[b][:, a * 64:(a + 1) * 64], wx[:, kc, blk * 128:(blk + 1) * 128], hT[:, kc, SC0:SC0 + 64], start=(kc == 0), stop=(kc == 7),
                           r=["wx", ("hT", 16)], w=[BK(b)])
                nblk = len(blks)
                CP("dve", raws[:, blks[0]:blks[0] + nblk, 48:112], banks[b][:, 0:nblk * 64].rearrange("p (a c) -> p a c", a=nblk), r=[BK(b)], w=[("raws2", half)])
            for blk in range(12):
                sl = blk % 2
                hf = 0 if blk < 8 else 1
                conv_block(lambda i, blk=blk: raws[:, blk, i * 16:i * 16 + 64], 64, blk, 2 + blk % 2, thc[sl][:, 0:64], xbss[:, blk, :],
                           [("raws", hf), ("raws2", hf)], ("xbss", blk))
            XTS = [("xbss", blk) for blk in range(12)]
            rawtm = big[1][:, 0:1536]
            for cb_ in range(3):
                b = 4 + (cb_ % 2)
                for kc in range(8):
                    MM(banks[b][0:64, :], hT[:, kc, SC0:SC0 + 64], wx[:, kc, cb_ * 512:(cb_ + 1) * 512], start=(kc == 0), stop=(kc == 7),
                       r=["wx", ("hT", 16)], w=[BK(b)])
                CP("dve", rawtm[0:64, cb_ * 512:(cb_ + 1) * 512], banks[b][0:64, :], r=[BK(b)], w=[("big", 1)])
            for l in range(1, 4):
                DMA("sp", ncs[:, l - 1, :], rawtm[l * 16:(l + 1) * 16, :], d_big, r=[("big", 1)])

            SEQM = cb[:, B_SEQM:B_SEQM + 1024].rearrange("p (s c) -> p s c", s=16)
            SEQINDB = cb[0:64, B_SEQINDB:B_SEQINDB + 16]

            def prefetch_A_weights(part):
                if part == 0:
                    PF["d_wA"] = P.dsem("wA")
                wq_p = abf[:, 2 * O_T:2 * O_T + 4096].rearrange("p (k g c) -> p k g c", k=8, g=4)
                wk_p = abf[:, 2 * (O_T + 2048):2 * (O_T + 2048) + 1024].rearrange("p (k c) -> p k c", k=8)
                wv_p = abf[:, 2 * (O_T + 2560):2 * (O_T + 2560) + 1024].rearrange("p (k c) -> p k c", k=8)
                wsrc_p = w_in[:, 0:512].rearrange("(kc p) (kv g d) -> p kc kv g d", p=128, kv=2, g=4)
                if part < 4:
                    g = part
                    for kv in range(2):
                        DMA("pool", wq_p[:, :, g, kv * 64:(kv + 1) * 64], wsrc_p[:, :, kv, g, :], PF["d_wA"], w=[("wq", g)])
                else:
                    DMA("pool", wk_p, w_in[:, 512:640].rearrange("(kc p) n -> p kc n", p=128), PF["d_wA"], w=["wk"])
                    DMA("pool", wv_p, w_in[:, 640:768].rearrange("(kc p) n -> p kc n", p=128), PF["d_wA"], w=["wv"])

            def seq_loop():
                MM(banks[3][0:16, 32:48], dA_all[0:64, 16, :], cf[0:64, C_SEQIND:C_SEQIND + 16], r=["dA"] + CONST, w=[BK(3)])
                CP("act", cdl[0:16, :], banks[3][0:16, 32:48], r=[BK(3)], w=["cdl"])
                selblk = cf[0:16, C_SELBLK:C_SELBLK + 1024].rearrange("p (a c) -> p a c", a=8)
                for blk in range(8):
                    MM(banks[3][:, 64 + blk * 16:64 + (blk + 1) * 16], selblk[:, blk, :], cdl[0:16, :], r=["cdl"] + CONST, w=[BK(3)])
                ACT(cdcol, banks[3][:, 64:192].rearrange("p (a s) -> p a s", a=8), AF.Exp, r=[BK(3)], w=["cdcol"])

                def load_h0(s_):
                    DMA("sp", h0[s_ % 2], sssm[s_].rearrange("(blk q) n -> q blk n", q=128), d_h0[s_ % 2], w=[("h0", s_ % 2)])

                load_h0(0)
                for s_ in range(16):
                    sl = s_ % 2
                    if s_ + 1 < 16:
                        load_h0(s_ + 1)
                    if 4 <= s_ <= 8:
                        prefetch_A_weights(s_ - 4)
                    TT("pool", CTm[sl], xbss[:, 10:12, :], SEQM[:, s_, :].unsqueeze(1).to_broadcast([128, 2, 64]), ALU.mult,
                       r=XTS + CONST, w=[("CTm", sl)])
                    TS("pool", Bm[sl][0:64].rearrange("p g n -> p (g n)"), B_tm[0:64].rearrange("p g n -> p (g n)"), SEQINDB[:, s_:s_ + 1], None, ALU.mult,
                       r=["B_tm"] + CONST, w=[("Bm", sl)])
                    for half in range(2):
                        b = half
                        for a in range(4):
                            TR(banks[b][:, a * 128:(a + 1) * 128], h0[sl][:, half * 4 + a, :], idf, r=[("h0", sl)] + CONST, w=[BK(b)])
                        CP("act", h0T[sl][:, half * 512:(half + 1) * 512], banks[b][:, :], r=[BK(b)], w=[("h0T", sl, half)])
                    for g in range(2):
                        MM(banks[4 + g][0:64, :], CTm[sl][:, g, :], h0T[sl][:, g * 512:(g + 1) * 512], start=(s_ == 0), stop=(s_ == 15),
                           r=[("CTm", sl), ("h0T", sl, g)], w=[BK(4 + g)])
                    for half in range(2):
                        b = 6 + half
                        for a in range(4):
                            blk = half * 4 + a
                            MM(banks[b][:, a * 128:(a + 1) * 128], xdd[0:64, blk * 128:(blk + 1) * 128], Bm[sl][0:64, blk // 4, :],
                               r=["xdd", ("Bm", sl)], w=[BK(b)])
                    for blk in range(8):
                        b = 6 + blk // 4
                        STT("dve", h0[sl][:, blk, :], h0[sl][:, blk, :], cdcol[:, blk, s_:s_ + 1], banks[b][:, (blk % 4) * 128:(blk % 4 + 1) * 128],
                            ALU.mult, ALU.add, r=[("h0", sl), "cdcol", BK(b)], w=[("h0", sl)])
                    DMA("sp", nss[s_].rearrange("(blk q) n -> q blk n", q=128), h0[sl], d_h0[sl], r=[("h0", sl)], w=[("h0", sl)])

            front_decay_a(16, 64, sample=True)
            front_decay_b(16, 64, sample=True)
            ssd_chunk(64, 16,
                      lambda blk: xbss[:, blk, :],
                      lambda g: xbss[:, 8 + g, :],
                      lambda g: xbss[:, 10 + g, :],
                      XTS, SC0, sample=True, seq_loop=seq_loop)
            dump("moT", moT[:, :, :], [("moT", i) for i in range(17)], BF16)

        MO_ALL = [("moT", i) for i in range(17)]

        def phase_C():
            A = Bump(arena, abf, [(R_T[0], C0_OFF)])
            wo_p = abf[:, 2 * WO_OFF:2 * WO_OFF + 8 * 1024].rearrange("p (k c) -> p k c", k=8)
            PF["d_wD"] = P.dsem("wD")
            wga0, wgb0, woa0, wob0 = c0_views()
            wga = [wga0, A.bf(8 * 256).rearrange("p (k c) -> p k c", k=8)]
            wgb = [wgb0, A.bf(8 * 256).rearrange("p (k c) -> p k c", k=8)]
            woa = [woa0, A.bf(4 * 256).rearrange("p (k c) -> p k c", k=4)]
            wob = [wob0, A.bf(8 * 256).rearrange("p (k c) -> p k c", k=8)]
            tA = [A.bf(512) for _ in range(2)]
            tB = [A.bf(512) for _ in range(2)]
            u = [A.f32(512) for _ in range(2)]
            v = [A.f32(512) for _ in range(2)]
            d_w = [PF["d_wC0"], P.dsem("wC1")]
            woa_src = w_oa.rearrange("(kv g d) n -> kv d g n", kv=2, g=4)
            def load_c(qi):
                ws = qi % 2
                cs = slice(qi * 256, (qi + 1) * 256)
                DMA("pool", wga[ws], w_in[:, 3344 + qi * 256:3344 + (qi + 1) * 256].rearrange("(kc p) n -> p kc n", p=128), d_w[ws], w=[("wC", ws)])
                DMA("pool", wgb[ws], w_in[:, 4368 + qi * 256:4368 + (qi + 1) * 256].rearrange("(kc p) n -> p kc n", p=128), d_w[ws], w=[("wC", ws)])
                for kv in range(2):
                    DMA("pool", woa[ws][kv * 64:(kv + 1) * 64, :, :], woa_src[kv][:, :, cs], d_w[ws], w=[("wC", ws)])
                DMA("pool", wob[ws], w_ob[:, cs].rearrange("(kc p) n -> p kc n", p=128), d_w[ws], w=[("wC", ws)])

            load_c(1)
            it = 0
            for qi in range(4):
                ws = qi % 2
                if qi >= 1 and qi + 1 < 4:
                    load_c(qi + 1)
                if qi == 3:
                    DMA("pool", wo_p, w_o.rearrange("(kc p) n -> p kc n", p=128), PF["d_wD"], w=["wo"])
                for blk in range(2):
                    dm = qi * 2 + blk
                    bs = slice(blk * 128, (blk + 1) * 128)
                    for T in range(5):
                        c0 = T * 512
                        n = 512 if T < 4 else 64
                        s = it % 2
                        it += 1
                        bgA, bpA, bgB, bpB = 4 * s, 4 * s + 1, 4 * s + 2, 4 * s + 3
                        htk = hT_toks(c0, n)
                        tk = list(range(c0 // 128, (c0 + n + 127) // 128))
                        for kc in range(8):
                            MM(banks[bgA][:, 0:n], wga[ws][:, kc, bs], hT[:, kc, c0:c0 + n], start=(kc == 0), stop=(kc == 7), r=[("wC", ws)] + htk, w=[BK(bgA)])
                        for g in range(4):
                            MM(banks[bpA][:, 0:n], woa[ws][:, g, bs], aoT[:, g, c0:c0 + n], start=(g == 0), stop=(g == 3),
                               r=[("wC", ws)] + [("aoT", t) for t in tk], w=[BK(bpA)])
                        for kc in range(8):
                            MM(banks[bgB][:, 0:n], wgb[ws][:, kc, bs], hT[:, kc, c0:c0 + n], start=(kc == 0), stop=(kc == 7), r=[("wC", ws)] + htk, w=[BK(bgB)])
                        for kc in range(8):
                            MM(banks[bpB][:, 0:n], wob[ws][:, kc, bs], moT[:, kc, c0:c0 + n], start=(kc == 0), stop=(kc == 7),
                               r=[("wC", ws)] + [("moT", t) for t in tk], w=[BK(bpB)])
                        ACT(tA[s][:, 0:n], banks[bgA][:, 0:n], AF.Tanh, r=[BK(bgA)], w=[("tA", s)], scale=0.5)
                        ACT(tB[s][:, 0:n], banks[bgB][:, 0:n], AF.Tanh, r=[BK(bgB)], w=[("tB", s)], scale=0.5)
                        STT("dve", u[s][:, 0:n], tA[s][:, 0:n], 1.0, banks[bpA][:, 0:n], ALU.add, ALU.mult, r=[("tA", s), BK(bpA)], w=[("u", s)])
                        STT("dve", v[s][:, 0:n], tB[s][:, 0:n], 1.0, banks[bpB][:, 0:n], ALU.add, ALU.mult, r=[("tB", s), BK(bpB)], w=[("v", s)])
                        TT("dve", mgT[:, dm, c0:c0 + n], u[s][:, 0:n], v[s][:, 0:n], ALU.add, r=[("u", s), ("v", s)], w=[("mgT", dm, T)])
            dump("mgT", mgT[:, :, :], [("mgT", dm, T) for dm in range(8) for T in range(5)], BF16)

        def mg_toks(t):
            return [("mgT", dm, t // 4) for dm in range(8)]

        def phase_D():
            A = Bump(arena, abf, [(R_T[0], E0_OFF)])
            wo = abf[:, 2 * WO_OFF:2 * WO_OFF + 8 * 1024].rearrange("p (k c) -> p k c", k=8)
            wup0 = abf[:, 2 * E0_OFF:2 * E0_OFF + 4096].rearrange("p (k c) -> p k c", k=8)
            wdn0 = abf[:, 2 * E0_OFF + 4096:2 * E0_OFF + 8192].rearrange("p (k c) -> p k c", k=4)
            PF["d_wE0"] = P.dsem("wE0")
            nw = A.f32(1024)
            xt = [A.f32(1024) for _ in range(2)]
            hb = [A.bf(1024) for _ in range(2)]
            junk = A.bf(1024)
            st_ = A.f32(16)
            d_x = [P.dsem("xD0"), P.dsem("xD1")]
            DMA("pool", wup0, w_up[:, 0:512].rearrange("(kc q) n -> q kc n", q=128), PF["d_wE0"], w=[("wE", 0)])
            DMA("pool", wdn0, w_down[0:512, :].rearrange("(f q) n -> q f n", q=128), PF["d_wE0"], w=[("wE", 0)])
            DMA("sp", nw, vecs[:, 1, :], P.dsem("nwD"), w=["nw2"])
            def stage1(t):
                s = t % 2
                rows = rows_of(t)
                DMA("sp", xt[s][0:rows], xin[t * 128:t * 128 + rows, :], d_x[s], w=[("xtD", s)])
                for half in range(2):
                    b = 2 * s + half
                    for kc in range(8):
                        MM(banks[b][0:rows, :], mgT[:, kc, t * 128:t * 128 + rows], wo[:, kc, half * 512:(half + 1) * 512], start=(kc == 0), stop=(kc == 7),
                           r=["wo"] + mg_toks(t), w=[BK(b)])
                    STT("dve", x1[0:rows, t, half * 512:(half + 1) * 512], banks[b][0:rows, :], 0.5, xt[s][0:rows, half * 512:(half + 1) * 512],
                        ALU.mult, ALU.add, r=[BK(b), ("xtD", s)], w=[("x1", t)])
                s3 = t % 3
                ss = st_[:, s3:s3 + 1]
                ms = st_[:, 4 + s3:5 + s3]
                rs = st_[:, 8 + s3:9 + s3]
                ACT(junk[0:rows], x1[0:rows, t, :], AF.Square, r=[("x1", t)], w=["junkD", ("ssD", s3)], accum_out=ss[0:rows])
                rstd_from_ss(ss[0:rows], ms[0:rows], rs[0:rows], 1.0 / D, EPS, [("ssD", s3)], ("msD", s3), ("rsD", s3))

            def stage2a(t):
                s = t % 2
                rows = rows_of(t)
                s3 = t % 3
                rs = st_[:, 8 + s3:9 + s3]
                STT("dve", hb[s][0:rows], x1[0:rows, t, :], rs[0:rows], nw[0:rows], ALU.mult, ALU.mult, r=[("x1", t), ("rsD", s3), "nw2"], w=[("hbD", s)])

            def stage2b(t):
                s = t % 2
                rows = rows_of(t)
                b = 4 + s
                pv = bankb[b][:, :].rearrange("p (a c) -> p a c", a=8)
                for kc in range(8):
                    TR(pv[:, kc, 0:rows], hb[s][0:rows, kc * 128:(kc + 1) * 128], idb[0:rows, 0:rows], r=[("hbD", s)] + CONST, w=[BK(b)])
                CP("act", hT[:, :, t * 128:t * 128 + rows], pv[:, :, 0:rows], r=[BK(b)], w=[("hmT", t)])

            for t in range(NTL + 2):
                if t >= 2:
                    stage2a(t - 2)
                if t < NTL:
                    stage1(t)
                if t >= 2:
                    stage2b(t - 2)
            dump("x1", x1[:, :, :], [("x1", t) for t in range(NTL)])

        def hm_toks(c0, n):
            return [("hmT", t) for t in range(c0 // 128, (c0 + n + 127) // 128)]

        def phase_E():
            A = Bump(arena, abf, [(R_T[0], E0_OFF), R_G])
            NP = 8
            wup = [abf[:, 2 * E0_OFF:2 * E0_OFF + 4096].rearrange("p (k c) -> p k c", k=8), A.bf(8 * 512).rearrange("p (k c) -> p k c", k=8)]
            wdn = [abf[:, 2 * E0_OFF + 4096:2 * E0_OFF + 8192].rearrange("p (k c) -> p k c", k=4), A.bf(4 * 1024).rearrange("p (k c) -> p k c", k=4)]
            rl = [A.bf(512) for _ in range(2)]
            upT = [A.bf(4 * 512).rearrange("p (k c) -> p k c", k=4) for _ in range(2)]
            nw = A.f32(1024)
            yt = [A.f32(1024) for _ in range(2)]
            junk = A.bf(1024)
            st_ = A.f32(16)
            d_w = [PF["d_wE0"], P.dsem("wE1")]
            d_n = P.dsem("nwE")
            d_y = [P.dsem("yE0"), P.dsem("yE1")]
            DMA("sp", nw, vecs[:, 2, :], d_n, w=["nw3"])
            def load_w(p):
                ws = p % 2
                DMA("pool", wup[ws], w_up[:, p * 512:(p + 1) * 512].rearrange("(kc q) n -> q kc n", q=128), d_w[ws], w=[("wE", ws)])
                DMA("pool", wdn[ws], w_down[p * 512:(p + 1) * 512, :].rearrange("(f q) n -> q f n", q=128), d_w[ws], w=[("wE", ws)])

            tiles = [(p, T) for p in range(NP) for T in range(5)]
            dn_ctr = [0]

            def emit_up(k):
                p, T = tiles[k]
                ws, us = p % 2, k % 2
                c0 = T * 512
                n = 512 if T < 4 else 64
                for ffc in range(4):
                    b = ffc % 2
                    for kc in range(8):
                        MM(banks[b][:, 0:n], wup[ws][:, kc, ffc * 128:(ffc + 1) * 128], hT[:, kc, c0:c0 + n], start=(kc == 0), stop=(kc == 7),
                           r=[("wE", ws)] + hm_toks(c0, n), w=[BK(b)])
                    ACT(rl[b][:, 0:n], banks[b][:, 0:n], AF.Relu, r=[BK(b)], w=[("rl", b)])
                    TT("dve", upT[us][:, ffc, 0:n], rl[b][:, 0:n], rl[b][:, 0:n], ALU.mult, r=[("rl", b)], w=[("upT", us, ffc)])

            def emit_down(k):
                p, T = tiles[k]
                ws, us = p % 2, k % 2
                c0 = T * 512
                n = 512 if T < 4 else 64
                for t in range(c0 // 128, (c0 + n + 127) // 128):
                    rows = rows_of(t)
                    j0 = t * 128 - c0
                    for half in range(2):
                        b = 2 + (dn_ctr[0] % 4)
                        dn_ctr[0] += 1
                        for ffc in range(4):
                            MM(banks[b][0:rows, :], upT[us][:, ffc, j0:j0 + rows], wdn[ws][:, ffc, half * 512:(half + 1) * 512], start=(ffc == 0), stop=(ffc == 3),
                               r=[("wE", ws), ("upT", us, ffc)], w=[BK(b)])
                        hs = slice(half * 512, (half + 1) * 512)
                        TT("dve", x1[0:rows, t, hs], x1[0:rows, t, hs], banks[b][0:rows, :], ALU.add, r=[("x1", t), BK(b)], w=[("x1", t)])
                    if p == NP - 1:
                        s = t % 2
                        ss = st_[:, s:s + 1]
                        ms = st_[:, 4 + s:5 + s]
                        rs = st_[:, 8 + s:9 + s]
                        ACT(junk[0:rows], x1[0:rows, t, :], AF.Square, r=[("x1", t)], w=["junkE", ("ssE", s)], accum_out=ss[0:rows])
                        rstd_from_ss(ss[0:rows], ms[0:rows], rs[0:rows], 1.0 / D, EPS, [("ssE", s)], ("msE", s), ("rsE", s))
                        STT("dve", yt[s][0:rows], x1[0:rows, t, :], rs[0:rows], nw[0:rows], ALU.mult, ALU.mult, r=[("x1", t), ("rsE", s), "nw3"], w=[("yt", s)])
                        DMA("sp", yout[t * 128:t * 128 + rows, :], yt[s][0:rows], d_y[s], r=[("yt", s)])
                if T == 4 and p + 2 < NP:
                    load_w(p + 2)

            load_w(1)
            emit_up(0)
            for k in range(len(tiles)):
                if k + 1 < len(tiles):
                    emit_up(k + 1)
                emit_down(k)

        phases = [("N", phase_N), ("B", phase_B), ("A", phase_A), ("C", phase_C), ("D", phase_D), ("E", phase_E)]
        for name, fn in phases:
            try:
                fn()
            except StopBuild:
                P.barrier()
                break
            P.barrier()
            if stop_after == name:
                break
        P.emit()
    return nc, dbg_outs


def _constants():
    f = np.zeros((128, NF), np.float32)
    i = np.arange(128)
    f[:, C_IDF:C_IDF + 128] = np.eye(128, dtype=np.float32)
    f[:, C_UI:C_UI + 128] = (i[:, None] <= i[None, :])
    f[:, C_SL:C_SL + 128] = (i[:, None] > i[None, :])
    f[:, C_ONES:C_ONES + 128] = 1.0
    tok = np.arange(64)
    l_, s_ = tok // 16, tok % 16
    same = s_[:, None] == s_[None, :]
    f[0:64, C_UIS:C_UIS + 64] = same & (l_[:, None] <= l_[None, :])
    f[0:64, C_SLS:C_SLS + 64] = same & (l_[:, None] > l_[None, :])
    f[0:64, C_SEQIND:C_SEQIND + 16] = (s_[:, None] == np.arange(16)[None, :])
    sel = np.zeros((16, 8, 128), np.float32)
    for h in range(16):
        sel[h, h // 2, (h % 2) * 64:(h % 2 + 1) * 64] = 1.0
    f[0:16, C_SELBLK:C_SELBLK + 1024] = sel.reshape(16, 1024)
    b = np.zeros((128, NB), np.float32)
    b[:, B_IDB:B_IDB + 128] = np.eye(128)
    b[:, B_MP:B_MP + 128] = (i[:, None] >= i[None, :])
    b[:, B_MC:B_MC + 128] = (i[:, None] <= i[None, :])
    b[:, B_MCACHE:B_MCACHE + 4] = (i[:, None] >= np.arange(4)[None, :])
    b[0:64, B_MNEW:B_MNEW + 64] = same & (l_[:, None] <= l_[None, :])
    b[:, B_UIB:B_UIB + 128] = (i[:, None] <= i[None, :])
    b[0:64, B_UISB:B_UISB + 64] = same & (l_[:, None] <= l_[None, :])
    op = np.zeros((128, 2, 128), np.float32)
    op[:, 0, 0:64] = 1.0
    op[:, 1, 64:128] = 1.0
    b[:, B_ONESPAD:B_ONESPAD + 256] = op.reshape(128, 256)
    seqm = (np.arange(16)[:, None] == s_[None, :]).astype(np.float32)
    b[:, B_SEQM:B_SEQM + 1024] = np.broadcast_to(seqm.reshape(1, 1024), (128, 1024))
    b[0:64, B_SEQINDB:B_SEQINDB + 16] = (s_[:, None] == np.arange(16)[None, :])
    return f, b.astype(ml_dtypes.bfloat16)


_CACHE = {}


def _get_program(stop_after=None, dbg=()):
    key = (stop_after, tuple(dbg))
    if key not in _CACHE:
        _CACHE[key] = build_program(stop_after, dbg)
    return _CACHE[key]


def make_in_maps(inp):
    f32 = np.float32
    g = lambda k: np.ascontiguousarray(np.asarray(inp[k], dtype=f32))
    x_prompt, x_sample = g("x_prompt"), g("x_sample")
    ck, cv = g("cache_swa_k")[0], g("cache_swa_v")[0]
    sconv, sssm = g("state_conv")[0], g("state_ssm")[0]
    cf, cbb = _constants()
    hv = np.stack([g("dt_bias")[0], g("a_log")[0], g("d_skip")[0]]).reshape(1, 48)
    cf[:, C_HV:C_HV + 48] = hv
    cw = g("conv_w")[0]
    cf[:, C_CONVW:C_CONVW + 48] = cw.reshape(4, 12, 128).transpose(2, 1, 0).reshape(128, 48)
    cf[:, C_CONVB:C_CONVB + 12] = g("conv_b")[0].reshape(12, 128).T
    sk = g("sinks")[0]
    cf[:, C_SINKS:C_SINKS + 4] = sk.reshape(2, 4)[np.arange(128) // 64]
    vecs = np.stack([g("norm1")[0], g("norm2")[0], g("final_norm"), g("ssm_norm")[0]])
    vecs = np.ascontiguousarray(np.broadcast_to(vecs[None], (128, 4, D)))
    shared = dict(w_in=g("w_in")[0], w_oa=g("w_oa")[0], w_ob=g("w_ob")[0], w_o=g("w_o")[0], w_up=g("w_up")[0],
                  w_down=g("w_down")[0], vecs=vecs, cstf=cf, cstb=cbb)
    maps = []
    for c in range(NCORES):
        ss = slice(16 * c, 16 * c + 16)
        xs = x_sample[ss].transpose(1, 0, 2).reshape(64, D)
        m = dict(shared)
        m["xin"] = np.ascontiguousarray(np.concatenate([x_prompt[c], xs], axis=0))
        m["ck"] = np.ascontiguousarray(ck[ss].reshape(16, 128, 128))
        m["cv"] = np.ascontiguousarray(cv[ss].reshape(16, 128, 128))
        m["sconv"] = np.ascontiguousarray(sconv[ss])
        m["sssm"] = np.ascontiguousarray(sssm[ss].reshape(16, 1024, 128))
        maps.append(m)
    return maps


def kernel(**inputs):
    nc, _ = _get_program()
    maps = make_in_maps(inputs)
    res = run_bass_kernel_spmd(nc, maps, core_ids=list(range(NCORES)))
    R = res.results
    f32 = np.float32
    y_prompt = np.stack([np.asarray(R[c]["yout"], f32)[0:2048] for c in range(NCORES)])
    y_sample = np.concatenate([np.asarray(R[c]["yout"], f32)[2048:].reshape(4, 16, D).transpose(1, 0, 2) for c in range(NCORES)], axis=0)
    nkp = np.stack([np.asarray(R[c]["nkp"], f32).reshape(128, 2, 64) for c in range(NCORES)])[None]
    nvp = np.stack([np.asarray(R[c]["nvp"], f32).reshape(128, 2, 64) for c in range(NCORES)])[None]
    ncp = np.stack([np.asarray(R[c]["ncp"], f32) for c in range(NCORES)])[None]
    nsp = np.stack([np.asarray(R[c]["nsp"], f32).reshape(16, 64, 128) for c in range(NCORES)])[None]
    nks = np.concatenate([np.asarray(R[c]["nks"], f32).reshape(16, 128, 2, 64) for c in range(NCORES)], axis=0)[None]
    nvs = np.concatenate([np.asarray(R[c]["nvs"], f32).reshape(16, 128, 2, 64) for c in range(NCORES)], axis=0)[None]
    ncs = np.concatenate([np.asarray(R[c]["ncs"], f32) for c in range(NCORES)], axis=0)[None]
    nss = np.concatenate([np.asarray(R[c]["nss"], f32).reshape(16, 16, 64, 128) for c in range(NCORES)], axis=0)[None]
    return (y_prompt, y_sample, nkp, nvp, ncp, nsp, nks, nvs, ncs, nss)
```
